# Optimizing a Trainium2 kernel written in Bass

```python
import math
import jax, jax.numpy as jnp
from jax import lax
import numpy as np

D_MODEL = 1024
BATCH = 16
SEQ = 4096
DEPTH = 2

CHUNK = 64
HEAD_DIM = 64
EPS = 1e-6
A_WIDTH = D_MODEL // 2
A_HEADS = A_WIDTH // HEAD_DIM
A_LEFT_CHUNKS = 8
A_BAND = (A_LEFT_CHUNKS + 1) * CHUNK
A_MAX_REL = 128
B_WIDTH = D_MODEL - A_WIDTH
B_GROUP = 16
B_GROUPS = B_WIDTH // B_GROUP
B_STATE = 64
DT_MIN = 1e-3
DT_MAX = 1e-1
C_WIDTH = D_MODEL // 2
C_HEADS = 4
C_HEAD_DIM = C_WIDTH // C_HEADS
ROPE_BASE = 10000.0
D_WIDTH = D_MODEL - C_WIDTH
D_HEADS = D_WIDTH // HEAD_DIM
SB_BLOCK = 128
FF_HIDDEN = ((-(-8 * D_MODEL // 3)) + 255) // 256 * 256
N_EVEN = (DEPTH + 1) // 2
N_ODD = DEPTH // 2
EVEN_IN = 3 * A_WIDTH + B_WIDTH
ODD_IN = 4 * C_WIDTH + 3 * D_WIDTH

kernel_name = "hybrid_streaming_encoder_block"


def rms_norm(x, g):
    xf = x.astype(jnp.float32)
    y = xf * lax.rsqrt(jnp.mean(xf * xf, axis=-1, keepdims=True) + EPS)
    return (y * g.astype(jnp.float32)).astype(x.dtype)


def modulate(h, shift, scale):
    return h * (1.0 + scale[:, None, :]) + shift[:, None, :]


def swiglu(h, w_in, w_out):
    gate, up = jnp.split(h @ w_in, 2, axis=-1)
    return (jax.nn.silu(gate) * up) @ w_out


def rotary(x, pos):
    half = x.shape[-1] // 2
    freqs = ROPE_BASE ** (-jnp.arange(half, dtype=jnp.float32) / half)
    ang = pos.astype(jnp.float32)[:, None] * freqs[None, :]
    cos = jnp.cos(ang)[None, :, None, :]
    sin = jnp.sin(ang)[None, :, None, :]
    x1, x2 = x[..., :half], x[..., half:]
    return jnp.concatenate([x1 * cos - x2 * sin, x1 * sin + x2 * cos], axis=-1)


def chunked_relpos_attention(q, k, v, q_gain, k_gain, rel_bias):
    bsz, L, H, dh = q.shape
    n_chunks = L // CHUNK
    pad = A_LEFT_CHUNKS * CHUNK
    q = rms_norm(q, q_gain)
    k = rms_norm(k, k_gain)
    kp = jnp.pad(k, ((0, 0), (pad, 0), (0, 0), (0, 0)))
    vp = jnp.pad(v, ((0, 0), (pad, 0), (0, 0), (0, 0)))
    qc = q.reshape(bsz, n_chunks, CHUNK, H, dh)
    rel = (pad + jnp.arange(CHUNK))[:, None] - jnp.arange(A_BAND)[None, :]
    idx = jnp.clip(rel, -A_MAX_REL, A_MAX_REL) + A_MAX_REL
    bias = rel_bias.astype(jnp.float32)[:, idx]
    scale = dh ** -0.5

    def one_chunk(ci):
        kb = lax.dynamic_slice_in_dim(kp, ci * CHUNK, A_BAND, axis=1)
        vb = lax.dynamic_slice_in_dim(vp, ci * CHUNK, A_BAND, axis=1)
        qb = lax.dynamic_index_in_dim(qc, ci, axis=1, keepdims=False)
        s = jnp.einsum('bqhd,bkhd->bhqk', qb, kb).astype(jnp.float32) * scale + bias
        valid = (ci * CHUNK - pad + jnp.arange(A_BAND)) >= 0
        s = jnp.where(valid[None, None, None, :], s, -1e30)
        p = jax.nn.softmax(s, axis=-1).astype(vb.dtype)
        return jnp.einsum('bhqk,bkhd->bqhd', p, vb)

    out = lax.map(one_chunk, jnp.arange(n_chunks))
    return out.transpose(1, 0, 2, 3, 4).reshape(bsz, L, H * dh)


def s5_mixer(u, lam_re, lam_im, log_dt, b_re, b_im, c_re, c_im, d_skip, w_glu, b_glu):
    f32 = jnp.float32
    bsz, L, _ = u.shape
    ug = u.astype(f32).reshape(bsz, L, B_GROUPS, B_GROUP)
    lam_re = lam_re.astype(f32)
    lam_im = lam_im.astype(f32)
    b_re = b_re.astype(f32)
    b_im = b_im.astype(f32)
    dt = jnp.exp(log_dt.astype(f32))[:, None]
    mag = jnp.exp(lam_re * dt)
    ang = lam_im * dt
    abar_re = mag * jnp.cos(ang)
    abar_im = mag * jnp.sin(ang)
    den = lam_re * lam_re + lam_im * lam_im
    f_re = ((abar_re - 1.0) * lam_re + abar_im * lam_im) / den
    f_im = (abar_im * lam_re - (abar_re - 1.0) * lam_im) / den
    bbar_re = f_re[..., None] * b_re - f_im[..., None] * b_im
    bbar_im = f_re[..., None] * b_im + f_im[..., None] * b_re
    bu_re = jnp.einsum('blgh,gph->lbgp', ug, bbar_re)
    bu_im = jnp.einsum('blgh,gph->lbgp', ug, bbar_im)
    a_re = jnp.broadcast_to(abar_re[None, None], (L, 1, B_GROUPS, B_STATE))
    a_im = jnp.broadcast_to(abar_im[None, None], (L, 1, B_GROUPS, B_STATE))

    def combine(e1, e2):
        a1r, a1i, b1r, b1i = e1
        a2r, a2i, b2r, b2i = e2
        ar = a1r * a2r - a1i * a2i
        ai = a1r * a2i + a1i * a2r
        br = a2r * b1r - a2i * b1i + b2r
        bi = a2r * b1i + a2i * b1r + b2i
        return ar, ai, br, bi

    _, _, xr, xi = lax.associative_scan(combine, (a_re, a_im, bu_re, bu_im), axis=0)
    y = (jnp.einsum('lbgp,ghp->blgh', xr, c_re.astype(f32))
         - jnp.einsum('lbgp,ghp->blgh', xi, c_im.astype(f32))
         + d_skip.astype(f32) * ug)
    y = jax.nn.gelu(y.reshape(bsz, L, B_WIDTH))
    gate = jax.nn.sigmoid(y @ w_glu.astype(f32) + b_glu.astype(f32))
    return (y * gate).astype(u.dtype)


def retention_mixer(q, k, v, g, norm_g):
    f32 = jnp.float32
    bsz, L, _ = q.shape
    n_chunks = L // CHUNK
    pos = jnp.arange(L)
    shp = (bsz, L, C_HEADS, C_HEAD_DIM)
    q = rotary(q.astype(f32).reshape(shp), pos)
    k = rotary(k.astype(f32).reshape(shp), pos) * (C_HEAD_DIM ** -0.5)
    v = v.astype(f32).reshape(shp)
    log_gamma = jnp.log(1.0 - 2.0 ** (-5.0 - jnp.arange(C_HEADS, dtype=f32)))
    idx = jnp.arange(CHUNK, dtype=f32)
    intra_decay = jnp.exp(log_gamma[:, None, None] * jnp.abs(idx[:, None] - idx[None, :]))
    q_decay = jnp.exp(log_gamma[:, None] * (idx + 1.0))
    k_decay = jnp.exp(log_gamma[:, None] * (CHUNK - 1.0 - idx))
    chunk_decay = jnp.exp(log_gamma * CHUNK)

    def to_chunks(t):
        return t.reshape(bsz, n_chunks, CHUNK, C_HEADS, C_HEAD_DIM).transpose(1, 0, 3, 2, 4)

    def step(state, inp):
        qn, kn, vn = inp
        scores = jnp.einsum('bhid,bhjd->bhij', qn, kn) * intra_decay
        out = (jnp.einsum('bhij,bhjd->bhid', scores, vn)
               + jnp.einsum('bhid,bhde->bhie', qn * q_decay[:, :, None], state))
        state = (state * chunk_decay[:, None, None]
                 + jnp.einsum('bhjd,bhje->bhde', kn * k_decay[:, :, None], vn))
        return state, out

    state0 = jnp.zeros((bsz, C_HEADS, C_HEAD_DIM, C_HEAD_DIM), f32)
    _, o = lax.scan(step, state0, (to_chunks(q), to_chunks(k), to_chunks(v)))
    o = o.transpose(1, 0, 3, 2, 4).reshape(shp)
    mu = jnp.mean(o, axis=-1, keepdims=True)
    var = jnp.mean(jnp.square(o - mu), axis=-1, keepdims=True)
    o = ((o - mu) * lax.rsqrt(var + EPS)).reshape(bsz, L, C_WIDTH) * norm_g.astype(f32)
    return (jax.nn.silu(g.astype(f32)) * o).astype(g.dtype)


def stick_breaking_mixer(q, k, v):
    bsz, L, _ = q.shape
    shp = (bsz, L, D_HEADS, HEAD_DIM)
    q, k, v = q.reshape(shp), k.reshape(shp), v.reshape(shp)
    scale = HEAD_DIM ** -0.5
    outs = []
    for blk in range(L // SB_BLOCK):
        t0 = blk * SB_BLOCK
        t1 = t0 + SB_BLOCK
        z = jnp.einsum('bqhd,bkhd->bhqk', q[:, t0:t1], k[:, :t1]).astype(jnp.float32) * scale
        causal = jnp.arange(t1)[None, :] < (t0 + jnp.arange(SB_BLOCK))[:, None]
        log_1m_beta = jnp.where(causal, -jax.nn.softplus(z), 0.0)
        between = lax.cumsum(log_1m_beta, axis=3, reverse=True) - log_1m_beta
        w = jnp.where(causal, jnp.exp(jax.nn.log_sigmoid(z) + between), 0.0)
        outs.append(jnp.einsum('bhqk,bkhd->bqhd', w.astype(v.dtype), v[:, :t1]))
    return jnp.concatenate(outs, axis=1).reshape(bsz, L, D_WIDTH)


def even_mixer(h, w_in, w_out, q_gain, k_gain, rel_bias, lam_re, lam_im, log_dt,
               b_re, b_im, c_re, c_im, d_skip, w_glu, b_glu):
    bsz, L, _ = h.shape
    q, k, v, u = jnp.split(h @ w_in, [A_WIDTH, 2 * A_WIDTH, 3 * A_WIDTH], axis=-1)
    shp = (bsz, L, A_HEADS, HEAD_DIM)
    a_out = chunked_relpos_attention(q.reshape(shp), k.reshape(shp), v.reshape(shp),
                                     q_gain, k_gain, rel_bias)
    b_out = s5_mixer(u, lam_re, lam_im, log_dt, b_re, b_im, c_re, c_im, d_skip, w_glu, b_glu)
    return jnp.concatenate([a_out, b_out], axis=-1) @ w_out


def odd_mixer(h, w_in, w_out, norm_g):
    offs = [C_WIDTH, 2 * C_WIDTH, 3 * C_WIDTH, 4 * C_WIDTH,
            4 * C_WIDTH + D_WIDTH, 4 * C_WIDTH + 2 * D_WIDTH]
    cq, ck, cv, cg, dq, dk, dv = jnp.split(h @ w_in, offs, axis=-1)
    c_out = retention_mixer(cq, ck, cv, cg, norm_g)
    d_out = stick_breaking_mixer(dq, dk, dv)
    return jnp.concatenate([c_out, d_out], axis=-1) @ w_out


def setup_inputs(seed: int = 0) -> dict:
    key = jax.random.key(seed)
    ks = jax.random.split(key, 26)
    f32 = jnp.float32
    D = D_MODEL

    def nrm(k, shape, s):
        return jax.random.normal(k, shape, f32) * s

    return {
        "x": nrm(ks[0], (BATCH, SEQ, D), 1.0),
        "c": nrm(ks[1], (BATCH, D), 1.0),
        "ada_w": nrm(ks[2], (DEPTH, D, 6 * D), 0.5 * D ** -0.5),
        "ada_b": nrm(ks[3], (DEPTH, 6 * D), 0.01),
        "ln_mix_g": 1.0 + nrm(ks[4], (DEPTH, D), 0.01),
        "ln_ffn_g": 1.0 + nrm(ks[5], (DEPTH, D), 0.01),
        "ffn_w_in": nrm(ks[6], (DEPTH, D, 2 * FF_HIDDEN), D ** -0.5),
        "ffn_w_out": nrm(ks[7], (DEPTH, FF_HIDDEN, D), FF_HIDDEN ** -0.5),
        "ab_w_in": nrm(ks[8], (N_EVEN, D, EVEN_IN), D ** -0.5),
        "ab_w_out": nrm(ks[9], (N_EVEN, A_WIDTH + B_WIDTH, D), (A_WIDTH + B_WIDTH) ** -0.5),
        "a_q_gain": 1.0 + nrm(ks[10], (N_EVEN, HEAD_DIM), 0.01),
        "a_k_gain": 1.0 + nrm(ks[11], (N_EVEN, HEAD_DIM), 0.01),
        "a_rel_bias": nrm(ks[12], (N_EVEN, A_HEADS, 2 * A_MAX_REL + 1), 0.5),
        "s5_lambda_re": -0.5 + nrm(ks[13], (N_EVEN, B_GROUPS, B_STATE), 0.01),
        "s5_lambda_im": math.pi * jnp.arange(B_STATE, dtype=f32) + nrm(ks[14], (N_EVEN, B_GROUPS, B_STATE), 0.01),
        "s5_log_dt": jax.random.uniform(ks[15], (N_EVEN, B_GROUPS), f32, math.log(DT_MIN), math.log(DT_MAX)),
        "s5_b_re": nrm(ks[16], (N_EVEN, B_GROUPS, B_STATE, B_GROUP), (2 * B_GROUP) ** -0.5),
        "s5_b_im": nrm(ks[17], (N_EVEN, B_GROUPS, B_STATE, B_GROUP), (2 * B_GROUP) ** -0.5),
        "s5_c_re": nrm(ks[18], (N_EVEN, B_GROUPS, B_GROUP, B_STATE), 2.0 * B_STATE ** -0.5),
        "s5_c_im": nrm(ks[19], (N_EVEN, B_GROUPS, B_GROUP, B_STATE), 2.0 * B_STATE ** -0.5),
        "s5_d": nrm(ks[20], (N_EVEN, B_GROUPS, B_GROUP), 0.5),
        "s5_w_glu": nrm(ks[21], (N_EVEN, B_WIDTH, B_WIDTH), B_WIDTH ** -0.5),
        "s5_b_glu": nrm(ks[22], (N_EVEN, B_WIDTH), 0.01),
        "cd_w_in": nrm(ks[23], (N_ODD, D, ODD_IN), D ** -0.5),
        "cd_w_out": nrm(ks[24], (N_ODD, C_WIDTH + D_WIDTH, D), (C_WIDTH + D_WIDTH) ** -0.5),
        "ret_norm_g": 1.0 + nrm(ks[25], (N_ODD, C_WIDTH), 0.01),
    }


def reference(x, c, ada_w, ada_b, ln_mix_g, ln_ffn_g, ffn_w_in, ffn_w_out,
              ab_w_in, ab_w_out, a_q_gain, a_k_gain, a_rel_bias,
              s5_lambda_re, s5_lambda_im, s5_log_dt, s5_b_re, s5_b_im, s5_c_re, s5_c_im,
              s5_d, s5_w_glu, s5_b_glu, cd_w_in, cd_w_out, ret_norm_g):
    cond = jax.nn.silu(c)
    for layer in range(DEPTH):
        mod = cond @ ada_w[layer] + ada_b[layer]
        sh1, sc1, g1, sh2, sc2, g2 = jnp.split(mod, 6, axis=-1)
        h = modulate(rms_norm(x, ln_mix_g[layer]), sh1, sc1)
        i = layer // 2
        if layer % 2 == 0:
            mix = even_mixer(h, ab_w_in[i], ab_w_out[i], a_q_gain[i], a_k_gain[i], a_rel_bias[i],
                             s5_lambda_re[i], s5_lambda_im[i], s5_log_dt[i], s5_b_re[i], s5_b_im[i],
                             s5_c_re[i], s5_c_im[i], s5_d[i], s5_w_glu[i], s5_b_glu[i])
        else:
            mix = odd_mixer(h, cd_w_in[i], cd_w_out[i], ret_norm_g[i])
        x = x + g1[:, None, :] * mix
        h = modulate(rms_norm(x, ln_ffn_g[layer]), sh2, sc2)
        x = x + g2[:, None, :] * swiglu(h, ffn_w_in[layer], ffn_w_out[layer])
    return x
```

```python
import contextlib
from contextlib import ExitStack
import numpy as np
import concourse.bass as bass
import concourse.mybir as mybir
from concourse.bass_utils import run_bass_kernel_spmd

F32 = mybir.dt.float32
BF16 = mybir.dt.bfloat16
I32 = mybir.dt.int32
AF = mybir.ActivationFunctionType
ALU = mybir.AluOpType
AX = mybir.AxisListType

ENGS = ['pe', 'act', 'dve', 'pool', 'sp']
NDS = 6


class Sched:
    def __init__(self, nc, stack, same_engine_sync=True):
        self.nc = nc
        self.ops = {e: [] for e in ENGS}
        self.cnt = {e: 0 for e in ENGS}
        self.seen = {e: {} for e in ENGS}
        self.lastw = {}
        self.readers = {}
        self.same = same_engine_sync
        self.csem = {e: stack.enter_context(nc.semaphore("c_" + e)) for e in ['pe', 'act', 'dve', 'pool']}
        self.dsem = {q: [stack.enter_context(nc.semaphore("d_%s%d" % (q, i))) for i in range(NDS)]
                     for q in ['sp', 'pool', 'act']}
        self.dma_n = {q: 0 for q in ['sp', 'pool', 'act']}
        self.out_tokens = []

    def _deps(self, reads, writes):
        deps = []
        for k in reads:
            if k in self.lastw:
                deps.append(self.lastw[k])
        for k in writes:
            if k in self.lastw:
                deps.append(self.lastw[k])
            deps.extend(self.readers.get(k, []))
        return deps

    def _waits(self, eng, deps):
        need = {}
        for (semkey, sem, val, deng) in deps:
            if deng == eng and semkey[0] == 'c' and (eng == 'pe' or not self.same):
                continue
            if self.seen[eng].get(semkey, 0) >= val:
                continue
            if semkey not in need or need[semkey][1] < val:
                need[semkey] = (sem, val)
        for semkey, (sem, val) in need.items():
            self.seen[eng][semkey] = val
        return list(need.values())

    def _record(self, tok, reads, writes):
        for k in reads:
            self.readers.setdefault(k, []).append(tok)
        for k in writes:
            self.lastw[k] = tok
            self.readers[k] = []

    def op(self, eng, fn, reads=(), writes=()):
        deps = self._deps(reads, writes)
        waits = self._waits(eng, deps)
        self.cnt[eng] += 1
        tok = (('c', eng), self.csem[eng], self.cnt[eng], eng)
        self.ops[eng].append((waits, fn, (self.csem[eng], 1)))
        self._record(tok, reads, writes)
        return tok

    def dma(self, q, fn, reads=(), writes=(), is_output=False):
        deps = self._deps(reads, writes)
        n = self.dma_n[q]
        self.dma_n[q] += 1
        slot = n % NDS
        sem = self.dsem[q][slot]
        semkey = ('d', q, slot)
        prev = 16 * (n // NDS)
        if prev > 0:
            deps.append((semkey, sem, prev, 'dma'))
        waits = self._waits(q, deps)
        tok = (semkey, sem, prev + 16, 'dma')
        self.ops[q].append((waits, fn, (sem, 16)))
        self._record(tok, reads, writes)
        if is_output:
            self.out_tokens.append(tok)
        return tok

    def flush(self, block):
        toks = []
        for e in ['pe', 'act', 'dve', 'pool']:
            if self.cnt[e] > 0:
                toks.append((('c', e), self.csem[e], self.cnt[e], 'x'))
        for q in ['sp', 'pool', 'act']:
            n = self.dma_n[q]
            for j in range(max(0, n - NDS), n):
                slot = j % NDS
                toks.append((('d', q, slot), self.dsem[q][slot], 16 * (j // NDS + 1), 'dma'))
        for e in ENGS:
            self.ops[e].append((self._waits(e, toks), None, None))
        self.lastw = {}
        self.readers = {}

        def run(eng_name):
            lst = self.ops[eng_name]

            def body(e):
                for (waits, fn, inc) in lst:
                    for (sem, val) in waits:
                        e.wait_ge(sem, val)
                    if fn is None:
                        continue
                    ins = fn(e)
                    ins.then_inc(inc[0], inc[1])
            return body

        block.tensor(run('pe'))
        block.scalar(run('act'))
        block.vector(run('dve'))
        block.gpsimd(run('pool'))
        block.sync(run('sp'))
        self.ops = {e: [] for e in ENGS}


EPS = 1e-6
S_LEN = 4096
NSEQ = 2
T_CORE = NSEQ * S_LEN
D = 1024
FF = 2816


def apx(ap, extra):
    return bass.AP(tensor=ap.tensor, offset=ap.offset, ap=[list(a) for a in ap.ap] + [list(e) for e in extra])


class Ctx:
    def __init__(self, nc, debug_out=()):
        self.nc = nc
        self.st = ExitStack()
        self.S = Sched(nc, self.st)
        self.dr = {}
        self.debug_out = set(debug_out)
        self.uid = 0

    def dram_in(self, name, shape, dt=F32):
        self.dr[name] = self.nc.dram_tensor(name, list(shape), dt, kind="ExternalInput").ap()
        return self.dr[name]

    def dram_out(self, name, shape, dt=F32):
        self.dr[name] = self.nc.dram_tensor(name, list(shape), dt, kind="ExternalOutput").ap()
        return self.dr[name]

    def dram_scr(self, name, shape, dt):
        kind = "ExternalOutput" if name in self.debug_out else "Internal"
        self.dr[name] = self.nc.dram_tensor(name, list(shape), dt, kind=kind).ap()
        return self.dr[name]

    def flush(self):
        with self.nc.Block() as block:
            self.S.flush(block)


class Phase:
    def __init__(self, cx, name):
        self.cx = cx
        self.nc = cx.nc
        self.S = cx.S
        self.name = name
        self.st = ExitStack()

    def sb(self, name, shape, dt):
        return self.st.enter_context(self.nc.sbuf_tensor(self.name + "_" + name, list(shape), dt))

    def ps(self, name, shape, dt=F32):
        return self.st.enter_context(self.nc.psum_tensor(self.name + "_" + name, list(shape), dt))

    def close(self):
        self.cx.flush()
        self.st.close()


def mm(S, out, lhsT, rhs, start, stop, r, w):
    S.op('pe', lambda e: e.matmul(out, lhsT, rhs, start=start, stop=stop), r, w)


def tr(S, out, in_, ident, r, w):
    S.op('pe', lambda e: e.transpose(out, in_, ident), r, w)


def act(S, out, in_, func, r, w, scale=1.0, bias=None, accum_out=None, eng='act'):
    kw = {}
    if bias is not None:
        kw['bias'] = bias
    if accum_out is not None:
        kw['accum_out'] = accum_out
    S.op('act', lambda e: e.activation(out=out, in_=in_, func=func, scale=scale, **kw), r, w)


def ts(S, eng, out, in0, s1, s2, op0, op1, r, w, accum_out=None):
    if op1 is None:
        S.op(eng, lambda e: e.tensor_scalar(out, in0, s1, None, op0), r, w)
    elif accum_out is not None:
        S.op(eng, lambda e: e.tensor_scalar(out, in0, s1, s2, op0, op1, accum_out), r, w)
    else:
        S.op(eng, lambda e: e.tensor_scalar(out, in0, s1, s2, op0, op1), r, w)


def tt(S, eng, out, in0, in1, op, r, w):
    S.op(eng, lambda e: e.tensor_tensor(out, in0, in1, op), r, w)


def stt(S, out, in0, scalar, in1, op0, op1, r, w):
    S.op('dve', lambda e: e.scalar_tensor_tensor(out, in0, scalar, in1, op0, op1), r, w)


def cp(S, eng, out, in_, r, w):
    if eng == 'act':
        S.op('act', lambda e: e.copy(out, in_), r, w)
    else:
        S.op(eng, lambda e: e.tensor_copy(out, in_), r, w)


def memset(S, eng, ap, val, w):
    S.op(eng, lambda e: e.memset(ap, val), (), w)


def dma(S, q, out, in_, r, w, is_output=False, slow=False):
    if slow:
        S.dma(q, lambda e: e.dma_start(out=out, in_=in_, allow_slow_non_contiguous=True), r, w, is_output)
    else:
        S.dma(q, lambda e: e.dma_start(out=out, in_=in_), r, w, is_output)


def load_w(S, ph, name, w_dram, K, N, q='pool', nsplit=None):
    kc = K // 128
    t = ph.sb(name, [128, kc, N], BF16)
    src = w_dram.rearrange("(c p) n -> p c n", p=128)
    for c in range(kc):
        dma(S, q, t[:, c, :], src[:, c, :], (), [(name, c)])
    return t


def phase_adaln(cx):
    nc, S, dr = cx.nc, cx.S, cx.dr
    ph = Phase(cx, "p0")
    cT = ph.sb("cT", [128, 8, 2], F32)
    condT = ph.sb("condT", [128, 8, 2], BF16)
    condbc = ph.sb("condbc", [128, 8, 2, 128], BF16)
    aw = ph.sb("aw", [128, 8, 6144], BF16)
    abT = ph.sb("abT", [128, 48], F32)
    lng = ph.sb("lng", [128, 2, 8], F32)
    abbc = ph.sb("abbc", [128, 2, 1024], F32)
    modsb = ph.sb("modsb", [128, 4, 8, 2], F32)
    tmp = ph.sb("tmp", [128, 8, 2], F32)
    gsb = [ph.sb("gsb%d" % i, [128, 1024], F32) for i in range(2)]
    pm = ph.ps("pm", [128, 32, 2], F32)
    pg = [ph.ps("pg%d" % i, [128, 512], F32) for i in range(2)]

    dma(S, 'sp', cT[:], dr["cT"], (), ["cT"])
    act(S, condT[:], cT[:], AF.Silu, ["cT"], ["condT"])
    cp(S, 'dve', condbc[:], apx(condT[:], [[0, 128]]), ["condT"], ["condbc"])
    gi = 0
    for l in range(2):
        src = dr["ada_w"][l].rearrange("(c p) n -> p c n", p=128)
        for c in range(8):
            dma(S, 'pool', aw[:, c, :], src[:, c, :], (), [("aw", c)])
        dma(S, 'sp', abT[:], dr["ada_bT"][l], (), ["abT"])
        dma(S, 'sp', lng[:, 0, :], dr["ln_gT"][0, l], (), ["lng"])
        dma(S, 'sp', lng[:, 1, :], dr["ln_gT"][1, l], (), ["lng"])
        for wi, blk in enumerate((2, 5)):
            dma(S, 'sp', abbc[:, wi, :], dr["ada_b"][l:l + 1, blk * 1024:(blk + 1) * 1024].partition_broadcast(128)
                if False else apx_pb(dr["ada_b"][l, blk * 1024:(blk + 1) * 1024]), (), ["abbc"])
        for jj, blk in enumerate((0, 1, 3, 4)):
            for fc in range(8):
                col = blk * 1024 + fc * 128
                for k in range(8):
                    mm(S, pm[:, jj * 8 + fc, :], aw[:, k, col:col + 128], condT[:, k, :], k == 0, k == 7,
                       [("aw", k), "condT"], ["pm"])
        for jj, blk in enumerate((0, 1, 3, 4)):
            bias = apx(abT[:, blk * 8:(blk + 1) * 8], [[0, 2]])
            if jj in (0, 2):
                tt(S, 'dve', modsb[:, jj + 1], pm[:, jj * 8:(jj + 1) * 8, :], bias, ALU.add, ["pm", "abT"], ["modsb"])
            else:
                tt(S, 'dve', tmp[:], pm[:, jj * 8:(jj + 1) * 8, :], bias, ALU.add, ["pm", "abT"], ["tmp"])
                stt(S, modsb[:, jj - 1], tmp[:], 1.0, apx(lng[:, jj // 2, :], [[0, 2]]), ALU.add, ALU.mult,
                    ["tmp", "lng"], ["modsb"])
        dma(S, 'sp', dr["modfm"][l], modsb[:], ["modsb"], [("modfm", l)])
        for b in range(2):
            for wi, blk in enumerate((2, 5)):
                g = gsb[gi % 2]
                gk = ("gsb", gi % 2)
                for half in range(2):
                    p = pg[half]
                    col = blk * 1024 + half * 512
                    for k in range(8):
                        mm(S, p[:], condbc[:, k, b, :], aw[:, k, col:col + 512], k == 0, k == 7,
                           [("aw", k), "condbc"], [("pg", half)])
                    tt(S, 'dve', g[:, half * 512:(half + 1) * 512], p[:], abbc[:, wi, half * 512:(half + 1) * 512],
                       ALU.add, [("pg", half), "abbc"], [gk])
                dma(S, 'sp', dr["gbc"][l, b, wi], g[:], [gk], [("gbc", l, b, wi)])
                gi += 1
    ph.close()


def apx_pb(ap1d):
    return bass.AP(tensor=ap1d.tensor, offset=ap1d.offset, ap=[[0, 128]] + [list(a) for a in ap1d.ap])


class NormT:
    def __init__(self, ph, nsub, ident):
        self.ph, self.S, self.nsub, self.ident = ph, ph.S, nsub, ident
        self.junk = ph.sb("nt_junk", [128, 1024], BF16)
        self.ss = [ph.sb("nt_ss%d" % i, [128, nsub], F32) for i in range(2)]
        self.rstd = [ph.sb("nt_rstd%d" % i, [128, nsub], F32) for i in range(2)]
        self.mhalf = ph.sb("nt_mhalf", [128, nsub], F32)
        self.xn = ph.sb("nt_xn", [128, nsub, 1024], BF16)
        self.tp = [ph.ps("nt_tp%d" % i, [128, nsub * 128], BF16) for i in range(2)]
        memset(self.S, 'pool', self.mhalf[:], -0.5, ["nt_mhalf"])
        self.n = 0

    def run(self, xt, xkey, hT, hkey, A, B, b):
        self.run_a(xt, xkey)
        self.run_b(hT, hkey, A, B, b)

    def run_a(self, xt, xkey):
        S, nsub = self.S, self.nsub
        par = self.n % 2
        self.n += 1
        ss, rstd = self.ss[par], self.rstd[par]
        for s in range(nsub):
            act(S, self.junk[:], xt[:, s, :], AF.Square, [xkey], ["nt_junk", ("nt_ss", par, s)], accum_out=ss[:, s:s + 1])
        ts(S, 'dve', rstd[:], ss[:], 1.0 / 1024, EPS, ALU.mult, ALU.add, [("nt_ss", par, s) for s in range(nsub)], [("nt_rstd", par)])
        tt(S, 'pool', rstd[:], rstd[:], self.mhalf[:], ALU.pow, [("nt_rstd", par), "nt_mhalf"], [("nt_rstd", par)])
        for s in range(nsub):
            ts(S, 'dve' if s % 2 == 0 else 'pool', self.xn[:, s, :], xt[:, s, :], rstd[:, s:s + 1], None, ALU.mult, None,
               [xkey, ("nt_rstd", par)], [("nt_xn", s)])

    def run_b(self, hT, hkey, A, B, b):
        S, nsub = self.S, self.nsub
        for c in range(8):
            tp = self.tp[c % 2]
            for s in range(nsub):
                tr(S, tp[:, s * 128:(s + 1) * 128], self.xn[:, s, c * 128:(c + 1) * 128], self.ident[:],
                   [("nt_xn", s), "ident"], [("nt_tp", c % 2)])
            if c % 2 == 0:
                ts(S, 'dve', hT[:, c, :], tp[:], A[:, c, b:b + 1], B[:, c, b:b + 1], ALU.mult, ALU.add,
                   [("nt_tp", c % 2), "modAB"], [(hkey, c)])
            else:
                act(S, hT[:, c, :], tp[:], AF.Identity, [("nt_tp", c % 2), "modAB"], [(hkey, c)],
                    scale=A[:, c, b:b + 1], bias=B[:, c, b:b + 1])


def load_consts(ph, S, dr):
    ident = ph.sb("ident", [128, 128], BF16)
    dma(S, 'pool', ident[:], dr["ident"], (), ["ident"])
    return ident


def phase_p1_l0(cx, ntiles=16):
    nc, S, dr = cx.nc, cx.S, cx.dr
    ph = Phase(cx, "p1a")
    ident = load_consts(ph, S, dr)
    bones = ph.sb("bones", [128, 128], BF16)
    dma(S, 'pool', bones[:], dr["blockones"], (), ["bones"])
    W = load_w(S, ph, "W", dr["ab_w_in"], 1024, 2048)
    wkeys = [("W", c) for c in range(8)]
    modAB = ph.sb("modAB", [128, 4, 8, 2], F32)
    dma(S, 'sp', modAB[:], dr["modfm"][0], [("modfm", 0)], ["modAB"])
    qkg = ph.sb("qkg", [128, 2], F32)
    dma(S, 'sp', qkg[:], dr["qkg"], (), ["qkg"])
    cb = ph.sb("cbias", [128, 2], F32)
    memset(S, 'pool', cb[:, 0:1], 64 * EPS, ["cbias"])
    memset(S, 'pool', cb[:, 1:2], EPS, ["cbias"])
    nt = NormT(ph, 4, ident)
    xt = [ph.sb("xt%d" % i, [128, 4, 1024], F32) for i in range(2)]
    hT = [ph.sb("hT%d" % i, [128, 8, 512], BF16) for i in range(2)]
    qkst = [ph.sb("qkst%d" % i, [128, 8, 512], BF16) for i in range(2)]
    vust = [ph.sb("vust%d" % i, [128, 4, 1024], BF16) for i in range(2)]
    sqk = [ph.sb("sqk%d" % i, [128, 512], BF16) for i in range(3)]
    rs = [ph.sb("rs%d" % i, [128, 512], F32) for i in range(3)]
    pq = [ph.ps("pq%d" % i, [128, 512], F32) for i in range(3)]
    pss = [ph.ps("pss%d" % i, [128, 512], F32) for i in range(1)]
    pv = [ph.ps("pv%d" % i, [128, 512], F32) for i in range(2)]
    qkT_d = dr["qkT"].rearrange("c p t -> p c t")
    def pre_a1(ti):
        par = ti % 2
        t0 = ti * 512
        dma(S, 'sp', xt[par][:], dr["x"][t0:t0 + 512, :].rearrange("(s p) d -> p s d", p=128), (), [("xt", par)])

    def pre_a2(ti):
        par = ti % 2
        nt.run_a(xt[par], ("xt", par))

    def pre_b(ti):
        par = ti % 2
        nt.run_b(hT[par], ("hT", par), modAB[:, 0], modAB[:, 1], ti // 8)

    pre_a1(0)
    pre_a2(0)
    pre_b(0)
    for ti in range(ntiles):
        par = ti % 2
        b = ti // 8
        t0 = ti * 512
        if ti + 1 < ntiles:
            pre_a1(ti + 1)
        hkeys = [(("hT", par), c) for c in range(8)]
        def qk_tail(oc):
            i3 = oc % 3
            p = pq[i3]
            pk = ("pq", i3)
            isk = 1 if oc >= 4 else 0
            sk = ("sqk", i3)
            mm(S, pss[0][:], bones[:], sqk[i3][:], True, True, [sk, "bones"], ["pss"])
            rk = ("rs", i3)
            act(S, rs[i3][:], pss[0][:], AF.Sqrt, ["pss", "cbias"], [rk],
                scale=(1.0 / 64 if isk else 1.0), bias=cb[:, isk:isk + 1])
            S.op('dve', (lambda o: (lambda e: e.reciprocal(o, o)))(rs[i3][:]), [rk], [rk])
            stt(S, qkst[par][:, oc, :], p[:], qkg[:, isk:isk + 1], rs[i3][:], ALU.mult, ALU.mult,
                [pk, rk, "qkg"], [("qkst", par)])

        for oc in range(8):
            i3 = oc % 3
            p = pq[i3]
            pk = ("pq", i3)
            for k in range(8):
                mm(S, p[:], W[:, k, oc * 128:(oc + 1) * 128], hT[par][:, k, :], k == 0, k == 7,
                   [wkeys[k], hkeys[k]], [pk])
            act(S, sqk[i3][:], p[:], AF.Square, [pk], [("sqk", i3)])
            if oc >= 1:
                qk_tail(oc - 1)
            if oc == 3 and ti + 1 < ntiles:
                pre_a2(ti + 1)
        tails_left = [7]
        for vi, (col, dname) in enumerate(((1024, "v0"), (1536, "u0"))):
            for s in range(4):
                p = pv[s % 2]
                pk = ("pv", s % 2)
                for k in range(8):
                    mm(S, p[:], hT[par][:, k, s * 128:(s + 1) * 128], W[:, k, col:col + 512], k == 0, k == 7,
                       [wkeys[k], hkeys[k]], [pk])
                if tails_left:
                    qk_tail(tails_left.pop(0))
                    if not tails_left:
                        dma(S, 'sp', qkT_d[:, :, t0:t0 + 512], qkst[par][:], [("qkst", par)], [("qkT", ti)])
                cp(S, 'act' if s % 2 == 0 else 'dve', vust[par][:, s, vi * 512:(vi + 1) * 512], p[:], [pk], [("vust", par, vi)])
            dma(S, 'sp', dr[dname][t0:t0 + 512, :].rearrange("(s p) d -> p s d", p=128),
                vust[par][:, :, vi * 512:(vi + 1) * 512], [("vust", par, vi)], [(dname, ti)])
        if ti + 1 < ntiles:
            pre_b(ti + 1)
    ph.close()


def phase_p3(cx, l, cat_name, x_name, out_name, wo_name, ntiles=32, final=False):
    nc, S, dr = cx.nc, cx.S, cx.dr
    ph = Phase(cx, "p3_%d" % l)
    ident = load_consts(ph, S, dr)
    Wo = load_w(S, ph, "Wo", dr[wo_name], 1024, 1024)
    Win = load_w(S, ph, "Win", dr["ffn_w_in"][l], 1024, 2 * FF)
    Wout = load_w(S, ph, "Wout", dr["ffn_w_out"][l], FF, 1024)
    modAB = ph.sb("modAB", [128, 4, 8, 2], F32)
    dma(S, 'sp', modAB[:], dr["modfm"][l], [("modfm", l)], ["modAB"])
    gb = ph.sb("gb", [128, 2, 1024], F32)
    nt = NormT(ph, 2, ident)
    xt2 = [ph.sb("xt%d" % i, [128, 2, 1024], F32) for i in range(2)]
    ct = [ph.sb("ct%d" % i, [128, 8, 256], BF16) for i in range(2)]
    hT = ph.sb("hT", [128, 8, 256], BF16)
    hact = ph.sb("hact", [128, 22, 256], BF16)
    sil = [ph.sb("sil%d" % i, [128, 256], F32) for i in range(2)]
    tmp = ph.sb("tmp", [128, 1024], F32)
    po = ph.ps("po", [128, 1024], F32)
    pw = ph.ps("pw", [128, 1024], F32)
    pgu = [ph.ps("pgu%d" % i, [128, 2, 256], F32) for i in range(2)]
    cat_d = dr[cat_name].rearrange("(c p) t -> p c t", p=128)
    hkeys = [("hT", c) for c in range(8)]

    def load(ti):
        par = ti % 2
        b = ti // 16
        t0 = ti * 256
        if ti % 16 == 0:
            for wi in range(2):
                dma(S, 'sp', gb[:, wi, :], dr["gbc"][l, b, wi], [("gbc", l, b, wi)], ["gb"])
        dma(S, 'sp', xt2[par][:], dr[x_name][t0:t0 + 256, :].rearrange("(s p) d -> p s d", p=128), [(x_name, ti)], [("xt", par)])
        dma(S, 'sp', ct[par][:], cat_d[:, :, t0:t0 + 256], [(cat_name, ti)], [("ct", par)])

    def outproj(ti):
        par = ti % 2
        xt = xt2[par]
        for s in range(2):
            for half in range(2):
                for k in range(8):
                    mm(S, po[:, half * 512:(half + 1) * 512], ct[par][:, k, s * 128:(s + 1) * 128],
                       Wo[:, k, half * 512:(half + 1) * 512], k == 0, k == 7, [("Wo", k), ("ct", par)], [("po", half)])
            tt(S, 'dve', tmp[:], po[:], gb[:, 0, :], ALU.mult, [("po", 0), ("po", 1), "gb"], ["tmp"])
            tt(S, 'pool', xt[:, s, :], xt[:, s, :], tmp[:], ALU.add, ["tmp", ("xt", par)], [("xt", par)])

    def ffn_in(ti):
        for j in range(22):
            gu = pgu[j % 2]
            gk = ("pgu", j % 2)
            for hh in range(2):
                col = hh * FF + j * 128
                for k in range(8):
                    mm(S, gu[:, hh, :], Win[:, k, col:col + 128], hT[:, k, :], k == 0, k == 7,
                       [("Win", k), hkeys[k]], [gk])
            sk = ("sil", j % 2)
            act(S, sil[j % 2][:], gu[:, 0, :], AF.Silu, [gk], [sk])
            tt(S, 'dve', hact[:, j, :], gu[:, 1, :], sil[j % 2][:], ALU.mult, [gk, sk], [("hact", j)])

    def ffn_out(ti):
        par = ti % 2
        xt = xt2[par]
        t0 = ti * 256
        for s in range(2):
            for half in range(2):
                for j in range(22):
                    mm(S, pw[:, half * 512:(half + 1) * 512], hact[:, j, s * 128:(s + 1) * 128],
                       Wout[:, j, half * 512:(half + 1) * 512], j == 0, j == 21, [("Wout", j), ("hact", j)], [("pw", half)])
            tt(S, 'dve', tmp[:], pw[:], gb[:, 1, :], ALU.mult, [("pw", 0), ("pw", 1), "gb"], ["tmp"])
            tt(S, 'pool', xt[:, s, :], xt[:, s, :], tmp[:], ALU.add, ["tmp", ("xt", par)], [("xt", par)])
        dma(S, 'sp', dr[out_name][t0:t0 + 256, :].rearrange("(s p) d -> p s d", p=128), xt[:], [("xt", par)], [(out_name, ti)],
            is_output=final)

    load(0)
    outproj(0)
    nt.run_a(xt2[0], ("xt", 0))
    nt.run_b(hT, "hT", modAB[:, 2], modAB[:, 3], 0)
    for ti in range(ntiles):
        nxt = ti + 1 < ntiles
        if nxt and (ti + 1) % 16 != 0:
            load(ti + 1)
        ffn_in(ti)
        if nxt and (ti + 1) % 16 != 0:
            outproj(ti + 1)
            nt.run_a(xt2[(ti + 1) % 2], ("xt", (ti + 1) % 2))
        ffn_out(ti)
        if nxt:
            if (ti + 1) % 16 == 0:
                load(ti + 1)
                outproj(ti + 1)
                nt.run_a(xt2[(ti + 1) % 2], ("xt", (ti + 1) % 2))
            nt.run_b(hT, "hT", modAB[:, 2], modAB[:, 3], (ti + 1) // 16)
    ph.close()


def phase_attn(cx, nseq=NSEQ, nqb=32):
    nc, S, dr = cx.nc, cx.S, cx.dr
    ph = Phase(cx, "pa")
    ident = load_consts(ph, S, dr)
    NEG = -30000.0
    Er = ph.sb("Er", [8, 257], F32)
    E = ph.sb("E", [8, 1024], F32)
    c256 = ph.sb("c256", [128, 8], F32)
    dma(S, 'sp', Er[:], dr["rel_bias"], (), ["Er"])
    rb = dr["rel_bias"]
    dma(S, 'sp', c256[:], bass.AP(tensor=rb.tensor, offset=rb.offset + 256, ap=[[0, 128], [257, 8]]), (), ["c256"], slow=True)
    memset(S, 'dve', E[:], 0.0, ["E"])
    ts(S, 'dve', E[:, 0:767], E[:, 0:767], Er[:, 256:257], None, ALU.add, None, ["E", "Er"], ["E"])
    cp(S, 'dve', E[:, 767:1024], Er[:, ::-1], ["E", "Er"], ["E"])
    dma(S, 'sp', dr["relext"], E[:], ["E"], ["relext"])
    BT = ph.sb("BT", [128, 5, 8, 128], F32)
    memset(S, 'pool', BT[:], 0.0, ["BT"])
    ext = dr["relext"]
    for j in range(3):
        tt(S, 'pool', BT[:, j, :, :], BT[:, j, :, :], apx(c256[:, :], [[0, 128]]), ALU.add, ["BT", "c256"], ["BT"])
    for j in (3, 4):
        for h in range(8):
            src = bass.AP(tensor=ext.tensor, offset=ext.offset + h * 1024 + 1023 - (5 - j) * 128 - 127, ap=[[1, 128], [1, 128]])
            dma(S, 'sp', BT[:, j, h, :], src, ["relext", "BT"], ["BT"])
    memset(S, 'pool', BT[0:64, 0, :, 0:64], NEG, ["BT"])
    memset(S, 'pool', BT[64:128, 4, :, 64:128], NEG, ["BT"])
    qT = ph.sb("qT", [128, 4, S_LEN], BF16)
    kT = ph.sb("kT", [128, 4, S_LEN], BF16)
    Vr = ph.sb("Vr", [128, 32, 512], BF16)
    Va = ph.sb("Va", [128, 32, 8, 65], BF16)
    memset(S, 'pool', Va[:, :, :, 64:65], 1.0, ["Va1"])
    NBA = 3
    sbf = [ph.sb("sbf%d" % i, [128, 5, 128], F32) for i in range(NBA)]
    pT = [ph.sb("pT%d" % i, [128, 5, 128], BF16) for i in range(NBA)]
    rc = ph.sb("rc", [128, 8], F32)
    ao = ph.sb("ao", [128, 512], BF16)
    aT = [ph.sb("aT%d" % i, [128, 4, 512], BF16) for i in range(2)]
    ps = [ph.ps("ps%d" % i, [128, 8, 128], F32) for i in range(2)]
    po = [ph.ps("po%d" % i, [128, 4, 65], F32) for i in range(2)]
    tp = ph.ps("tp", [128, 4, 128], BF16)
    qk_d = dr["qkT"].rearrange("c p t -> p c t")
    cat_d = dr["catT0"].rearrange("(c p) t -> p c t", p=128)
    hn = 0
    for b in range(nseq):
        tb = b * S_LEN
        for c in range(4):
            dma(S, 'sp', qT[:, c, :], qk_d[:, c, tb:tb + S_LEN], [("qkT", i) for i in range(b * 8, b * 8 + 8)], ["qT"])
            dma(S, 'sp', kT[:, c, :], qk_d[:, 4 + c, tb:tb + S_LEN], [("qkT", i) for i in range(b * 8, b * 8 + 8)], ["kT"])
        for c in range(4):
            dma(S, 'sp', Vr[:, c * 8:(c + 1) * 8, :],
                dr["v0"][tb + c * 1024: tb + (c + 1) * 1024, :].rearrange("(s p) d -> p s d", p=128),
                [("v0", i) for i in range(b * 8, b * 8 + 8)], ["Vr"])
        for c in range(4):
            cp(S, 'pool', Va[:, c * 8:(c + 1) * 8, :, 0:64], Vr[:, c * 8:(c + 1) * 8, :].rearrange("p s (h d) -> p s h d", h=8),
               ["Vr"], ["Va"])
        units = [(qb, h) for qb in range(nqb) for h in range(8)]
        bufi = {}

        def st1(u):
            nonlocal hn
            qb, h = u
            kb0 = max(0, qb - 4)
            nkb = qb - kb0 + 1
            j0 = 5 - nkb
            pr, base = h // 2, 64 * (h % 2)
            par = hn % NBA
            p_s, pk = ps[hn % 2], ("ps", hn % 2)
            hn += 1
            bufi[u] = par
            for j in range(nkb):
                kb = kb0 + j
                mm(S, p_s[:, j, :], kT[base:base + 64, pr, kb * 128:(kb + 1) * 128],
                   qT[base:base + 64, pr, qb * 128:(qb + 1) * 128], True, True, ["kT", "qT"], [pk])
            tt(S, 'dve', sbf[par][:, 0:nkb, :], p_s[:, 0:nkb, :], BT[:, j0:5, h, ::-1], ALU.add, [pk, "BT"], [("sbf", par)])
            act(S, pT[par][:, 0:nkb, :], sbf[par][:, 0:nkb, :], AF.Exp, [("sbf", par)], [("pT", par)])

        def st2(u):
            qb, h = u
            kb0 = max(0, qb - 4)
            nkb = qb - kb0 + 1
            par = bufi.pop(u)
            for j in range(nkb):
                kb = kb0 + j
                mm(S, po[h // 4][:, h % 4, :], pT[par][:, j, :], Va[:, kb, h, :], j == 0, j == nkb - 1,
                   [("pT", par), "Va", "Va1"], [("po", h // 4)])
            if h == 7:
                epi(qb)

        def epi(qb):
            for g in range(2):
                S.op('dve', (lambda o, i: (lambda e: e.reciprocal(o, i)))(rc[:, g * 4:(g + 1) * 4], po[g][:, :, 64]),
                     [("po", g)], [("rc", g)])
                tt(S, 'dve', ao[:, g * 256:(g + 1) * 256].rearrange("p (h d) -> p h d", h=4), po[g][:, :, 0:64],
                   apx(rc[:, g * 4:(g + 1) * 4], [[0, 64]]), ALU.mult, [("po", g), ("rc", g)], [("ao", g)])
            for c in range(4):
                tr(S, tp[:, c, :], ao[:, c * 128:(c + 1) * 128], ident[:], [("ao", c // 2), "ident"], ["tp"])
            apar = (qb // 4) % 2
            cp(S, 'act', aT[apar][:, :, (qb % 4) * 128:(qb % 4 + 1) * 128], tp[:], ["tp"], [("aT", apar)])
            if qb % 4 == 3 or qb == nqb - 1:
                q0 = (qb // 4) * 4
                n = (qb - q0 + 1) * 128
                dma(S, 'sp', cat_d[:, 0:4, tb + q0 * 128: tb + q0 * 128 + n], aT[apar][:, :, 0:n], [("aT", apar)],
                    [("catT0", "a", b, qb // 4)])

        SK = 2
        for i in range(len(units) + SK):
            if i < len(units):
                st1(units[i])
            if i - SK >= 0:
                st2(units[i - SK])
    ph.close()


def phase_p1_l1(cx, x_name, ntiles=16):
    nc, S, dr = cx.nc, cx.S, cx.dr
    ph = Phase(cx, "p1b")
    ident = load_consts(ph, S, dr)
    W = load_w(S, ph, "W", dr["cd_w_in"], 1024, 3584)
    wkeys = [("W", c) for c in range(8)]
    modAB = ph.sb("modAB", [128, 4, 8, 2], F32)
    dma(S, 'sp', modAB[:], dr["modfm"][1], [("modfm", 1)], ["modAB"])
    QD = ph.sb("QD", [128, 4], F32)
    KD = ph.sb("KD", [128, 4], F32)
    dma(S, 'sp', QD[:], dr["ret_qd"], (), ["QD"])
    dma(S, 'sp', KD[:], dr["ret_kd"], (), ["KD"])
    nt = NormT(ph, 4, ident)
    xt = [ph.sb("xt%d" % i, [128, 4, 1024], F32) for i in range(2)]
    hT2 = [ph.sb("hT%d" % i, [128, 8, 512], BF16) for i in range(2)]
    rot = [ph.sb("rot%d" % i, [128, 4, 2, 64], F32) for i in range(2)]
    t12 = [ph.sb("t12_%d" % i, [128, 4, 64], F32) for i in range(4)]
    R = [ph.sb("R%d" % i, [128, 4, 2, 64], F32) for i in range(2)]
    qkb = ph.sb("qkb", [128, 3, 512], BF16)
    fst = [ph.sb("fst%d" % i, [128, 12, 512], BF16) for i in range(2)]
    tst = [ph.sb("tst%d" % i, [128, 4, 3, 512], BF16) for i in range(2)]
    dst = [ph.sb("dst%d" % i, [128, 8, 512], BF16) for i in range(2)]
    dvs = [ph.sb("dvs%d" % i, [128, 4, 512], BF16) for i in range(2)]
    NPT = 4
    pt = [ph.ps("pt%d" % i, [128, 512], F32) for i in range(NPT)]
    ptr2 = [ph.ps("ptr%d" % i, [128, 4, 128], BF16) for i in range(2)]
    cf_d = dr["c_fm"].rearrange("k p t -> p k t")
    ct_d = dr["c_tok"]
    dq_d = dr["d_qkT"].rearrange("c p t -> p c t")
    pn = 0
    def pre_a(ti):
        par = ti % 2
        t0 = ti * 512
        pos0 = t0 % S_LEN
        xk = ("xt", par)
        dma(S, 'sp', xt[par][:], dr[x_name][t0:t0 + 512, :].rearrange("(s p) d -> p s d", p=128), [(x_name, 2 * ti), (x_name, 2 * ti + 1)], [xk])
        dma(S, 'sp', rot[par][:], dr["rot"][pos0:pos0 + 512].rearrange("(s p) a f -> p s a f", p=128), (), [("rot", par)])

    def pre_a2(ti):
        par = ti % 2
        nt.run_a(xt[par], ("xt", par))

    def pre_b(ti):
        par = ti % 2
        nt.run_b(hT2[par], ("hT", par), modAB[:, 0], modAB[:, 1], ti // 8)

    pre_a(0)
    pre_a2(0)
    pre_b(0)
    trn = 0
    for ti in range(ntiles):
        par = ti % 2
        b = ti // 8
        t0 = ti * 512
        if ti + 1 < ntiles:
            pre_a(ti + 1)
        hT = hT2[par]
        hkeys = [(("hT", par), c) for c in range(8)]

        def tokmm(s, col):
            nonlocal pn
            p = pt[pn % NPT]
            pk = ("pt", pn % NPT)
            pn += 1
            for k in range(8):
                mm(S, p[:], hT[:, k, s * 128:(s + 1) * 128], W[:, k, col:col + 512], k == 0, k == 7, [wkeys[k], hkeys[k]], [pk])
            return p, pk

        def fm_chunk(oc):
            nonlocal pn
            p = pt[pn % NPT]
            pk = ("pt", pn % NPT)
            pn += 1
            col = 2048 + oc * 128
            for k in range(8):
                mm(S, p[:], W[:, k, col:col + 128], hT[:, k, :], k == 0, k == 7, [wkeys[k], hkeys[k]], [pk])
            cp(S, 'act' if oc % 2 == 0 else 'dve', dst[par][:, oc, :], p[:], [pk], [("dst", par)])

        for s in range(4):
            cosv = rot[par][:, s, 0, :]
            sinv = rot[par][:, s, 1, :]
            cos4 = bass.AP(tensor=cosv.tensor, offset=cosv.offset, ap=[list(cosv.ap[0]), [0, 4], list(cosv.ap[1])])
            sin4 = bass.AP(tensor=sinv.tensor, offset=sinv.offset, ap=[list(sinv.ap[0]), [0, 4], list(sinv.ap[1])])
            for qi, col in enumerate((0, 512)):
                p, pk = tokmm(s, col)
                pv4 = p[:].rearrange("p (h a f) -> p h a f", h=4, a=2)
                x1, x2 = pv4[:, :, 0, :], pv4[:, :, 1, :]
                rk = ("R", qi)
                tt(S, 'dve', t12[0][:], x1, cos4, ALU.mult, [pk, ("rot", par)], ["t0"])
                tt(S, 'dve', t12[1][:], x2, sin4, ALU.mult, [pk, ("rot", par)], ["t1"])
                tt(S, 'dve', t12[2][:], x1, sin4, ALU.mult, [pk, ("rot", par)], ["t2"])
                tt(S, 'dve', t12[3][:], x2, cos4, ALU.mult, [pk, ("rot", par)], ["t3"])
                tt(S, 'pool', R[qi][:, :, 0, :], t12[0][:], t12[1][:], ALU.subtract, ["t0", "t1"], [rk])
                tt(S, 'pool', R[qi][:, :, 1, :], t12[2][:], t12[3][:], ALU.add, ["t2", "t3"], [rk])
            Rq = R[0][:].rearrange("p h a f -> p h (a f)")
            Rk = R[1][:].rearrange("p h a f -> p h (a f)")
            q3 = qkb[:].rearrange("p k (h d) -> p k h d", h=4)
            cp(S, 'pool', q3[:, 0], Rq, [("R", 0)], [("qkb", 0)])
            tt(S, 'pool', q3[:, 1], Rq, apx(QD[:, :], [[0, 128]]), ALU.mult, [("R", 0), "QD"], [("qkb", 1)])
            ts(S, 'pool', q3[:, 2], Rk, 128.0 ** -0.5, None, ALU.mult, None, [("R", 1)], [("qkb", 2)])
            tt(S, 'pool', tst[par][:, s, 0, :].rearrange("p (h d) -> p h d", h=4), Rk, apx(KD[:, :], [[0, 128]]), ALU.mult,
               [("R", 1), "KD"], [("tst", par)])
            p, pk = tokmm(s, 1024)
            cp(S, 'act', tst[par][:, s, 1, :], p[:], [pk], [("tst", par)])
            p, pk = tokmm(s, 1536)
            act(S, tst[par][:, s, 2, :], p[:], AF.Silu, [pk], [("tst", par)])
            p, pk = tokmm(s, 3072)
            cp(S, 'act', dvs[par][:, s, :], p[:], [pk], [("dvs", par)])
            for oc in (2 * s, 2 * s + 1):
                fm_chunk(oc)
            for kind in range(3):
                ptr = ptr2[trn % 2]
                ptk = ("ptr", trn % 2)
                for h in range(4):
                    tr(S, ptr[:, h, :], qkb[:, kind, h * 128:(h + 1) * 128], ident[:], [("qkb", kind), "ident"], [ptk])
                cp(S, 'act' if trn % 2 == 0 else 'dve', fst[par][:, kind * 4:(kind + 1) * 4, s * 128:(s + 1) * 128], ptr[:],
                   [ptk], [("fst", par)])
                trn += 1
            if s == 1 and ti + 1 < ntiles:
                pre_a2(ti + 1)
        dma(S, 'sp', cf_d[:, :, t0:t0 + 512], fst[par][:], [("fst", par)], [("c_fm", ti)])
        dma(S, 'sp', ct_d[t0:t0 + 512].rearrange("(s p) k d -> p s k d", p=128), tst[par][:], [("tst", par)], [("c_tok", ti)])
        dma(S, 'sp', dq_d[:, :, t0:t0 + 512], dst[par][:], [("dst", par)], [("d_qkT", ti)])
        dma(S, 'sp', dr["d_v"][t0:t0 + 512, :].rearrange("(s p) d -> p s d", p=128), dvs[par][:], [("dvs", par)], [("d_v", ti)])
        if ti + 1 < ntiles:
            pre_b(ti + 1)
    ph.close()


def phase_sb(cx, nseq=NSEQ, nblk=32):
    nc, S, dr = cx.nc, cx.S, cx.dr
    ph = Phase(cx, "pd")
    ident = load_consts(ph, S, dr)
    mask = ph.sb("mask", [128, 256], F32)
    dma(S, 'sp', mask[:], dr["sb_mask"], (), ["mask"])
    ones = ph.sb("ones", [128, 256], F32)
    memset(S, 'pool', ones[:], 1.0, ["ones"])
    one1 = ph.sb("one1", [128, 1], F32)
    memset(S, 'pool', one1[:], 1.0, ["one1"])
    qT = ph.sb("qT", [128, 4, S_LEN], BF16)
    kT = ph.sb("kT", [128, 4, S_LEN], BF16)
    V = ph.sb("V", [128, 32, 512], BF16)
    ex = [ph.sb("ex%d" % i, [128, 256], F32) for i in range(8)]
    sp = [ph.sb("sp%d" % i, [128, 256], F32) for i in range(8)]
    Rc = [ph.sb("Rc%d" % i, [128, 256], F32) for i in range(8)]
    lw = [ph.sb("lw%d" % i, [128, 256], F32) for i in range(8)]
    wm = [ph.sb("wm%d" % i, [128, 256], BF16) for i in range(8)]
    wT = [ph.sb("wT%d" % i, [128, 2, 128], BF16) for i in range(8)]
    do_b = ph.sb("do_b", [128, 512], BF16)
    dT = [ph.sb("dT%d" % i, [128, 4, 512], BF16) for i in range(2)]
    pz = [ph.ps("pz%d" % i, [128, 256], F32) for i in range(4)]
    pwt = [ph.ps("pwt%d" % i, [128, 2, 128], BF16) for i in range(2)]
    po = ph.ps("po", [128, 8, 64], F32)
    ptp = ph.ps("ptp", [128, 4, 128], BF16)
    qk_d = dr["d_qkT"].rearrange("c p t -> p c t")
    cat_d = dr["catT1"].rearrange("(c p) t -> p c t", p=128)
    hn = 0
    for b in range(nseq):
        tb = b * S_LEN
        rk = [("d_qkT", i) for i in range(b * 8, b * 8 + 8)]
        for c in range(4):
            dma(S, 'sp', qT[:, c, :], qk_d[:, c, tb:tb + S_LEN], rk, ["qT"])
            dma(S, 'sp', kT[:, c, :], qk_d[:, 4 + c, tb:tb + S_LEN], rk, ["kT"])
        for c in range(4):
            dma(S, 'sp', V[:, c * 8:(c + 1) * 8, :],
                dr["d_v"][tb + c * 1024: tb + (c + 1) * 1024, :].rearrange("(s p) d -> p s d", p=128),
                [("d_v", i) for i in range(b * 8, b * 8 + 8)], ["V"])
        units = [(blk, h) for blk in range(nblk) for h in range(8)]
        bufi = {}

        def geom(blk):
            nk = 1 if blk == 0 else 2
            return nk, nk * 128, (blk + 1 - nk) * 128, 256 - nk * 128

        def stA(u):
            nonlocal hn
            blk, h = u
            nk, W_, k0, m0 = geom(blk)
            pr, base = h // 2, 64 * (h % 2)
            par = hn % 8
            z, zk = pz[hn % 4], ("pz", hn % 4)
            bufi[u] = (par, hn % 2)
            hn += 1
            mm(S, z[:, 0:W_], qT[base:base + 64, pr, blk * 128:(blk + 1) * 128], kT[base:base + 64, pr, k0:k0 + W_],
               True, True, ["qT", "kT"], [zk])
            act(S, ex[par][:, 0:W_], z[:, 0:W_], AF.Exp, [zk], [("ex", par)], scale=0.125)
            act(S, sp[par][:, 0:W_], ex[par][:, 0:W_], AF.Ln, [("ex", par), "one1"], [("sp", par)], bias=one1[:, 0:1])
            tt(S, 'pool', sp[par][:, 0:W_], sp[par][:, 0:W_], mask[:, m0:256], ALU.mult, [("sp", par), "mask"], [("sp", par)])
            S.op('dve', (lambda o, d0, d1: (lambda e: e.tensor_tensor_scan(o, d0, d1, 0.0, ALU.mult, ALU.add)))(
                Rc[par][:, 0:W_][:, ::-1], ones[:, 0:W_], sp[par][:, 0:W_][:, ::-1]), [("sp", par), "ones"], [("Rc", par)])
            stt(S, lw[par][:, 0:W_], z[:, 0:W_], 0.125, Rc[par][:, 0:W_], ALU.mult, ALU.subtract, [zk, ("Rc", par)], [("lw", par)])
            act(S, lw[par][:, 0:W_], lw[par][:, 0:W_], AF.Exp, [("lw", par)], [("lw", par)])
            tt(S, 'pool', wm[par][:, 0:W_], lw[par][:, 0:W_], mask[:, m0:256], ALU.mult, [("lw", par), "mask"], [("wm", par)])

        def stB(u):
            blk, h = u
            nk, W_, k0, m0 = geom(blk)
            par, p2 = bufi[u]
            pw2, pwk = pwt[p2], ("pwt", p2)
            for n in range(nk):
                tr(S, pw2[:, n, :], wm[par][:, n * 128:(n + 1) * 128], ident[:], [("wm", par), "ident"], [pwk])
            cp(S, 'act', wT[par][:, 0:nk, :], pw2[:, 0:nk, :], [pwk], [("wT", par)])

        def stC(u):
            blk, h = u
            nk, W_, k0, m0 = geom(blk)
            par, p2 = bufi.pop(u)
            for n in range(nk):
                kb = blk + 1 - nk + n
                mm(S, po[:, h, :], wT[par][:, n, :], V[:, kb, h * 64:(h + 1) * 64], n == 0, n == nk - 1,
                   [("wT", par), "V"], ["po"])
            if h == 7:
                epi(blk)

        def epi(blk):
            cp(S, 'dve', do_b[:], po[:].rearrange("p h d -> p (h d)"), ["po"], ["do_b"])
            for c in range(4):
                tr(S, ptp[:, c, :], do_b[:, c * 128:(c + 1) * 128], ident[:], ["do_b", "ident"], ["ptp"])
            apar = (blk // 4) % 2
            cp(S, 'act', dT[apar][:, :, (blk % 4) * 128:(blk % 4 + 1) * 128], ptp[:], ["ptp"], [("dT", apar)])
            if blk % 4 == 3 or blk == nblk - 1:
                q0 = (blk // 4) * 4
                n = (blk - q0 + 1) * 128
                dma(S, 'sp', cat_d[:, 4:8, tb + q0 * 128: tb + q0 * 128 + n], dT[apar][:, :, 0:n], [("dT", apar)],
                    [("catT1", "d", b, blk // 4)])

        for i in range(len(units) + 4):
            if i < len(units):
                stA(units[i])
            if 0 <= i - 2 < len(units):
                stB(units[i - 2])
            if 0 <= i - 4 < len(units):
                stC(units[i - 4])
    ph.close()


def phase_ret(cx, nseq=NSEQ, nblk=32):
    nc, S, dr = cx.nc, cx.S, cx.dr
    ph = Phase(cx, "pc")
    ident = load_consts(ph, S, dr)
    decT = ph.sb("decT", [128, 4, 128], F32)
    dma(S, 'sp', decT[:], dr["ret_decT"], (), ["decT"])
    ng = ph.sb("ng", [128, 512], F32)
    rn = dr["ret_norm_g"]
    dma(S, 'sp', ng[:], bass.AP(tensor=rn.tensor, offset=rn.offset, ap=[[0, 128], [1, 512]]), (), ["ng"])
    mhalf = ph.sb("mhalf", [128, 4], F32)
    memset(S, 'pool', mhalf[:], -0.5, ["mhalf"])
    SEG = 8
    fm = [ph.sb("fm%d" % i, [128, 12, SEG * 128], BF16) for i in range(2)]
    tk = [ph.sb("tk%d" % i, [128, SEG, 3, 512], BF16) for i in range(2)]
    st32 = [ph.sb("st32_%d" % h, [128, 128], F32) for h in range(4)]
    stb = [[ph.sb("stb_%d_%d" % (h, i), [128, 128], BF16) for i in range(2)] for h in range(4)]
    PT = [ph.sb("PT%d" % i, [128, 128], BF16) for i in range(2)]
    osb = ph.sb("osb", [128, 4, 128], F32)
    sq = ph.sb("sq", [128, 4, 128], F32)
    s12 = ph.sb("s12", [128, 2, 4], F32)
    mv = ph.sb("mv", [128, 3, 4], F32)
    cn = ph.sb("cn", [128, 512], F32)
    co_b = ph.sb("co_b", [128, 512], BF16)
    cT = [ph.sb("cT%d" % i, [128, 4, 512], BF16) for i in range(2)]
    pS = [ph.ps("pS%d" % i, [128, 128], F32) for i in range(2)]
    pO = ph.ps("pO", [128, 4, 128], F32)
    pK = [ph.ps("pK%d" % i, [128, 128], F32) for i in range(2)]
    ptp = ph.ps("ptp", [128, 4, 128], BF16)
    cf_d = dr["c_fm"].rearrange("k p t -> p k t")
    ct_d = dr["c_tok"]
    cat_d = dr["catT1"].rearrange("(c p) t -> p c t", p=128)
    gam = [1.0 - 2.0 ** (-5.0 - h) for h in range(4)]
    sn = 0
    kn = 0
    for b in range(nseq):
        tb = b * S_LEN
        for h in range(4):
            memset(S, 'pool', st32[h][:], 0.0, [("st32", h)])
            memset(S, 'pool', stb[h][0][:], 0.0, [("stb", h, 0)])
        scnt = [0, 0, 0, 0]
        for blk in range(nblk):
            seg, sb_ = blk // SEG, blk % SEG
            sp_ = seg % 2
            if sb_ == 0:
                t0 = tb + seg * SEG * 128
                nb = min(SEG, nblk - seg * SEG)
                rkeys = [("c_fm", (t0 // 512) + i) for i in range(2)]
                dma(S, 'sp', fm[sp_][:, :, 0:nb * 128], cf_d[:, :, t0:t0 + nb * 128], rkeys, [("fm", sp_)])
                dma(S, 'sp', tk[sp_][:, 0:nb], ct_d[t0:t0 + nb * 128].rearrange("(s p) k d -> p s k d", p=128),
                    [("c_tok", (t0 // 512) + i) for i in range(2)], [("tk", sp_)])
            cs = slice(sb_ * 128, (sb_ + 1) * 128)
            for h in range(4):
                hs = slice(h * 128, (h + 1) * 128)
                par = sn % 2
                sn += 1
                mm(S, pS[par][:], fm[sp_][:, 8 + h, cs], fm[sp_][:, 0 + h, cs], True, True, [("fm", sp_)], [("pS", par)])
                tt(S, 'dve', PT[par][:], pS[par][:], decT[:, h, :], ALU.mult, [("pS", par), "decT"], [("PT", par)])
                mm(S, pO[:, h, :], PT[par][:], tk[sp_][:, sb_, 1, hs], True, False, [("PT", par), ("tk", sp_)], ["pO"])
                for half in range(2):
                    ps_ = slice(half * 64, (half + 1) * 64)
                    cur = scnt[h] % 2
                    mm(S, pO[ps_, h, :], fm[sp_][:, 4 + h, sb_ * 128 + half * 64: sb_ * 128 + (half + 1) * 64], stb[h][cur][:],
                       False, half == 1, [("fm", sp_), ("stb", h, cur)], ["pO"])
                    kp = kn % 2
                    kn += 1
                    mm(S, pK[kp][:], tk[sp_][ps_, sb_, 0, hs], tk[sp_][ps_, sb_, 1, hs], True, True, [("tk", sp_)], [("pK", kp)])
                    stt(S, st32[h][:], st32[h][:], gam[h] ** 64, pK[kp][:], ALU.mult, ALU.add, [("pK", kp), ("st32", h)], [("st32", h)])
                    cp(S, 'act', stb[h][1 - cur][:], st32[h][:], [("st32", h)], [("stb", h, 1 - cur)])
                    scnt[h] += 1
            cp(S, 'act', osb[:], pO[:], ["pO"], ["osb"])
            S.op('dve', lambda e: e.reduce_sum(s12[:, 0, :], osb[:], axis=AX.X), ["osb"], [("s12", 0)])
            tt(S, 'pool', sq[:], osb[:], osb[:], ALU.mult, ["osb"], ["sq"])
            S.op('dve', lambda e: e.reduce_sum(s12[:, 1, :], sq[:], axis=AX.X), ["sq"], [("s12", 1)])
            ts(S, 'dve', mv[:, 0, :], s12[:, 0, :], 1.0 / 128, None, ALU.mult, None, [("s12", 0)], [("mv", 0)])
            tt(S, 'dve', mv[:, 1, :], mv[:, 0, :], mv[:, 0, :], ALU.mult, [("mv", 0)], [("mv", 1)])
            stt(S, mv[:, 2, :], s12[:, 1, :], 1.0 / 128, mv[:, 1, :], ALU.mult, ALU.subtract, [("s12", 1), ("mv", 1)], [("mv", 2)])
            ts(S, 'dve', mv[:, 2, :], mv[:, 2, :], EPS, None, ALU.add, None, [("mv", 2)], [("mv", 2)])
            tt(S, 'pool', mv[:, 2, :], mv[:, 2, :], mhalf[:], ALU.pow, [("mv", 2), "mhalf"], [("mv", 2)])
            cn3 = cn[:].rearrange("p (h d) -> p h d", h=4)
            tt(S, 'dve', cn3, osb[:], apx(mv[:, 0, :], [[0, 128]]), ALU.subtract, ["osb", ("mv", 0)], ["cn"])
            tt(S, 'dve', cn3, cn3, apx(mv[:, 2, :], [[0, 128]]), ALU.mult, ["cn", ("mv", 2)], ["cn"])
            tt(S, 'pool', cn[:], cn[:], ng[:], ALU.mult, ["cn", "ng"], ["cn"])
            tt(S, 'pool', co_b[:], cn[:], tk[sp_][:, sb_, 2, :], ALU.mult, ["cn", ("tk", sp_)], ["co_b"])
            for c in range(4):
                tr(S, ptp[:, c, :], co_b[:, c * 128:(c + 1) * 128], ident[:], ["co_b", "ident"], ["ptp"])
            apar = (blk // 4) % 2
            cp(S, 'act', cT[apar][:, :, (blk % 4) * 128:(blk % 4 + 1) * 128], ptp[:], ["ptp"], [("cT", apar)])
            if blk % 4 == 3 or blk == nblk - 1:
                q0 = (blk // 4) * 4
                n = (blk - q0 + 1) * 128
                dma(S, 'sp', cat_d[:, 0:4, tb + q0 * 128: tb + q0 * 128 + n], cT[apar][:, :, 0:n], [("cT", apar)],
                    [("catT1", "c", b, blk // 4)])
    ph.close()


def fap(t, off, dims, p0=0, pn=None):
    base = t[:]
    pstep, pcnt = base.ap[0]
    if pn is None:
        pn = pcnt - p0
    return bass.AP(tensor=base.tensor, offset=base.offset + p0 * pstep + off, ap=[[pstep, pn]] + [list(d) for d in dims])


def phase_s5(cx, nseq=NSEQ):
    nc, S, dr = cx.nc, cx.S, cx.dr
    ph = Phase(cx, "pb")
    ident = load_consts(ph, S, dr)
    identf = ph.sb("identf", [128, 128], F32)
    dma(S, 'sp', identf[:], dr["ident"], (), ["identf"])
    W_intra = ph.sb("W_intra", [128, 32, 2, 256], BF16)
    W_BU = ph.sb("W_BU", [128, 32, 2, 2, 64], BF16)
    W_CX = ph.sb("W_CX", [64, 2, 32, 256], BF16)
    Aa = ph.sb("Aa", [64, 2, 32], F32)
    A2 = ph.sb("A2", [64, 2, 32], F32)
    Wg = load_w(S, ph, "Wg", dr["s5_w_glu"], 512, 512)
    bg = ph.sb("bg", [128, 4], F32)
    dma(S, 'sp', bg[:], dr["s5_bgT"], (), ["bg"])
    memset(S, 'pool', W_intra[:], 0.0, ["W_intra"])

    pp = Phase(cx, "pbp")
    cnt = [0]

    def T(shape=(64, 32)):
        cnt[0] += 1
        return pp.sb("t%d" % cnt[0], list(shape), F32)

    def k(t):
        return t.name if hasattr(t, "name") else id(t)

    def mul(o, a, b_):
        tt(S, 'dve', o[:], a[:], b_[:], ALU.mult, [k(a), k(b_)], [k(o)])

    def add(o, a, b_):
        tt(S, 'dve', o[:], a[:], b_[:], ALU.add, [k(a), k(b_)], [k(o)])

    def sub(o, a, b_):
        tt(S, 'dve', o[:], a[:], b_[:], ALU.subtract, [k(a), k(b_)], [k(o)])

    def tsa(o, a, s1, s2, op0, op1):
        ts(S, 'dve', o[:], a[:], s1, s2, op0, op1, [k(a)], [k(o)])

    lre, lim, ldt = T(), T(), T()
    dma(S, 'sp', lre[:], dr["s5_lamT_re"], (), [k(lre)])
    dma(S, 'sp', lim[:], dr["s5_lamT_im"], (), [k(lim)])
    dma(S, 'sp', ldt[:], dr["s5_ldt_bc"], (), [k(ldt)])
    dt = T()
    act(S, dt[:], ldt[:], AF.Exp, [k(ldt)], [k(dt)])
    lr, ang, mag = T(), T(), T()
    mul(lr, lre, dt)
    mul(ang, lim, dt)
    act(S, mag[:], lr[:], AF.Exp, [k(lr)], [k(mag)])
    kf, r = T(), T()
    MAGIC = 12582912.0
    tsa(kf, ang, 1.0 / (2 * np.pi), None, ALU.mult, None)
    tsa(kf, kf, MAGIC, None, ALU.add, None)
    tsa(kf, kf, MAGIC, None, ALU.subtract, None)
    C1 = 6.28125
    C2 = 2 * np.pi - C1
    stt(S, r[:], kf[:], -C1, ang[:], ALU.mult, ALU.add, [k(kf), k(ang)], [k(r)])
    stt(S, r[:], kf[:], -C2, r[:], ALU.mult, ALU.add, [k(kf), k(r)], [k(r)])
    y, y2, sn, cs, tmp = T(), T(), T(), T(), T()
    tsa(y, r, 0.125, None, ALU.mult, None)
    mul(y2, y, y)
    f = [1.0]
    for i in range(1, 12):
        f.append(f[-1] * i)
    tsa(sn, y2, 1.0 / f[9], -1.0 / f[7], ALU.mult, ALU.add)
    for c_ in (1.0 / f[5], -1.0 / f[3], 1.0):
        mul(sn, sn, y2)
        tsa(sn, sn, c_, None, ALU.add, None)
    mul(sn, sn, y)
    tsa(cs, y2, -1.0 / f[10], 1.0 / f[8], ALU.mult, ALU.add)
    for c_ in (-1.0 / f[6], 1.0 / f[4], -0.5, 1.0):
        mul(cs, cs, y2)
        tsa(cs, cs, c_, None, ALU.add, None)
    for _ in range(3):
        mul(tmp, sn, cs)
        mul(cs, sn, sn)
        tsa(sn, tmp, 2.0, None, ALU.mult, None)
        tsa(cs, cs, -2.0, 1.0, ALU.mult, ALU.add)
    ar, ai = T(), T()
    mul(ar, mag, cs)
    mul(ai, mag, sn)
    am1, den, fre, fim, t1, t2 = T(), T(), T(), T(), T(), T()
    tsa(am1, ar, -1.0, None, ALU.add, None)
    mul(den, lre, lre)
    mul(t1, lim, lim)
    add(den, den, t1)
    S.op('dve', lambda e: e.reciprocal(den[:], den[:]), [k(den)], [k(den)])
    mul(t1, am1, lre)
    mul(t2, ai, lim)
    add(fre, t1, t2)
    mul(fre, fre, den)
    mul(t1, ai, lre)
    mul(t2, am1, lim)
    sub(fim, t1, t2)
    mul(fim, fim, den)
    pr = pp.sb("pr", [64, 32, 17], F32)
    pi_ = pp.sb("pi", [64, 32, 17], F32)
    memset(S, 'dve', pr[:, :, 0:1], 1.0, ["pr"])
    memset(S, 'dve', pi_[:, :, 0:1], 0.0, ["pi"])
    q1, q2 = T(), T()
    for j in range(16):
        tt(S, 'dve', q1[:], pr[:, :, j], ar[:], ALU.mult, ["pr", k(ar)], [k(q1)])
        tt(S, 'dve', q2[:], pi_[:, :, j], ai[:], ALU.mult, ["pi", k(ai)], [k(q2)])
        tt(S, 'dve', pr[:, :, j + 1], q1[:], q2[:], ALU.subtract, [k(q1), k(q2)], ["pr"])
        tt(S, 'dve', q1[:], pr[:, :, j], ai[:], ALU.mult, ["pr", k(ai)], [k(q1)])
        tt(S, 'dve', q2[:], pi_[:, :, j], ar[:], ALU.mult, ["pi", k(ar)], [k(q2)])
        tt(S, 'dve', pi_[:, :, j + 1], q1[:], q2[:], ALU.add, [k(q1), k(q2)], ["pi"])
    for ri in range(2):
        cp(S, 'dve', Aa[:, ri, :], pr[:, :, 16], ["pr"], ["Aa"])
    ts(S, 'dve', A2[:, 0, :], pi_[:, :, 16], -1.0, None, ALU.mult, None, ["pi"], ["A2"])
    cp(S, 'dve', A2[:, 1, :], pi_[:, :, 16], ["pi"], ["A2"])
    bre, bim = pp.sb("bre", [64, 32, 16], F32), pp.sb("bim", [64, 32, 16], F32)
    cre, cim = pp.sb("cre", [64, 32, 16], F32), pp.sb("cim", [64, 32, 16], F32)
    dma(S, 'sp', bre[:], dr["s5_bT_re"], (), ["bre"])
    dma(S, 'sp', bim[:], dr["s5_bT_im"], (), ["bim"])
    dma(S, 'sp', cre[:], dr["s5_cT_re"], (), ["cre"])
    dma(S, 'sp', cim[:], dr["s5_cT_im"], (), ["cim"])
    Bre, Bim, nBim, u1, u2 = [pp.sb(n_, [64, 32, 16], F32) for n_ in ("Bre", "Bim", "nBim", "u1", "u2")]
    fre_b, fim_b = apx(fre[:], [[0, 16]]), apx(fim[:], [[0, 16]])
    tt(S, 'dve', u1[:], bre[:], fre_b, ALU.mult, ["bre", k(fre)], ["u1"])
    tt(S, 'dve', u2[:], bim[:], fim_b, ALU.mult, ["bim", k(fim)], ["u2"])
    tt(S, 'dve', Bre[:], u1[:], u2[:], ALU.subtract, ["u1", "u2"], ["Bre"])
    tt(S, 'dve', u1[:], bim[:], fre_b, ALU.mult, ["bim", k(fre)], ["u1"])
    tt(S, 'dve', u2[:], bre[:], fim_b, ALU.mult, ["bre", k(fim)], ["u2"])
    tt(S, 'dve', Bim[:], u1[:], u2[:], ALU.add, ["u1", "u2"], ["Bim"])
    ts(S, 'dve', nBim[:], Bim[:], -1.0, None, ALU.mult, None, ["Bim"], ["nBim"])
    HG = 16
    pp1 = Phase(cx, "pbp1")
    T1 = pp1.sb("T1", [64, HG, 17, 16], F32)
    T2 = pp1.sb("T2", [64, HG, 17, 16], F32)
    T3 = pp1.sb("T3", [64, HG, 17, 16], F32)
    Kb = pp1.sb("Kb", [16, 32, 256], BF16)
    dbc = pp1.sb("dbc", [16, 32, 16], F32)
    dma(S, 'sp', dbc[:], dr["s5_d_bc"][0:16], (), ["dbc"])
    Dg = pp1.sb("Dg", [16, 32, 16], F32)
    tt(S, 'dve', Dg[:], dbc[:], fap(identf, 0, [(0, 32), (1, 16)], 0, 16), ALU.mult, ["dbc", "identf"], ["Dg"])
    pk = [pp1.ps("pk%d" % i, [16, 2, 256], F32) for i in range(2)]
    for gh in range(2):
        gs = slice(gh * HG, (gh + 1) * HG)
        Cre_b = fap(cre, gh * HG * 16, [(16, HG), (0, 17), (1, 16)])
        Cim_b = fap(cim, gh * HG * 16, [(16, HG), (0, 17), (1, 16)])
        pr_b = fap(pr, gh * HG * 17, [(17, HG), (1, 17), (0, 16)])
        pi_b = fap(pi_, gh * HG * 17, [(17, HG), (1, 17), (0, 16)])
        tt(S, 'dve', T1[:], Cre_b, pr_b, ALU.mult, ["cre", "pr"], ["T1"])
        tt(S, 'dve', T3[:], Cim_b, pi_b, ALU.mult, ["cim", "pi"], ["T3"])
        tt(S, 'pool', T1[:], T1[:], T3[:], ALU.subtract, ["T1", "T3"], ["T1"])
        tt(S, 'dve', T2[:], Cre_b, pi_b, ALU.mult, ["cre", "pi"], ["T2"])
        tt(S, 'dve', T3[:], Cim_b, pr_b, ALU.mult, ["cim", "pr"], ["T3"])
        tt(S, 'pool', T2[:], T2[:], T3[:], ALU.add, ["T2", "T3"], ["T2"])
        cp(S, 'act', W_CX[:, 0, gs, :].rearrange("p g (t h) -> p g t h", t=16), T1[:, :, 1:17, :], ["T1"], ["W_CX"])
        ts(S, 'pool', W_CX[:, 1, gs, :].rearrange("p g (t h) -> p g t h", t=16), T2[:, :, 1:17, :], -1.0, None, ALU.mult, None,
           ["T2"], ["W_CX"])
        for gl in range(HG):
            g = gh * HG + gl
            pkt = pk[(g // 2) % 2]
            pkk = ("pk", (g // 2) % 2)
            mm(S, pkt[:, g % 2, :], Bre[:, g, :], T1[:, gl, 0:16, :].rearrange("p t h -> p (t h)"), True, False, ["Bre", "T1"], [pkk])
            mm(S, pkt[:, g % 2, :], nBim[:, g, :], T2[:, gl, 0:16, :].rearrange("p t h -> p (t h)"), False, True, ["nBim", "T2"], [pkk])
            if g % 2 == 1:
                cp(S, 'act', Kb[:, g - 1:g + 1, 16:256], pkt[:, :, 16:256], [pkk], ["Kb"])
                tt(S, 'dve', Kb[:, g - 1:g + 1, 0:16], pkt[:, :, 0:16], Dg[:, g - 1:g + 1, :], ALU.add, [pkk, "Dg"], ["Kb"])
    dma(S, 'sp', dr["s5_kall"], Kb[:], ["Kb"], ["kall_d"])
    kd = dr["s5_kall"]
    for s in range(16):
        half, sl = s // 8, s % 8
        n = (16 - s) * 16
        dma(S, 'sp', W_intra[16 * sl:16 * sl + 16, :, half, 16 * s:256], kd[:, :, 0:n], ["kall_d", "W_intra"], ["W_intra"])
    pp1.close()
    pp2 = Phase(cx, "pbp2")
    WTb = [pp2.sb("WTb%d" % i, [64, 32, 256], BF16) for i in range(2)]
    ptw = [pp2.ps("ptw%d" % i, [128, 8, 64], BF16) for i in range(2)]
    Q1 = pp2.sb("Q1", [64, HG, 16, 16], F32)
    Q2 = pp2.sb("Q2", [64, HG, 16, 16], F32)
    for gh in range(2):
        gs = slice(gh * HG, (gh + 1) * HG)
        prr = fap(pr, gh * HG * 17 + 15, [(17, HG), (-1, 16), (0, 16)])
        pir = fap(pi_, gh * HG * 17 + 15, [(17, HG), (-1, 16), (0, 16)])
        Bre_b = fap(Bre, gh * HG * 16, [(16, HG), (0, 16), (1, 16)])
        Bim_b = fap(Bim, gh * HG * 16, [(16, HG), (0, 16), (1, 16)])
        tt(S, 'dve', Q1[:], prr, Bre_b, ALU.mult, ["pr", "Bre"], ["Q1"])
        tt(S, 'dve', Q2[:], pir, Bim_b, ALU.mult, ["pi", "Bim"], ["Q2"])
        tt(S, 'pool', WTb[0][:, gs, :].rearrange("p g (s h) -> p g s h", s=16), Q1[:], Q2[:], ALU.subtract, ["Q1", "Q2"], [("WTb", 0)])
        tt(S, 'dve', Q1[:], prr, Bim_b, ALU.mult, ["pr", "Bim"], ["Q1"])
        tt(S, 'dve', Q2[:], pir, Bre_b, ALU.mult, ["pi", "Bre"], ["Q2"])
        tt(S, 'pool', WTb[1][:, gs, :].rearrange("p g (s h) -> p g s h", s=16), Q1[:], Q2[:], ALU.add, ["Q1", "Q2"], [("WTb", 1)])
    for g2 in range(16):
        pt_ = ptw[g2 % 2]
        for gi in range(2):
            g = 2 * g2 + gi
            for half in range(2):
                for ri in range(2):
                    tr(S, pt_[:, gi * 4 + half * 2 + ri, :], WTb[ri][:, g, half * 128:(half + 1) * 128], ident[0:64, 0:64],
                       [("WTb", ri), "ident"], [("ptw", g2 % 2)])
        cp(S, 'act' if g2 % 2 == 0 else 'dve', W_BU[:, 2 * g2:2 * g2 + 2].rearrange("p g a r q -> p (g a r) q"), pt_[:],
           [("ptw", g2 % 2)], ["W_BU"])
    pp2.close()
    pp.close()

    RA = ph.sb("RA", [128, 16384], BF16)
    RB = ph.sb("RB", [128, 16384], BF16)
    RC = ph.sb("RC", [128, 16448], BF16)
    Ublk = RA[:].rearrange("p (cb s ch) -> p cb s ch", cb=2, s=16)
    Xb = RA[0:64, :].rearrange("p (r g c) -> p r g c", r=2, g=32)
    UT = RB[:].rearrange("p (g a c) -> p g a c", g=32, a=2)
    ygT = RB[:].rearrange("p (k t) -> p k t", k=4)
    GX = RC[0:64, :].bitcast(F32).rearrange("p (r g c) -> p r g c", r=2, g=16)
    Ytok = RC[:, 0:16384].rearrange("p (cb t ch) -> p cb t ch", cb=2, t=16)
    P1 = ph.sb("P1", [64, 2, 16], F32)
    P2 = ph.sb("P2", [64, 2, 16], F32)
    sg = [ph.sb("sg%d" % i, [128, 512], F32) for i in range(2)]
    boT = [ph.sb("boT%d" % i, [128, 4, 512], BF16) for i in range(2)]
    ptu = [ph.ps("ptu%d" % i, [128, 8, 128], BF16) for i in range(2)]
    pG = [ph.ps("pG%d" % i, [64, 2, 256], F32) for i in range(2)]
    pY = [ph.ps("pY%d" % i, [128, 256], F32) for i in range(2)]
    pL = [ph.ps("pL%d" % i, [128, 512], F32) for i in range(2)]
    cat_d = dr["catT0"].rearrange("(c p) t -> p c t", p=128)
    for b in range(nseq):
        tb = b * S_LEN
        dma(S, 'sp', RA[:].rearrange("p (cb x) -> p cb x", cb=2),
            dr["u0"][tb:tb + S_LEN, :].rearrange("(cb c s) ch -> c cb (s ch)", cb=2, c=128),
            [("u0", i) for i in range(b * 8, b * 8 + 8)], ["RA"])
        for cb in range(2):
            cp(S, 'dve' if cb == 0 else 'pool', fap(RC, cb * 8192, [(256, 32), (16, 16), (1, 16)]),
               fap(RA, cb * 8192, [(16, 32), (512, 16), (1, 16)]), ["RA"], ["RC"])
        for g2 in range(16):
            pt_ = ptu[g2 % 2]
            for gi in range(2):
                g = 2 * g2 + gi
                for half in range(2):
                    for cb in range(2):
                        tr(S, pt_[:, gi * 4 + half * 2 + cb, :], fap(RC, cb * 8192 + g * 256 + half * 128, [(1, 128)]), ident[:],
                           ["RC", "ident"], [("ptu", g2 % 2)])
            cp(S, 'act' if g2 % 2 == 0 else 'dve', UT[:, 2 * g2:2 * g2 + 2].rearrange("p g a c -> p (g a c)"),
               pt_[:].rearrange("p a c -> p (a c)"), [("ptu", g2 % 2)], ["RB"])
        for gh in range(2):
            memset(S, 'pool', GX[:, :, :, 0:1], 0.0, ["RC"])
            for gl in range(16):
                g = gh * 16 + gl
                pg_ = pG[gl % 2]
                for ri in range(2):
                    for half in range(2):
                        mm(S, pg_[:, ri, :], W_BU[:, g, half, ri, :], UT[:, g, half, :], half == 0, half == 1,
                           ["W_BU", "RB"], [("pG", gl % 2)])
                cp(S, 'act' if gl % 2 == 0 else 'dve', GX[:, :, gl, 1:257], pg_[:], [("pG", gl % 2)], ["RC"])
            Aa_h = Aa[:, :, gh * 16:(gh + 1) * 16]
            A2_h = A2[:, :, gh * 16:(gh + 1) * 16]
            for c in range(256):
                Xc = GX[:, :, :, c]
                Xs = GX[:, ::-1, :, c]
                tt(S, 'dve', P1[:], Xc, Aa_h, ALU.mult, ["RC", "Aa"], ["P1"])
                tt(S, 'dve', P2[:], Xs, A2_h, ALU.mult, ["RC", "A2"], ["P2"])
                tt(S, 'dve', P1[:], P1[:], P2[:], ALU.add, ["P1", "P2"], ["P1"])
                tt(S, 'dve', GX[:, :, :, c + 1], GX[:, :, :, c + 1], P1[:], ALU.add, ["RC", "P1"], ["RC"])
            cp(S, 'pool', Xb[:, :, gh * 16:(gh + 1) * 16, :], GX[:, :, :, 0:256], ["RC"], ["RA"])
        yn = 0
        for g in range(32):
            for cb in range(2):
                py = pY[yn % 2]
                pyk = ("pY", yn % 2)
                yn += 1
                cs_ = slice(cb * 128, (cb + 1) * 128)
                mm(S, py[:], UT[:, g, 0, cs_], W_intra[:, g, 0, :], True, False, ["RB", "W_intra"], [pyk])
                mm(S, py[:], UT[:, g, 1, cs_], W_intra[:, g, 1, :], False, False, ["RB", "W_intra"], [pyk])
                mm(S, py[:], Xb[:, 0, g, cs_], W_CX[:, 0, g, :], False, False, ["RA", "W_CX"], [pyk])
                mm(S, py[:], Xb[:, 1, g, cs_], W_CX[:, 1, g, :], False, True, ["RA", "W_CX"], [pyk])
                act(S, Ytok[:, cb, :, 16 * g:16 * g + 16], py[:].rearrange("p (t h) -> p t h", t=16), AF.Gelu_apprx_tanh, [pyk], ["RC"])
        tn = 0
        for cb in range(2):
            for kc in range(4):
                for th in range(2):
                    pt_ = ptu[tn % 2]
                    ptk = ("ptu", tn % 2)
                    tn += 1
                    for tl in range(8):
                        t_ = th * 8 + tl
                        tr(S, pt_[:, tl, :], Ytok[:, cb, t_, kc * 128:(kc + 1) * 128], ident[:], ["RC", "ident"], [ptk])
                    dst = fap(RB, kc * 4096 + cb * 2048 + th * 8, [(1, 8), (16, 128)])
                    cp(S, 'act' if tn % 2 == 0 else 'dve', dst, pt_[:], [ptk], ["RB"])
        ln = 0
        for ti in range(8):
            tsl = slice(ti * 512, (ti + 1) * 512)
            bp = ti % 2
            for oc in range(4):
                pl = pL[ln % 2]
                plk = ("pL", ln % 2)
                sgt = sg[ln % 2]
                sgk = ("sg", ln % 2)
                ln += 1
                for kc in range(4):
                    mm(S, pl[:], Wg[:, kc, oc * 128:(oc + 1) * 128], ygT[:, kc, tsl], kc == 0, kc == 3, [("Wg", kc), "RB"], [plk])
                act(S, sgt[:], pl[:], AF.Sigmoid, [plk, "bg"], [sgk], bias=bg[:, oc:oc + 1])
                tt(S, 'dve', boT[bp][:, oc, :], sgt[:], ygT[:, oc, tsl], ALU.mult, [sgk, "RB"], [("boT", bp)])
            dma(S, 'sp', cat_d[:, 4:8, tb + ti * 512: tb + (ti + 1) * 512], boT[bp][:], [("boT", bp)], [("catT0", "b", b, ti)])
    ph.close()


def prep_core(inp, core):
    b0 = 2 * core
    d = {}
    d["x"] = np.ascontiguousarray(inp["x"][b0:b0 + 2].reshape(8192, 1024))
    c2 = inp["c"][b0:b0 + 2]
    d["cT"] = np.ascontiguousarray(c2.T.reshape(8, 128, 2).transpose(1, 0, 2))
    d["ada_w"] = inp["ada_w"]
    d["ada_b"] = inp["ada_b"]
    d["ada_bT"] = np.ascontiguousarray(inp["ada_b"].reshape(2, 48, 128).transpose(0, 2, 1))
    lg = np.stack([inp["ln_mix_g"], inp["ln_ffn_g"]], 0)
    d["ln_gT"] = np.ascontiguousarray(lg.reshape(2, 2, 8, 128).transpose(0, 1, 3, 2))
    return d

def prep_l0(inp, d):
    d["ab_w_in"] = inp["ab_w_in"][0]
    d["qkg"] = np.ascontiguousarray(np.stack([np.tile(inp["a_q_gain"][0], 2), np.tile(inp["a_k_gain"][0], 2)], 1))
    d["ident"] = np.eye(128, dtype=np.float32)
    bo = np.zeros((128, 128), np.float32); bo[:64, :64] = 1; bo[64:, 64:] = 1
    d["blockones"] = bo
    return d

def prep_consts(d):
    d["ident"] = np.eye(128, dtype=np.float32)
    pos = np.arange(4096, dtype=np.float64)[:, None]
    fr = 10000.0 ** (-np.arange(64, dtype=np.float64) / 64)[None, :]
    ang = pos * fr
    d["rot"] = np.stack([np.cos(ang), np.sin(ang)], 1).astype(np.float32)
    lg = np.log(1.0 - 2.0 ** (-5.0 - np.arange(4, dtype=np.float64)))
    idx = np.arange(128) % 64
    d["ret_qd"] = np.exp(lg[None, :] * (idx[:, None] + 1.0)).astype(np.float32)
    d["ret_kd"] = (np.exp(lg[None, :] * (63.0 - idx[:, None])) * 128.0 ** -0.5).astype(np.float32)
    j = np.arange(128)[:, None]; i = np.arange(128)[None, :]
    same = (j // 64) == (i // 64)
    dec = np.exp(lg[:, None, None] * np.abs(i - j)[None]) * same[None]
    d["ret_decT"] = np.ascontiguousarray(dec.transpose(1, 0, 2)).astype(np.float32)
    m = np.ones((128, 256), np.float32)
    m[:, 128:] = (np.arange(128)[None, :] < np.arange(128)[:, None]).astype(np.float32)
    d["sb_mask"] = m
    return d

def prep_s5(inp, d):
    d["s5_lamT_re"] = np.ascontiguousarray(inp["s5_lambda_re"][0].T)
    d["s5_lamT_im"] = np.ascontiguousarray(inp["s5_lambda_im"][0].T)
    d["s5_ldt_bc"] = np.ascontiguousarray(np.broadcast_to(inp["s5_log_dt"][0][None, :], (64, 32)))
    d["s5_bT_re"] = np.ascontiguousarray(inp["s5_b_re"][0].transpose(1, 0, 2))
    d["s5_bT_im"] = np.ascontiguousarray(inp["s5_b_im"][0].transpose(1, 0, 2))
    d["s5_cT_re"] = np.ascontiguousarray(inp["s5_c_re"][0].transpose(2, 0, 1))
    d["s5_cT_im"] = np.ascontiguousarray(inp["s5_c_im"][0].transpose(2, 0, 1))
    d["s5_d_bc"] = np.ascontiguousarray(np.broadcast_to(inp["s5_d"][0][None], (128, 32, 16)))
    d["s5_w_glu"] = inp["s5_w_glu"][0]
    d["s5_bgT"] = np.ascontiguousarray(inp["s5_b_glu"][0].reshape(4, 128).T)
    return d


IN_SPECS = [
    ("x", [8192, 1024]), ("cT", [128, 8, 2]), ("ada_w", [2, 1024, 6144]), ("ada_b", [2, 6144]), ("ada_bT", [2, 128, 48]),
    ("ln_gT", [2, 2, 128, 8]), ("ab_w_in", [1024, 2048]), ("qkg", [128, 2]), ("ident", [128, 128]), ("blockones", [128, 128]),
    ("rel_bias", [8, 257]),
    ("s5_lamT_re", [64, 32]), ("s5_lamT_im", [64, 32]), ("s5_ldt_bc", [64, 32]), ("s5_bT_re", [64, 32, 16]), ("s5_bT_im", [64, 32, 16]),
    ("s5_cT_re", [64, 32, 16]), ("s5_cT_im", [64, 32, 16]), ("s5_d_bc", [128, 32, 16]), ("s5_w_glu", [512, 512]), ("s5_bgT", [128, 4]),
    ("ab_w_out", [1024, 1024]), ("ffn_w_in", [2, 1024, 5632]), ("ffn_w_out", [2, 2816, 1024]),
    ("cd_w_in", [1024, 3584]), ("cd_w_out", [1024, 1024]), ("ret_norm_g", [512]),
    ("rot", [4096, 2, 64]), ("ret_qd", [128, 4]), ("ret_kd", [128, 4]), ("ret_decT", [128, 4, 128]), ("sb_mask", [128, 256]),
]


def build_program():
    nc = bass.Bass("TRN2", target_bir_lowering=False)
    cx = Ctx(nc)
    for n_, s_ in IN_SPECS:
        cx.dram_in(n_, s_)
    cx.dram_out("out", [T_CORE, 1024])
    cx.dram_scr("modfm", [2, 128, 4, 8, 2], F32)
    cx.dram_scr("gbc", [2, 2, 2, 128, 1024], F32)
    cx.dram_scr("qkT", [8, 128, T_CORE], BF16)
    cx.dram_scr("v0", [T_CORE, 512], BF16)
    cx.dram_scr("u0", [T_CORE, 512], BF16)
    cx.dram_scr("relext", [8, 1024], F32)
    cx.dram_scr("s5_kall", [16, 32, 256], BF16)
    cx.dram_scr("catT0", [1024, T_CORE], BF16)
    cx.dram_scr("x1", [T_CORE, 1024], F32)
    cx.dram_scr("c_fm", [12, 128, T_CORE], BF16)
    cx.dram_scr("c_tok", [T_CORE, 3, 512], BF16)
    cx.dram_scr("d_qkT", [8, 128, T_CORE], BF16)
    cx.dram_scr("d_v", [T_CORE, 512], BF16)
    cx.dram_scr("catT1", [1024, T_CORE], BF16)
    with cx.st:
        phase_adaln(cx)
        phase_p1_l0(cx)
        phase_attn(cx)
        phase_s5(cx)
        phase_p3(cx, 0, "catT0", "x", "x1", "ab_w_out")
        phase_p1_l1(cx, "x1")
        phase_sb(cx)
        phase_ret(cx)
        phase_p3(cx, 1, "catT1", "x1", "out", "cd_w_out", final=True)
    return nc


def prep_all(inp, core):
    d = prep_core(inp, core)
    prep_l0(inp, d)
    prep_consts(d)
    prep_s5(inp, d)
    d["rel_bias"] = inp["a_rel_bias"][0]
    d["ab_w_out"] = inp["ab_w_out"][0]
    d["ffn_w_in"] = inp["ffn_w_in"]
    d["ffn_w_out"] = inp["ffn_w_out"]
    d["cd_w_in"] = inp["cd_w_in"][0]
    d["cd_w_out"] = inp["cd_w_out"][0]
    d["ret_norm_g"] = inp["ret_norm_g"][0]
    return {k_: np.ascontiguousarray(np.asarray(d[k_], dtype=np.float32)) for k_, _ in IN_SPECS}


def kernel(**inputs):
    inp = {k_: np.asarray(v_) for k_, v_ in inputs.items()}
    nc = build_program()
    in_maps = [prep_all(inp, core) for core in range(8)]
    res = run_bass_kernel_spmd(nc, in_maps, core_ids=list(range(8)))
    outs = [np.asarray(res.results[i]["out"]).reshape(2, S_LEN, 1024) for i in range(8)]
    return np.concatenate(outs, axis=0).astype(np.float32)
```

```python
import contextlib
from contextlib import ExitStack
import numpy as np
import concourse.bass as bass
import concourse.mybir as mybir
from concourse.bass_utils import run_bass_kernel_spmd

F32 = mybir.dt.float32
BF16 = mybir.dt.bfloat16
I32 = mybir.dt.int32
AF = mybir.ActivationFunctionType
ALU = mybir.AluOpType
AX = mybir.AxisListType

ENGS = ['pe', 'act', 'dve', 'pool', 'sp']
NDS = 6


class Sched:
    def __init__(self, nc, stack, same_engine_sync=True):
        self.nc = nc
        self.ops = {e: [] for e in ENGS}
        self.cnt = {e: 0 for e in ENGS}
        self.seen = {e: {} for e in ENGS}
        self.lastw = {}
        self.readers = {}
        self.same = same_engine_sync
        self.csem = {e: stack.enter_context(nc.semaphore("c_" + e)) for e in ['pe', 'act', 'dve', 'pool']}
        self.dsem = {q: [stack.enter_context(nc.semaphore("d_%s%d" % (q, i))) for i in range(NDS)]
                     for q in ['sp', 'pool', 'act']}
        self.dma_n = {q: 0 for q in ['sp', 'pool', 'act']}
        self.out_tokens = []

    def _deps(self, reads, writes):
        deps = []
        for k in reads:
            if k in self.lastw:
                deps.append(self.lastw[k])
        for k in writes:
            if k in self.lastw:
                deps.append(self.lastw[k])
            deps.extend(self.readers.get(k, []))
        return deps

    def _waits(self, eng, deps):
        need = {}
        for (semkey, sem, val, deng) in deps:
            if deng == eng and semkey[0] == 'c' and (eng == 'pe' or not self.same):
                continue
            if self.seen[eng].get(semkey, 0) >= val:
                continue
            if semkey not in need or need[semkey][1] < val:
                need[semkey] = (sem, val)
        for semkey, (sem, val) in need.items():
            self.seen[eng][semkey] = val
        return list(need.values())

    def _record(self, tok, reads, writes):
        for k in reads:
            self.readers.setdefault(k, []).append(tok)
        for k in writes:
            self.lastw[k] = tok
            self.readers[k] = []

    def op(self, eng, fn, reads=(), writes=()):
        deps = self._deps(reads, writes)
        waits = self._waits(eng, deps)
        self.cnt[eng] += 1
        tok = (('c', eng), self.csem[eng], self.cnt[eng], eng)
        self.ops[eng].append((waits, fn, (self.csem[eng], 1)))
        self._record(tok, reads, writes)
        return tok

    def dma(self, q, fn, reads=(), writes=(), is_output=False):
        deps = self._deps(reads, writes)
        n = self.dma_n[q]
        self.dma_n[q] += 1
        slot = n % NDS
        sem = self.dsem[q][slot]
        semkey = ('d', q, slot)
        prev = 16 * (n // NDS)
        if prev > 0:
            deps.append((semkey, sem, prev, 'dma'))
        waits = self._waits(q, deps)
        tok = (semkey, sem, prev + 16, 'dma')
        self.ops[q].append((waits, fn, (sem, 16)))
        self._record(tok, reads, writes)
        if is_output:
            self.out_tokens.append(tok)
        return tok

    def flush(self, block):
        toks = []
        for e in ['pe', 'act', 'dve', 'pool']:
            if self.cnt[e] > 0:
                toks.append((('c', e), self.csem[e], self.cnt[e], 'x'))
        for q in ['sp', 'pool', 'act']:
            n = self.dma_n[q]
            for j in range(max(0, n - NDS), n):
                slot = j % NDS
                toks.append((('d', q, slot), self.dsem[q][slot], 16 * (j // NDS + 1), 'dma'))
        for e in ENGS:
            self.ops[e].append((self._waits(e, toks), None, None))
        self.lastw = {}
        self.readers = {}

        def run(eng_name):
            lst = self.ops[eng_name]

            def body(e):
                for (waits, fn, inc) in lst:
                    for (sem, val) in waits:
                        e.wait_ge(sem, val)
                    if fn is None:
                        continue
                    ins = fn(e)
                    ins.then_inc(inc[0], inc[1])
            return body

        block.tensor(run('pe'))
        block.scalar(run('act'))
        block.vector(run('dve'))
        block.gpsimd(run('pool'))
        block.sync(run('sp'))
        self.ops = {e: [] for e in ENGS}


EPS = 1e-6
S_LEN = 4096
NSEQ = 2
T_CORE = NSEQ * S_LEN
D = 1024
FF = 2816


def apx(ap, extra):
    return bass.AP(tensor=ap.tensor, offset=ap.offset, ap=[list(a) for a in ap.ap] + [list(e) for e in extra])


class Ctx:
    def __init__(self, nc, debug_out=()):
        self.nc = nc
        self.st = ExitStack()
        self.S = Sched(nc, self.st)
        self.dr = {}
        self.debug_out = set(debug_out)
        self.uid = 0

    def dram_in(self, name, shape, dt=F32):
        self.dr[name] = self.nc.dram_tensor(name, list(shape), dt, kind="ExternalInput").ap()
        return self.dr[name]

    def dram_out(self, name, shape, dt=F32):
        self.dr[name] = self.nc.dram_tensor(name, list(shape), dt, kind="ExternalOutput").ap()
        return self.dr[name]

    def dram_scr(self, name, shape, dt):
        kind = "ExternalOutput" if name in self.debug_out else "Internal"
        self.dr[name] = self.nc.dram_tensor(name, list(shape), dt, kind=kind).ap()
        return self.dr[name]

    def flush(self):
        with self.nc.Block() as block:
            self.S.flush(block)


class Phase:
    def __init__(self, cx, name):
        self.cx = cx
        self.nc = cx.nc
        self.S = cx.S
        self.name = name
        self.st = ExitStack()

    def sb(self, name, shape, dt):
        return self.st.enter_context(self.nc.sbuf_tensor(self.name + "_" + name, list(shape), dt))

    def ps(self, name, shape, dt=F32):
        return self.st.enter_context(self.nc.psum_tensor(self.name + "_" + name, list(shape), dt))

    def close(self):
        self.cx.flush()
        self.st.close()


def mm(S, out, lhsT, rhs, start, stop, r, w):
    S.op('pe', lambda e: e.matmul(out, lhsT, rhs, start=start, stop=stop), r, w)


def tr(S, out, in_, ident, r, w):
    S.op('pe', lambda e: e.transpose(out, in_, ident), r, w)


def act(S, out, in_, func, r, w, scale=1.0, bias=None, accum_out=None, eng='act'):
    kw = {}
    if bias is not None:
        kw['bias'] = bias
    if accum_out is not None:
        kw['accum_out'] = accum_out
    S.op('act', lambda e: e.activation(out=out, in_=in_, func=func, scale=scale, **kw), r, w)


def ts(S, eng, out, in0, s1, s2, op0, op1, r, w, accum_out=None):
    if op1 is None:
        S.op(eng, lambda e: e.tensor_scalar(out, in0, s1, None, op0), r, w)
    elif accum_out is not None:
        S.op(eng, lambda e: e.tensor_scalar(out, in0, s1, s2, op0, op1, accum_out), r, w)
    else:
        S.op(eng, lambda e: e.tensor_scalar(out, in0, s1, s2, op0, op1), r, w)


def tt(S, eng, out, in0, in1, op, r, w):
    S.op(eng, lambda e: e.tensor_tensor(out, in0, in1, op), r, w)


def stt(S, out, in0, scalar, in1, op0, op1, r, w):
    S.op('dve', lambda e: e.scalar_tensor_tensor(out, in0, scalar, in1, op0, op1), r, w)


def cp(S, eng, out, in_, r, w):
    if eng == 'act':
        S.op('act', lambda e: e.copy(out, in_), r, w)
    else:
        S.op(eng, lambda e: e.tensor_copy(out, in_), r, w)


def memset(S, eng, ap, val, w):
    S.op(eng, lambda e: e.memset(ap, val), (), w)


def dma(S, q, out, in_, r, w, is_output=False, slow=False):
    if slow:
        S.dma(q, lambda e: e.dma_start(out=out, in_=in_, allow_slow_non_contiguous=True), r, w, is_output)
    else:
        S.dma(q, lambda e: e.dma_start(out=out, in_=in_), r, w, is_output)


def load_w(S, ph, name, w_dram, K, N, q='pool', nsplit=None):
    kc = K // 128
    t = ph.sb(name, [128, kc, N], BF16)
    src = w_dram.rearrange("(c p) n -> p c n", p=128)
    for c in range(kc):
        dma(S, q, t[:, c, :], src[:, c, :], (), [(name, c)])
    return t


def phase_adaln(cx):
    nc, S, dr = cx.nc, cx.S, cx.dr
    ph = Phase(cx, "p0")
    cT = ph.sb("cT", [128, 8, 2], F32)
    condT = ph.sb("condT", [128, 8, 2], BF16)
    condbc = ph.sb("condbc", [128, 8, 2, 128], BF16)
    aw = ph.sb("aw", [128, 8, 6144], BF16)
    abT = ph.sb("abT", [128, 48], F32)
    lng = ph.sb("lng", [128, 2, 8], F32)
    abbc = ph.sb("abbc", [128, 2, 1024], F32)
    modsb = ph.sb("modsb", [128, 4, 8, 2], F32)
    tmp = ph.sb("tmp", [128, 8, 2], F32)
    gsb = [ph.sb("gsb%d" % i, [128, 1024], F32) for i in range(2)]
    pm = ph.ps("pm", [128, 32, 2], F32)
    pg = [ph.ps("pg%d" % i, [128, 512], F32) for i in range(2)]

    dma(S, 'sp', cT[:], dr["cT"], (), ["cT"])
    act(S, condT[:], cT[:], AF.Silu, ["cT"], ["condT"])
    cp(S, 'dve', condbc[:], apx(condT[:], [[0, 128]]), ["condT"], ["condbc"])
    gi = 0
    for l in range(2):
        src = dr["ada_w"][l].rearrange("(c p) n -> p c n", p=128)
        for c in range(8):
            dma(S, 'pool', aw[:, c, :], src[:, c, :], (), [("aw", c)])
        dma(S, 'sp', abT[:], dr["ada_bT"][l], (), ["abT"])
        dma(S, 'sp', lng[:, 0, :], dr["ln_gT"][0, l], (), ["lng"])
        dma(S, 'sp', lng[:, 1, :], dr["ln_gT"][1, l], (), ["lng"])
        for wi, blk in enumerate((2, 5)):
            dma(S, 'sp', abbc[:, wi, :], dr["ada_b"][l:l + 1, blk * 1024:(blk + 1) * 1024].partition_broadcast(128)
                if False else apx_pb(dr["ada_b"][l, blk * 1024:(blk + 1) * 1024]), (), ["abbc"])
        for jj, blk in enumerate((0, 1, 3, 4)):
            for fc in range(8):
                col = blk * 1024 + fc * 128
                for k in range(8):
                    mm(S, pm[:, jj * 8 + fc, :], aw[:, k, col:col + 128], condT[:, k, :], k == 0, k == 7,
                       [("aw", k), "condT"], ["pm"])
        for jj, blk in enumerate((0, 1, 3, 4)):
            bias = apx(abT[:, blk * 8:(blk + 1) * 8], [[0, 2]])
            if jj in (0, 2):
                tt(S, 'dve', modsb[:, jj + 1], pm[:, jj * 8:(jj + 1) * 8, :], bias, ALU.add, ["pm", "abT"], ["modsb"])
            else:
                tt(S, 'dve', tmp[:], pm[:, jj * 8:(jj + 1) * 8, :], bias, ALU.add, ["pm", "abT"], ["tmp"])
                stt(S, modsb[:, jj - 1], tmp[:], 1.0, apx(lng[:, jj // 2, :], [[0, 2]]), ALU.add, ALU.mult,
                    ["tmp", "lng"], ["modsb"])
        dma(S, 'sp', dr["modfm"][l], modsb[:], ["modsb"], [("modfm", l)])
        for b in range(2):
            for wi, blk in enumerate((2, 5)):
                g = gsb[gi % 2]
                gk = ("gsb", gi % 2)
                for half in range(2):
                    p = pg[half]
                    col = blk * 1024 + half * 512
                    for k in range(8):
                        mm(S, p[:], condbc[:, k, b, :], aw[:, k, col:col + 512], k == 0, k == 7,
                           [("aw", k), "condbc"], [("pg", half)])
                    tt(S, 'dve', g[:, half * 512:(half + 1) * 512], p[:], abbc[:, wi, half * 512:(half + 1) * 512],
                       ALU.add, [("pg", half), "abbc"], [gk])
                dma(S, 'sp', dr["gbc"][l, b, wi], g[:], [gk], [("gbc", l, b, wi)])
                gi += 1
    ph.close()


def apx_pb(ap1d):
    return bass.AP(tensor=ap1d.tensor, offset=ap1d.offset, ap=[[0, 128]] + [list(a) for a in ap1d.ap])


class NormT:
    def __init__(self, ph, nsub, ident):
        self.ph, self.S, self.nsub, self.ident = ph, ph.S, nsub, ident
        self.junk = ph.sb("nt_junk", [128, 1024], BF16)
        self.ss = [ph.sb("nt_ss%d" % i, [128, nsub], F32) for i in range(2)]
        self.rstd = [ph.sb("nt_rstd%d" % i, [128, nsub], F32) for i in range(2)]
        self.mhalf = ph.sb("nt_mhalf", [128, nsub], F32)
        self.xn = ph.sb("nt_xn", [128, nsub, 1024], BF16)
        self.tp = [ph.ps("nt_tp%d" % i, [128, nsub * 128], BF16) for i in range(2)]
        memset(self.S, 'pool', self.mhalf[:], -0.5, ["nt_mhalf"])
        self.n = 0

    def run(self, xt, xkey, hT, hkey, A, B, b):
        self.run_a(xt, xkey)
        self.run_b(hT, hkey, A, B, b)

    def run_a(self, xt, xkey):
        S, nsub = self.S, self.nsub
        par = self.n % 2
        self.n += 1
        ss, rstd = self.ss[par], self.rstd[par]
        for s in range(nsub):
            act(S, self.junk[:], xt[:, s, :], AF.Square, [xkey], ["nt_junk", ("nt_ss", par, s)], accum_out=ss[:, s:s + 1])
        ts(S, 'dve', rstd[:], ss[:], 1.0 / 1024, EPS, ALU.mult, ALU.add, [("nt_ss", par, s) for s in range(nsub)], [("nt_rstd", par)])
        tt(S, 'pool', rstd[:], rstd[:], self.mhalf[:], ALU.pow, [("nt_rstd", par), "nt_mhalf"], [("nt_rstd", par)])
        for s in range(nsub):
            ts(S, 'dve' if s % 2 == 0 else 'pool', self.xn[:, s, :], xt[:, s, :], rstd[:, s:s + 1], None, ALU.mult, None,
               [xkey, ("nt_rstd", par)], [("nt_xn", s)])

    def run_b(self, hT, hkey, A, B, b):
        S, nsub = self.S, self.nsub
        for c in range(8):
            tp = self.tp[c % 2]
            for s in range(nsub):
                tr(S, tp[:, s * 128:(s + 1) * 128], self.xn[:, s, c * 128:(c + 1) * 128], self.ident[:],
                   [("nt_xn", s), "ident"], [("nt_tp", c % 2)])
            if c % 2 == 0:
                ts(S, 'dve', hT[:, c, :], tp[:], A[:, c, b:b + 1], B[:, c, b:b + 1], ALU.mult, ALU.add,
                   [("nt_tp", c % 2), "modAB"], [(hkey, c)])
            else:
                act(S, hT[:, c, :], tp[:], AF.Identity, [("nt_tp", c % 2), "modAB"], [(hkey, c)],
                    scale=A[:, c, b:b + 1], bias=B[:, c, b:b + 1])


def load_consts(ph, S, dr):
    ident = ph.sb("ident", [128, 128], BF16)
    dma(S, 'pool', ident[:], dr["ident"], (), ["ident"])
    return ident


def phase_p1_l0(cx, ntiles=16):
    nc, S, dr = cx.nc, cx.S, cx.dr
    ph = Phase(cx, "p1a")
    ident = load_consts(ph, S, dr)
    bones = ph.sb("bones", [128, 128], BF16)
    dma(S, 'pool', bones[:], dr["blockones"], (), ["bones"])
    W = load_w(S, ph, "W", dr["ab_w_in"], 1024, 2048)
    wkeys = [("W", c) for c in range(8)]
    modAB = ph.sb("modAB", [128, 4, 8, 2], F32)
    dma(S, 'sp', modAB[:], dr["modfm"][0], [("modfm", 0)], ["modAB"])
    qkg = ph.sb("qkg", [128, 2], F32)
    dma(S, 'sp', qkg[:], dr["qkg"], (), ["qkg"])
    cb = ph.sb("cbias", [128, 2], F32)
    memset(S, 'pool', cb[:, 0:1], 64 * EPS, ["cbias"])
    memset(S, 'pool', cb[:, 1:2], EPS, ["cbias"])
    nt = NormT(ph, 4, ident)
    xt = [ph.sb("xt%d" % i, [128, 4, 1024], F32) for i in range(2)]
    hT = [ph.sb("hT%d" % i, [128, 8, 512], BF16) for i in range(2)]
    qkst = [ph.sb("qkst%d" % i, [128, 8, 512], BF16) for i in range(2)]
    vust = [ph.sb("vust%d" % i, [128, 4, 1024], BF16) for i in range(2)]
    sqk = [ph.sb("sqk%d" % i, [128, 512], BF16) for i in range(3)]
    rs = [ph.sb("rs%d" % i, [128, 512], F32) for i in range(3)]
    pq = [ph.ps("pq%d" % i, [128, 512], F32) for i in range(3)]
    pss = [ph.ps("pss%d" % i, [128, 512], F32) for i in range(1)]
    pv = [ph.ps("pv%d" % i, [128, 512], F32) for i in range(2)]
    qkT_d = dr["qkT"].rearrange("c p t -> p c t")
    def pre_a1(ti):
        par = ti % 2
        t0 = ti * 512
        dma(S, 'sp', xt[par][:], dr["x"][t0:t0 + 512, :].rearrange("(s p) d -> p s d", p=128), (), [("xt", par)])

    def pre_a2(ti):
        par = ti % 2
        nt.run_a(xt[par], ("xt", par))

    def pre_b(ti):
        par = ti % 2
        nt.run_b(hT[par], ("hT", par), modAB[:, 0], modAB[:, 1], ti // 8)

    pre_a1(0)
    pre_a2(0)
    pre_b(0)
    for ti in range(ntiles):
        par = ti % 2
        b = ti // 8
        t0 = ti * 512
        if ti + 1 < ntiles:
            pre_a1(ti + 1)
        hkeys = [(("hT", par), c) for c in range(8)]
        def qk_tail(oc):
            i3 = oc % 3
            p = pq[i3]
            pk = ("pq", i3)
            isk = 1 if oc >= 4 else 0
            sk = ("sqk", i3)
            mm(S, pss[0][:], bones[:], sqk[i3][:], True, True, [sk, "bones"], ["pss"])
            rk = ("rs", i3)
            act(S, rs[i3][:], pss[0][:], AF.Sqrt, ["pss", "cbias"], [rk],
                scale=(1.0 / 64 if isk else 1.0), bias=cb[:, isk:isk + 1])
            S.op('dve', (lambda o: (lambda e: e.reciprocal(o, o)))(rs[i3][:]), [rk], [rk])
            stt(S, qkst[par][:, oc, :], p[:], qkg[:, isk:isk + 1], rs[i3][:], ALU.mult, ALU.mult,
                [pk, rk, "qkg"], [("qkst", par)])

        for oc in range(8):
            i3 = oc % 3
            p = pq[i3]
            pk = ("pq", i3)
            for k in range(8):
                mm(S, p[:], W[:, k, oc * 128:(oc + 1) * 128], hT[par][:, k, :], k == 0, k == 7,
                   [wkeys[k], hkeys[k]], [pk])
            act(S, sqk[i3][:], p[:], AF.Square, [pk], [("sqk", i3)])
            if oc >= 1:
                qk_tail(oc - 1)
            if oc == 3 and ti + 1 < ntiles:
                pre_a2(ti + 1)
        tails_left = [7]
        for vi, (col, dname) in enumerate(((1024, "v0"), (1536, "u0"))):
            for s in range(4):
                p = pv[s % 2]
                pk = ("pv", s % 2)
                for k in range(8):
                    mm(S, p[:], hT[par][:, k, s * 128:(s + 1) * 128], W[:, k, col:col + 512], k == 0, k == 7,
                       [wkeys[k], hkeys[k]], [pk])
                if tails_left:
                    qk_tail(tails_left.pop(0))
                    if not tails_left:
                        dma(S, 'sp', qkT_d[:, :, t0:t0 + 512], qkst[par][:], [("qkst", par)], [("qkT", ti)])
                cp(S, 'act' if s % 2 == 0 else 'dve', vust[par][:, s, vi * 512:(vi + 1) * 512], p[:], [pk], [("vust", par, vi)])
            dma(S, 'sp', dr[dname][t0:t0 + 512, :].rearrange("(s p) d -> p s d", p=128),
                vust[par][:, :, vi * 512:(vi + 1) * 512], [("vust", par, vi)], [(dname, ti)])
        if ti + 1 < ntiles:
            pre_b(ti + 1)
    ph.close()


def phase_p3(cx, l, cat_name, x_name, out_name, wo_name, ntiles=32, final=False):
    nc, S, dr = cx.nc, cx.S, cx.dr
    ph = Phase(cx, "p3_%d" % l)
    ident = load_consts(ph, S, dr)
    Wo = load_w(S, ph, "Wo", dr[wo_name], 1024, 1024)
    Win = load_w(S, ph, "Win", dr["ffn_w_in"][l], 1024, 2 * FF)
    Wout = load_w(S, ph, "Wout", dr["ffn_w_out"][l], FF, 1024)
    modAB = ph.sb("modAB", [128, 4, 8, 2], F32)
    dma(S, 'sp', modAB[:], dr["modfm"][l], [("modfm", l)], ["modAB"])
    gb = ph.sb("gb", [128, 2, 1024], F32)
    nt = NormT(ph, 2, ident)
    xt2 = [ph.sb("xt%d" % i, [128, 2, 1024], F32) for i in range(2)]
    ct = [ph.sb("ct%d" % i, [128, 8, 256], BF16) for i in range(2)]
    hT = ph.sb("hT", [128, 8, 256], BF16)
    hact = ph.sb("hact", [128, 22, 256], BF16)
    sil = [ph.sb("sil%d" % i, [128, 256], F32) for i in range(2)]
    tmp = ph.sb("tmp", [128, 1024], F32)
    po = ph.ps("po", [128, 1024], F32)
    pw = ph.ps("pw", [128, 1024], F32)
    pgu = [ph.ps("pgu%d" % i, [128, 2, 256], F32) for i in range(2)]
    cat_d = dr[cat_name].rearrange("(c p) t -> p c t", p=128)
    hkeys = [("hT", c) for c in range(8)]

    def load(ti):
        par = ti % 2
        b = ti // 16
        t0 = ti * 256
        if ti % 16 == 0:
            for wi in range(2):
                dma(S, 'sp', gb[:, wi, :], dr["gbc"][l, b, wi], [("gbc", l, b, wi)], ["gb"])
        dma(S, 'sp', xt2[par][:], dr[x_name][t0:t0 + 256, :].rearrange("(s p) d -> p s d", p=128), [(x_name, ti)], [("xt", par)])
        dma(S, 'sp', ct[par][:], cat_d[:, :, t0:t0 + 256], [(cat_name, ti)], [("ct", par)])

    def outproj(ti):
        par = ti % 2
        xt = xt2[par]
        for s in range(2):
            for half in range(2):
                for k in range(8):
                    mm(S, po[:, half * 512:(half + 1) * 512], ct[par][:, k, s * 128:(s + 1) * 128],
                       Wo[:, k, half * 512:(half + 1) * 512], k == 0, k == 7, [("Wo", k), ("ct", par)], [("po", half)])
            tt(S, 'dve', tmp[:], po[:], gb[:, 0, :], ALU.mult, [("po", 0), ("po", 1), "gb"], ["tmp"])
            tt(S, 'pool', xt[:, s, :], xt[:, s, :], tmp[:], ALU.add, ["tmp", ("xt", par)], [("xt", par)])

    def ffn_in(ti):
        for j in range(22):
            gu = pgu[j % 2]
            gk = ("pgu", j % 2)
            for hh in range(2):
                col = hh * FF + j * 128
                for k in range(8):
                    mm(S, gu[:, hh, :], Win[:, k, col:col + 128], hT[:, k, :], k == 0, k == 7,
                       [("Win", k), hkeys[k]], [gk])
            sk = ("sil", j % 2)
            act(S, sil[j % 2][:], gu[:, 0, :], AF.Silu, [gk], [sk])
            tt(S, 'dve', hact[:, j, :], gu[:, 1, :], sil[j % 2][:], ALU.mult, [gk, sk], [("hact", j)])

    def ffn_out(ti):
        par = ti % 2
        xt = xt2[par]
        t0 = ti * 256
        for s in range(2):
            for half in range(2):
                for j in range(22):
                    mm(S, pw[:, half * 512:(half + 1) * 512], hact[:, j, s * 128:(s + 1) * 128],
                       Wout[:, j, half * 512:(half + 1) * 512], j == 0, j == 21, [("Wout", j), ("hact", j)], [("pw", half)])
            tt(S, 'dve', tmp[:], pw[:], gb[:, 1, :], ALU.mult, [("pw", 0), ("pw", 1), "gb"], ["tmp"])
            tt(S, 'pool', xt[:, s, :], xt[:, s, :], tmp[:], ALU.add, ["tmp", ("xt", par)], [("xt", par)])
        dma(S, 'sp', dr[out_name][t0:t0 + 256, :].rearrange("(s p) d -> p s d", p=128), xt[:], [("xt", par)], [(out_name, ti)],
            is_output=final)

    load(0)
    outproj(0)
    nt.run_a(xt2[0], ("xt", 0))
    nt.run_b(hT, "hT", modAB[:, 2], modAB[:, 3], 0)
    for ti in range(ntiles):
        nxt = ti + 1 < ntiles
        if nxt and (ti + 1) % 16 != 0:
            load(ti + 1)
        ffn_in(ti)
        if nxt and (ti + 1) % 16 != 0:
            outproj(ti + 1)
            nt.run_a(xt2[(ti + 1) % 2], ("xt", (ti + 1) % 2))
        ffn_out(ti)
        if nxt:
            if (ti + 1) % 16 == 0:
                load(ti + 1)
                outproj(ti + 1)
                nt.run_a(xt2[(ti + 1) % 2], ("xt", (ti + 1) % 2))
            nt.run_b(hT, "hT", modAB[:, 2], modAB[:, 3], (ti + 1) // 16)
    ph.close()


def phase_attn(cx, nseq=NSEQ, nqb=32):
    nc, S, dr = cx.nc, cx.S, cx.dr
    ph = Phase(cx, "pa")
    ident = load_consts(ph, S, dr)
    NEG = -30000.0
    Er = ph.sb("Er", [8, 257], F32)
    E = ph.sb("E", [8, 1024], F32)
    c256 = ph.sb("c256", [128, 8], F32)
    dma(S, 'sp', Er[:], dr["rel_bias"], (), ["Er"])
    rb = dr["rel_bias"]
    dma(S, 'sp', c256[:], bass.AP(tensor=rb.tensor, offset=rb.offset + 256, ap=[[0, 128], [257, 8]]), (), ["c256"], slow=True)
    memset(S, 'dve', E[:], 0.0, ["E"])
    ts(S, 'dve', E[:, 0:767], E[:, 0:767], Er[:, 256:257], None, ALU.add, None, ["E", "Er"], ["E"])
    cp(S, 'dve', E[:, 767:1024], Er[:, ::-1], ["E", "Er"], ["E"])
    dma(S, 'sp', dr["relext"], E[:], ["E"], ["relext"])
    BT = ph.sb("BT", [128, 5, 8, 128], F32)
    memset(S, 'pool', BT[:], 0.0, ["BT"])
    ext = dr["relext"]
    for j in range(3):
        tt(S, 'pool', BT[:, j, :, :], BT[:, j, :, :], apx(c256[:, :], [[0, 128]]), ALU.add, ["BT", "c256"], ["BT"])
    for j in (3, 4):
        for h in range(8):
            src = bass.AP(tensor=ext.tensor, offset=ext.offset + h * 1024 + 1023 - (5 - j) * 128 - 127, ap=[[1, 128], [1, 128]])
            dma(S, 'sp', BT[:, j, h, :], src, ["relext", "BT"], ["BT"])
    memset(S, 'pool', BT[0:64, 0, :, 0:64], NEG, ["BT"])
    memset(S, 'pool', BT[64:128, 4, :, 64:128], NEG, ["BT"])
    qT = ph.sb("qT", [128, 4, S_LEN], BF16)
    kT = ph.sb("kT", [128, 4, S_LEN], BF16)
    Vr = ph.sb("Vr", [128, 32, 512], BF16)
    Va = ph.sb("Va", [128, 32, 8, 65], BF16)
    memset(S, 'pool', Va[:, :, :, 64:65], 1.0, ["Va1"])
    NBA = 3
    sbf = [ph.sb("sbf%d" % i, [128, 5, 128], F32) for i in range(NBA)]
    pT = [ph.sb("pT%d" % i, [128, 5, 128], BF16) for i in range(NBA)]
    rc = ph.sb("rc", [128, 8], F32)
    ao = ph.sb("ao", [128, 512], BF16)
    aT = [ph.sb("aT%d" % i, [128, 4, 512], BF16) for i in range(2)]
    ps = [ph.ps("ps%d" % i, [128, 8, 128], F32) for i in range(2)]
    po = [ph.ps("po%d" % i, [128, 4, 65], F32) for i in range(2)]
    tp = ph.ps("tp", [128, 4, 128], BF16)
    qk_d = dr["qkT"].rearrange("c p t -> p c t")
    cat_d = dr["catT0"].rearrange("(c p) t -> p c t", p=128)
    hn = 0
    for b in range(nseq):
        tb = b * S_LEN
        for c in range(4):
            dma(S, 'sp', qT[:, c, :], qk_d[:, c, tb:tb + S_LEN], [("qkT", i) for i in range(b * 8, b * 8 + 8)], ["qT"])
            dma(S, 'sp', kT[:, c, :], qk_d[:, 4 + c, tb:tb + S_LEN], [("qkT", i) for i in range(b * 8, b * 8 + 8)], ["kT"])
        for c in range(4):
            dma(S, 'sp', Vr[:, c * 8:(c + 1) * 8, :],
                dr["v0"][tb + c * 1024: tb + (c + 1) * 1024, :].rearrange("(s p) d -> p s d", p=128),
                [("v0", i) for i in range(b * 8, b * 8 + 8)], ["Vr"])
        for c in range(4):
            cp(S, 'pool', Va[:, c * 8:(c + 1) * 8, :, 0:64], Vr[:, c * 8:(c + 1) * 8, :].rearrange("p s (h d) -> p s h d", h=8),
               ["Vr"], ["Va"])
        units = [(qb, h) for qb in range(nqb) for h in range(8)]
        bufi = {}

        def st1(u):
            nonlocal hn
            qb, h = u
            kb0 = max(0, qb - 4)
            nkb = qb - kb0 + 1
            j0 = 5 - nkb
            pr, base = h // 2, 64 * (h % 2)
            par = hn % NBA
            p_s, pk = ps[hn % 2], ("ps", hn % 2)
            hn += 1
            bufi[u] = par
            for j in range(nkb):
                kb = kb0 + j
                mm(S, p_s[:, j, :], kT[base:base + 64, pr, kb * 128:(kb + 1) * 128],
                   qT[base:base + 64, pr, qb * 128:(qb + 1) * 128], True, True, ["kT", "qT"], [pk])
            tt(S, 'dve', sbf[par][:, 0:nkb, :], p_s[:, 0:nkb, :], BT[:, j0:5, h, ::-1], ALU.add, [pk, "BT"], [("sbf", par)])
            act(S, pT[par][:, 0:nkb, :], sbf[par][:, 0:nkb, :], AF.Exp, [("sbf", par)], [("pT", par)])

        def st2(u):
            qb, h = u
            kb0 = max(0, qb - 4)
            nkb = qb - kb0 + 1
            par = bufi.pop(u)
            for j in range(nkb):
                kb = kb0 + j
                mm(S, po[h // 4][:, h % 4, :], pT[par][:, j, :], Va[:, kb, h, :], j == 0, j == nkb - 1,
                   [("pT", par), "Va", "Va1"], [("po", h // 4)])
            if h == 7:
                epi(qb)

        def epi(qb):
            for g in range(2):
                S.op('dve', (lambda o, i: (lambda e: e.reciprocal(o, i)))(rc[:, g * 4:(g + 1) * 4], po[g][:, :, 64]),
                     [("po", g)], [("rc", g)])
                tt(S, 'dve', ao[:, g * 256:(g + 1) * 256].rearrange("p (h d) -> p h d", h=4), po[g][:, :, 0:64],
                   apx(rc[:, g * 4:(g + 1) * 4], [[0, 64]]), ALU.mult, [("po", g), ("rc", g)], [("ao", g)])
            for c in range(4):
                tr(S, tp[:, c, :], ao[:, c * 128:(c + 1) * 128], ident[:], [("ao", c // 2), "ident"], ["tp"])
            apar = (qb // 4) % 2
            cp(S, 'act', aT[apar][:, :, (qb % 4) * 128:(qb % 4 + 1) * 128], tp[:], ["tp"], [("aT", apar)])
            if qb % 4 == 3 or qb == nqb - 1:
                q0 = (qb // 4) * 4
                n = (qb - q0 + 1) * 128
                dma(S, 'sp', cat_d[:, 0:4, tb + q0 * 128: tb + q0 * 128 + n], aT[apar][:, :, 0:n], [("aT", apar)],
                    [("catT0", "a", b, qb // 4)])

        SK = 2
        for i in range(len(units) + SK):
            if i < len(units):
                st1(units[i])
            if i - SK >= 0:
                st2(units[i - SK])
    ph.close()


def phase_p1_l1(cx, x_name, ntiles=16):
    nc, S, dr = cx.nc, cx.S, cx.dr
    ph = Phase(cx, "p1b")
    ident = load_consts(ph, S, dr)
    W = load_w(S, ph, "W", dr["cd_w_in"], 1024, 3584)
    wkeys = [("W", c) for c in range(8)]
    modAB = ph.sb("modAB", [128, 4, 8, 2], F32)
    dma(S, 'sp', modAB[:], dr["modfm"][1], [("modfm", 1)], ["modAB"])
    QD = ph.sb("QD", [128, 4], F32)
    KD = ph.sb("KD", [128, 4], F32)
    dma(S, 'sp', QD[:], dr["ret_qd"], (), ["QD"])
    dma(S, 'sp', KD[:], dr["ret_kd"], (), ["KD"])
    nt = NormT(ph, 4, ident)
    xt = [ph.sb("xt%d" % i, [128, 4, 1024], F32) for i in range(2)]
    hT2 = [ph.sb("hT%d" % i, [128, 8, 512], BF16) for i in range(2)]
    rot = [ph.sb("rot%d" % i, [128, 4, 2, 64], F32) for i in range(2)]
    t12 = [ph.sb("t12_%d" % i, [128, 4, 64], F32) for i in range(4)]
    R = [ph.sb("R%d" % i, [128, 4, 2, 64], F32) for i in range(2)]
    qkb = ph.sb("qkb", [128, 3, 512], BF16)
    fst = [ph.sb("fst%d" % i, [128, 12, 512], BF16) for i in range(2)]
    tst = [ph.sb("tst%d" % i, [128, 4, 3, 512], BF16) for i in range(2)]
    dst = [ph.sb("dst%d" % i, [128, 8, 512], BF16) for i in range(2)]
    dvs = [ph.sb("dvs%d" % i, [128, 4, 512], BF16) for i in range(2)]
    NPT = 4
    pt = [ph.ps("pt%d" % i, [128, 512], F32) for i in range(NPT)]
    ptr2 = [ph.ps("ptr%d" % i, [128, 4, 128], BF16) for i in range(2)]
    cf_d = dr["c_fm"].rearrange("k p t -> p k t")
    ct_d = dr["c_tok"]
    dq_d = dr["d_qkT"].rearrange("c p t -> p c t")
    pn = 0
    def pre_a(ti):
        par = ti % 2
        t0 = ti * 512
        pos0 = t0 % S_LEN
        xk = ("xt", par)
        dma(S, 'sp', xt[par][:], dr[x_name][t0:t0 + 512, :].rearrange("(s p) d -> p s d", p=128), [(x_name, 2 * ti), (x_name, 2 * ti + 1)], [xk])
        dma(S, 'sp', rot[par][:], dr["rot"][pos0:pos0 + 512].rearrange("(s p) a f -> p s a f", p=128), (), [("rot", par)])

    def pre_a2(ti):
        par = ti % 2
        nt.run_a(xt[par], ("xt", par))

    def pre_b(ti):
        par = ti % 2
        nt.run_b(hT2[par], ("hT", par), modAB[:, 0], modAB[:, 1], ti // 8)

    pre_a(0)
    pre_a2(0)
    pre_b(0)
    trn = 0
    for ti in range(ntiles):
        par = ti % 2
        b = ti // 8
        t0 = ti * 512
        if ti + 1 < ntiles:
            pre_a(ti + 1)
        hT = hT2[par]
        hkeys = [(("hT", par), c) for c in range(8)]

        def tokmm(s, col):
            nonlocal pn
            p = pt[pn % NPT]
            pk = ("pt", pn % NPT)
            pn += 1
            for k in range(8):
                mm(S, p[:], hT[:, k, s * 128:(s + 1) * 128], W[:, k, col:col + 512], k == 0, k == 7, [wkeys[k], hkeys[k]], [pk])
            return p, pk

        def fm_chunk(oc):
            nonlocal pn
            p = pt[pn % NPT]
            pk = ("pt", pn % NPT)
            pn += 1
            col = 2048 + oc * 128
            for k in range(8):
                mm(S, p[:], W[:, k, col:col + 128], hT[:, k, :], k == 0, k == 7, [wkeys[k], hkeys[k]], [pk])
            cp(S, 'act' if oc % 2 == 0 else 'dve', dst[par][:, oc, :], p[:], [pk], [("dst", par)])

        for s in range(4):
            cosv = rot[par][:, s, 0, :]
            sinv = rot[par][:, s, 1, :]
            cos4 = bass.AP(tensor=cosv.tensor, offset=cosv.offset, ap=[list(cosv.ap[0]), [0, 4], list(cosv.ap[1])])
            sin4 = bass.AP(tensor=sinv.tensor, offset=sinv.offset, ap=[list(sinv.ap[0]), [0, 4], list(sinv.ap[1])])
            for qi, col in enumerate((0, 512)):
                p, pk = tokmm(s, col)
                pv4 = p[:].rearrange("p (h a f) -> p h a f", h=4, a=2)
                x1, x2 = pv4[:, :, 0, :], pv4[:, :, 1, :]
                rk = ("R", qi)
                tt(S, 'dve', t12[0][:], x1, cos4, ALU.mult, [pk, ("rot", par)], ["t0"])
                tt(S, 'dve', t12[1][:], x2, sin4, ALU.mult, [pk, ("rot", par)], ["t1"])
                tt(S, 'dve', t12[2][:], x1, sin4, ALU.mult, [pk, ("rot", par)], ["t2"])
                tt(S, 'dve', t12[3][:], x2, cos4, ALU.mult, [pk, ("rot", par)], ["t3"])
                tt(S, 'pool', R[qi][:, :, 0, :], t12[0][:], t12[1][:], ALU.subtract, ["t0", "t1"], [rk])
                tt(S, 'pool', R[qi][:, :, 1, :], t12[2][:], t12[3][:], ALU.add, ["t2", "t3"], [rk])
            Rq = R[0][:].rearrange("p h a f -> p h (a f)")
            Rk = R[1][:].rearrange("p h a f -> p h (a f)")
            q3 = qkb[:].rearrange("p k (h d) -> p k h d", h=4)
            cp(S, 'pool', q3[:, 0], Rq, [("R", 0)], [("qkb", 0)])
            tt(S, 'pool', q3[:, 1], Rq, apx(QD[:, :], [[0, 128]]), ALU.mult, [("R", 0), "QD"], [("qkb", 1)])
            ts(S, 'pool', q3[:, 2], Rk, 128.0 ** -0.5, None, ALU.mult, None, [("R", 1)], [("qkb", 2)])
            tt(S, 'pool', tst[par][:, s, 0, :].rearrange("p (h d) -> p h d", h=4), Rk, apx(KD[:, :], [[0, 128]]), ALU.mult,
               [("R", 1), "KD"], [("tst", par)])
            p, pk = tokmm(s, 1024)
            cp(S, 'act', tst[par][:, s, 1, :], p[:], [pk], [("tst", par)])
            p, pk = tokmm(s, 1536)
            act(S, tst[par][:, s, 2, :], p[:], AF.Silu, [pk], [("tst", par)])
            p, pk = tokmm(s, 3072)
            cp(S, 'act', dvs[par][:, s, :], p[:], [pk], [("dvs", par)])
            for oc in (2 * s, 2 * s + 1):
                fm_chunk(oc)
            for kind in range(3):
                ptr = ptr2[trn % 2]
                ptk = ("ptr", trn % 2)
                for h in range(4):
                    tr(S, ptr[:, h, :], qkb[:, kind, h * 128:(h + 1) * 128], ident[:], [("qkb", kind), "ident"], [ptk])
                cp(S, 'act' if trn % 2 == 0 else 'dve', fst[par][:, kind * 4:(kind + 1) * 4, s * 128:(s + 1) * 128], ptr[:],
                   [ptk], [("fst", par)])
                trn += 1
            if s == 1 and ti + 1 < ntiles:
                pre_a2(ti + 1)
        dma(S, 'sp', cf_d[:, :, t0:t0 + 512], fst[par][:], [("fst", par)], [("c_fm", ti)])
        dma(S, 'sp', ct_d[t0:t0 + 512].rearrange("(s p) k d -> p s k d", p=128), tst[par][:], [("tst", par)], [("c_tok", ti)])
        dma(S, 'sp', dq_d[:, :, t0:t0 + 512], dst[par][:], [("dst", par)], [("d_qkT", ti)])
        dma(S, 'sp', dr["d_v"][t0:t0 + 512, :].rearrange("(s p) d -> p s d", p=128), dvs[par][:], [("dvs", par)], [("d_v", ti)])
        if ti + 1 < ntiles:
            pre_b(ti + 1)
    ph.close()


def phase_sb(cx, nseq=NSEQ, nblk=32):
    nc, S, dr = cx.nc, cx.S, cx.dr
    ph = Phase(cx, "pd")
    ident = load_consts(ph, S, dr)
    mask = ph.sb("mask", [128, 256], F32)
    dma(S, 'sp', mask[:], dr["sb_mask"], (), ["mask"])
    ones = ph.sb("ones", [128, 256], F32)
    memset(S, 'pool', ones[:], 1.0, ["ones"])
    one1 = ph.sb("one1", [128, 1], F32)
    memset(S, 'pool', one1[:], 1.0, ["one1"])
    qT = ph.sb("qT", [128, 4, S_LEN], BF16)
    kT = ph.sb("kT", [128, 4, S_LEN], BF16)
    V = ph.sb("V", [128, 32, 512], BF16)
    ex = [ph.sb("ex%d" % i, [128, 256], F32) for i in range(8)]
    sp = [ph.sb("sp%d" % i, [128, 256], F32) for i in range(8)]
    Rc = [ph.sb("Rc%d" % i, [128, 256], F32) for i in range(8)]
    lw = [ph.sb("lw%d" % i, [128, 256], F32) for i in range(8)]
    wm = [ph.sb("wm%d" % i, [128, 256], BF16) for i in range(8)]
    wT = [ph.sb("wT%d" % i, [128, 2, 128], BF16) for i in range(8)]
    do_b = ph.sb("do_b", [128, 512], BF16)
    dT = [ph.sb("dT%d" % i, [128, 4, 512], BF16) for i in range(2)]
    pz = [ph.ps("pz%d" % i, [128, 256], F32) for i in range(4)]
    pwt = [ph.ps("pwt%d" % i, [128, 2, 128], BF16) for i in range(2)]
    po = ph.ps("po", [128, 8, 64], F32)
    ptp = ph.ps("ptp", [128, 4, 128], BF16)
    qk_d = dr["d_qkT"].rearrange("c p t -> p c t")
    cat_d = dr["catT1"].rearrange("(c p) t -> p c t", p=128)
    hn = 0
    for b in range(nseq):
        tb = b * S_LEN
        rk = [("d_qkT", i) for i in range(b * 8, b * 8 + 8)]
        for c in range(4):
            dma(S, 'sp', qT[:, c, :], qk_d[:, c, tb:tb + S_LEN], rk, ["qT"])
            dma(S, 'sp', kT[:, c, :], qk_d[:, 4 + c, tb:tb + S_LEN], rk, ["kT"])
        for c in range(4):
            dma(S, 'sp', V[:, c * 8:(c + 1) * 8, :],
                dr["d_v"][tb + c * 1024: tb + (c + 1) * 1024, :].rearrange("(s p) d -> p s d", p=128),
                [("d_v", i) for i in range(b * 8, b * 8 + 8)], ["V"])
        units = [(blk, h) for blk in range(nblk) for h in range(8)]
        bufi = {}

        def geom(blk):
            nk = 1 if blk == 0 else 2
            return nk, nk * 128, (blk + 1 - nk) * 128, 256 - nk * 128

        def stA(u):
            nonlocal hn
            blk, h = u
            nk, W_, k0, m0 = geom(blk)
            pr, base = h // 2, 64 * (h % 2)
            par = hn % 8
            z, zk = pz[hn % 4], ("pz", hn % 4)
            bufi[u] = (par, hn % 2)
            hn += 1
            mm(S, z[:, 0:W_], qT[base:base + 64, pr, blk * 128:(blk + 1) * 128], kT[base:base + 64, pr, k0:k0 + W_],
               True, True, ["qT", "kT"], [zk])
            act(S, ex[par][:, 0:W_], z[:, 0:W_], AF.Exp, [zk], [("ex", par)], scale=0.125)
            act(S, sp[par][:, 0:W_], ex[par][:, 0:W_], AF.Ln, [("ex", par), "one1"], [("sp", par)], bias=one1[:, 0:1])
            tt(S, 'dve', sp[par][:, 0:W_], sp[par][:, 0:W_], mask[:, m0:256], ALU.mult, [("sp", par), "mask"], [("sp", par)])
            S.op('dve', (lambda o, d0, d1: (lambda e: e.tensor_tensor_scan(o, d0, d1, 0.0, ALU.mult, ALU.add)))(
                Rc[par][:, 0:W_][:, ::-1], ones[:, 0:W_], sp[par][:, 0:W_][:, ::-1]), [("sp", par), "ones"], [("Rc", par)])
            stt(S, lw[par][:, 0:W_], z[:, 0:W_], 0.125, Rc[par][:, 0:W_], ALU.mult, ALU.subtract, [zk, ("Rc", par)], [("lw", par)])
            act(S, lw[par][:, 0:W_], lw[par][:, 0:W_], AF.Exp, [("lw", par)], [("lw", par)])
            tt(S, 'pool', wm[par][:, 0:W_], lw[par][:, 0:W_], mask[:, m0:256], ALU.mult, [("lw", par), "mask"], [("wm", par)])

        def stB(u):
            blk, h = u
            nk, W_, k0, m0 = geom(blk)
            par, p2 = bufi[u]
            pw2, pwk = pwt[p2], ("pwt", p2)
            for n in range(nk):
                tr(S, pw2[:, n, :], wm[par][:, n * 128:(n + 1) * 128], ident[:], [("wm", par), "ident"], [pwk])
            cp(S, 'act', wT[par][:, 0:nk, :], pw2[:, 0:nk, :], [pwk], [("wT", par)])

        def stC(u):
            blk, h = u
            nk, W_, k0, m0 = geom(blk)
            par, p2 = bufi.pop(u)
            for n in range(nk):
                kb = blk + 1 - nk + n
                mm(S, po[:, h, :], wT[par][:, n, :], V[:, kb, h * 64:(h + 1) * 64], n == 0, n == nk - 1,
                   [("wT", par), "V"], ["po"])
            if h == 7:
                epi(blk)

        def epi(blk):
            cp(S, 'dve', do_b[:], po[:].rearrange("p h d -> p (h d)"), ["po"], ["do_b"])
            for c in range(4):
                tr(S, ptp[:, c, :], do_b[:, c * 128:(c + 1) * 128], ident[:], ["do_b", "ident"], ["ptp"])
            apar = (blk // 4) % 2
            cp(S, 'act', dT[apar][:, :, (blk % 4) * 128:(blk % 4 + 1) * 128], ptp[:], ["ptp"], [("dT", apar)])
            if blk % 4 == 3 or blk == nblk - 1:
                q0 = (blk // 4) * 4
                n = (blk - q0 + 1) * 128
                dma(S, 'sp', cat_d[:, 4:8, tb + q0 * 128: tb + q0 * 128 + n], dT[apar][:, :, 0:n], [("dT", apar)],
                    [("catT1", "d", b, blk // 4)])

        for i in range(len(units) + 4):
            if i < len(units):
                stA(units[i])
            if 0 <= i - 2 < len(units):
                stB(units[i - 2])
            if 0 <= i - 4 < len(units):
                stC(units[i - 4])
    ph.close()


def phase_ret(cx, nseq=NSEQ, nblk=32):
    nc, S, dr = cx.nc, cx.S, cx.dr
    ph = Phase(cx, "pc")
    ident = load_consts(ph, S, dr)
    decT = ph.sb("decT", [128, 4, 128], F32)
    dma(S, 'sp', decT[:], dr["ret_decT"], (), ["decT"])
    ng = ph.sb("ng", [128, 512], F32)
    rn = dr["ret_norm_g"]
    dma(S, 'sp', ng[:], bass.AP(tensor=rn.tensor, offset=rn.offset, ap=[[0, 128], [1, 512]]), (), ["ng"])
    mhalf = ph.sb("mhalf", [128, 4], F32)
    memset(S, 'pool', mhalf[:], -0.5, ["mhalf"])
    SEG = 8
    fm = [ph.sb("fm%d" % i, [128, 12, SEG * 128], BF16) for i in range(2)]
    tk = [ph.sb("tk%d" % i, [128, SEG, 3, 512], BF16) for i in range(2)]
    st32 = [ph.sb("st32_%d" % h, [128, 128], F32) for h in range(4)]
    stb = [[ph.sb("stb_%d_%d" % (h, i), [128, 128], BF16) for i in range(2)] for h in range(4)]
    PT = [ph.sb("PT%d" % i, [128, 4, 128], BF16) for i in range(2)]
    osb = ph.sb("osb", [128, 4, 128], F32)
    sq = ph.sb("sq", [128, 4, 128], F32)
    s12 = ph.sb("s12", [128, 2, 4], F32)
    mv = ph.sb("mv", [128, 3, 4], F32)
    cn = ph.sb("cn", [128, 512], F32)
    co_b = ph.sb("co_b", [128, 512], BF16)
    cT = [ph.sb("cT%d" % i, [128, 4, 512], BF16) for i in range(2)]
    pS = [ph.ps("pS%d" % i, [128, 4, 128], F32) for i in range(1)]
    pO = [ph.ps("pO%d" % i, [128, 4, 128], F32) for i in range(2)]
    pK = [ph.ps("pK%d" % i, [128, 128], F32) for i in range(4)]
    ptp = ph.ps("ptp", [128, 4, 128], BF16)
    cf_d = dr["c_fm"].rearrange("k p t -> p k t")
    ct_d = dr["c_tok"]
    cat_d = dr["catT1"].rearrange("(c p) t -> p c t", p=128)
    gam = [1.0 - 2.0 ** (-5.0 - h) for h in range(4)]
    sn = 0
    kn = 0
    for b in range(nseq):
        tb = b * S_LEN
        for h in range(4):
            memset(S, 'pool', st32[h][:], 0.0, [("st32", h)])
            memset(S, 'pool', stb[h][0][:], 0.0, [("stb", h, 0)])
        scnt = [0, 0, 0, 0]
        for blk in range(nblk):
            seg, sb_ = blk // SEG, blk % SEG
            sp_ = seg % 2
            if sb_ == 0:
                t0 = tb + seg * SEG * 128
                nb = min(SEG, nblk - seg * SEG)
                rkeys = [("c_fm", (t0 // 512) + i) for i in range(2)]
                dma(S, 'sp', fm[sp_][:, :, 0:nb * 128], cf_d[:, :, t0:t0 + nb * 128], rkeys, [("fm", sp_)])
                dma(S, 'sp', tk[sp_][:, 0:nb], ct_d[t0:t0 + nb * 128].rearrange("(s p) k d -> p s k d", p=128),
                    [("c_tok", (t0 // 512) + i) for i in range(2)], [("tk", sp_)])
            cs = slice(sb_ * 128, (sb_ + 1) * 128)
            bp = blk % 2
            po_ = pO[bp]
            pok = ("pO", bp)
            for h in range(4):
                mm(S, pS[0][:, h, :], fm[sp_][:, 8 + h, cs], fm[sp_][:, 0 + h, cs], True, True, [("fm", sp_)], ["pS"])
            tt(S, 'dve', PT[bp][:], pS[0][:], decT[:], ALU.mult, ["pS", "decT"], [("PT", bp)])
            for h in range(4):
                hs = slice(h * 128, (h + 1) * 128)
                mm(S, po_[:, h, :], PT[bp][:, h, :], tk[sp_][:, sb_, 1, hs], h == 0, False, [("PT", bp), ("tk", sp_)], [pok])
            for half in range(2):
                ps_ = slice(half * 64, (half + 1) * 64)
                kp = kn % 2
                kn += 1
                for h in range(4):
                    hs = slice(h * 128, (h + 1) * 128)
                    cur = scnt[h] % 2
                    mm(S, po_[ps_, h, :], fm[sp_][:, 4 + h, sb_ * 128 + half * 64: sb_ * 128 + (half + 1) * 64], stb[h][cur][:],
                       False, h == 3, [("fm", sp_), ("stb", h, cur)], [pok])
                    mm(S, pK[h][:], tk[sp_][ps_, sb_, 0, hs], tk[sp_][ps_, sb_, 1, hs], True, True, [("tk", sp_)], [("pK", h)])
                    stt(S, st32[h][:], st32[h][:], gam[h] ** 64, pK[h][:], ALU.mult, ALU.add, [("pK", h), ("st32", h)], [("st32", h)])
                    cp(S, 'act', stb[h][1 - cur][:], st32[h][:], [("st32", h)], [("stb", h, 1 - cur)])
                    scnt[h] += 1
            cp(S, 'act', osb[:], po_[:], [pok], ["osb"])
            S.op('dve', lambda e: e.reduce_sum(s12[:, 0, :], osb[:], axis=AX.X), ["osb"], [("s12", 0)])
            tt(S, 'pool', sq[:], osb[:], osb[:], ALU.mult, ["osb"], ["sq"])
            S.op('dve', lambda e: e.reduce_sum(s12[:, 1, :], sq[:], axis=AX.X), ["sq"], [("s12", 1)])
            ts(S, 'dve', mv[:, 0, :], s12[:, 0, :], 1.0 / 128, None, ALU.mult, None, [("s12", 0)], [("mv", 0)])
            tt(S, 'dve', mv[:, 1, :], mv[:, 0, :], mv[:, 0, :], ALU.mult, [("mv", 0)], [("mv", 1)])
            stt(S, mv[:, 2, :], s12[:, 1, :], 1.0 / 128, mv[:, 1, :], ALU.mult, ALU.subtract, [("s12", 1), ("mv", 1)], [("mv", 2)])
            ts(S, 'dve', mv[:, 2, :], mv[:, 2, :], EPS, None, ALU.add, None, [("mv", 2)], [("mv", 2)])
            tt(S, 'pool', mv[:, 2, :], mv[:, 2, :], mhalf[:], ALU.pow, [("mv", 2), "mhalf"], [("mv", 2)])
            cn3 = cn[:].rearrange("p (h d) -> p h d", h=4)
            tt(S, 'dve', cn3, osb[:], apx(mv[:, 0, :], [[0, 128]]), ALU.subtract, ["osb", ("mv", 0)], ["cn"])
            tt(S, 'dve', cn3, cn3, apx(mv[:, 2, :], [[0, 128]]), ALU.mult, ["cn", ("mv", 2)], ["cn"])
            tt(S, 'pool', cn[:], cn[:], ng[:], ALU.mult, ["cn", "ng"], ["cn"])
            tt(S, 'pool', co_b[:], cn[:], tk[sp_][:, sb_, 2, :], ALU.mult, ["cn", ("tk", sp_)], ["co_b"])
            for c in range(4):
                tr(S, ptp[:, c, :], co_b[:, c * 128:(c + 1) * 128], ident[:], ["co_b", "ident"], ["ptp"])
            apar = (blk // 4) % 2
            cp(S, 'act', cT[apar][:, :, (blk % 4) * 128:(blk % 4 + 1) * 128], ptp[:], ["ptp"], [("cT", apar)])
            if blk % 4 == 3 or blk == nblk - 1:
                q0 = (blk // 4) * 4
                n = (blk - q0 + 1) * 128
                dma(S, 'sp', cat_d[:, 0:4, tb + q0 * 128: tb + q0 * 128 + n], cT[apar][:, :, 0:n], [("cT", apar)],
                    [("catT1", "c", b, blk // 4)])
    ph.close()


def fap(t, off, dims, p0=0, pn=None):
    base = t[:]
    pstep, pcnt = base.ap[0]
    if pn is None:
        pn = pcnt - p0
    return bass.AP(tensor=base.tensor, offset=base.offset + p0 * pstep + off, ap=[[pstep, pn]] + [list(d) for d in dims])


def phase_s5(cx, nseq=NSEQ):
    nc, S, dr = cx.nc, cx.S, cx.dr
    ph = Phase(cx, "pb")
    ident = load_consts(ph, S, dr)
    identf = ph.sb("identf", [128, 128], F32)
    dma(S, 'sp', identf[:], dr["ident"], (), ["identf"])
    W_intra = ph.sb("W_intra", [128, 32, 2, 256], BF16)
    W_BU = ph.sb("W_BU", [128, 32, 2, 2, 64], BF16)
    W_CX = ph.sb("W_CX", [64, 2, 32, 256], BF16)
    Aa = ph.sb("Aa", [64, 2, 32], F32)
    A2 = ph.sb("A2", [64, 2, 32], F32)
    Wg = load_w(S, ph, "Wg", dr["s5_w_glu"], 512, 512)
    bg = ph.sb("bg", [128, 4], F32)
    dma(S, 'sp', bg[:], dr["s5_bgT"], (), ["bg"])
    memset(S, 'pool', W_intra[:], 0.0, ["W_intra"])

    pp = Phase(cx, "pbp")
    cnt = [0]

    def T(shape=(64, 32)):
        cnt[0] += 1
        return pp.sb("t%d" % cnt[0], list(shape), F32)

    def k(t):
        return t.name if hasattr(t, "name") else id(t)

    def mul(o, a, b_):
        tt(S, 'dve', o[:], a[:], b_[:], ALU.mult, [k(a), k(b_)], [k(o)])

    def add(o, a, b_):
        tt(S, 'dve', o[:], a[:], b_[:], ALU.add, [k(a), k(b_)], [k(o)])

    def sub(o, a, b_):
        tt(S, 'dve', o[:], a[:], b_[:], ALU.subtract, [k(a), k(b_)], [k(o)])

    def tsa(o, a, s1, s2, op0, op1):
        ts(S, 'dve', o[:], a[:], s1, s2, op0, op1, [k(a)], [k(o)])

    lre, lim, ldt = T(), T(), T()
    dma(S, 'sp', lre[:], dr["s5_lamT_re"], (), [k(lre)])
    dma(S, 'sp', lim[:], dr["s5_lamT_im"], (), [k(lim)])
    dma(S, 'sp', ldt[:], dr["s5_ldt_bc"], (), [k(ldt)])
    dt = T()
    act(S, dt[:], ldt[:], AF.Exp, [k(ldt)], [k(dt)])
    lr, ang, mag = T(), T(), T()
    mul(lr, lre, dt)
    mul(ang, lim, dt)
    act(S, mag[:], lr[:], AF.Exp, [k(lr)], [k(mag)])
    kf, r = T(), T()
    MAGIC = 12582912.0
    tsa(kf, ang, 1.0 / (2 * np.pi), None, ALU.mult, None)
    tsa(kf, kf, MAGIC, None, ALU.add, None)
    tsa(kf, kf, MAGIC, None, ALU.subtract, None)
    C1 = 6.28125
    C2 = 2 * np.pi - C1
    stt(S, r[:], kf[:], -C1, ang[:], ALU.mult, ALU.add, [k(kf), k(ang)], [k(r)])
    stt(S, r[:], kf[:], -C2, r[:], ALU.mult, ALU.add, [k(kf), k(r)], [k(r)])
    y, y2, sn, cs, tmp = T(), T(), T(), T(), T()
    tsa(y, r, 0.125, None, ALU.mult, None)
    mul(y2, y, y)
    f = [1.0]
    for i in range(1, 12):
        f.append(f[-1] * i)
    tsa(sn, y2, 1.0 / f[9], -1.0 / f[7], ALU.mult, ALU.add)
    for c_ in (1.0 / f[5], -1.0 / f[3], 1.0):
        mul(sn, sn, y2)
        tsa(sn, sn, c_, None, ALU.add, None)
    mul(sn, sn, y)
    tsa(cs, y2, -1.0 / f[10], 1.0 / f[8], ALU.mult, ALU.add)
    for c_ in (-1.0 / f[6], 1.0 / f[4], -0.5, 1.0):
        mul(cs, cs, y2)
        tsa(cs, cs, c_, None, ALU.add, None)
    for _ in range(3):
        mul(tmp, sn, cs)
        mul(cs, sn, sn)
        tsa(sn, tmp, 2.0, None, ALU.mult, None)
        tsa(cs, cs, -2.0, 1.0, ALU.mult, ALU.add)
    ar, ai = T(), T()
    mul(ar, mag, cs)
    mul(ai, mag, sn)
    am1, den, fre, fim, t1, t2 = T(), T(), T(), T(), T(), T()
    tsa(am1, ar, -1.0, None, ALU.add, None)
    mul(den, lre, lre)
    mul(t1, lim, lim)
    add(den, den, t1)
    S.op('dve', lambda e: e.reciprocal(den[:], den[:]), [k(den)], [k(den)])
    mul(t1, am1, lre)
    mul(t2, ai, lim)
    add(fre, t1, t2)
    mul(fre, fre, den)
    mul(t1, ai, lre)
    mul(t2, am1, lim)
    sub(fim, t1, t2)
    mul(fim, fim, den)
    pr = pp.sb("pr", [64, 32, 17], F32)
    pi_ = pp.sb("pi", [64, 32, 17], F32)
    memset(S, 'dve', pr[:, :, 0:1], 1.0, ["pr"])
    memset(S, 'dve', pi_[:, :, 0:1], 0.0, ["pi"])
    q1, q2 = T(), T()
    for j in range(16):
        tt(S, 'dve', q1[:], pr[:, :, j], ar[:], ALU.mult, ["pr", k(ar)], [k(q1)])
        tt(S, 'dve', q2[:], pi_[:, :, j], ai[:], ALU.mult, ["pi", k(ai)], [k(q2)])
        tt(S, 'dve', pr[:, :, j + 1], q1[:], q2[:], ALU.subtract, [k(q1), k(q2)], ["pr"])
        tt(S, 'dve', q1[:], pr[:, :, j], ai[:], ALU.mult, ["pr", k(ai)], [k(q1)])
        tt(S, 'dve', q2[:], pi_[:, :, j], ar[:], ALU.mult, ["pi", k(ar)], [k(q2)])
        tt(S, 'dve', pi_[:, :, j + 1], q1[:], q2[:], ALU.add, [k(q1), k(q2)], ["pi"])
    for ri in range(2):
        cp(S, 'dve', Aa[:, ri, :], pr[:, :, 16], ["pr"], ["Aa"])
    ts(S, 'dve', A2[:, 0, :], pi_[:, :, 16], -1.0, None, ALU.mult, None, ["pi"], ["A2"])
    cp(S, 'dve', A2[:, 1, :], pi_[:, :, 16], ["pi"], ["A2"])
    bre, bim = pp.sb("bre", [64, 32, 16], F32), pp.sb("bim", [64, 32, 16], F32)
    cre, cim = pp.sb("cre", [64, 32, 16], F32), pp.sb("cim", [64, 32, 16], F32)
    dma(S, 'sp', bre[:], dr["s5_bT_re"], (), ["bre"])
    dma(S, 'sp', bim[:], dr["s5_bT_im"], (), ["bim"])
    dma(S, 'sp', cre[:], dr["s5_cT_re"], (), ["cre"])
    dma(S, 'sp', cim[:], dr["s5_cT_im"], (), ["cim"])
    Bre, Bim, nBim, u1, u2 = [pp.sb(n_, [64, 32, 16], F32) for n_ in ("Bre", "Bim", "nBim", "u1", "u2")]
    fre_b, fim_b = apx(fre[:], [[0, 16]]), apx(fim[:], [[0, 16]])
    tt(S, 'dve', u1[:], bre[:], fre_b, ALU.mult, ["bre", k(fre)], ["u1"])
    tt(S, 'dve', u2[:], bim[:], fim_b, ALU.mult, ["bim", k(fim)], ["u2"])
    tt(S, 'dve', Bre[:], u1[:], u2[:], ALU.subtract, ["u1", "u2"], ["Bre"])
    tt(S, 'dve', u1[:], bim[:], fre_b, ALU.mult, ["bim", k(fre)], ["u1"])
    tt(S, 'dve', u2[:], bre[:], fim_b, ALU.mult, ["bre", k(fim)], ["u2"])
    tt(S, 'dve', Bim[:], u1[:], u2[:], ALU.add, ["u1", "u2"], ["Bim"])
    ts(S, 'dve', nBim[:], Bim[:], -1.0, None, ALU.mult, None, ["Bim"], ["nBim"])
    HG = 16
    pp1 = Phase(cx, "pbp1")
    T1 = pp1.sb("T1", [64, HG, 17, 16], F32)
    T2 = pp1.sb("T2", [64, HG, 17, 16], F32)
    T3 = pp1.sb("T3", [64, HG, 17, 16], F32)
    Kb = pp1.sb("Kb", [16, 32, 256], BF16)
    dbc = pp1.sb("dbc", [16, 32, 16], F32)
    dma(S, 'sp', dbc[:], dr["s5_d_bc"][0:16], (), ["dbc"])
    Dg = pp1.sb("Dg", [16, 32, 16], F32)
    tt(S, 'dve', Dg[:], dbc[:], fap(identf, 0, [(0, 32), (1, 16)], 0, 16), ALU.mult, ["dbc", "identf"], ["Dg"])
    pk = [pp1.ps("pk%d" % i, [16, 2, 256], F32) for i in range(2)]
    for gh in range(2):
        gs = slice(gh * HG, (gh + 1) * HG)
        Cre_b = fap(cre, gh * HG * 16, [(16, HG), (0, 17), (1, 16)])
        Cim_b = fap(cim, gh * HG * 16, [(16, HG), (0, 17), (1, 16)])
        pr_b = fap(pr, gh * HG * 17, [(17, HG), (1, 17), (0, 16)])
        pi_b = fap(pi_, gh * HG * 17, [(17, HG), (1, 17), (0, 16)])
        tt(S, 'dve', T1[:], Cre_b, pr_b, ALU.mult, ["cre", "pr"], ["T1"])
        tt(S, 'dve', T3[:], Cim_b, pi_b, ALU.mult, ["cim", "pi"], ["T3"])
        tt(S, 'pool', T1[:], T1[:], T3[:], ALU.subtract, ["T1", "T3"], ["T1"])
        tt(S, 'dve', T2[:], Cre_b, pi_b, ALU.mult, ["cre", "pi"], ["T2"])
        tt(S, 'dve', T3[:], Cim_b, pr_b, ALU.mult, ["cim", "pr"], ["T3"])
        tt(S, 'pool', T2[:], T2[:], T3[:], ALU.add, ["T2", "T3"], ["T2"])
        cp(S, 'act', W_CX[:, 0, gs, :].rearrange("p g (t h) -> p g t h", t=16), T1[:, :, 1:17, :], ["T1"], ["W_CX"])
        ts(S, 'pool', W_CX[:, 1, gs, :].rearrange("p g (t h) -> p g t h", t=16), T2[:, :, 1:17, :], -1.0, None, ALU.mult, None,
           ["T2"], ["W_CX"])
        for gl in range(HG):
            g = gh * HG + gl
            pkt = pk[(g // 2) % 2]
            pkk = ("pk", (g // 2) % 2)
            mm(S, pkt[:, g % 2, :], Bre[:, g, :], T1[:, gl, 0:16, :].rearrange("p t h -> p (t h)"), True, False, ["Bre", "T1"], [pkk])
            mm(S, pkt[:, g % 2, :], nBim[:, g, :], T2[:, gl, 0:16, :].rearrange("p t h -> p (t h)"), False, True, ["nBim", "T2"], [pkk])
            if g % 2 == 1:
                cp(S, 'act', Kb[:, g - 1:g + 1, 16:256], pkt[:, :, 16:256], [pkk], ["Kb"])
                tt(S, 'dve', Kb[:, g - 1:g + 1, 0:16], pkt[:, :, 0:16], Dg[:, g - 1:g + 1, :], ALU.add, [pkk, "Dg"], ["Kb"])
    dma(S, 'sp', dr["s5_kall"], Kb[:], ["Kb"], ["kall_d"])
    kd = dr["s5_kall"]
    for s in range(16):
        half, sl = s // 8, s % 8
        n = (16 - s) * 16
        dma(S, 'sp', W_intra[16 * sl:16 * sl + 16, :, half, 16 * s:256], kd[:, :, 0:n], ["kall_d", "W_intra"], ["W_intra"])
    pp1.close()
    pp2 = Phase(cx, "pbp2")
    WTb = [pp2.sb("WTb%d" % i, [64, 32, 256], BF16) for i in range(2)]
    ptw = [pp2.ps("ptw%d" % i, [128, 8, 64], BF16) for i in range(2)]
    Q1 = pp2.sb("Q1", [64, HG, 16, 16], F32)
    Q2 = pp2.sb("Q2", [64, HG, 16, 16], F32)
    for gh in range(2):
        gs = slice(gh * HG, (gh + 1) * HG)
        prr = fap(pr, gh * HG * 17 + 15, [(17, HG), (-1, 16), (0, 16)])
        pir = fap(pi_, gh * HG * 17 + 15, [(17, HG), (-1, 16), (0, 16)])
        Bre_b = fap(Bre, gh * HG * 16, [(16, HG), (0, 16), (1, 16)])
        Bim_b = fap(Bim, gh * HG * 16, [(16, HG), (0, 16), (1, 16)])
        tt(S, 'dve', Q1[:], prr, Bre_b, ALU.mult, ["pr", "Bre"], ["Q1"])
        tt(S, 'dve', Q2[:], pir, Bim_b, ALU.mult, ["pi", "Bim"], ["Q2"])
        tt(S, 'pool', WTb[0][:, gs, :].rearrange("p g (s h) -> p g s h", s=16), Q1[:], Q2[:], ALU.subtract, ["Q1", "Q2"], [("WTb", 0)])
        tt(S, 'dve', Q1[:], prr, Bim_b, ALU.mult, ["pr", "Bim"], ["Q1"])
        tt(S, 'dve', Q2[:], pir, Bre_b, ALU.mult, ["pi", "Bre"], ["Q2"])
        tt(S, 'pool', WTb[1][:, gs, :].rearrange("p g (s h) -> p g s h", s=16), Q1[:], Q2[:], ALU.add, ["Q1", "Q2"], [("WTb", 1)])
    for g2 in range(16):
        pt_ = ptw[g2 % 2]
        for gi in range(2):
            g = 2 * g2 + gi
            for half in range(2):
                for ri in range(2):
                    tr(S, pt_[:, gi * 4 + half * 2 + ri, :], WTb[ri][:, g, half * 128:(half + 1) * 128], ident[0:64, 0:64],
                       [("WTb", ri), "ident"], [("ptw", g2 % 2)])
        cp(S, 'act' if g2 % 2 == 0 else 'dve', W_BU[:, 2 * g2:2 * g2 + 2].rearrange("p g a r q -> p (g a r) q"), pt_[:],
           [("ptw", g2 % 2)], ["W_BU"])
    pp2.close()
    pp.close()

    RA = ph.sb("RA", [128, 16384], BF16)
    RB = ph.sb("RB", [128, 16384], BF16)
    RC = ph.sb("RC", [128, 16448], BF16)
    Ublk = RA[:].rearrange("p (cb s ch) -> p cb s ch", cb=2, s=16)
    Xb = RA[0:64, :].rearrange("p (r g c) -> p r g c", r=2, g=32)
    UT = RB[:].rearrange("p (g a c) -> p g a c", g=32, a=2)
    ygT = RB[:].rearrange("p (k t) -> p k t", k=4)
    GX = RC[0:64, :].bitcast(F32).rearrange("p (r g c) -> p r g c", r=2, g=16)
    Ytok = RC[:, 0:16384].rearrange("p (cb t ch) -> p cb t ch", cb=2, t=16)
    P1 = ph.sb("P1", [64, 2, 16], F32)
    P2 = ph.sb("P2", [64, 2, 16], F32)
    sg = [ph.sb("sg%d" % i, [128, 512], F32) for i in range(2)]
    boT = [ph.sb("boT%d" % i, [128, 4, 512], BF16) for i in range(2)]
    ptu = [ph.ps("ptu%d" % i, [128, 8, 128], BF16) for i in range(2)]
    pG = [ph.ps("pG%d" % i, [64, 2, 256], F32) for i in range(2)]
    pY = [ph.ps("pY%d" % i, [128, 256], F32) for i in range(2)]
    pL = [ph.ps("pL%d" % i, [128, 512], F32) for i in range(2)]
    cat_d = dr["catT0"].rearrange("(c p) t -> p c t", p=128)
    for b in range(nseq):
        tb = b * S_LEN
        dma(S, 'sp', RA[:].rearrange("p (cb x) -> p cb x", cb=2),
            dr["u0"][tb:tb + S_LEN, :].rearrange("(cb c s) ch -> c cb (s ch)", cb=2, c=128),
            [("u0", i) for i in range(b * 8, b * 8 + 8)], ["RA"])
        for cb in range(2):
            cp(S, 'dve' if cb == 0 else 'pool', fap(RC, cb * 8192, [(256, 32), (16, 16), (1, 16)]),
               fap(RA, cb * 8192, [(16, 32), (512, 16), (1, 16)]), ["RA"], ["RC"])
        for g2 in range(16):
            pt_ = ptu[g2 % 2]
            for gi in range(2):
                g = 2 * g2 + gi
                for half in range(2):
                    for cb in range(2):
                        tr(S, pt_[:, gi * 4 + half * 2 + cb, :], fap(RC, cb * 8192 + g * 256 + half * 128, [(1, 128)]), ident[:],
                           ["RC", "ident"], [("ptu", g2 % 2)])
            cp(S, 'act' if g2 % 2 == 0 else 'dve', UT[:, 2 * g2:2 * g2 + 2].rearrange("p g a c -> p (g a c)"),
               pt_[:].rearrange("p a c -> p (a c)"), [("ptu", g2 % 2)], ["RB"])
        for gh in range(2):
            memset(S, 'pool', GX[:, :, :, 0:1], 0.0, ["RC"])
            for gl in range(16):
                g = gh * 16 + gl
                pg_ = pG[gl % 2]
                for ri in range(2):
                    for half in range(2):
                        mm(S, pg_[:, ri, :], W_BU[:, g, half, ri, :], UT[:, g, half, :], half == 0, half == 1,
                           ["W_BU", "RB"], [("pG", gl % 2)])
                cp(S, 'act' if gl % 2 == 0 else 'dve', GX[:, :, gl, 1:257], pg_[:], [("pG", gl % 2)], ["RC"])
            Aa_h = Aa[:, :, gh * 16:(gh + 1) * 16]
            A2_h = A2[:, :, gh * 16:(gh + 1) * 16]
            for c in range(256):
                Xc = GX[:, :, :, c]
                Xs = GX[:, ::-1, :, c]
                tt(S, 'dve', P1[:], Xc, Aa_h, ALU.mult, ["RC", "Aa"], ["P1"])
                tt(S, 'dve', P2[:], Xs, A2_h, ALU.mult, ["RC", "A2"], ["P2"])
                tt(S, 'dve', P1[:], P1[:], P2[:], ALU.add, ["P1", "P2"], ["P1"])
                tt(S, 'dve', GX[:, :, :, c + 1], GX[:, :, :, c + 1], P1[:], ALU.add, ["RC", "P1"], ["RC"])
            cp(S, 'pool', Xb[:, :, gh * 16:(gh + 1) * 16, :], GX[:, :, :, 0:256], ["RC"], ["RA"])
        yn = 0
        for g in range(32):
            for cb in range(2):
                py = pY[yn % 2]
                pyk = ("pY", yn % 2)
                yn += 1
                cs_ = slice(cb * 128, (cb + 1) * 128)
                mm(S, py[:], UT[:, g, 0, cs_], W_intra[:, g, 0, :], True, False, ["RB", "W_intra"], [pyk])
                mm(S, py[:], UT[:, g, 1, cs_], W_intra[:, g, 1, :], False, False, ["RB", "W_intra"], [pyk])
                mm(S, py[:], Xb[:, 0, g, cs_], W_CX[:, 0, g, :], False, False, ["RA", "W_CX"], [pyk])
                mm(S, py[:], Xb[:, 1, g, cs_], W_CX[:, 1, g, :], False, True, ["RA", "W_CX"], [pyk])
                act(S, Ytok[:, cb, :, 16 * g:16 * g + 16], py[:].rearrange("p (t h) -> p t h", t=16), AF.Gelu_apprx_tanh, [pyk], ["RC"])
        tn = 0
        for cb in range(2):
            for kc in range(4):
                for th in range(2):
                    pt_ = ptu[tn % 2]
                    ptk = ("ptu", tn % 2)
                    tn += 1
                    for tl in range(8):
                        t_ = th * 8 + tl
                        tr(S, pt_[:, tl, :], Ytok[:, cb, t_, kc * 128:(kc + 1) * 128], ident[:], ["RC", "ident"], [ptk])
                    dst = fap(RB, kc * 4096 + cb * 2048 + th * 8, [(1, 8), (16, 128)])
                    cp(S, 'act' if tn % 2 == 0 else 'dve', dst, pt_[:], [ptk], ["RB"])
        ln = 0
        for ti in range(8):
            tsl = slice(ti * 512, (ti + 1) * 512)
            bp = ti % 2
            for oc in range(4):
                pl = pL[ln % 2]
                plk = ("pL", ln % 2)
                sgt = sg[ln % 2]
                sgk = ("sg", ln % 2)
                ln += 1
                for kc in range(4):
                    mm(S, pl[:], Wg[:, kc, oc * 128:(oc + 1) * 128], ygT[:, kc, tsl], kc == 0, kc == 3, [("Wg", kc), "RB"], [plk])
                act(S, sgt[:], pl[:], AF.Sigmoid, [plk, "bg"], [sgk], bias=bg[:, oc:oc + 1])
                tt(S, 'dve', boT[bp][:, oc, :], sgt[:], ygT[:, oc, tsl], ALU.mult, [sgk, "RB"], [("boT", bp)])
            dma(S, 'sp', cat_d[:, 4:8, tb + ti * 512: tb + (ti + 1) * 512], boT[bp][:], [("boT", bp)], [("catT0", "b", b, ti)])
    ph.close()


def prep_core(inp, core):
    b0 = 2 * core
    d = {}
    d["x"] = np.ascontiguousarray(inp["x"][b0:b0 + 2].reshape(8192, 1024))
    c2 = inp["c"][b0:b0 + 2]
    d["cT"] = np.ascontiguousarray(c2.T.reshape(8, 128, 2).transpose(1, 0, 2))
    d["ada_w"] = inp["ada_w"]
    d["ada_b"] = inp["ada_b"]
    d["ada_bT"] = np.ascontiguousarray(inp["ada_b"].reshape(2, 48, 128).transpose(0, 2, 1))
    lg = np.stack([inp["ln_mix_g"], inp["ln_ffn_g"]], 0)
    d["ln_gT"] = np.ascontiguousarray(lg.reshape(2, 2, 8, 128).transpose(0, 1, 3, 2))
    return d

def prep_l0(inp, d):
    d["ab_w_in"] = inp["ab_w_in"][0]
    d["qkg"] = np.ascontiguousarray(np.stack([np.tile(inp["a_q_gain"][0], 2), np.tile(inp["a_k_gain"][0], 2)], 1))
    d["ident"] = np.eye(128, dtype=np.float32)
    bo = np.zeros((128, 128), np.float32); bo[:64, :64] = 1; bo[64:, 64:] = 1
    d["blockones"] = bo
    return d

def prep_consts(d):
    d["ident"] = np.eye(128, dtype=np.float32)
    pos = np.arange(4096, dtype=np.float64)[:, None]
    fr = 10000.0 ** (-np.arange(64, dtype=np.float64) / 64)[None, :]
    ang = pos * fr
    d["rot"] = np.stack([np.cos(ang), np.sin(ang)], 1).astype(np.float32)
    lg = np.log(1.0 - 2.0 ** (-5.0 - np.arange(4, dtype=np.float64)))
    idx = np.arange(128) % 64
    d["ret_qd"] = np.exp(lg[None, :] * (idx[:, None] + 1.0)).astype(np.float32)
    d["ret_kd"] = (np.exp(lg[None, :] * (63.0 - idx[:, None])) * 128.0 ** -0.5).astype(np.float32)
    j = np.arange(128)[:, None]; i = np.arange(128)[None, :]
    same = (j // 64) == (i // 64)
    dec = np.exp(lg[:, None, None] * np.abs(i - j)[None]) * same[None]
    d["ret_decT"] = np.ascontiguousarray(dec.transpose(1, 0, 2)).astype(np.float32)
    m = np.ones((128, 256), np.float32)
    m[:, 128:] = (np.arange(128)[None, :] < np.arange(128)[:, None]).astype(np.float32)
    d["sb_mask"] = m
    return d

def prep_s5(inp, d):
    d["s5_lamT_re"] = np.ascontiguousarray(inp["s5_lambda_re"][0].T)
    d["s5_lamT_im"] = np.ascontiguousarray(inp["s5_lambda_im"][0].T)
    d["s5_ldt_bc"] = np.ascontiguousarray(np.broadcast_to(inp["s5_log_dt"][0][None, :], (64, 32)))
    d["s5_bT_re"] = np.ascontiguousarray(inp["s5_b_re"][0].transpose(1, 0, 2))
    d["s5_bT_im"] = np.ascontiguousarray(inp["s5_b_im"][0].transpose(1, 0, 2))
    d["s5_cT_re"] = np.ascontiguousarray(inp["s5_c_re"][0].transpose(2, 0, 1))
    d["s5_cT_im"] = np.ascontiguousarray(inp["s5_c_im"][0].transpose(2, 0, 1))
    d["s5_d_bc"] = np.ascontiguousarray(np.broadcast_to(inp["s5_d"][0][None], (128, 32, 16)))
    d["s5_w_glu"] = inp["s5_w_glu"][0]
    d["s5_bgT"] = np.ascontiguousarray(inp["s5_b_glu"][0].reshape(4, 128).T)
    return d


IN_SPECS = [
    ("x", [8192, 1024]), ("cT", [128, 8, 2]), ("ada_w", [2, 1024, 6144]), ("ada_b", [2, 6144]), ("ada_bT", [2, 128, 48]),
    ("ln_gT", [2, 2, 128, 8]), ("ab_w_in", [1024, 2048]), ("qkg", [128, 2]), ("ident", [128, 128]), ("blockones", [128, 128]),
    ("rel_bias", [8, 257]),
    ("s5_lamT_re", [64, 32]), ("s5_lamT_im", [64, 32]), ("s5_ldt_bc", [64, 32]), ("s5_bT_re", [64, 32, 16]), ("s5_bT_im", [64, 32, 16]),
    ("s5_cT_re", [64, 32, 16]), ("s5_cT_im", [64, 32, 16]), ("s5_d_bc", [128, 32, 16]), ("s5_w_glu", [512, 512]), ("s5_bgT", [128, 4]),
    ("ab_w_out", [1024, 1024]), ("ffn_w_in", [2, 1024, 5632]), ("ffn_w_out", [2, 2816, 1024]),
    ("cd_w_in", [1024, 3584]), ("cd_w_out", [1024, 1024]), ("ret_norm_g", [512]),
    ("rot", [4096, 2, 64]), ("ret_qd", [128, 4]), ("ret_kd", [128, 4]), ("ret_decT", [128, 4, 128]), ("sb_mask", [128, 256]),
]


def build_program():
    nc = bass.Bass("TRN2", target_bir_lowering=False)
    cx = Ctx(nc)
    for n_, s_ in IN_SPECS:
        cx.dram_in(n_, s_)
    cx.dram_out("out", [T_CORE, 1024])
    cx.dram_scr("modfm", [2, 128, 4, 8, 2], F32)
    cx.dram_scr("gbc", [2, 2, 2, 128, 1024], F32)
    cx.dram_scr("qkT", [8, 128, T_CORE], BF16)
    cx.dram_scr("v0", [T_CORE, 512], BF16)
    cx.dram_scr("u0", [T_CORE, 512], BF16)
    cx.dram_scr("relext", [8, 1024], F32)
    cx.dram_scr("s5_kall", [16, 32, 256], BF16)
    cx.dram_scr("catT0", [1024, T_CORE], BF16)
    cx.dram_scr("x1", [T_CORE, 1024], F32)
    cx.dram_scr("c_fm", [12, 128, T_CORE], BF16)
    cx.dram_scr("c_tok", [T_CORE, 3, 512], BF16)
    cx.dram_scr("d_qkT", [8, 128, T_CORE], BF16)
    cx.dram_scr("d_v", [T_CORE, 512], BF16)
    cx.dram_scr("catT1", [1024, T_CORE], BF16)
    with cx.st:
        phase_adaln(cx)
        phase_p1_l0(cx)
        phase_attn(cx)
        phase_s5(cx)
        phase_p3(cx, 0, "catT0", "x", "x1", "ab_w_out")
        phase_p1_l1(cx, "x1")
        phase_sb(cx)
        phase_ret(cx)
        phase_p3(cx, 1, "catT1", "x1", "out", "cd_w_out", final=True)
    return nc


def prep_all(inp, core):
    d = prep_core(inp, core)
    prep_l0(inp, d)
    prep_consts(d)
    prep_s5(inp, d)
    d["rel_bias"] = inp["a_rel_bias"][0]
    d["ab_w_out"] = inp["ab_w_out"][0]
    d["ffn_w_in"] = inp["ffn_w_in"]
    d["ffn_w_out"] = inp["ffn_w_out"]
    d["cd_w_in"] = inp["cd_w_in"][0]
    d["cd_w_out"] = inp["cd_w_out"][0]
    d["ret_norm_g"] = inp["ret_norm_g"][0]
    return {k_: np.ascontiguousarray(np.asarray(d[k_], dtype=np.float32)) for k_, _ in IN_SPECS}


def kernel(**inputs):
    inp = {k_: np.asarray(v_) for k_, v_ in inputs.items()}
    nc = build_program()
    in_maps = [prep_all(inp, core) for core in range(8)]
    res = run_bass_kernel_spmd(nc, in_maps, core_ids=list(range(8)))
    outs = [np.asarray(res.results[i]["out"]).reshape(2, S_LEN, 1024) for i in range(8)]
    return np.concatenate(outs, axis=0).astype(np.float32)
```

```python
import contextlib
from contextlib import ExitStack
import numpy as np
import concourse.bass as bass
import concourse.mybir as mybir
from concourse.bass_utils import run_bass_kernel_spmd

F32 = mybir.dt.float32
BF16 = mybir.dt.bfloat16
I32 = mybir.dt.int32
AF = mybir.ActivationFunctionType
ALU = mybir.AluOpType
AX = mybir.AxisListType

ENGS = ['pe', 'act', 'dve', 'pool', 'sp']
NDS = 6


class Sched:
    def __init__(self, nc, stack, same_engine_sync=True):
        self.nc = nc
        self.ops = {e: [] for e in ENGS}
        self.cnt = {e: 0 for e in ENGS}
        self.seen = {e: {} for e in ENGS}
        self.lastw = {}
        self.readers = {}
        self.same = same_engine_sync
        self.csem = {e: stack.enter_context(nc.semaphore("c_" + e)) for e in ['pe', 'act', 'dve', 'pool']}
        self.dsem = {q: [stack.enter_context(nc.semaphore("d_%s%d" % (q, i))) for i in range(NDS)]
                     for q in ['sp', 'pool', 'act']}
        self.dma_n = {q: 0 for q in ['sp', 'pool', 'act']}
        self.out_tokens = []

    def _deps(self, reads, writes):
        deps = []
        for k in reads:
            if k in self.lastw:
                deps.append(self.lastw[k])
        for k in writes:
            if k in self.lastw:
                deps.append(self.lastw[k])
            deps.extend(self.readers.get(k, []))
        return deps

    def _waits(self, eng, deps):
        need = {}
        for (semkey, sem, val, deng) in deps:
            if deng == eng and semkey[0] == 'c' and (eng == 'pe' or not self.same):
                continue
            if self.seen[eng].get(semkey, 0) >= val:
                continue
            if semkey not in need or need[semkey][1] < val:
                need[semkey] = (sem, val)
        for semkey, (sem, val) in need.items():
            self.seen[eng][semkey] = val
        return list(need.values())

    def _record(self, tok, reads, writes):
        for k in reads:
            self.readers.setdefault(k, []).append(tok)
        for k in writes:
            self.lastw[k] = tok
            self.readers[k] = []

    def op(self, eng, fn, reads=(), writes=()):
        deps = self._deps(reads, writes)
        waits = self._waits(eng, deps)
        self.cnt[eng] += 1
        tok = (('c', eng), self.csem[eng], self.cnt[eng], eng)
        self.ops[eng].append((waits, fn, (self.csem[eng], 1)))
        self._record(tok, reads, writes)
        return tok

    def dma(self, q, fn, reads=(), writes=(), is_output=False):
        deps = self._deps(reads, writes)
        n = self.dma_n[q]
        self.dma_n[q] += 1
        slot = n % NDS
        sem = self.dsem[q][slot]
        semkey = ('d', q, slot)
        prev = 16 * (n // NDS)
        if prev > 0:
            deps.append((semkey, sem, prev, 'dma'))
        waits = self._waits(q, deps)
        tok = (semkey, sem, prev + 16, 'dma')
        self.ops[q].append((waits, fn, (sem, 16)))
        self._record(tok, reads, writes)
        if is_output:
            self.out_tokens.append(tok)
        return tok

    def flush(self, block):
        toks = []
        for e in ['pe', 'act', 'dve', 'pool']:
            if self.cnt[e] > 0:
                toks.append((('c', e), self.csem[e], self.cnt[e], 'x'))
        for q in ['sp', 'pool', 'act']:
            n = self.dma_n[q]
            for j in range(max(0, n - NDS), n):
                slot = j % NDS
                toks.append((('d', q, slot), self.dsem[q][slot], 16 * (j // NDS + 1), 'dma'))
        for e in ENGS:
            self.ops[e].append((self._waits(e, toks), None, None))
        self.lastw = {}
        self.readers = {}

        def run(eng_name):
            lst = self.ops[eng_name]

            def body(e):
                for (waits, fn, inc) in lst:
                    for (sem, val) in waits:
                        e.wait_ge(sem, val)
                    if fn is None:
                        continue
                    ins = fn(e)
                    ins.then_inc(inc[0], inc[1])
            return body

        block.tensor(run('pe'))
        block.scalar(run('act'))
        block.vector(run('dve'))
        block.gpsimd(run('pool'))
        block.sync(run('sp'))
        self.ops = {e: [] for e in ENGS}


EPS = 1e-6
S_LEN = 4096
NSEQ = 2
T_CORE = NSEQ * S_LEN
D = 1024
FF = 2816


def apx(ap, extra):
    return bass.AP(tensor=ap.tensor, offset=ap.offset, ap=[list(a) for a in ap.ap] + [list(e) for e in extra])


class Ctx:
    def __init__(self, nc, debug_out=()):
        self.nc = nc
        self.st = ExitStack()
        self.S = Sched(nc, self.st)
        self.dr = {}
        self.debug_out = set(debug_out)
        self.uid = 0

    def dram_in(self, name, shape, dt=F32):
        self.dr[name] = self.nc.dram_tensor(name, list(shape), dt, kind="ExternalInput").ap()
        return self.dr[name]

    def dram_out(self, name, shape, dt=F32):
        self.dr[name] = self.nc.dram_tensor(name, list(shape), dt, kind="ExternalOutput").ap()
        return self.dr[name]

    def dram_scr(self, name, shape, dt):
        kind = "ExternalOutput" if name in self.debug_out else "Internal"
        self.dr[name] = self.nc.dram_tensor(name, list(shape), dt, kind=kind).ap()
        return self.dr[name]

    def flush(self):
        with self.nc.Block() as block:
            self.S.flush(block)


class Phase:
    def __init__(self, cx, name):
        self.cx = cx
        self.nc = cx.nc
        self.S = cx.S
        self.name = name
        self.st = ExitStack()

    def sb(self, name, shape, dt):
        return self.st.enter_context(self.nc.sbuf_tensor(self.name + "_" + name, list(shape), dt))

    def ps(self, name, shape, dt=F32):
        return self.st.enter_context(self.nc.psum_tensor(self.name + "_" + name, list(shape), dt))

    def close(self):
        self.cx.flush()
        self.st.close()


def mm(S, out, lhsT, rhs, start, stop, r, w):
    S.op('pe', lambda e: e.matmul(out, lhsT, rhs, start=start, stop=stop), r, w)


def tr(S, out, in_, ident, r, w):
    S.op('pe', lambda e: e.transpose(out, in_, ident), r, w)


def act(S, out, in_, func, r, w, scale=1.0, bias=None, accum_out=None, eng='act'):
    kw = {}
    if bias is not None:
        kw['bias'] = bias
    if accum_out is not None:
        kw['accum_out'] = accum_out
    S.op('act', lambda e: e.activation(out=out, in_=in_, func=func, scale=scale, **kw), r, w)


def ts(S, eng, out, in0, s1, s2, op0, op1, r, w, accum_out=None):
    if op1 is None:
        S.op(eng, lambda e: e.tensor_scalar(out, in0, s1, None, op0), r, w)
    elif accum_out is not None:
        S.op(eng, lambda e: e.tensor_scalar(out, in0, s1, s2, op0, op1, accum_out), r, w)
    else:
        S.op(eng, lambda e: e.tensor_scalar(out, in0, s1, s2, op0, op1), r, w)


def tt(S, eng, out, in0, in1, op, r, w):
    S.op(eng, lambda e: e.tensor_tensor(out, in0, in1, op), r, w)


def stt(S, out, in0, scalar, in1, op0, op1, r, w):
    S.op('dve', lambda e: e.scalar_tensor_tensor(out, in0, scalar, in1, op0, op1), r, w)


def cp(S, eng, out, in_, r, w):
    if eng == 'act':
        S.op('act', lambda e: e.copy(out, in_), r, w)
    else:
        S.op(eng, lambda e: e.tensor_copy(out, in_), r, w)


def memset(S, eng, ap, val, w):
    S.op(eng, lambda e: e.memset(ap, val), (), w)


def dma(S, q, out, in_, r, w, is_output=False, slow=False):
    if slow:
        S.dma(q, lambda e: e.dma_start(out=out, in_=in_, allow_slow_non_contiguous=True), r, w, is_output)
    else:
        S.dma(q, lambda e: e.dma_start(out=out, in_=in_), r, w, is_output)


def load_w(S, ph, name, w_dram, K, N, q='pool', nsplit=None):
    kc = K // 128
    t = ph.sb(name, [128, kc, N], BF16)
    src = w_dram.rearrange("(c p) n -> p c n", p=128)
    for c in range(kc):
        dma(S, q, t[:, c, :], src[:, c, :], (), [(name, c)])
    return t


def phase_adaln(cx):
    nc, S, dr = cx.nc, cx.S, cx.dr
    ph = Phase(cx, "p0")
    cT = ph.sb("cT", [128, 8, 2], F32)
    condT = ph.sb("condT", [128, 8, 2], BF16)
    condbc = ph.sb("condbc", [128, 8, 2, 128], BF16)
    aw = ph.sb("aw", [128, 8, 6144], BF16)
    abT = ph.sb("abT", [128, 48], F32)
    lng = ph.sb("lng", [128, 2, 8], F32)
    abbc = ph.sb("abbc", [128, 2, 1024], F32)
    modsb = ph.sb("modsb", [128, 4, 8, 2], F32)
    tmp = ph.sb("tmp", [128, 8, 2], F32)
    gsb = [ph.sb("gsb%d" % i, [128, 1024], F32) for i in range(2)]
    pm = ph.ps("pm", [128, 32, 2], F32)
    pg = [ph.ps("pg%d" % i, [128, 512], F32) for i in range(2)]

    dma(S, 'sp', cT[:], dr["cT"], (), ["cT"])
    act(S, condT[:], cT[:], AF.Silu, ["cT"], ["condT"])
    cp(S, 'dve', condbc[:], apx(condT[:], [[0, 128]]), ["condT"], ["condbc"])
    gi = 0
    for l in range(2):
        src = dr["ada_w"][l].rearrange("(c p) n -> p c n", p=128)
        for c in range(8):
            dma(S, 'pool', aw[:, c, :], src[:, c, :], (), [("aw", c)])
        dma(S, 'sp', abT[:], dr["ada_bT"][l], (), ["abT"])
        dma(S, 'sp', lng[:, 0, :], dr["ln_gT"][0, l], (), ["lng"])
        dma(S, 'sp', lng[:, 1, :], dr["ln_gT"][1, l], (), ["lng"])
        for wi, blk in enumerate((2, 5)):
            dma(S, 'sp', abbc[:, wi, :], dr["ada_b"][l:l + 1, blk * 1024:(blk + 1) * 1024].partition_broadcast(128)
                if False else apx_pb(dr["ada_b"][l, blk * 1024:(blk + 1) * 1024]), (), ["abbc"])
        for jj, blk in enumerate((0, 1, 3, 4)):
            for fc in range(8):
                col = blk * 1024 + fc * 128
                for k in range(8):
                    mm(S, pm[:, jj * 8 + fc, :], aw[:, k, col:col + 128], condT[:, k, :], k == 0, k == 7,
                       [("aw", k), "condT"], ["pm"])
        for jj, blk in enumerate((0, 1, 3, 4)):
            bias = apx(abT[:, blk * 8:(blk + 1) * 8], [[0, 2]])
            if jj in (0, 2):
                tt(S, 'dve', modsb[:, jj + 1], pm[:, jj * 8:(jj + 1) * 8, :], bias, ALU.add, ["pm", "abT"], ["modsb"])
            else:
                tt(S, 'dve', tmp[:], pm[:, jj * 8:(jj + 1) * 8, :], bias, ALU.add, ["pm", "abT"], ["tmp"])
                stt(S, modsb[:, jj - 1], tmp[:], 1.0, apx(lng[:, jj // 2, :], [[0, 2]]), ALU.add, ALU.mult,
                    ["tmp", "lng"], ["modsb"])
        dma(S, 'sp', dr["modfm"][l], modsb[:], ["modsb"], [("modfm", l)])
        for b in range(2):
            for wi, blk in enumerate((2, 5)):
                g = gsb[gi % 2]
                gk = ("gsb", gi % 2)
                for half in range(2):
                    p = pg[half]
                    col = blk * 1024 + half * 512
                    for k in range(8):
                        mm(S, p[:], condbc[:, k, b, :], aw[:, k, col:col + 512], k == 0, k == 7,
                           [("aw", k), "condbc"], [("pg", half)])
                    tt(S, 'dve', g[:, half * 512:(half + 1) * 512], p[:], abbc[:, wi, half * 512:(half + 1) * 512],
                       ALU.add, [("pg", half), "abbc"], [gk])
                dma(S, 'sp', dr["gbc"][l, b, wi], g[:], [gk], [("gbc", l, b, wi)])
                gi += 1
    ph.close()


def apx_pb(ap1d):
    return bass.AP(tensor=ap1d.tensor, offset=ap1d.offset, ap=[[0, 128]] + [list(a) for a in ap1d.ap])


class NormT:
    def __init__(self, ph, nsub, ident):
        self.ph, self.S, self.nsub, self.ident = ph, ph.S, nsub, ident
        self.junk = ph.sb("nt_junk", [128, 1024], BF16)
        self.ss = [ph.sb("nt_ss%d" % i, [128, nsub], F32) for i in range(2)]
        self.rstd = [ph.sb("nt_rstd%d" % i, [128, nsub], F32) for i in range(2)]
        self.mhalf = ph.sb("nt_mhalf", [128, nsub], F32)
        self.xn = ph.sb("nt_xn", [128, nsub, 1024], BF16)
        self.tp = [ph.ps("nt_tp%d" % i, [128, nsub * 128], BF16) for i in range(2)]
        memset(self.S, 'pool', self.mhalf[:], -0.5, ["nt_mhalf"])
        self.n = 0

    def run(self, xt, xkey, hT, hkey, A, B, b):
        self.run_a(xt, xkey)
        self.run_b(hT, hkey, A, B, b)

    def run_a(self, xt, xkey):
        S, nsub = self.S, self.nsub
        par = self.n % 2
        self.n += 1
        ss, rstd = self.ss[par], self.rstd[par]
        for s in range(nsub):
            act(S, self.junk[:], xt[:, s, :], AF.Square, [xkey], ["nt_junk", ("nt_ss", par, s)], accum_out=ss[:, s:s + 1])
        ts(S, 'dve', rstd[:], ss[:], 1.0 / 1024, EPS, ALU.mult, ALU.add, [("nt_ss", par, s) for s in range(nsub)], [("nt_rstd", par)])
        tt(S, 'pool', rstd[:], rstd[:], self.mhalf[:], ALU.pow, [("nt_rstd", par), "nt_mhalf"], [("nt_rstd", par)])
        for s in range(nsub):
            ts(S, 'dve' if s % 2 == 0 else 'pool', self.xn[:, s, :], xt[:, s, :], rstd[:, s:s + 1], None, ALU.mult, None,
               [xkey, ("nt_rstd", par)], [("nt_xn", s)])

    def run_b(self, hT, hkey, A, B, b):
        S, nsub = self.S, self.nsub
        for c in range(8):
            tp = self.tp[c % 2]
            for s in range(nsub):
                tr(S, tp[:, s * 128:(s + 1) * 128], self.xn[:, s, c * 128:(c + 1) * 128], self.ident[:],
                   [("nt_xn", s), "ident"], [("nt_tp", c % 2)])
            if c % 2 == 0:
                ts(S, 'dve', hT[:, c, :], tp[:], A[:, c, b:b + 1], B[:, c, b:b + 1], ALU.mult, ALU.add,
                   [("nt_tp", c % 2), "modAB"], [(hkey, c)])
            else:
                act(S, hT[:, c, :], tp[:], AF.Identity, [("nt_tp", c % 2), "modAB"], [(hkey, c)],
                    scale=A[:, c, b:b + 1], bias=B[:, c, b:b + 1])


def load_consts(ph, S, dr):
    ident = ph.sb("ident", [128, 128], BF16)
    dma(S, 'pool', ident[:], dr["ident"], (), ["ident"])
    return ident


def phase_p1_l0(cx, ntiles=16):
    nc, S, dr = cx.nc, cx.S, cx.dr
    ph = Phase(cx, "p1a")
    ident = load_consts(ph, S, dr)
    bones = ph.sb("bones", [128, 128], BF16)
    dma(S, 'pool', bones[:], dr["blockones"], (), ["bones"])
    W = load_w(S, ph, "W", dr["ab_w_in"], 1024, 2048)
    wkeys = [("W", c) for c in range(8)]
    modAB = ph.sb("modAB", [128, 4, 8, 2], F32)
    dma(S, 'sp', modAB[:], dr["modfm"][0], [("modfm", 0)], ["modAB"])
    qkg = ph.sb("qkg", [128, 2], F32)
    dma(S, 'sp', qkg[:], dr["qkg"], (), ["qkg"])
    cb = ph.sb("cbias", [128, 2], F32)
    memset(S, 'pool', cb[:, 0:1], 64 * EPS, ["cbias"])
    memset(S, 'pool', cb[:, 1:2], EPS, ["cbias"])
    nt = NormT(ph, 4, ident)
    xt = [ph.sb("xt%d" % i, [128, 4, 1024], F32) for i in range(2)]
    hT = [ph.sb("hT%d" % i, [128, 8, 512], BF16) for i in range(2)]
    qkst = [ph.sb("qkst%d" % i, [128, 8, 512], BF16) for i in range(2)]
    vust = [ph.sb("vust%d" % i, [128, 4, 1024], BF16) for i in range(2)]
    sqk = [ph.sb("sqk%d" % i, [128, 512], BF16) for i in range(3)]
    rs = [ph.sb("rs%d" % i, [128, 512], F32) for i in range(3)]
    pq = [ph.ps("pq%d" % i, [128, 512], F32) for i in range(3)]
    pss = [ph.ps("pss%d" % i, [128, 512], F32) for i in range(1)]
    pv = [ph.ps("pv%d" % i, [128, 512], F32) for i in range(2)]
    qkT_d = dr["qkT"].rearrange("c p t -> p c t")
    def pre_a1(ti):
        par = ti % 2
        t0 = ti * 512
        dma(S, 'sp', xt[par][:], dr["x"][t0:t0 + 512, :].rearrange("(s p) d -> p s d", p=128), (), [("xt", par)])

    def pre_a2(ti):
        par = ti % 2
        nt.run_a(xt[par], ("xt", par))

    def pre_b(ti):
        par = ti % 2
        nt.run_b(hT[par], ("hT", par), modAB[:, 0], modAB[:, 1], ti // 8)

    pre_a1(0)
    pre_a2(0)
    pre_b(0)
    for ti in range(ntiles):
        par = ti % 2
        b = ti // 8
        t0 = ti * 512
        if ti + 1 < ntiles:
            pre_a1(ti + 1)
        hkeys = [(("hT", par), c) for c in range(8)]
        def qk_tail(oc):
            i3 = oc % 3
            p = pq[i3]
            pk = ("pq", i3)
            isk = 1 if oc >= 4 else 0
            sk = ("sqk", i3)
            mm(S, pss[0][:], bones[:], sqk[i3][:], True, True, [sk, "bones"], ["pss"])
            rk = ("rs", i3)
            act(S, rs[i3][:], pss[0][:], AF.Sqrt, ["pss", "cbias"], [rk],
                scale=(1.0 / 64 if isk else 1.0), bias=cb[:, isk:isk + 1])
            S.op('dve', (lambda o: (lambda e: e.reciprocal(o, o)))(rs[i3][:]), [rk], [rk])
            stt(S, qkst[par][:, oc, :], p[:], qkg[:, isk:isk + 1], rs[i3][:], ALU.mult, ALU.mult,
                [pk, rk, "qkg"], [("qkst", par)])

        for oc in range(8):
            i3 = oc % 3
            p = pq[i3]
            pk = ("pq", i3)
            for k in range(8):
                mm(S, p[:], W[:, k, oc * 128:(oc + 1) * 128], hT[par][:, k, :], k == 0, k == 7,
                   [wkeys[k], hkeys[k]], [pk])
            act(S, sqk[i3][:], p[:], AF.Square, [pk], [("sqk", i3)])
            if oc >= 1:
                qk_tail(oc - 1)
            if oc == 3 and ti + 1 < ntiles:
                pre_a2(ti + 1)
        tails_left = [7]
        for vi, (col, dname) in enumerate(((1024, "v0"), (1536, "u0"))):
            for s in range(4):
                p = pv[s % 2]
                pk = ("pv", s % 2)
                for k in range(8):
                    mm(S, p[:], hT[par][:, k, s * 128:(s + 1) * 128], W[:, k, col:col + 512], k == 0, k == 7,
                       [wkeys[k], hkeys[k]], [pk])
                if tails_left:
                    qk_tail(tails_left.pop(0))
                    if not tails_left:
                        dma(S, 'sp', qkT_d[:, :, t0:t0 + 512], qkst[par][:], [("qkst", par)], [("qkT", ti)])
                cp(S, 'act' if s % 2 == 0 else 'dve', vust[par][:, s, vi * 512:(vi + 1) * 512], p[:], [pk], [("vust", par, vi)])
            dma(S, 'sp', dr[dname][t0:t0 + 512, :].rearrange("(s p) d -> p s d", p=128),
                vust[par][:, :, vi * 512:(vi + 1) * 512], [("vust", par, vi)], [(dname, ti)])
        if ti + 1 < ntiles:
            pre_b(ti + 1)
    ph.close()


def phase_p3(cx, l, cat_name, x_name, out_name, wo_name, ntiles=32, final=False):
    nc, S, dr = cx.nc, cx.S, cx.dr
    ph = Phase(cx, "p3_%d" % l)
    ident = load_consts(ph, S, dr)
    Wo = load_w(S, ph, "Wo", dr[wo_name], 1024, 1024)
    Win = load_w(S, ph, "Win", dr["ffn_w_in"][l], 1024, 2 * FF)
    Wout = load_w(S, ph, "Wout", dr["ffn_w_out"][l], FF, 1024)
    modAB = ph.sb("modAB", [128, 4, 8, 2], F32)
    dma(S, 'sp', modAB[:], dr["modfm"][l], [("modfm", l)], ["modAB"])
    gb = ph.sb("gb", [128, 2, 1024], F32)
    nt = NormT(ph, 2, ident)
    xt2 = [ph.sb("xt%d" % i, [128, 2, 1024], F32) for i in range(2)]
    ct = [ph.sb("ct%d" % i, [128, 8, 256], BF16) for i in range(2)]
    hT = ph.sb("hT", [128, 8, 256], BF16)
    hact = ph.sb("hact", [128, 22, 256], BF16)
    sil = [ph.sb("sil%d" % i, [128, 256], F32) for i in range(2)]
    tmp = ph.sb("tmp", [128, 1024], F32)
    po = ph.ps("po", [128, 1024], F32)
    pw = ph.ps("pw", [128, 1024], F32)
    pgu = [ph.ps("pgu%d" % i, [128, 2, 256], F32) for i in range(2)]
    cat_d = dr[cat_name].rearrange("(c p) t -> p c t", p=128)
    hkeys = [("hT", c) for c in range(8)]

    def load(ti):
        par = ti % 2
        b = ti // 16
        t0 = ti * 256
        if ti % 16 == 0:
            for wi in range(2):
                dma(S, 'sp', gb[:, wi, :], dr["gbc"][l, b, wi], [("gbc", l, b, wi)], ["gb"])
        dma(S, 'sp', xt2[par][:], dr[x_name][t0:t0 + 256, :].rearrange("(s p) d -> p s d", p=128), [(x_name, ti)], [("xt", par)])
        dma(S, 'sp', ct[par][:], cat_d[:, :, t0:t0 + 256], [(cat_name, ti)], [("ct", par)])

    def outproj(ti):
        par = ti % 2
        xt = xt2[par]
        for s in range(2):
            for half in range(2):
                for k in range(8):
                    mm(S, po[:, half * 512:(half + 1) * 512], ct[par][:, k, s * 128:(s + 1) * 128],
                       Wo[:, k, half * 512:(half + 1) * 512], k == 0, k == 7, [("Wo", k), ("ct", par)], [("po", half)])
            tt(S, 'dve', tmp[:], po[:], gb[:, 0, :], ALU.mult, [("po", 0), ("po", 1), "gb"], ["tmp"])
            tt(S, 'pool', xt[:, s, :], xt[:, s, :], tmp[:], ALU.add, ["tmp", ("xt", par)], [("xt", par)])

    def ffn_in(ti):
        for j in range(22):
            gu = pgu[j % 2]
            gk = ("pgu", j % 2)
            for hh in range(2):
                col = hh * FF + j * 128
                for k in range(8):
                    mm(S, gu[:, hh, :], Win[:, k, col:col + 128], hT[:, k, :], k == 0, k == 7,
                       [("Win", k), hkeys[k]], [gk])
            sk = ("sil", j % 2)
            act(S, sil[j % 2][:], gu[:, 0, :], AF.Silu, [gk], [sk])
            tt(S, 'dve', hact[:, j, :], gu[:, 1, :], sil[j % 2][:], ALU.mult, [gk, sk], [("hact", j)])

    def ffn_out(ti):
        par = ti % 2
        xt = xt2[par]
        t0 = ti * 256
        for s in range(2):
            for half in range(2):
                for j in range(22):
                    mm(S, pw[:, half * 512:(half + 1) * 512], hact[:, j, s * 128:(s + 1) * 128],
                       Wout[:, j, half * 512:(half + 1) * 512], j == 0, j == 21, [("Wout", j), ("hact", j)], [("pw", half)])
            tt(S, 'dve', tmp[:], pw[:], gb[:, 1, :], ALU.mult, [("pw", 0), ("pw", 1), "gb"], ["tmp"])
            tt(S, 'pool', xt[:, s, :], xt[:, s, :], tmp[:], ALU.add, ["tmp", ("xt", par)], [("xt", par)])
        dma(S, 'sp', dr[out_name][t0:t0 + 256, :].rearrange("(s p) d -> p s d", p=128), xt[:], [("xt", par)], [(out_name, ti)],
            is_output=final)

    load(0)
    outproj(0)
    nt.run_a(xt2[0], ("xt", 0))
    nt.run_b(hT, "hT", modAB[:, 2], modAB[:, 3], 0)
    for ti in range(ntiles):
        nxt = ti + 1 < ntiles
        if nxt and (ti + 1) % 16 != 0:
            load(ti + 1)
        ffn_in(ti)
        if nxt and (ti + 1) % 16 != 0:
            outproj(ti + 1)
            nt.run_a(xt2[(ti + 1) % 2], ("xt", (ti + 1) % 2))
        ffn_out(ti)
        if nxt:
            if (ti + 1) % 16 == 0:
                load(ti + 1)
                outproj(ti + 1)
                nt.run_a(xt2[(ti + 1) % 2], ("xt", (ti + 1) % 2))
            nt.run_b(hT, "hT", modAB[:, 2], modAB[:, 3], (ti + 1) // 16)
    ph.close()


def phase_attn(cx, nseq=NSEQ, nqb=32):
    nc, S, dr = cx.nc, cx.S, cx.dr
    ph = Phase(cx, "pa")
    ident = load_consts(ph, S, dr)
    NEG = -30000.0
    Er = ph.sb("Er", [8, 257], F32)
    E = ph.sb("E", [8, 1024], F32)
    c256 = ph.sb("c256", [128, 8], F32)
    dma(S, 'sp', Er[:], dr["rel_bias"], (), ["Er"])
    rb = dr["rel_bias"]
    dma(S, 'sp', c256[:], bass.AP(tensor=rb.tensor, offset=rb.offset + 256, ap=[[0, 128], [257, 8]]), (), ["c256"], slow=True)
    memset(S, 'dve', E[:], 0.0, ["E"])
    ts(S, 'dve', E[:, 0:767], E[:, 0:767], Er[:, 256:257], None, ALU.add, None, ["E", "Er"], ["E"])
    cp(S, 'dve', E[:, 767:1024], Er[:, ::-1], ["E", "Er"], ["E"])
    dma(S, 'sp', dr["relext"], E[:], ["E"], ["relext"])
    BT = ph.sb("BT", [128, 5, 8, 128], F32)
    memset(S, 'pool', BT[:], 0.0, ["BT"])
    ext = dr["relext"]
    for j in range(3):
        tt(S, 'pool', BT[:, j, :, :], BT[:, j, :, :], apx(c256[:, :], [[0, 128]]), ALU.add, ["BT", "c256"], ["BT"])
    for j in (3, 4):
        for h in range(8):
            src = bass.AP(tensor=ext.tensor, offset=ext.offset + h * 1024 + 1023 - (5 - j) * 128 - 127, ap=[[1, 128], [1, 128]])
            dma(S, 'sp', BT[:, j, h, :], src, ["relext", "BT"], ["BT"])
    memset(S, 'pool', BT[0:64, 0, :, 0:64], NEG, ["BT"])
    memset(S, 'pool', BT[64:128, 4, :, 64:128], NEG, ["BT"])
    qT = ph.sb("qT", [128, 4, S_LEN], BF16)
    kT = ph.sb("kT", [128, 4, S_LEN], BF16)
    Vr = ph.sb("Vr", [128, 32, 512], BF16)
    Va = ph.sb("Va", [128, 32, 8, 65], BF16)
    memset(S, 'pool', Va[:, :, :, 64:65], 1.0, ["Va1"])
    NBA = 3
    sbf = [ph.sb("sbf%d" % i, [128, 5, 128], F32) for i in range(NBA)]
    pT = [ph.sb("pT%d" % i, [128, 5, 128], BF16) for i in range(NBA)]
    rc = ph.sb("rc", [128, 8], F32)
    ao = ph.sb("ao", [128, 512], BF16)
    aT = [ph.sb("aT%d" % i, [128, 4, 512], BF16) for i in range(2)]
    ps = [ph.ps("ps%d" % i, [128, 8, 128], F32) for i in range(2)]
    po = [ph.ps("po%d" % i, [128, 4, 65], F32) for i in range(2)]
    tp = ph.ps("tp", [128, 4, 128], BF16)
    qk_d = dr["qkT"].rearrange("c p t -> p c t")
    cat_d = dr["catT0"].rearrange("(c p) t -> p c t", p=128)
    hn = 0
    for b in range(nseq):
        tb = b * S_LEN
        for c in range(4):
            dma(S, 'sp', qT[:, c, :], qk_d[:, c, tb:tb + S_LEN], [("qkT", i) for i in range(b * 8, b * 8 + 8)], ["qT"])
            dma(S, 'sp', kT[:, c, :], qk_d[:, 4 + c, tb:tb + S_LEN], [("qkT", i) for i in range(b * 8, b * 8 + 8)], ["kT"])
        for c in range(4):
            dma(S, 'sp', Vr[:, c * 8:(c + 1) * 8, :],
                dr["v0"][tb + c * 1024: tb + (c + 1) * 1024, :].rearrange("(s p) d -> p s d", p=128),
                [("v0", i) for i in range(b * 8, b * 8 + 8)], ["Vr"])
        for c in range(4):
            cp(S, 'pool', Va[:, c * 8:(c + 1) * 8, :, 0:64], Vr[:, c * 8:(c + 1) * 8, :].rearrange("p s (h d) -> p s h d", h=8),
               ["Vr"], ["Va"])
        units = [(qb, h) for qb in range(nqb) for h in range(8)]
        bufi = {}

        def st1(u):
            nonlocal hn
            qb, h = u
            kb0 = max(0, qb - 4)
            nkb = qb - kb0 + 1
            j0 = 5 - nkb
            pr, base = h // 2, 64 * (h % 2)
            par = hn % NBA
            p_s, pk = ps[hn % 2], ("ps", hn % 2)
            hn += 1
            bufi[u] = par
            for j in range(nkb):
                kb = kb0 + j
                mm(S, p_s[:, j, :], kT[base:base + 64, pr, kb * 128:(kb + 1) * 128],
                   qT[base:base + 64, pr, qb * 128:(qb + 1) * 128], True, True, ["kT", "qT"], [pk])
            tt(S, 'dve', sbf[par][:, 0:nkb, :], p_s[:, 0:nkb, :], BT[:, j0:5, h, ::-1], ALU.add, [pk, "BT"], [("sbf", par)])
            act(S, pT[par][:, 0:nkb, :], sbf[par][:, 0:nkb, :], AF.Exp, [("sbf", par)], [("pT", par)])

        def st2(u):
            qb, h = u
            kb0 = max(0, qb - 4)
            nkb = qb - kb0 + 1
            par = bufi.pop(u)
            for j in range(nkb):
                kb = kb0 + j
                mm(S, po[h // 4][:, h % 4, :], pT[par][:, j, :], Va[:, kb, h, :], j == 0, j == nkb - 1,
                   [("pT", par), "Va", "Va1"], [("po", h // 4)])
            if h == 7:
                epi(qb)

        def epi(qb):
            for g in range(2):
                S.op('dve', (lambda o, i: (lambda e: e.reciprocal(o, i)))(rc[:, g * 4:(g + 1) * 4], po[g][:, :, 64]),
                     [("po", g)], [("rc", g)])
                tt(S, 'dve', ao[:, g * 256:(g + 1) * 256].rearrange("p (h d) -> p h d", h=4), po[g][:, :, 0:64],
                   apx(rc[:, g * 4:(g + 1) * 4], [[0, 64]]), ALU.mult, [("po", g), ("rc", g)], [("ao", g)])
            for c in range(4):
                tr(S, tp[:, c, :], ao[:, c * 128:(c + 1) * 128], ident[:], [("ao", c // 2), "ident"], ["tp"])
            apar = (qb // 4) % 2
            cp(S, 'act', aT[apar][:, :, (qb % 4) * 128:(qb % 4 + 1) * 128], tp[:], ["tp"], [("aT", apar)])
            if qb % 4 == 3 or qb == nqb - 1:
                q0 = (qb // 4) * 4
                n = (qb - q0 + 1) * 128
                dma(S, 'sp', cat_d[:, 0:4, tb + q0 * 128: tb + q0 * 128 + n], aT[apar][:, :, 0:n], [("aT", apar)],
                    [("catT0", "a", b, qb // 4)])

        SK = 2
        for i in range(len(units) + SK):
            if i < len(units):
                st1(units[i])
            if i - SK >= 0:
                st2(units[i - SK])
    ph.close()


def phase_p1_l1(cx, x_name, ntiles=16):
    nc, S, dr = cx.nc, cx.S, cx.dr
    ph = Phase(cx, "p1b")
    ident = load_consts(ph, S, dr)
    W = load_w(S, ph, "W", dr["cd_w_in"], 1024, 3584)
    wkeys = [("W", c) for c in range(8)]
    modAB = ph.sb("modAB", [128, 4, 8, 2], F32)
    dma(S, 'sp', modAB[:], dr["modfm"][1], [("modfm", 1)], ["modAB"])
    QD = ph.sb("QD", [128, 4], F32)
    KD = ph.sb("KD", [128, 4], F32)
    dma(S, 'sp', QD[:], dr["ret_qd"], (), ["QD"])
    dma(S, 'sp', KD[:], dr["ret_kd"], (), ["KD"])
    nt = NormT(ph, 4, ident)
    xt = [ph.sb("xt%d" % i, [128, 4, 1024], F32) for i in range(2)]
    hT2 = [ph.sb("hT%d" % i, [128, 8, 512], BF16) for i in range(2)]
    rot = [ph.sb("rot%d" % i, [128, 4, 2, 64], F32) for i in range(2)]
    t12 = [ph.sb("t12_%d" % i, [128, 4, 64], F32) for i in range(4)]
    R = [ph.sb("R%d" % i, [128, 4, 2, 64], F32) for i in range(2)]
    qkb = ph.sb("qkb", [128, 3, 512], BF16)
    fst = [ph.sb("fst%d" % i, [128, 12, 512], BF16) for i in range(2)]
    tst = [ph.sb("tst%d" % i, [128, 4, 3, 512], BF16) for i in range(2)]
    dst = [ph.sb("dst%d" % i, [128, 8, 512], BF16) for i in range(2)]
    dvs = [ph.sb("dvs%d" % i, [128, 4, 512], BF16) for i in range(2)]
    NPT = 4
    pt = [ph.ps("pt%d" % i, [128, 512], F32) for i in range(NPT)]
    ptr2 = [ph.ps("ptr%d" % i, [128, 4, 128], BF16) for i in range(2)]
    cf_d = dr["c_fm"].rearrange("k p t -> p k t")
    ct_d = dr["c_tok"]
    dq_d = dr["d_qkT"].rearrange("c p t -> p c t")
    pn = 0
    def pre_a(ti):
        par = ti % 2
        t0 = ti * 512
        pos0 = t0 % S_LEN
        xk = ("xt", par)
        dma(S, 'sp', xt[par][:], dr[x_name][t0:t0 + 512, :].rearrange("(s p) d -> p s d", p=128), [(x_name, 2 * ti), (x_name, 2 * ti + 1)], [xk])
        dma(S, 'sp', rot[par][:], dr["rot"][pos0:pos0 + 512].rearrange("(s p) a f -> p s a f", p=128), (), [("rot", par)])

    def pre_a2(ti):
        par = ti % 2
        nt.run_a(xt[par], ("xt", par))

    def pre_b(ti):
        par = ti % 2
        nt.run_b(hT2[par], ("hT", par), modAB[:, 0], modAB[:, 1], ti // 8)

    pre_a(0)
    pre_a2(0)
    pre_b(0)
    trn = 0
    for ti in range(ntiles):
        par = ti % 2
        b = ti // 8
        t0 = ti * 512
        if ti + 1 < ntiles:
            pre_a(ti + 1)
        hT = hT2[par]
        hkeys = [(("hT", par), c) for c in range(8)]

        def tokmm(s, col):
            nonlocal pn
            p = pt[pn % NPT]
            pk = ("pt", pn % NPT)
            pn += 1
            for k in range(8):
                mm(S, p[:], hT[:, k, s * 128:(s + 1) * 128], W[:, k, col:col + 512], k == 0, k == 7, [wkeys[k], hkeys[k]], [pk])
            return p, pk

        def fm_chunk(oc):
            nonlocal pn
            p = pt[pn % NPT]
            pk = ("pt", pn % NPT)
            pn += 1
            col = 2048 + oc * 128
            for k in range(8):
                mm(S, p[:], W[:, k, col:col + 128], hT[:, k, :], k == 0, k == 7, [wkeys[k], hkeys[k]], [pk])
            cp(S, 'act' if oc % 2 == 0 else 'dve', dst[par][:, oc, :], p[:], [pk], [("dst", par)])

        for s in range(4):
            cosv = rot[par][:, s, 0, :]
            sinv = rot[par][:, s, 1, :]
            cos4 = bass.AP(tensor=cosv.tensor, offset=cosv.offset, ap=[list(cosv.ap[0]), [0, 4], list(cosv.ap[1])])
            sin4 = bass.AP(tensor=sinv.tensor, offset=sinv.offset, ap=[list(sinv.ap[0]), [0, 4], list(sinv.ap[1])])
            for qi, col in enumerate((0, 512)):
                p, pk = tokmm(s, col)
                pv4 = p[:].rearrange("p (h a f) -> p h a f", h=4, a=2)
                x1, x2 = pv4[:, :, 0, :], pv4[:, :, 1, :]
                rk = ("R", qi)
                tt(S, 'dve', t12[0][:], x1, cos4, ALU.mult, [pk, ("rot", par)], ["t0"])
                tt(S, 'dve', t12[1][:], x2, sin4, ALU.mult, [pk, ("rot", par)], ["t1"])
                tt(S, 'dve', t12[2][:], x1, sin4, ALU.mult, [pk, ("rot", par)], ["t2"])
                tt(S, 'dve', t12[3][:], x2, cos4, ALU.mult, [pk, ("rot", par)], ["t3"])
                tt(S, 'pool', R[qi][:, :, 0, :], t12[0][:], t12[1][:], ALU.subtract, ["t0", "t1"], [rk])
                tt(S, 'pool', R[qi][:, :, 1, :], t12[2][:], t12[3][:], ALU.add, ["t2", "t3"], [rk])
            Rq = R[0][:].rearrange("p h a f -> p h (a f)")
            Rk = R[1][:].rearrange("p h a f -> p h (a f)")
            q3 = qkb[:].rearrange("p k (h d) -> p k h d", h=4)
            cp(S, 'act', q3[:, 0], Rq, [("R", 0)], [("qkb", 0)])
            tt(S, 'dve', q3[:, 1], Rq, apx(QD[:, :], [[0, 128]]), ALU.mult, [("R", 0), "QD"], [("qkb", 1)])
            act(S, q3[:, 2], Rk, AF.Copy, [("R", 1)], [("qkb", 2)], scale=128.0 ** -0.5)
            tt(S, 'pool', tst[par][:, s, 0, :].rearrange("p (h d) -> p h d", h=4), Rk, apx(KD[:, :], [[0, 128]]), ALU.mult,
               [("R", 1), "KD"], [("tst", par)])
            p, pk = tokmm(s, 1024)
            cp(S, 'act', tst[par][:, s, 1, :], p[:], [pk], [("tst", par)])
            p, pk = tokmm(s, 1536)
            act(S, tst[par][:, s, 2, :], p[:], AF.Silu, [pk], [("tst", par)])
            p, pk = tokmm(s, 3072)
            cp(S, 'act', dvs[par][:, s, :], p[:], [pk], [("dvs", par)])
            for oc in (2 * s, 2 * s + 1):
                fm_chunk(oc)
            for kind in range(3):
                ptr = ptr2[trn % 2]
                ptk = ("ptr", trn % 2)
                for h in range(4):
                    tr(S, ptr[:, h, :], qkb[:, kind, h * 128:(h + 1) * 128], ident[:], [("qkb", kind), "ident"], [ptk])
                cp(S, 'act' if trn % 2 == 0 else 'dve', fst[par][:, kind * 4:(kind + 1) * 4, s * 128:(s + 1) * 128], ptr[:],
                   [ptk], [("fst", par)])
                trn += 1
            if s == 1 and ti + 1 < ntiles:
                pre_a2(ti + 1)
        dma(S, 'sp', cf_d[:, :, t0:t0 + 512], fst[par][:], [("fst", par)], [("c_fm", ti)])
        dma(S, 'sp', ct_d[t0:t0 + 512].rearrange("(s p) k d -> p s k d", p=128), tst[par][:], [("tst", par)], [("c_tok", ti)])
        dma(S, 'sp', dq_d[:, :, t0:t0 + 512], dst[par][:], [("dst", par)], [("d_qkT", ti)])
        dma(S, 'sp', dr["d_v"][t0:t0 + 512, :].rearrange("(s p) d -> p s d", p=128), dvs[par][:], [("dvs", par)], [("d_v", ti)])
        if ti + 1 < ntiles:
            pre_b(ti + 1)
    ph.close()


def phase_sb(cx, nseq=NSEQ, nblk=32):
    nc, S, dr = cx.nc, cx.S, cx.dr
    ph = Phase(cx, "pd")
    ident = load_consts(ph, S, dr)
    mask = ph.sb("mask", [128, 256], F32)
    dma(S, 'sp', mask[:], dr["sb_mask"], (), ["mask"])
    ones = ph.sb("ones", [128, 256], F32)
    memset(S, 'pool', ones[:], 1.0, ["ones"])
    one1 = ph.sb("one1", [128, 1], F32)
    memset(S, 'pool', one1[:], 1.0, ["one1"])
    qT = ph.sb("qT", [128, 4, S_LEN], BF16)
    kT = ph.sb("kT", [128, 4, S_LEN], BF16)
    V = ph.sb("V", [128, 32, 512], BF16)
    ex = [ph.sb("ex%d" % i, [128, 256], F32) for i in range(8)]
    sp = [ph.sb("sp%d" % i, [128, 256], F32) for i in range(8)]
    Rc = [ph.sb("Rc%d" % i, [128, 256], F32) for i in range(8)]
    lw = [ph.sb("lw%d" % i, [128, 256], F32) for i in range(8)]
    wm = [ph.sb("wm%d" % i, [128, 256], BF16) for i in range(8)]
    wT = [ph.sb("wT%d" % i, [128, 2, 128], BF16) for i in range(8)]
    do_b = ph.sb("do_b", [128, 512], BF16)
    dT = [ph.sb("dT%d" % i, [128, 4, 512], BF16) for i in range(2)]
    pz = [ph.ps("pz%d" % i, [128, 256], F32) for i in range(4)]
    pwt = [ph.ps("pwt%d" % i, [128, 2, 128], BF16) for i in range(2)]
    po = ph.ps("po", [128, 8, 64], F32)
    ptp = ph.ps("ptp", [128, 4, 128], BF16)
    qk_d = dr["d_qkT"].rearrange("c p t -> p c t")
    cat_d = dr["catT1"].rearrange("(c p) t -> p c t", p=128)
    hn = 0
    for b in range(nseq):
        tb = b * S_LEN
        rk = [("d_qkT", i) for i in range(b * 8, b * 8 + 8)]
        for c in range(4):
            dma(S, 'sp', qT[:, c, :], qk_d[:, c, tb:tb + S_LEN], rk, ["qT"])
            dma(S, 'sp', kT[:, c, :], qk_d[:, 4 + c, tb:tb + S_LEN], rk, ["kT"])
        for c in range(4):
            dma(S, 'sp', V[:, c * 8:(c + 1) * 8, :],
                dr["d_v"][tb + c * 1024: tb + (c + 1) * 1024, :].rearrange("(s p) d -> p s d", p=128),
                [("d_v", i) for i in range(b * 8, b * 8 + 8)], ["V"])
        units = [(blk, h) for blk in range(nblk) for h in range(8)]
        bufi = {}

        def geom(blk):
            nk = 1 if blk == 0 else 2
            return nk, nk * 128, (blk + 1 - nk) * 128, 256 - nk * 128

        def stA(u):
            nonlocal hn
            blk, h = u
            nk, W_, k0, m0 = geom(blk)
            pr, base = h // 2, 64 * (h % 2)
            par = hn % 8
            z, zk = pz[hn % 4], ("pz", hn % 4)
            bufi[u] = (par, hn % 2)
            hn += 1
            mm(S, z[:, 0:W_], qT[base:base + 64, pr, blk * 128:(blk + 1) * 128], kT[base:base + 64, pr, k0:k0 + W_],
               True, True, ["qT", "kT"], [zk])
            act(S, ex[par][:, 0:W_], z[:, 0:W_], AF.Exp, [zk], [("ex", par)], scale=0.125)
            act(S, sp[par][:, 0:W_], ex[par][:, 0:W_], AF.Ln, [("ex", par), "one1"], [("sp", par)], bias=one1[:, 0:1])
            tt(S, 'dve', sp[par][:, 0:W_], sp[par][:, 0:W_], mask[:, m0:256], ALU.mult, [("sp", par), "mask"], [("sp", par)])
            S.op('dve', (lambda o, d0, d1: (lambda e: e.tensor_tensor_scan(o, d0, d1, 0.0, ALU.mult, ALU.add)))(
                Rc[par][:, 0:W_][:, ::-1], ones[:, 0:W_], sp[par][:, 0:W_][:, ::-1]), [("sp", par), "ones"], [("Rc", par)])
            stt(S, lw[par][:, 0:W_], z[:, 0:W_], 0.125, Rc[par][:, 0:W_], ALU.mult, ALU.subtract, [zk, ("Rc", par)], [("lw", par)])

        def stA2(u):
            blk, h = u
            nk, W_, k0, m0 = geom(blk)
            par, p2 = bufi[u]
            act(S, lw[par][:, 0:W_], lw[par][:, 0:W_], AF.Exp, [("lw", par)], [("lw", par)])
            tt(S, 'pool', wm[par][:, 0:W_], lw[par][:, 0:W_], mask[:, m0:256], ALU.mult, [("lw", par), "mask"], [("wm", par)])

        def stB(u):
            blk, h = u
            nk, W_, k0, m0 = geom(blk)
            par, p2 = bufi[u]
            pw2, pwk = pwt[p2], ("pwt", p2)
            for n in range(nk):
                tr(S, pw2[:, n, :], wm[par][:, n * 128:(n + 1) * 128], ident[:], [("wm", par), "ident"], [pwk])
            cp(S, 'act', wT[par][:, 0:nk, :], pw2[:, 0:nk, :], [pwk], [("wT", par)])

        def stC(u):
            blk, h = u
            nk, W_, k0, m0 = geom(blk)
            par, p2 = bufi.pop(u)
            for n in range(nk):
                kb = blk + 1 - nk + n
                mm(S, po[:, h, :], wT[par][:, n, :], V[:, kb, h * 64:(h + 1) * 64], n == 0, n == nk - 1,
                   [("wT", par), "V"], ["po"])
            if h == 7:
                epi(blk)

        def epi(blk):
            cp(S, 'dve', do_b[:], po[:].rearrange("p h d -> p (h d)"), ["po"], ["do_b"])
            for c in range(4):
                tr(S, ptp[:, c, :], do_b[:, c * 128:(c + 1) * 128], ident[:], ["do_b", "ident"], ["ptp"])
            apar = (blk // 4) % 2
            cp(S, 'act', dT[apar][:, :, (blk % 4) * 128:(blk % 4 + 1) * 128], ptp[:], ["ptp"], [("dT", apar)])
            if blk % 4 == 3 or blk == nblk - 1:
                q0 = (blk // 4) * 4
                n = (blk - q0 + 1) * 128
                dma(S, 'sp', cat_d[:, 4:8, tb + q0 * 128: tb + q0 * 128 + n], dT[apar][:, :, 0:n], [("dT", apar)],
                    [("catT1", "d", b, blk // 4)])

        for i in range(len(units) + 5):
            if i < len(units):
                stA(units[i])
            if 0 <= i - 1 < len(units):
                stA2(units[i - 1])
            if 0 <= i - 3 < len(units):
                stB(units[i - 3])
            if 0 <= i - 5 < len(units):
                stC(units[i - 5])
    ph.close()


def phase_ret(cx, nseq=NSEQ, nblk=32):
    nc, S, dr = cx.nc, cx.S, cx.dr
    ph = Phase(cx, "pc")
    ident = load_consts(ph, S, dr)
    decT = ph.sb("decT", [128, 4, 128], F32)
    dma(S, 'sp', decT[:], dr["ret_decT"], (), ["decT"])
    ng = ph.sb("ng", [128, 512], F32)
    rn = dr["ret_norm_g"]
    dma(S, 'sp', ng[:], bass.AP(tensor=rn.tensor, offset=rn.offset, ap=[[0, 128], [1, 512]]), (), ["ng"])
    mhalf = ph.sb("mhalf", [128, 4], F32)
    memset(S, 'pool', mhalf[:], -0.5, ["mhalf"])
    SEG = 8
    fm = [ph.sb("fm%d" % i, [128, 12, SEG * 128], BF16) for i in range(2)]
    tk = [ph.sb("tk%d" % i, [128, SEG, 3, 512], BF16) for i in range(2)]
    st32 = [ph.sb("st32_%d" % h, [128, 128], F32) for h in range(4)]
    stb = [[ph.sb("stb_%d_%d" % (h, i), [128, 128], BF16) for i in range(2)] for h in range(4)]
    PT = [ph.sb("PT%d" % i, [128, 4, 128], BF16) for i in range(2)]
    osb = ph.sb("osb", [128, 4, 128], F32)
    sq = ph.sb("sq", [128, 4, 128], F32)
    s12 = ph.sb("s12", [128, 2, 4], F32)
    mv = ph.sb("mv", [128, 3, 4], F32)
    cn = ph.sb("cn", [128, 512], F32)
    co_b = ph.sb("co_b", [128, 512], BF16)
    cT = [ph.sb("cT%d" % i, [128, 4, 512], BF16) for i in range(2)]
    pS = [ph.ps("pS%d" % i, [128, 4, 128], F32) for i in range(1)]
    pO = [ph.ps("pO%d" % i, [128, 4, 128], F32) for i in range(2)]
    pK = [ph.ps("pK%d" % i, [128, 128], F32) for i in range(4)]
    ptp = ph.ps("ptp", [128, 4, 128], BF16)
    cf_d = dr["c_fm"].rearrange("k p t -> p k t")
    ct_d = dr["c_tok"]
    cat_d = dr["catT1"].rearrange("(c p) t -> p c t", p=128)
    gam = [1.0 - 2.0 ** (-5.0 - h) for h in range(4)]
    sn = 0
    kn = 0
    for b in range(nseq):
        tb = b * S_LEN
        for h in range(4):
            memset(S, 'pool', st32[h][:], 0.0, [("st32", h)])
            memset(S, 'pool', stb[h][0][:], 0.0, [("stb", h, 0)])
        scnt = [0, 0, 0, 0]
        for blk in range(nblk):
            seg, sb_ = blk // SEG, blk % SEG
            sp_ = seg % 2
            if sb_ == 0:
                t0 = tb + seg * SEG * 128
                nb = min(SEG, nblk - seg * SEG)
                rkeys = [("c_fm", (t0 // 512) + i) for i in range(2)]
                dma(S, 'sp', fm[sp_][:, :, 0:nb * 128], cf_d[:, :, t0:t0 + nb * 128], rkeys, [("fm", sp_)])
                dma(S, 'sp', tk[sp_][:, 0:nb], ct_d[t0:t0 + nb * 128].rearrange("(s p) k d -> p s k d", p=128),
                    [("c_tok", (t0 // 512) + i) for i in range(2)], [("tk", sp_)])
            cs = slice(sb_ * 128, (sb_ + 1) * 128)
            bp = blk % 2
            po_ = pO[bp]
            pok = ("pO", bp)
            for h in range(4):
                mm(S, pS[0][:, h, :], fm[sp_][:, 8 + h, cs], fm[sp_][:, 0 + h, cs], True, True, [("fm", sp_)], ["pS"])
            tt(S, 'dve', PT[bp][:], pS[0][:], decT[:], ALU.mult, ["pS", "decT"], [("PT", bp)])
            for h in range(4):
                hs = slice(h * 128, (h + 1) * 128)
                mm(S, po_[:, h, :], PT[bp][:, h, :], tk[sp_][:, sb_, 1, hs], h == 0, False, [("PT", bp), ("tk", sp_)], [pok])
            for half in range(2):
                ps_ = slice(half * 64, (half + 1) * 64)
                kp = kn % 2
                kn += 1
                for h in range(4):
                    hs = slice(h * 128, (h + 1) * 128)
                    cur = scnt[h] % 2
                    mm(S, po_[ps_, h, :], fm[sp_][:, 4 + h, sb_ * 128 + half * 64: sb_ * 128 + (half + 1) * 64], stb[h][cur][:],
                       False, h == 3, [("fm", sp_), ("stb", h, cur)], [pok])
                    mm(S, pK[h][:], tk[sp_][ps_, sb_, 0, hs], tk[sp_][ps_, sb_, 1, hs], True, True, [("tk", sp_)], [("pK", h)])
                    stt(S, st32[h][:], st32[h][:], gam[h] ** 64, pK[h][:], ALU.mult, ALU.add, [("pK", h), ("st32", h)], [("st32", h)])
                    cp(S, 'act', stb[h][1 - cur][:], st32[h][:], [("st32", h)], [("stb", h, 1 - cur)])
                    scnt[h] += 1
            cp(S, 'act', osb[:], po_[:], [pok], ["osb"])
            S.op('dve', lambda e: e.reduce_sum(s12[:, 0, :], osb[:], axis=AX.X), ["osb"], [("s12", 0)])
            tt(S, 'pool', sq[:], osb[:], osb[:], ALU.mult, ["osb"], ["sq"])
            S.op('dve', lambda e: e.reduce_sum(s12[:, 1, :], sq[:], axis=AX.X), ["sq"], [("s12", 1)])
            ts(S, 'dve', mv[:, 0, :], s12[:, 0, :], 1.0 / 128, None, ALU.mult, None, [("s12", 0)], [("mv", 0)])
            tt(S, 'dve', mv[:, 1, :], mv[:, 0, :], mv[:, 0, :], ALU.mult, [("mv", 0)], [("mv", 1)])
            stt(S, mv[:, 2, :], s12[:, 1, :], 1.0 / 128, mv[:, 1, :], ALU.mult, ALU.subtract, [("s12", 1), ("mv", 1)], [("mv", 2)])
            ts(S, 'dve', mv[:, 2, :], mv[:, 2, :], EPS, None, ALU.add, None, [("mv", 2)], [("mv", 2)])
            tt(S, 'pool', mv[:, 2, :], mv[:, 2, :], mhalf[:], ALU.pow, [("mv", 2), "mhalf"], [("mv", 2)])
            cn3 = cn[:].rearrange("p (h d) -> p h d", h=4)
            tt(S, 'dve', cn3, osb[:], apx(mv[:, 0, :], [[0, 128]]), ALU.subtract, ["osb", ("mv", 0)], ["cn"])
            tt(S, 'dve', cn3, cn3, apx(mv[:, 2, :], [[0, 128]]), ALU.mult, ["cn", ("mv", 2)], ["cn"])
            tt(S, 'pool', cn[:], cn[:], ng[:], ALU.mult, ["cn", "ng"], ["cn"])
            tt(S, 'pool', co_b[:], cn[:], tk[sp_][:, sb_, 2, :], ALU.mult, ["cn", ("tk", sp_)], ["co_b"])
            for c in range(4):
                tr(S, ptp[:, c, :], co_b[:, c * 128:(c + 1) * 128], ident[:], ["co_b", "ident"], ["ptp"])
            apar = (blk // 4) % 2
            cp(S, 'act', cT[apar][:, :, (blk % 4) * 128:(blk % 4 + 1) * 128], ptp[:], ["ptp"], [("cT", apar)])
            if blk % 4 == 3 or blk == nblk - 1:
                q0 = (blk // 4) * 4
                n = (blk - q0 + 1) * 128
                dma(S, 'sp', cat_d[:, 0:4, tb + q0 * 128: tb + q0 * 128 + n], cT[apar][:, :, 0:n], [("cT", apar)],
                    [("catT1", "c", b, blk // 4)])
    ph.close()


def fap(t, off, dims, p0=0, pn=None):
    base = t[:]
    pstep, pcnt = base.ap[0]
    if pn is None:
        pn = pcnt - p0
    return bass.AP(tensor=base.tensor, offset=base.offset + p0 * pstep + off, ap=[[pstep, pn]] + [list(d) for d in dims])


def phase_s5(cx, nseq=NSEQ):
    nc, S, dr = cx.nc, cx.S, cx.dr
    ph = Phase(cx, "pb")
    ident = load_consts(ph, S, dr)
    identf = ph.sb("identf", [128, 128], F32)
    dma(S, 'sp', identf[:], dr["ident"], (), ["identf"])
    W_intra = ph.sb("W_intra", [128, 32, 2, 256], BF16)
    W_BU = ph.sb("W_BU", [128, 32, 2, 2, 64], BF16)
    W_CX = ph.sb("W_CX", [64, 2, 32, 256], BF16)
    Aa = ph.sb("Aa", [64, 2, 32], F32)
    A2 = ph.sb("A2", [64, 2, 32], F32)
    Wg = load_w(S, ph, "Wg", dr["s5_w_glu"], 512, 512)
    bg = ph.sb("bg", [128, 4], F32)
    dma(S, 'sp', bg[:], dr["s5_bgT"], (), ["bg"])
    memset(S, 'pool', W_intra[:], 0.0, ["W_intra"])

    pp = Phase(cx, "pbp")
    cnt = [0]

    def T(shape=(64, 32)):
        cnt[0] += 1
        return pp.sb("t%d" % cnt[0], list(shape), F32)

    def k(t):
        return t.name if hasattr(t, "name") else id(t)

    def mul(o, a, b_):
        tt(S, 'dve', o[:], a[:], b_[:], ALU.mult, [k(a), k(b_)], [k(o)])

    def add(o, a, b_):
        tt(S, 'dve', o[:], a[:], b_[:], ALU.add, [k(a), k(b_)], [k(o)])

    def sub(o, a, b_):
        tt(S, 'dve', o[:], a[:], b_[:], ALU.subtract, [k(a), k(b_)], [k(o)])

    def tsa(o, a, s1, s2, op0, op1):
        ts(S, 'dve', o[:], a[:], s1, s2, op0, op1, [k(a)], [k(o)])

    lre, lim, ldt = T(), T(), T()
    dma(S, 'sp', lre[:], dr["s5_lamT_re"], (), [k(lre)])
    dma(S, 'sp', lim[:], dr["s5_lamT_im"], (), [k(lim)])
    dma(S, 'sp', ldt[:], dr["s5_ldt_bc"], (), [k(ldt)])
    dt = T()
    act(S, dt[:], ldt[:], AF.Exp, [k(ldt)], [k(dt)])
    lr, ang, mag = T(), T(), T()
    mul(lr, lre, dt)
    mul(ang, lim, dt)
    act(S, mag[:], lr[:], AF.Exp, [k(lr)], [k(mag)])
    kf, r = T(), T()
    MAGIC = 12582912.0
    tsa(kf, ang, 1.0 / (2 * np.pi), None, ALU.mult, None)
    tsa(kf, kf, MAGIC, None, ALU.add, None)
    tsa(kf, kf, MAGIC, None, ALU.subtract, None)
    C1 = 6.28125
    C2 = 2 * np.pi - C1
    stt(S, r[:], kf[:], -C1, ang[:], ALU.mult, ALU.add, [k(kf), k(ang)], [k(r)])
    stt(S, r[:], kf[:], -C2, r[:], ALU.mult, ALU.add, [k(kf), k(r)], [k(r)])
    y, y2, sn, cs, tmp = T(), T(), T(), T(), T()
    tsa(y, r, 0.125, None, ALU.mult, None)
    mul(y2, y, y)
    f = [1.0]
    for i in range(1, 12):
        f.append(f[-1] * i)
    tsa(sn, y2, 1.0 / f[9], -1.0 / f[7], ALU.mult, ALU.add)
    for c_ in (1.0 / f[5], -1.0 / f[3], 1.0):
        mul(sn, sn, y2)
        tsa(sn, sn, c_, None, ALU.add, None)
    mul(sn, sn, y)
    tsa(cs, y2, -1.0 / f[10], 1.0 / f[8], ALU.mult, ALU.add)
    for c_ in (-1.0 / f[6], 1.0 / f[4], -0.5, 1.0):
        mul(cs, cs, y2)
        tsa(cs, cs, c_, None, ALU.add, None)
    for _ in range(3):
        mul(tmp, sn, cs)
        mul(cs, sn, sn)
        tsa(sn, tmp, 2.0, None, ALU.mult, None)
        tsa(cs, cs, -2.0, 1.0, ALU.mult, ALU.add)
    ar, ai = T(), T()
    mul(ar, mag, cs)
    mul(ai, mag, sn)
    am1, den, fre, fim, t1, t2 = T(), T(), T(), T(), T(), T()
    tsa(am1, ar, -1.0, None, ALU.add, None)
    mul(den, lre, lre)
    mul(t1, lim, lim)
    add(den, den, t1)
    S.op('dve', lambda e: e.reciprocal(den[:], den[:]), [k(den)], [k(den)])
    mul(t1, am1, lre)
    mul(t2, ai, lim)
    add(fre, t1, t2)
    mul(fre, fre, den)
    mul(t1, ai, lre)
    mul(t2, am1, lim)
    sub(fim, t1, t2)
    mul(fim, fim, den)
    pr = pp.sb("pr", [64, 32, 17], F32)
    pi_ = pp.sb("pi", [64, 32, 17], F32)
    memset(S, 'dve', pr[:, :, 0:1], 1.0, ["pr"])
    memset(S, 'dve', pi_[:, :, 0:1], 0.0, ["pi"])
    q1, q2 = T(), T()
    for j in range(16):
        tt(S, 'dve', q1[:], pr[:, :, j], ar[:], ALU.mult, ["pr", k(ar)], [k(q1)])
        tt(S, 'dve', q2[:], pi_[:, :, j], ai[:], ALU.mult, ["pi", k(ai)], [k(q2)])
        tt(S, 'dve', pr[:, :, j + 1], q1[:], q2[:], ALU.subtract, [k(q1), k(q2)], ["pr"])
        tt(S, 'dve', q1[:], pr[:, :, j], ai[:], ALU.mult, ["pr", k(ai)], [k(q1)])
        tt(S, 'dve', q2[:], pi_[:, :, j], ar[:], ALU.mult, ["pi", k(ar)], [k(q2)])
        tt(S, 'dve', pi_[:, :, j + 1], q1[:], q2[:], ALU.add, [k(q1), k(q2)], ["pi"])
    for ri in range(2):
        cp(S, 'dve', Aa[:, ri, :], pr[:, :, 16], ["pr"], ["Aa"])
    ts(S, 'dve', A2[:, 0, :], pi_[:, :, 16], -1.0, None, ALU.mult, None, ["pi"], ["A2"])
    cp(S, 'dve', A2[:, 1, :], pi_[:, :, 16], ["pi"], ["A2"])
    bre, bim = pp.sb("bre", [64, 32, 16], F32), pp.sb("bim", [64, 32, 16], F32)
    cre, cim = pp.sb("cre", [64, 32, 16], F32), pp.sb("cim", [64, 32, 16], F32)
    dma(S, 'sp', bre[:], dr["s5_bT_re"], (), ["bre"])
    dma(S, 'sp', bim[:], dr["s5_bT_im"], (), ["bim"])
    dma(S, 'sp', cre[:], dr["s5_cT_re"], (), ["cre"])
    dma(S, 'sp', cim[:], dr["s5_cT_im"], (), ["cim"])
    Bre, Bim, nBim, u1, u2 = [pp.sb(n_, [64, 32, 16], F32) for n_ in ("Bre", "Bim", "nBim", "u1", "u2")]
    fre_b, fim_b = apx(fre[:], [[0, 16]]), apx(fim[:], [[0, 16]])
    tt(S, 'dve', u1[:], bre[:], fre_b, ALU.mult, ["bre", k(fre)], ["u1"])
    tt(S, 'dve', u2[:], bim[:], fim_b, ALU.mult, ["bim", k(fim)], ["u2"])
    tt(S, 'dve', Bre[:], u1[:], u2[:], ALU.subtract, ["u1", "u2"], ["Bre"])
    tt(S, 'dve', u1[:], bim[:], fre_b, ALU.mult, ["bim", k(fre)], ["u1"])
    tt(S, 'dve', u2[:], bre[:], fim_b, ALU.mult, ["bre", k(fim)], ["u2"])
    tt(S, 'dve', Bim[:], u1[:], u2[:], ALU.add, ["u1", "u2"], ["Bim"])
    ts(S, 'dve', nBim[:], Bim[:], -1.0, None, ALU.mult, None, ["Bim"], ["nBim"])
    HG = 16
    pp1 = Phase(cx, "pbp1")
    T1 = pp1.sb("T1", [64, HG, 17, 16], F32)
    T2 = pp1.sb("T2", [64, HG, 17, 16], F32)
    T3 = pp1.sb("T3", [64, HG, 17, 16], F32)
    Kb = pp1.sb("Kb", [16, 32, 256], BF16)
    dbc = pp1.sb("dbc", [16, 32, 16], F32)
    dma(S, 'sp', dbc[:], dr["s5_d_bc"][0:16], (), ["dbc"])
    Dg = pp1.sb("Dg", [16, 32, 16], F32)
    tt(S, 'dve', Dg[:], dbc[:], fap(identf, 0, [(0, 32), (1, 16)], 0, 16), ALU.mult, ["dbc", "identf"], ["Dg"])
    pk = [pp1.ps("pk%d" % i, [16, 2, 256], F32) for i in range(2)]
    for gh in range(2):
        gs = slice(gh * HG, (gh + 1) * HG)
        Cre_b = fap(cre, gh * HG * 16, [(16, HG), (0, 17), (1, 16)])
        Cim_b = fap(cim, gh * HG * 16, [(16, HG), (0, 17), (1, 16)])
        pr_b = fap(pr, gh * HG * 17, [(17, HG), (1, 17), (0, 16)])
        pi_b = fap(pi_, gh * HG * 17, [(17, HG), (1, 17), (0, 16)])
        tt(S, 'dve', T1[:], Cre_b, pr_b, ALU.mult, ["cre", "pr"], ["T1"])
        tt(S, 'dve', T3[:], Cim_b, pi_b, ALU.mult, ["cim", "pi"], ["T3"])
        tt(S, 'pool', T1[:], T1[:], T3[:], ALU.subtract, ["T1", "T3"], ["T1"])
        tt(S, 'dve', T2[:], Cre_b, pi_b, ALU.mult, ["cre", "pi"], ["T2"])
        tt(S, 'dve', T3[:], Cim_b, pr_b, ALU.mult, ["cim", "pr"], ["T3"])
        tt(S, 'pool', T2[:], T2[:], T3[:], ALU.add, ["T2", "T3"], ["T2"])
        cp(S, 'act', W_CX[:, 0, gs, :].rearrange("p g (t h) -> p g t h", t=16), T1[:, :, 1:17, :], ["T1"], ["W_CX"])
        ts(S, 'pool', W_CX[:, 1, gs, :].rearrange("p g (t h) -> p g t h", t=16), T2[:, :, 1:17, :], -1.0, None, ALU.mult, None,
           ["T2"], ["W_CX"])
        for gl in range(HG):
            g = gh * HG + gl
            pkt = pk[(g // 2) % 2]
            pkk = ("pk", (g // 2) % 2)
            mm(S, pkt[:, g % 2, :], Bre[:, g, :], T1[:, gl, 0:16, :].rearrange("p t h -> p (t h)"), True, False, ["Bre", "T1"], [pkk])
            mm(S, pkt[:, g % 2, :], nBim[:, g, :], T2[:, gl, 0:16, :].rearrange("p t h -> p (t h)"), False, True, ["nBim", "T2"], [pkk])
            if g % 2 == 1:
                cp(S, 'act', Kb[:, g - 1:g + 1, 16:256], pkt[:, :, 16:256], [pkk], ["Kb"])
                tt(S, 'dve', Kb[:, g - 1:g + 1, 0:16], pkt[:, :, 0:16], Dg[:, g - 1:g + 1, :], ALU.add, [pkk, "Dg"], ["Kb"])
    dma(S, 'sp', dr["s5_kall"], Kb[:], ["Kb"], ["kall_d"])
    kd = dr["s5_kall"]
    for s in range(16):
        half, sl = s // 8, s % 8
        n = (16 - s) * 16
        dma(S, 'sp', W_intra[16 * sl:16 * sl + 16, :, half, 16 * s:256], kd[:, :, 0:n], ["kall_d", "W_intra"], ["W_intra"])
    pp1.close()
    pp2 = Phase(cx, "pbp2")
    WTb = [pp2.sb("WTb%d" % i, [64, 32, 256], BF16) for i in range(2)]
    ptw = [pp2.ps("ptw%d" % i, [128, 8, 64], BF16) for i in range(2)]
    Q1 = pp2.sb("Q1", [64, HG, 16, 16], F32)
    Q2 = pp2.sb("Q2", [64, HG, 16, 16], F32)
    for gh in range(2):
        gs = slice(gh * HG, (gh + 1) * HG)
        prr = fap(pr, gh * HG * 17 + 15, [(17, HG), (-1, 16), (0, 16)])
        pir = fap(pi_, gh * HG * 17 + 15, [(17, HG), (-1, 16), (0, 16)])
        Bre_b = fap(Bre, gh * HG * 16, [(16, HG), (0, 16), (1, 16)])
        Bim_b = fap(Bim, gh * HG * 16, [(16, HG), (0, 16), (1, 16)])
        tt(S, 'dve', Q1[:], prr, Bre_b, ALU.mult, ["pr", "Bre"], ["Q1"])
        tt(S, 'dve', Q2[:], pir, Bim_b, ALU.mult, ["pi", "Bim"], ["Q2"])
        tt(S, 'pool', WTb[0][:, gs, :].rearrange("p g (s h) -> p g s h", s=16), Q1[:], Q2[:], ALU.subtract, ["Q1", "Q2"], [("WTb", 0)])
        tt(S, 'dve', Q1[:], prr, Bim_b, ALU.mult, ["pr", "Bim"], ["Q1"])
        tt(S, 'dve', Q2[:], pir, Bre_b, ALU.mult, ["pi", "Bre"], ["Q2"])
        tt(S, 'pool', WTb[1][:, gs, :].rearrange("p g (s h) -> p g s h", s=16), Q1[:], Q2[:], ALU.add, ["Q1", "Q2"], [("WTb", 1)])
    for g2 in range(16):
        pt_ = ptw[g2 % 2]
        for gi in range(2):
            g = 2 * g2 + gi
            for half in range(2):
                for ri in range(2):
                    tr(S, pt_[:, gi * 4 + half * 2 + ri, :], WTb[ri][:, g, half * 128:(half + 1) * 128], ident[0:64, 0:64],
                       [("WTb", ri), "ident"], [("ptw", g2 % 2)])
        cp(S, 'act' if g2 % 2 == 0 else 'dve', W_BU[:, 2 * g2:2 * g2 + 2].rearrange("p g a r q -> p (g a r) q"), pt_[:],
           [("ptw", g2 % 2)], ["W_BU"])
    pp2.close()
    pp.close()

    RA = ph.sb("RA", [128, 16384], BF16)
    RB = ph.sb("RB", [128, 16384], BF16)
    RC = ph.sb("RC", [128, 16448], BF16)
    Ublk = RA[:].rearrange("p (cb s ch) -> p cb s ch", cb=2, s=16)
    Xb = RA[0:64, :].rearrange("p (r g c) -> p r g c", r=2, g=32)
    UT = RB[:].rearrange("p (g a c) -> p g a c", g=32, a=2)
    ygT = RB[:].rearrange("p (k t) -> p k t", k=4)
    GX = RC[0:64, :].bitcast(F32).rearrange("p (r g c) -> p r g c", r=2, g=16)
    Ytok = RC[:, 0:16384].rearrange("p (cb t ch) -> p cb t ch", cb=2, t=16)
    P1 = ph.sb("P1", [64, 2, 16], F32)
    P2 = ph.sb("P2", [64, 2, 16], F32)
    sg = [ph.sb("sg%d" % i, [128, 512], F32) for i in range(2)]
    boT = [ph.sb("boT%d" % i, [128, 4, 512], BF16) for i in range(2)]
    ptu = [ph.ps("ptu%d" % i, [128, 8, 128], BF16) for i in range(2)]
    pG = [ph.ps("pG%d" % i, [64, 2, 256], F32) for i in range(2)]
    pY = [ph.ps("pY%d" % i, [128, 256], F32) for i in range(2)]
    pL = [ph.ps("pL%d" % i, [128, 512], F32) for i in range(2)]
    cat_d = dr["catT0"].rearrange("(c p) t -> p c t", p=128)
    for b in range(nseq):
        tb = b * S_LEN
        dma(S, 'sp', RA[:].rearrange("p (cb x) -> p cb x", cb=2),
            dr["u0"][tb:tb + S_LEN, :].rearrange("(cb c s) ch -> c cb (s ch)", cb=2, c=128),
            [("u0", i) for i in range(b * 8, b * 8 + 8)], ["RA"])
        for cb in range(2):
            cp(S, 'dve' if cb == 0 else 'pool', fap(RC, cb * 8192, [(256, 32), (16, 16), (1, 16)]),
               fap(RA, cb * 8192, [(16, 32), (512, 16), (1, 16)]), ["RA"], ["RC"])
        for g2 in range(16):
            pt_ = ptu[g2 % 2]
            for gi in range(2):
                g = 2 * g2 + gi
                for half in range(2):
                    for cb in range(2):
                        tr(S, pt_[:, gi * 4 + half * 2 + cb, :], fap(RC, cb * 8192 + g * 256 + half * 128, [(1, 128)]), ident[:],
                           ["RC", "ident"], [("ptu", g2 % 2)])
            cp(S, 'act' if g2 % 2 == 0 else 'dve', UT[:, 2 * g2:2 * g2 + 2].rearrange("p g a c -> p (g a c)"),
               pt_[:].rearrange("p a c -> p (a c)"), [("ptu", g2 % 2)], ["RB"])
        for gh in range(2):
            memset(S, 'pool', GX[:, :, :, 0:1], 0.0, ["RC"])
            for gl in range(16):
                g = gh * 16 + gl
                pg_ = pG[gl % 2]
                for ri in range(2):
                    for half in range(2):
                        mm(S, pg_[:, ri, :], W_BU[:, g, half, ri, :], UT[:, g, half, :], half == 0, half == 1,
                           ["W_BU", "RB"], [("pG", gl % 2)])
                cp(S, 'act' if gl % 2 == 0 else 'dve', GX[:, :, gl, 1:257], pg_[:], [("pG", gl % 2)], ["RC"])
            Aa_h = Aa[:, :, gh * 16:(gh + 1) * 16]
            A2_h = A2[:, :, gh * 16:(gh + 1) * 16]
            for c in range(256):
                Xc = GX[:, :, :, c]
                Xs = GX[:, ::-1, :, c]
                tt(S, 'dve', P1[:], Xc, Aa_h, ALU.mult, ["RC", "Aa"], ["P1"])
                tt(S, 'dve', P2[:], Xs, A2_h, ALU.mult, ["RC", "A2"], ["P2"])
                tt(S, 'dve', P1[:], P1[:], P2[:], ALU.add, ["P1", "P2"], ["P1"])
                tt(S, 'dve', GX[:, :, :, c + 1], GX[:, :, :, c + 1], P1[:], ALU.add, ["RC", "P1"], ["RC"])
            cp(S, 'pool', Xb[:, :, gh * 16:(gh + 1) * 16, :], GX[:, :, :, 0:256], ["RC"], ["RA"])
        yn = 0
        for g in range(32):
            for cb in range(2):
                py = pY[yn % 2]
                pyk = ("pY", yn % 2)
                yn += 1
                cs_ = slice(cb * 128, (cb + 1) * 128)
                mm(S, py[:], UT[:, g, 0, cs_], W_intra[:, g, 0, :], True, False, ["RB", "W_intra"], [pyk])
                mm(S, py[:], UT[:, g, 1, cs_], W_intra[:, g, 1, :], False, False, ["RB", "W_intra"], [pyk])
                mm(S, py[:], Xb[:, 0, g, cs_], W_CX[:, 0, g, :], False, False, ["RA", "W_CX"], [pyk])
                mm(S, py[:], Xb[:, 1, g, cs_], W_CX[:, 1, g, :], False, True, ["RA", "W_CX"], [pyk])
                act(S, Ytok[:, cb, :, 16 * g:16 * g + 16], py[:].rearrange("p (t h) -> p t h", t=16), AF.Gelu_apprx_tanh, [pyk], ["RC"])
        tn = 0
        for cb in range(2):
            for kc in range(4):
                for th in range(2):
                    pt_ = ptu[tn % 2]
                    ptk = ("ptu", tn % 2)
                    tn += 1
                    for tl in range(8):
                        t_ = th * 8 + tl
                        tr(S, pt_[:, tl, :], Ytok[:, cb, t_, kc * 128:(kc + 1) * 128], ident[:], ["RC", "ident"], [ptk])
                    dst = fap(RB, kc * 4096 + cb * 2048 + th * 8, [(1, 8), (16, 128)])
                    cp(S, 'act' if tn % 2 == 0 else 'dve', dst, pt_[:], [ptk], ["RB"])
        ln = 0
        for ti in range(8):
            tsl = slice(ti * 512, (ti + 1) * 512)
            bp = ti % 2
            for oc in range(4):
                pl = pL[ln % 2]
                plk = ("pL", ln % 2)
                sgt = sg[ln % 2]
                sgk = ("sg", ln % 2)
                ln += 1
                for kc in range(4):
                    mm(S, pl[:], Wg[:, kc, oc * 128:(oc + 1) * 128], ygT[:, kc, tsl], kc == 0, kc == 3, [("Wg", kc), "RB"], [plk])
                act(S, sgt[:], pl[:], AF.Sigmoid, [plk, "bg"], [sgk], bias=bg[:, oc:oc + 1])
                tt(S, 'dve', boT[bp][:, oc, :], sgt[:], ygT[:, oc, tsl], ALU.mult, [sgk, "RB"], [("boT", bp)])
            dma(S, 'sp', cat_d[:, 4:8, tb + ti * 512: tb + (ti + 1) * 512], boT[bp][:], [("boT", bp)], [("catT0", "b", b, ti)])
    ph.close()


def prep_core(inp, core):
    b0 = 2 * core
    d = {}
    d["x"] = np.ascontiguousarray(inp["x"][b0:b0 + 2].reshape(8192, 1024))
    c2 = inp["c"][b0:b0 + 2]
    d["cT"] = np.ascontiguousarray(c2.T.reshape(8, 128, 2).transpose(1, 0, 2))
    d["ada_w"] = inp["ada_w"]
    d["ada_b"] = inp["ada_b"]
    d["ada_bT"] = np.ascontiguousarray(inp["ada_b"].reshape(2, 48, 128).transpose(0, 2, 1))
    lg = np.stack([inp["ln_mix_g"], inp["ln_ffn_g"]], 0)
    d["ln_gT"] = np.ascontiguousarray(lg.reshape(2, 2, 8, 128).transpose(0, 1, 3, 2))
    return d

def prep_l0(inp, d):
    d["ab_w_in"] = inp["ab_w_in"][0]
    d["qkg"] = np.ascontiguousarray(np.stack([np.tile(inp["a_q_gain"][0], 2), np.tile(inp["a_k_gain"][0], 2)], 1))
    d["ident"] = np.eye(128, dtype=np.float32)
    bo = np.zeros((128, 128), np.float32); bo[:64, :64] = 1; bo[64:, 64:] = 1
    d["blockones"] = bo
    return d

def prep_consts(d):
    d["ident"] = np.eye(128, dtype=np.float32)
    pos = np.arange(4096, dtype=np.float64)[:, None]
    fr = 10000.0 ** (-np.arange(64, dtype=np.float64) / 64)[None, :]
    ang = pos * fr
    d["rot"] = np.stack([np.cos(ang), np.sin(ang)], 1).astype(np.float32)
    lg = np.log(1.0 - 2.0 ** (-5.0 - np.arange(4, dtype=np.float64)))
    idx = np.arange(128) % 64
    d["ret_qd"] = np.exp(lg[None, :] * (idx[:, None] + 1.0)).astype(np.float32)
    d["ret_kd"] = (np.exp(lg[None, :] * (63.0 - idx[:, None])) * 128.0 ** -0.5).astype(np.float32)
    j = np.arange(128)[:, None]; i = np.arange(128)[None, :]
    same = (j // 64) == (i // 64)
    dec = np.exp(lg[:, None, None] * np.abs(i - j)[None]) * same[None]
    d["ret_decT"] = np.ascontiguousarray(dec.transpose(1, 0, 2)).astype(np.float32)
    m = np.ones((128, 256), np.float32)
    m[:, 128:] = (np.arange(128)[None, :] < np.arange(128)[:, None]).astype(np.float32)
    d["sb_mask"] = m
    return d

def prep_s5(inp, d):
    d["s5_lamT_re"] = np.ascontiguousarray(inp["s5_lambda_re"][0].T)
    d["s5_lamT_im"] = np.ascontiguousarray(inp["s5_lambda_im"][0].T)
    d["s5_ldt_bc"] = np.ascontiguousarray(np.broadcast_to(inp["s5_log_dt"][0][None, :], (64, 32)))
    d["s5_bT_re"] = np.ascontiguousarray(inp["s5_b_re"][0].transpose(1, 0, 2))
    d["s5_bT_im"] = np.ascontiguousarray(inp["s5_b_im"][0].transpose(1, 0, 2))
    d["s5_cT_re"] = np.ascontiguousarray(inp["s5_c_re"][0].transpose(2, 0, 1))
    d["s5_cT_im"] = np.ascontiguousarray(inp["s5_c_im"][0].transpose(2, 0, 1))
    d["s5_d_bc"] = np.ascontiguousarray(np.broadcast_to(inp["s5_d"][0][None], (128, 32, 16)))
    d["s5_w_glu"] = inp["s5_w_glu"][0]
    d["s5_bgT"] = np.ascontiguousarray(inp["s5_b_glu"][0].reshape(4, 128).T)
    return d


IN_SPECS = [
    ("x", [8192, 1024]), ("cT", [128, 8, 2]), ("ada_w", [2, 1024, 6144]), ("ada_b", [2, 6144]), ("ada_bT", [2, 128, 48]),
    ("ln_gT", [2, 2, 128, 8]), ("ab_w_in", [1024, 2048]), ("qkg", [128, 2]), ("ident", [128, 128]), ("blockones", [128, 128]),
    ("rel_bias", [8, 257]),
    ("s5_lamT_re", [64, 32]), ("s5_lamT_im", [64, 32]), ("s5_ldt_bc", [64, 32]), ("s5_bT_re", [64, 32, 16]), ("s5_bT_im", [64, 32, 16]),
    ("s5_cT_re", [64, 32, 16]), ("s5_cT_im", [64, 32, 16]), ("s5_d_bc", [128, 32, 16]), ("s5_w_glu", [512, 512]), ("s5_bgT", [128, 4]),
    ("ab_w_out", [1024, 1024]), ("ffn_w_in", [2, 1024, 5632]), ("ffn_w_out", [2, 2816, 1024]),
    ("cd_w_in", [1024, 3584]), ("cd_w_out", [1024, 1024]), ("ret_norm_g", [512]),
    ("rot", [4096, 2, 64]), ("ret_qd", [128, 4]), ("ret_kd", [128, 4]), ("ret_decT", [128, 4, 128]), ("sb_mask", [128, 256]),
]


def build_program():
    nc = bass.Bass("TRN2", target_bir_lowering=False)
    cx = Ctx(nc)
    for n_, s_ in IN_SPECS:
        cx.dram_in(n_, s_)
    cx.dram_out("out", [T_CORE, 1024])
    cx.dram_scr("modfm", [2, 128, 4, 8, 2], F32)
    cx.dram_scr("gbc", [2, 2, 2, 128, 1024], F32)
    cx.dram_scr("qkT", [8, 128, T_CORE], BF16)
    cx.dram_scr("v0", [T_CORE, 512], BF16)
    cx.dram_scr("u0", [T_CORE, 512], BF16)
    cx.dram_scr("relext", [8, 1024], F32)
    cx.dram_scr("s5_kall", [16, 32, 256], BF16)
    cx.dram_scr("catT0", [1024, T_CORE], BF16)
    cx.dram_scr("x1", [T_CORE, 1024], F32)
    cx.dram_scr("c_fm", [12, 128, T_CORE], BF16)
    cx.dram_scr("c_tok", [T_CORE, 3, 512], BF16)
    cx.dram_scr("d_qkT", [8, 128, T_CORE], BF16)
    cx.dram_scr("d_v", [T_CORE, 512], BF16)
    cx.dram_scr("catT1", [1024, T_CORE], BF16)
    with cx.st:
        phase_adaln(cx)
        phase_p1_l0(cx)
        phase_attn(cx)
        phase_s5(cx)
        phase_p3(cx, 0, "catT0", "x", "x1", "ab_w_out")
        phase_p1_l1(cx, "x1")
        phase_sb(cx)
        phase_ret(cx)
        phase_p3(cx, 1, "catT1", "x1", "out", "cd_w_out", final=True)
    return nc


def prep_all(inp, core):
    d = prep_core(inp, core)
    prep_l0(inp, d)
    prep_consts(d)
    prep_s5(inp, d)
    d["rel_bias"] = inp["a_rel_bias"][0]
    d["ab_w_out"] = inp["ab_w_out"][0]
    d["ffn_w_in"] = inp["ffn_w_in"]
    d["ffn_w_out"] = inp["ffn_w_out"]
    d["cd_w_in"] = inp["cd_w_in"][0]
    d["cd_w_out"] = inp["cd_w_out"][0]
    d["ret_norm_g"] = inp["ret_norm_g"][0]
    return {k_: np.ascontiguousarray(np.asarray(d[k_], dtype=np.float32)) for k_, _ in IN_SPECS}


def kernel(**inputs):
    inp = {k_: np.asarray(v_) for k_, v_ in inputs.items()}
    nc = build_program()
    in_maps = [prep_all(inp, core) for core in range(8)]
    res = run_bass_kernel_spmd(nc, in_maps, core_ids=list(range(8)))
    outs = [np.asarray(res.results[i]["out"]).reshape(2, S_LEN, 1024) for i in range(8)]
    return np.concatenate(outs, axis=0).astype(np.float32)
```

```python
import contextlib
from contextlib import ExitStack
import numpy as np
import concourse.bass as bass
import concourse.mybir as mybir
from concourse.bass_utils import run_bass_kernel_spmd

F32 = mybir.dt.float32
BF16 = mybir.dt.bfloat16
I32 = mybir.dt.int32
AF = mybir.ActivationFunctionType
ALU = mybir.AluOpType
AX = mybir.AxisListType

ENGS = ['pe', 'act', 'dve', 'pool', 'sp']
NDS = 6


class Sched:
    def __init__(self, nc, stack, same_engine_sync=True):
        self.nc = nc
        self.ops = {e: [] for e in ENGS}
        self.cnt = {e: 0 for e in ENGS}
        self.seen = {e: {} for e in ENGS}
        self.lastw = {}
        self.readers = {}
        self.same = same_engine_sync
        self.csem = {e: stack.enter_context(nc.semaphore("c_" + e)) for e in ['pe', 'act', 'dve', 'pool']}
        self.dsem = {q: [stack.enter_context(nc.semaphore("d_%s%d" % (q, i))) for i in range(NDS)]
                     for q in ['sp', 'pool', 'act']}
        self.dma_n = {q: 0 for q in ['sp', 'pool', 'act']}
        self.out_tokens = []

    def _deps(self, reads, writes):
        deps = []
        for k in reads:
            if k in self.lastw:
                deps.append(self.lastw[k])
        for k in writes:
            if k in self.lastw:
                deps.append(self.lastw[k])
            deps.extend(self.readers.get(k, []))
        return deps

    def _waits(self, eng, deps):
        need = {}
        for (semkey, sem, val, deng) in deps:
            if deng == eng and semkey[0] == 'c' and (eng == 'pe' or not self.same):
                continue
            if self.seen[eng].get(semkey, 0) >= val:
                continue
            if semkey not in need or need[semkey][1] < val:
                need[semkey] = (sem, val)
        for semkey, (sem, val) in need.items():
            self.seen[eng][semkey] = val
        return list(need.values())

    def _record(self, tok, reads, writes):
        for k in reads:
            self.readers.setdefault(k, []).append(tok)
        for k in writes:
            self.lastw[k] = tok
            self.readers[k] = []

    def op(self, eng, fn, reads=(), writes=()):
        deps = self._deps(reads, writes)
        waits = self._waits(eng, deps)
        self.cnt[eng] += 1
        tok = (('c', eng), self.csem[eng], self.cnt[eng], eng)
        self.ops[eng].append((waits, fn, (self.csem[eng], 1)))
        self._record(tok, reads, writes)
        return tok

    def dma(self, q, fn, reads=(), writes=(), is_output=False):
        deps = self._deps(reads, writes)
        n = self.dma_n[q]
        self.dma_n[q] += 1
        slot = n % NDS
        sem = self.dsem[q][slot]
        semkey = ('d', q, slot)
        prev = 16 * (n // NDS)
        if prev > 0:
            deps.append((semkey, sem, prev, 'dma'))
        waits = self._waits(q, deps)
        tok = (semkey, sem, prev + 16, 'dma')
        self.ops[q].append((waits, fn, (sem, 16)))
        self._record(tok, reads, writes)
        if is_output:
            self.out_tokens.append(tok)
        return tok

    def flush(self, block):
        toks = []
        for e in ['pe', 'act', 'dve', 'pool']:
            if self.cnt[e] > 0:
                toks.append((('c', e), self.csem[e], self.cnt[e], 'x'))
        for q in ['sp', 'pool', 'act']:
            n = self.dma_n[q]
            for j in range(max(0, n - NDS), n):
                slot = j % NDS
                toks.append((('d', q, slot), self.dsem[q][slot], 16 * (j // NDS + 1), 'dma'))
        for e in ENGS:
            self.ops[e].append((self._waits(e, toks), None, None))
        self.lastw = {}
        self.readers = {}

        def run(eng_name):
            lst = self.ops[eng_name]

            def body(e):
                for (waits, fn, inc) in lst:
                    for (sem, val) in waits:
                        e.wait_ge(sem, val)
                    if fn is None:
                        continue
                    ins = fn(e)
                    ins.then_inc(inc[0], inc[1])
            return body

        block.tensor(run('pe'))
        block.scalar(run('act'))
        block.vector(run('dve'))
        block.gpsimd(run('pool'))
        block.sync(run('sp'))
        self.ops = {e: [] for e in ENGS}


EPS = 1e-6
S_LEN = 4096
NSEQ = 2
T_CORE = NSEQ * S_LEN
D = 1024
FF = 2816


def apx(ap, extra):
    return bass.AP(tensor=ap.tensor, offset=ap.offset, ap=[list(a) for a in ap.ap] + [list(e) for e in extra])


class Ctx:
    def __init__(self, nc, debug_out=()):
        self.nc = nc
        self.st = ExitStack()
        self.S = Sched(nc, self.st)
        self.dr = {}
        self.debug_out = set(debug_out)
        self.uid = 0

    def dram_in(self, name, shape, dt=F32):
        self.dr[name] = self.nc.dram_tensor(name, list(shape), dt, kind="ExternalInput").ap()
        return self.dr[name]

    def dram_out(self, name, shape, dt=F32):
        self.dr[name] = self.nc.dram_tensor(name, list(shape), dt, kind="ExternalOutput").ap()
        return self.dr[name]

    def dram_scr(self, name, shape, dt):
        kind = "ExternalOutput" if name in self.debug_out else "Internal"
        self.dr[name] = self.nc.dram_tensor(name, list(shape), dt, kind=kind).ap()
        return self.dr[name]

    def flush(self):
        with self.nc.Block() as block:
            self.S.flush(block)


class Phase:
    def __init__(self, cx, name):
        self.cx = cx
        self.nc = cx.nc
        self.S = cx.S
        self.name = name
        self.st = ExitStack()

    def sb(self, name, shape, dt):
        return self.st.enter_context(self.nc.sbuf_tensor(self.name + "_" + name, list(shape), dt))

    def ps(self, name, shape, dt=F32):
        return self.st.enter_context(self.nc.psum_tensor(self.name + "_" + name, list(shape), dt))

    def close(self):
        self.cx.flush()
        self.st.close()


def mm(S, out, lhsT, rhs, start, stop, r, w):
    S.op('pe', lambda e: e.matmul(out, lhsT, rhs, start=start, stop=stop), r, w)


def tr(S, out, in_, ident, r, w):
    S.op('pe', lambda e: e.transpose(out, in_, ident), r, w)


def act(S, out, in_, func, r, w, scale=1.0, bias=None, accum_out=None, eng='act'):
    kw = {}
    if bias is not None:
        kw['bias'] = bias
    if accum_out is not None:
        kw['accum_out'] = accum_out
    S.op('act', lambda e: e.activation(out=out, in_=in_, func=func, scale=scale, **kw), r, w)


def ts(S, eng, out, in0, s1, s2, op0, op1, r, w, accum_out=None):
    if op1 is None:
        S.op(eng, lambda e: e.tensor_scalar(out, in0, s1, None, op0), r, w)
    elif accum_out is not None:
        S.op(eng, lambda e: e.tensor_scalar(out, in0, s1, s2, op0, op1, accum_out), r, w)
    else:
        S.op(eng, lambda e: e.tensor_scalar(out, in0, s1, s2, op0, op1), r, w)


def tt(S, eng, out, in0, in1, op, r, w):
    S.op(eng, lambda e: e.tensor_tensor(out, in0, in1, op), r, w)


def stt(S, out, in0, scalar, in1, op0, op1, r, w):
    S.op('dve', lambda e: e.scalar_tensor_tensor(out, in0, scalar, in1, op0, op1), r, w)


def cp(S, eng, out, in_, r, w):
    if eng == 'act':
        S.op('act', lambda e: e.copy(out, in_), r, w)
    else:
        S.op(eng, lambda e: e.tensor_copy(out, in_), r, w)


def memset(S, eng, ap, val, w):
    S.op(eng, lambda e: e.memset(ap, val), (), w)


def dma(S, q, out, in_, r, w, is_output=False, slow=False):
    if slow:
        S.dma(q, lambda e: e.dma_start(out=out, in_=in_, allow_slow_non_contiguous=True), r, w, is_output)
    else:
        S.dma(q, lambda e: e.dma_start(out=out, in_=in_), r, w, is_output)


def load_w(S, ph, name, w_dram, K, N, q='pool', nsplit=None):
    kc = K // 128
    t = ph.sb(name, [128, kc, N], BF16)
    src = w_dram.rearrange("(c p) n -> p c n", p=128)
    for c in range(kc):
        dma(S, q, t[:, c, :], src[:, c, :], (), [(name, c)])
    return t


def phase_adaln(cx):
    nc, S, dr = cx.nc, cx.S, cx.dr
    ph = Phase(cx, "p0")
    cT = ph.sb("cT", [128, 8, 2], F32)
    condT = ph.sb("condT", [128, 8, 2], BF16)
    condbc = ph.sb("condbc", [128, 8, 2, 128], BF16)
    aw = ph.sb("aw", [128, 8, 6144], BF16)
    abT = ph.sb("abT", [128, 48], F32)
    lng = ph.sb("lng", [128, 2, 8], F32)
    abbc = ph.sb("abbc", [128, 2, 1024], F32)
    modsb = ph.sb("modsb", [128, 4, 8, 2], F32)
    tmp = ph.sb("tmp", [128, 8, 2], F32)
    gsb = [ph.sb("gsb%d" % i, [128, 1024], F32) for i in range(2)]
    pm = ph.ps("pm", [128, 32, 2], F32)
    pg = [ph.ps("pg%d" % i, [128, 512], F32) for i in range(2)]

    dma(S, 'sp', cT[:], dr["cT"], (), ["cT"])
    act(S, condT[:], cT[:], AF.Silu, ["cT"], ["condT"])
    cp(S, 'dve', condbc[:], apx(condT[:], [[0, 128]]), ["condT"], ["condbc"])
    gi = 0
    for l in range(2):
        src = dr["ada_w"][l].rearrange("(c p) n -> p c n", p=128)
        for c in range(8):
            dma(S, 'pool', aw[:, c, :], src[:, c, :], (), [("aw", c)])
        dma(S, 'sp', abT[:], dr["ada_bT"][l], (), ["abT"])
        dma(S, 'sp', lng[:, 0, :], dr["ln_gT"][0, l], (), ["lng"])
        dma(S, 'sp', lng[:, 1, :], dr["ln_gT"][1, l], (), ["lng"])
        for wi, blk in enumerate((2, 5)):
            dma(S, 'sp', abbc[:, wi, :], dr["ada_b"][l:l + 1, blk * 1024:(blk + 1) * 1024].partition_broadcast(128)
                if False else apx_pb(dr["ada_b"][l, blk * 1024:(blk + 1) * 1024]), (), ["abbc"])
        for jj, blk in enumerate((0, 1, 3, 4)):
            for fc in range(8):
                col = blk * 1024 + fc * 128
                for k in range(8):
                    mm(S, pm[:, jj * 8 + fc, :], aw[:, k, col:col + 128], condT[:, k, :], k == 0, k == 7,
                       [("aw", k), "condT"], ["pm"])
        for jj, blk in enumerate((0, 1, 3, 4)):
            bias = apx(abT[:, blk * 8:(blk + 1) * 8], [[0, 2]])
            if jj in (0, 2):
                tt(S, 'dve', modsb[:, jj + 1], pm[:, jj * 8:(jj + 1) * 8, :], bias, ALU.add, ["pm", "abT"], ["modsb"])
            else:
                tt(S, 'dve', tmp[:], pm[:, jj * 8:(jj + 1) * 8, :], bias, ALU.add, ["pm", "abT"], ["tmp"])
                stt(S, modsb[:, jj - 1], tmp[:], 1.0, apx(lng[:, jj // 2, :], [[0, 2]]), ALU.add, ALU.mult,
                    ["tmp", "lng"], ["modsb"])
        dma(S, 'sp', dr["modfm"][l], modsb[:], ["modsb"], [("modfm", l)])
        for b in range(2):
            for wi, blk in enumerate((2, 5)):
                g = gsb[gi % 2]
                gk = ("gsb", gi % 2)
                for half in range(2):
                    p = pg[half]
                    col = blk * 1024 + half * 512
                    for k in range(8):
                        mm(S, p[:], condbc[:, k, b, :], aw[:, k, col:col + 512], k == 0, k == 7,
                           [("aw", k), "condbc"], [("pg", half)])
                    tt(S, 'dve', g[:, half * 512:(half + 1) * 512], p[:], abbc[:, wi, half * 512:(half + 1) * 512],
                       ALU.add, [("pg", half), "abbc"], [gk])
                dma(S, 'sp', dr["gbc"][l, b, wi], g[:], [gk], [("gbc", l, b, wi)])
                gi += 1
    ph.close()


def apx_pb(ap1d):
    return bass.AP(tensor=ap1d.tensor, offset=ap1d.offset, ap=[[0, 128]] + [list(a) for a in ap1d.ap])


class NormT:
    def __init__(self, ph, nsub, ident):
        self.ph, self.S, self.nsub, self.ident = ph, ph.S, nsub, ident
        self.junk = ph.sb("nt_junk", [128, 1024], BF16)
        self.ss = [ph.sb("nt_ss%d" % i, [128, nsub], F32) for i in range(2)]
        self.rstd = [ph.sb("nt_rstd%d" % i, [128, nsub], F32) for i in range(2)]
        self.mhalf = ph.sb("nt_mhalf", [128, nsub], F32)
        self.xn = ph.sb("nt_xn", [128, nsub, 1024], BF16)
        self.tp = [ph.ps("nt_tp%d" % i, [128, nsub * 128], BF16) for i in range(2)]
        memset(self.S, 'pool', self.mhalf[:], -0.5, ["nt_mhalf"])
        self.n = 0

    def run(self, xt, xkey, hT, hkey, A, B, b):
        self.run_a(xt, xkey)
        self.run_b(hT, hkey, A, B, b)

    def run_a(self, xt, xkey):
        S, nsub = self.S, self.nsub
        par = self.n % 2
        self.n += 1
        ss, rstd = self.ss[par], self.rstd[par]
        for s in range(nsub):
            act(S, self.junk[:], xt[:, s, :], AF.Square, [xkey], ["nt_junk", ("nt_ss", par, s)], accum_out=ss[:, s:s + 1])
        ts(S, 'dve', rstd[:], ss[:], 1.0 / 1024, EPS, ALU.mult, ALU.add, [("nt_ss", par, s) for s in range(nsub)], [("nt_rstd", par)])
        tt(S, 'pool', rstd[:], rstd[:], self.mhalf[:], ALU.pow, [("nt_rstd", par), "nt_mhalf"], [("nt_rstd", par)])
        for s in range(nsub):
            ts(S, 'dve' if s % 2 == 0 else 'pool', self.xn[:, s, :], xt[:, s, :], rstd[:, s:s + 1], None, ALU.mult, None,
               [xkey, ("nt_rstd", par)], [("nt_xn", s)])

    def run_b(self, hT, hkey, A, B, b):
        S, nsub = self.S, self.nsub
        for c in range(8):
            tp = self.tp[c % 2]
            for s in range(nsub):
                tr(S, tp[:, s * 128:(s + 1) * 128], self.xn[:, s, c * 128:(c + 1) * 128], self.ident[:],
                   [("nt_xn", s), "ident"], [("nt_tp", c % 2)])
            if c % 2 == 0:
                ts(S, 'dve', hT[:, c, :], tp[:], A[:, c, b:b + 1], B[:, c, b:b + 1], ALU.mult, ALU.add,
                   [("nt_tp", c % 2), "modAB"], [(hkey, c)])
            else:
                act(S, hT[:, c, :], tp[:], AF.Identity, [("nt_tp", c % 2), "modAB"], [(hkey, c)],
                    scale=A[:, c, b:b + 1], bias=B[:, c, b:b + 1])


def load_consts(ph, S, dr):
    ident = ph.sb("ident", [128, 128], BF16)
    dma(S, 'pool', ident[:], dr["ident"], (), ["ident"])
    return ident


def phase_p1_l0(cx, ntiles=16):
    nc, S, dr = cx.nc, cx.S, cx.dr
    ph = Phase(cx, "p1a")
    ident = load_consts(ph, S, dr)
    bones = ph.sb("bones", [128, 128], BF16)
    dma(S, 'pool', bones[:], dr["blockones"], (), ["bones"])
    W = load_w(S, ph, "W", dr["ab_w_in"], 1024, 2048)
    wkeys = [("W", c) for c in range(8)]
    modAB = ph.sb("modAB", [128, 4, 8, 2], F32)
    dma(S, 'sp', modAB[:], dr["modfm"][0], [("modfm", 0)], ["modAB"])
    qkg = ph.sb("qkg", [128, 2], F32)
    dma(S, 'sp', qkg[:], dr["qkg"], (), ["qkg"])
    cb = ph.sb("cbias", [128, 2], F32)
    memset(S, 'pool', cb[:, 0:1], 64 * EPS, ["cbias"])
    memset(S, 'pool', cb[:, 1:2], EPS, ["cbias"])
    nt = NormT(ph, 4, ident)
    xt = [ph.sb("xt%d" % i, [128, 4, 1024], F32) for i in range(2)]
    hT = [ph.sb("hT%d" % i, [128, 8, 512], BF16) for i in range(2)]
    qkst = [ph.sb("qkst%d" % i, [128, 8, 512], BF16) for i in range(2)]
    vust = [ph.sb("vust%d" % i, [128, 4, 1024], BF16) for i in range(2)]
    sqk = [ph.sb("sqk%d" % i, [128, 512], BF16) for i in range(3)]
    rs = [ph.sb("rs%d" % i, [128, 512], F32) for i in range(3)]
    pq = [ph.ps("pq%d" % i, [128, 512], F32) for i in range(3)]
    pss = [ph.ps("pss%d" % i, [128, 512], F32) for i in range(1)]
    pv = [ph.ps("pv%d" % i, [128, 512], F32) for i in range(2)]
    qkT_d = dr["qkT"].rearrange("c p t -> p c t")
    def pre_a1(ti):
        par = ti % 2
        t0 = ti * 512
        dma(S, 'sp', xt[par][:], dr["x"][t0:t0 + 512, :].rearrange("(s p) d -> p s d", p=128), (), [("xt", par)])

    def pre_a2(ti):
        par = ti % 2
        nt.run_a(xt[par], ("xt", par))

    def pre_b(ti):
        par = ti % 2
        nt.run_b(hT[par], ("hT", par), modAB[:, 0], modAB[:, 1], ti // 8)

    pre_a1(0)
    pre_a2(0)
    pre_b(0)
    for ti in range(ntiles):
        par = ti % 2
        b = ti // 8
        t0 = ti * 512
        if ti + 1 < ntiles:
            pre_a1(ti + 1)
        hkeys = [(("hT", par), c) for c in range(8)]
        def qk_tail(oc):
            i3 = oc % 3
            p = pq[i3]
            pk = ("pq", i3)
            isk = 1 if oc >= 4 else 0
            sk = ("sqk", i3)
            mm(S, pss[0][:], bones[:], sqk[i3][:], True, True, [sk, "bones"], ["pss"])
            rk = ("rs", i3)
            act(S, rs[i3][:], pss[0][:], AF.Sqrt, ["pss", "cbias"], [rk],
                scale=(1.0 / 64 if isk else 1.0), bias=cb[:, isk:isk + 1])
            S.op('dve', (lambda o: (lambda e: e.reciprocal(o, o)))(rs[i3][:]), [rk], [rk])
            stt(S, qkst[par][:, oc, :], p[:], qkg[:, isk:isk + 1], rs[i3][:], ALU.mult, ALU.mult,
                [pk, rk, "qkg"], [("qkst", par)])

        for oc in range(8):
            i3 = oc % 3
            p = pq[i3]
            pk = ("pq", i3)
            for k in range(8):
                mm(S, p[:], W[:, k, oc * 128:(oc + 1) * 128], hT[par][:, k, :], k == 0, k == 7,
                   [wkeys[k], hkeys[k]], [pk])
            act(S, sqk[i3][:], p[:], AF.Square, [pk], [("sqk", i3)])
            if oc >= 1:
                qk_tail(oc - 1)
            if oc == 3 and ti + 1 < ntiles:
                pre_a2(ti + 1)
        tails_left = [7]
        for vi, (col, dname) in enumerate(((1024, "v0"), (1536, "u0"))):
            for s in range(4):
                p = pv[s % 2]
                pk = ("pv", s % 2)
                for k in range(8):
                    mm(S, p[:], hT[par][:, k, s * 128:(s + 1) * 128], W[:, k, col:col + 512], k == 0, k == 7,
                       [wkeys[k], hkeys[k]], [pk])
                if tails_left:
                    qk_tail(tails_left.pop(0))
                    if not tails_left:
                        dma(S, 'sp', qkT_d[:, :, t0:t0 + 512], qkst[par][:], [("qkst", par)], [("qkT", ti)])
                cp(S, 'act' if s % 2 == 0 else 'dve', vust[par][:, s, vi * 512:(vi + 1) * 512], p[:], [pk], [("vust", par, vi)])
            dma(S, 'sp', dr[dname][t0:t0 + 512, :].rearrange("(s p) d -> p s d", p=128),
                vust[par][:, :, vi * 512:(vi + 1) * 512], [("vust", par, vi)], [(dname, ti)])
        if ti + 1 < ntiles:
            pre_b(ti + 1)
    ph.close()


def phase_p3(cx, l, cat_name, x_name, out_name, wo_name, ntiles=32, final=False):
    nc, S, dr = cx.nc, cx.S, cx.dr
    ph = Phase(cx, "p3_%d" % l)
    ident = load_consts(ph, S, dr)
    Wo = load_w(S, ph, "Wo", dr[wo_name], 1024, 1024)
    Win = load_w(S, ph, "Win", dr["ffn_w_in"][l], 1024, 2 * FF)
    Wout = load_w(S, ph, "Wout", dr["ffn_w_out"][l], FF, 1024)
    modAB = ph.sb("modAB", [128, 4, 8, 2], F32)
    dma(S, 'sp', modAB[:], dr["modfm"][l], [("modfm", l)], ["modAB"])
    gb = ph.sb("gb", [128, 2, 1024], F32)
    nt = NormT(ph, 2, ident)
    xt2 = [ph.sb("xt%d" % i, [128, 2, 1024], F32) for i in range(2)]
    ct = [ph.sb("ct%d" % i, [128, 8, 256], BF16) for i in range(2)]
    hT = ph.sb("hT", [128, 8, 256], BF16)
    hact = ph.sb("hact", [128, 22, 256], BF16)
    sil = [ph.sb("sil%d" % i, [128, 256], F32) for i in range(2)]
    tmp = ph.sb("tmp", [128, 1024], F32)
    po = ph.ps("po", [128, 1024], F32)
    pw = ph.ps("pw", [128, 1024], F32)
    pgu = [ph.ps("pgu%d" % i, [128, 2, 256], F32) for i in range(2)]
    cat_d = dr[cat_name].rearrange("(c p) t -> p c t", p=128)
    hkeys = [("hT", c) for c in range(8)]

    def load(ti):
        par = ti % 2
        b = ti // 16
        t0 = ti * 256
        if ti % 16 == 0:
            for wi in range(2):
                dma(S, 'sp', gb[:, wi, :], dr["gbc"][l, b, wi], [("gbc", l, b, wi)], ["gb"])
        dma(S, 'sp', xt2[par][:], dr[x_name][t0:t0 + 256, :].rearrange("(s p) d -> p s d", p=128), [(x_name, ti)], [("xt", par)])
        dma(S, 'sp', ct[par][:], cat_d[:, :, t0:t0 + 256], [(cat_name, ti)], [("ct", par)])

    def outproj(ti):
        par = ti % 2
        xt = xt2[par]
        for s in range(2):
            for half in range(2):
                for k in range(8):
                    mm(S, po[:, half * 512:(half + 1) * 512], ct[par][:, k, s * 128:(s + 1) * 128],
                       Wo[:, k, half * 512:(half + 1) * 512], k == 0, k == 7, [("Wo", k), ("ct", par)], [("po", half)])
            for half in range(2):
                hs_ = slice(half * 512, (half + 1) * 512)
                tt(S, 'dve', tmp[:, hs_], po[:, hs_], gb[:, 0, hs_], ALU.mult, [("po", half), "gb"], [("tmp", half)])
                tt(S, 'pool', xt[:, s, hs_], xt[:, s, hs_], tmp[:, hs_], ALU.add, [("tmp", half), ("xt", par)], [("xt", par)])

    def ffn_in(ti):
        for j in range(22):
            gu = pgu[j % 2]
            gk = ("pgu", j % 2)
            for hh in range(2):
                col = hh * FF + j * 128
                for k in range(8):
                    mm(S, gu[:, hh, :], Win[:, k, col:col + 128], hT[:, k, :], k == 0, k == 7,
                       [("Win", k), hkeys[k]], [gk])
            sk = ("sil", j % 2)
            act(S, sil[j % 2][:], gu[:, 0, :], AF.Silu, [gk], [sk])
            tt(S, 'dve', hact[:, j, :], gu[:, 1, :], sil[j % 2][:], ALU.mult, [gk, sk], [("hact", j)])

    def ffn_out(ti):
        par = ti % 2
        xt = xt2[par]
        t0 = ti * 256
        for s in range(2):
            for half in range(2):
                for j in range(22):
                    mm(S, pw[:, half * 512:(half + 1) * 512], hact[:, j, s * 128:(s + 1) * 128],
                       Wout[:, j, half * 512:(half + 1) * 512], j == 0, j == 21, [("Wout", j), ("hact", j)], [("pw", half)])
            for half in range(2):
                hs_ = slice(half * 512, (half + 1) * 512)
                tt(S, 'dve', tmp[:, hs_], pw[:, hs_], gb[:, 1, hs_], ALU.mult, [("pw", half), "gb"], [("tmp", half)])
                tt(S, 'pool', xt[:, s, hs_], xt[:, s, hs_], tmp[:, hs_], ALU.add, [("tmp", half), ("xt", par)], [("xt", par)])
        dma(S, 'sp', dr[out_name][t0:t0 + 256, :].rearrange("(s p) d -> p s d", p=128), xt[:], [("xt", par)], [(out_name, ti)],
            is_output=final)

    load(0)
    outproj(0)
    nt.run_a(xt2[0], ("xt", 0))
    nt.run_b(hT, "hT", modAB[:, 2], modAB[:, 3], 0)
    for ti in range(ntiles):
        nxt = ti + 1 < ntiles
        if nxt and (ti + 1) % 16 != 0:
            load(ti + 1)
        ffn_in(ti)
        if nxt and (ti + 1) % 16 != 0:
            outproj(ti + 1)
            nt.run_a(xt2[(ti + 1) % 2], ("xt", (ti + 1) % 2))
        ffn_out(ti)
        if nxt:
            if (ti + 1) % 16 == 0:
                load(ti + 1)
                outproj(ti + 1)
                nt.run_a(xt2[(ti + 1) % 2], ("xt", (ti + 1) % 2))
            nt.run_b(hT, "hT", modAB[:, 2], modAB[:, 3], (ti + 1) // 16)
    ph.close()


def phase_attn(cx, nseq=NSEQ, nqb=32):
    nc, S, dr = cx.nc, cx.S, cx.dr
    ph = Phase(cx, "pa")
    ident = load_consts(ph, S, dr)
    NEG = -30000.0
    Er = ph.sb("Er", [8, 257], F32)
    E = ph.sb("E", [8, 1024], F32)
    c256 = ph.sb("c256", [128, 8], F32)
    dma(S, 'sp', Er[:], dr["rel_bias"], (), ["Er"])
    rb = dr["rel_bias"]
    dma(S, 'sp', c256[:], bass.AP(tensor=rb.tensor, offset=rb.offset + 256, ap=[[0, 128], [257, 8]]), (), ["c256"], slow=True)
    memset(S, 'dve', E[:], 0.0, ["E"])
    ts(S, 'dve', E[:, 0:767], E[:, 0:767], Er[:, 256:257], None, ALU.add, None, ["E", "Er"], ["E"])
    cp(S, 'dve', E[:, 767:1024], Er[:, ::-1], ["E", "Er"], ["E"])
    dma(S, 'sp', dr["relext"], E[:], ["E"], ["relext"])
    BT = ph.sb("BT", [128, 5, 8, 128], F32)
    memset(S, 'pool', BT[:], 0.0, ["BT"])
    ext = dr["relext"]
    for j in range(3):
        tt(S, 'pool', BT[:, j, :, :], BT[:, j, :, :], apx(c256[:, :], [[0, 128]]), ALU.add, ["BT", "c256"], ["BT"])
    for j in (3, 4):
        for h in range(8):
            src = bass.AP(tensor=ext.tensor, offset=ext.offset + h * 1024 + 1023 - (5 - j) * 128 - 127, ap=[[1, 128], [1, 128]])
            dma(S, 'sp', BT[:, j, h, :], src, ["relext", "BT"], ["BT"])
    memset(S, 'pool', BT[0:64, 0, :, 0:64], NEG, ["BT"])
    memset(S, 'pool', BT[64:128, 4, :, 64:128], NEG, ["BT"])
    qT = ph.sb("qT", [128, 4, S_LEN], BF16)
    kT = ph.sb("kT", [128, 4, S_LEN], BF16)
    Vr = ph.sb("Vr", [128, 32, 512], BF16)
    Va = ph.sb("Va", [128, 32, 8, 65], BF16)
    memset(S, 'pool', Va[:, :, :, 64:65], 1.0, ["Va1"])
    NBA = 3
    sbf = [ph.sb("sbf%d" % i, [128, 5, 128], F32) for i in range(NBA)]
    pT = [ph.sb("pT%d" % i, [128, 5, 128], BF16) for i in range(NBA)]
    rc = ph.sb("rc", [128, 8], F32)
    ao = ph.sb("ao", [128, 512], BF16)
    aT = [ph.sb("aT%d" % i, [128, 4, 512], BF16) for i in range(2)]
    ps = [ph.ps("ps%d" % i, [128, 8, 128], F32) for i in range(2)]
    po = [ph.ps("po%d" % i, [128, 4, 65], F32) for i in range(2)]
    tp = ph.ps("tp", [128, 4, 128], BF16)
    qk_d = dr["qkT"].rearrange("c p t -> p c t")
    cat_d = dr["catT0"].rearrange("(c p) t -> p c t", p=128)
    hn = 0
    for b in range(nseq):
        tb = b * S_LEN
        for c in range(4):
            dma(S, 'sp', qT[:, c, :], qk_d[:, c, tb:tb + S_LEN], [("qkT", i) for i in range(b * 8, b * 8 + 8)], ["qT"])
            dma(S, 'sp', kT[:, c, :], qk_d[:, 4 + c, tb:tb + S_LEN], [("qkT", i) for i in range(b * 8, b * 8 + 8)], ["kT"])
        for c in range(4):
            dma(S, 'sp', Vr[:, c * 8:(c + 1) * 8, :],
                dr["v0"][tb + c * 1024: tb + (c + 1) * 1024, :].rearrange("(s p) d -> p s d", p=128),
                [("v0", i) for i in range(b * 8, b * 8 + 8)], ["Vr"])
        for c in range(4):
            cp(S, 'pool', Va[:, c * 8:(c + 1) * 8, :, 0:64], Vr[:, c * 8:(c + 1) * 8, :].rearrange("p s (h d) -> p s h d", h=8),
               ["Vr"], ["Va"])
        units = [(qb, h) for qb in range(nqb) for h in range(8)]
        bufi = {}

        def st1(u):
            nonlocal hn
            qb, h = u
            kb0 = max(0, qb - 4)
            nkb = qb - kb0 + 1
            j0 = 5 - nkb
            pr, base = h // 2, 64 * (h % 2)
            par = hn % NBA
            p_s, pk = ps[hn % 2], ("ps", hn % 2)
            hn += 1
            bufi[u] = par
            for j in range(nkb):
                kb = kb0 + j
                mm(S, p_s[:, j, :], kT[base:base + 64, pr, kb * 128:(kb + 1) * 128],
                   qT[base:base + 64, pr, qb * 128:(qb + 1) * 128], True, True, ["kT", "qT"], [pk])
            tt(S, 'dve', sbf[par][:, 0:nkb, :], p_s[:, 0:nkb, :], BT[:, j0:5, h, ::-1], ALU.add, [pk, "BT"], [("sbf", par)])
            act(S, pT[par][:, 0:nkb, :], sbf[par][:, 0:nkb, :], AF.Exp, [("sbf", par)], [("pT", par)])

        def st2(u):
            qb, h = u
            kb0 = max(0, qb - 4)
            nkb = qb - kb0 + 1
            par = bufi.pop(u)
            for j in range(nkb):
                kb = kb0 + j
                mm(S, po[h // 4][:, h % 4, :], pT[par][:, j, :], Va[:, kb, h, :], j == 0, j == nkb - 1,
                   [("pT", par), "Va", "Va1"], [("po", h // 4)])
            if h == 7:
                epi(qb)

        def epi(qb):
            for g in range(2):
                S.op('dve', (lambda o, i: (lambda e: e.reciprocal(o, i)))(rc[:, g * 4:(g + 1) * 4], po[g][:, :, 64]),
                     [("po", g)], [("rc", g)])
                tt(S, 'dve', ao[:, g * 256:(g + 1) * 256].rearrange("p (h d) -> p h d", h=4), po[g][:, :, 0:64],
                   apx(rc[:, g * 4:(g + 1) * 4], [[0, 64]]), ALU.mult, [("po", g), ("rc", g)], [("ao", g)])
            for c in range(4):
                tr(S, tp[:, c, :], ao[:, c * 128:(c + 1) * 128], ident[:], [("ao", c // 2), "ident"], ["tp"])
            apar = (qb // 4) % 2
            cp(S, 'act', aT[apar][:, :, (qb % 4) * 128:(qb % 4 + 1) * 128], tp[:], ["tp"], [("aT", apar)])
            if qb % 4 == 3 or qb == nqb - 1:
                q0 = (qb // 4) * 4
                n = (qb - q0 + 1) * 128
                dma(S, 'sp', cat_d[:, 0:4, tb + q0 * 128: tb + q0 * 128 + n], aT[apar][:, :, 0:n], [("aT", apar)],
                    [("catT0", "a", b, qb // 4)])

        SK = 2
        for i in range(len(units) + SK):
            if i < len(units):
                st1(units[i])
            if i - SK >= 0:
                st2(units[i - SK])
    ph.close()


def phase_p1_l1(cx, x_name, ntiles=16):
    nc, S, dr = cx.nc, cx.S, cx.dr
    ph = Phase(cx, "p1b")
    ident = load_consts(ph, S, dr)
    W = load_w(S, ph, "W", dr["cd_w_in"], 1024, 3584)
    wkeys = [("W", c) for c in range(8)]
    modAB = ph.sb("modAB", [128, 4, 8, 2], F32)
    dma(S, 'sp', modAB[:], dr["modfm"][1], [("modfm", 1)], ["modAB"])
    QD = ph.sb("QD", [128, 4], F32)
    KD = ph.sb("KD", [128, 4], F32)
    dma(S, 'sp', QD[:], dr["ret_qd"], (), ["QD"])
    dma(S, 'sp', KD[:], dr["ret_kd"], (), ["KD"])
    nt = NormT(ph, 4, ident)
    xt = [ph.sb("xt%d" % i, [128, 4, 1024], F32) for i in range(2)]
    hT2 = [ph.sb("hT%d" % i, [128, 8, 512], BF16) for i in range(2)]
    rot = [ph.sb("rot%d" % i, [128, 4, 2, 64], F32) for i in range(2)]
    t12 = [ph.sb("t12_%d" % i, [128, 4, 64], F32) for i in range(4)]
    R = [ph.sb("R%d" % i, [128, 4, 2, 64], F32) for i in range(2)]
    qkb = ph.sb("qkb", [128, 3, 512], BF16)
    fst = [ph.sb("fst%d" % i, [128, 12, 512], BF16) for i in range(2)]
    tst = [ph.sb("tst%d" % i, [128, 4, 3, 512], BF16) for i in range(2)]
    dst = [ph.sb("dst%d" % i, [128, 8, 512], BF16) for i in range(2)]
    dvs = [ph.sb("dvs%d" % i, [128, 4, 512], BF16) for i in range(2)]
    NPT = 4
    pt = [ph.ps("pt%d" % i, [128, 512], F32) for i in range(NPT)]
    ptr2 = [ph.ps("ptr%d" % i, [128, 4, 128], BF16) for i in range(2)]
    cf_d = dr["c_fm"].rearrange("k p t -> p k t")
    ct_d = dr["c_tok"]
    dq_d = dr["d_qkT"].rearrange("c p t -> p c t")
    pn = 0
    def pre_a(ti):
        par = ti % 2
        t0 = ti * 512
        pos0 = t0 % S_LEN
        xk = ("xt", par)
        dma(S, 'sp', xt[par][:], dr[x_name][t0:t0 + 512, :].rearrange("(s p) d -> p s d", p=128), [(x_name, 2 * ti), (x_name, 2 * ti + 1)], [xk])
        dma(S, 'sp', rot[par][:], dr["rot"][pos0:pos0 + 512].rearrange("(s p) a f -> p s a f", p=128), (), [("rot", par)])

    def pre_a2(ti):
        par = ti % 2
        nt.run_a(xt[par], ("xt", par))

    def pre_b(ti):
        par = ti % 2
        nt.run_b(hT2[par], ("hT", par), modAB[:, 0], modAB[:, 1], ti // 8)

    pre_a(0)
    pre_a2(0)
    pre_b(0)
    trn = 0
    for ti in range(ntiles):
        par = ti % 2
        b = ti // 8
        t0 = ti * 512
        if ti + 1 < ntiles:
            pre_a(ti + 1)
        hT = hT2[par]
        hkeys = [(("hT", par), c) for c in range(8)]

        def tokmm(s, col):
            nonlocal pn
            p = pt[pn % NPT]
            pk = ("pt", pn % NPT)
            pn += 1
            for k in range(8):
                mm(S, p[:], hT[:, k, s * 128:(s + 1) * 128], W[:, k, col:col + 512], k == 0, k == 7, [wkeys[k], hkeys[k]], [pk])
            return p, pk

        def fm_chunk(oc):
            nonlocal pn
            p = pt[pn % NPT]
            pk = ("pt", pn % NPT)
            pn += 1
            col = 2048 + oc * 128
            for k in range(8):
                mm(S, p[:], W[:, k, col:col + 128], hT[:, k, :], k == 0, k == 7, [wkeys[k], hkeys[k]], [pk])
            cp(S, 'act' if oc % 2 == 0 else 'dve', dst[par][:, oc, :], p[:], [pk], [("dst", par)])

        for s in range(4):
            cosv = rot[par][:, s, 0, :]
            sinv = rot[par][:, s, 1, :]
            cos4 = bass.AP(tensor=cosv.tensor, offset=cosv.offset, ap=[list(cosv.ap[0]), [0, 4], list(cosv.ap[1])])
            sin4 = bass.AP(tensor=sinv.tensor, offset=sinv.offset, ap=[list(sinv.ap[0]), [0, 4], list(sinv.ap[1])])
            for qi, col in enumerate((0, 512)):
                p, pk = tokmm(s, col)
                pv4 = p[:].rearrange("p (h a f) -> p h a f", h=4, a=2)
                x1, x2 = pv4[:, :, 0, :], pv4[:, :, 1, :]
                rk = ("R", qi)
                tt(S, 'dve', t12[0][:], x1, cos4, ALU.mult, [pk, ("rot", par)], ["t0"])
                tt(S, 'dve', t12[1][:], x2, sin4, ALU.mult, [pk, ("rot", par)], ["t1"])
                tt(S, 'dve', t12[2][:], x1, sin4, ALU.mult, [pk, ("rot", par)], ["t2"])
                tt(S, 'dve', t12[3][:], x2, cos4, ALU.mult, [pk, ("rot", par)], ["t3"])
                tt(S, 'pool', R[qi][:, :, 0, :], t12[0][:], t12[1][:], ALU.subtract, ["t0", "t1"], [rk])
                tt(S, 'pool', R[qi][:, :, 1, :], t12[2][:], t12[3][:], ALU.add, ["t2", "t3"], [rk])
            Rq = R[0][:].rearrange("p h a f -> p h (a f)")
            Rk = R[1][:].rearrange("p h a f -> p h (a f)")
            q3 = qkb[:].rearrange("p k (h d) -> p k h d", h=4)
            cp(S, 'act', q3[:, 0], Rq, [("R", 0)], [("qkb", 0)])
            tt(S, 'dve', q3[:, 1], Rq, apx(QD[:, :], [[0, 128]]), ALU.mult, [("R", 0), "QD"], [("qkb", 1)])
            act(S, q3[:, 2], Rk, AF.Copy, [("R", 1)], [("qkb", 2)], scale=128.0 ** -0.5)
            tt(S, 'pool', tst[par][:, s, 0, :].rearrange("p (h d) -> p h d", h=4), Rk, apx(KD[:, :], [[0, 128]]), ALU.mult,
               [("R", 1), "KD"], [("tst", par)])
            p, pk = tokmm(s, 1024)
            cp(S, 'act', tst[par][:, s, 1, :], p[:], [pk], [("tst", par)])
            p, pk = tokmm(s, 1536)
            act(S, tst[par][:, s, 2, :], p[:], AF.Silu, [pk], [("tst", par)])
            p, pk = tokmm(s, 3072)
            cp(S, 'act', dvs[par][:, s, :], p[:], [pk], [("dvs", par)])
            for oc in (2 * s, 2 * s + 1):
                fm_chunk(oc)
            for kind in range(3):
                ptr = ptr2[trn % 2]
                ptk = ("ptr", trn % 2)
                for h in range(4):
                    tr(S, ptr[:, h, :], qkb[:, kind, h * 128:(h + 1) * 128], ident[:], [("qkb", kind), "ident"], [ptk])
                cp(S, 'act' if trn % 2 == 0 else 'dve', fst[par][:, kind * 4:(kind + 1) * 4, s * 128:(s + 1) * 128], ptr[:],
                   [ptk], [("fst", par)])
                trn += 1
            if s == 1 and ti + 1 < ntiles:
                pre_a2(ti + 1)
        dma(S, 'sp', cf_d[:, :, t0:t0 + 512], fst[par][:], [("fst", par)], [("c_fm", ti)])
        dma(S, 'sp', ct_d[t0:t0 + 512].rearrange("(s p) k d -> p s k d", p=128), tst[par][:], [("tst", par)], [("c_tok", ti)])
        dma(S, 'sp', dq_d[:, :, t0:t0 + 512], dst[par][:], [("dst", par)], [("d_qkT", ti)])
        dma(S, 'sp', dr["d_v"][t0:t0 + 512, :].rearrange("(s p) d -> p s d", p=128), dvs[par][:], [("dvs", par)], [("d_v", ti)])
        if ti + 1 < ntiles:
            pre_b(ti + 1)
    ph.close()


def phase_sb(cx, nseq=NSEQ, nblk=32):
    nc, S, dr = cx.nc, cx.S, cx.dr
    ph = Phase(cx, "pd")
    ident = load_consts(ph, S, dr)
    mask = ph.sb("mask", [128, 256], F32)
    dma(S, 'sp', mask[:], dr["sb_mask"], (), ["mask"])
    ones = ph.sb("ones", [128, 256], F32)
    memset(S, 'pool', ones[:], 1.0, ["ones"])
    one1 = ph.sb("one1", [128, 1], F32)
    memset(S, 'pool', one1[:], 1.0, ["one1"])
    qT = ph.sb("qT", [128, 4, S_LEN], BF16)
    kT = ph.sb("kT", [128, 4, S_LEN], BF16)
    V = ph.sb("V", [128, 32, 512], BF16)
    ex = [ph.sb("ex%d" % i, [128, 256], F32) for i in range(8)]
    sp = [ph.sb("sp%d" % i, [128, 256], F32) for i in range(8)]
    Rc = [ph.sb("Rc%d" % i, [128, 256], F32) for i in range(8)]
    lw = [ph.sb("lw%d" % i, [128, 256], F32) for i in range(8)]
    wm = [ph.sb("wm%d" % i, [128, 256], BF16) for i in range(8)]
    wT = [ph.sb("wT%d" % i, [128, 2, 128], BF16) for i in range(8)]
    do_b = ph.sb("do_b", [128, 512], BF16)
    dT = [ph.sb("dT%d" % i, [128, 4, 512], BF16) for i in range(2)]
    pz = [ph.ps("pz%d" % i, [128, 256], F32) for i in range(4)]
    pwt = [ph.ps("pwt%d" % i, [128, 2, 128], BF16) for i in range(2)]
    po = ph.ps("po", [128, 8, 64], F32)
    ptp = ph.ps("ptp", [128, 4, 128], BF16)
    qk_d = dr["d_qkT"].rearrange("c p t -> p c t")
    cat_d = dr["catT1"].rearrange("(c p) t -> p c t", p=128)
    hn = 0
    for b in range(nseq):
        tb = b * S_LEN
        rk = [("d_qkT", i) for i in range(b * 8, b * 8 + 8)]
        for c in range(4):
            dma(S, 'sp', qT[:, c, :], qk_d[:, c, tb:tb + S_LEN], rk, ["qT"])
            dma(S, 'sp', kT[:, c, :], qk_d[:, 4 + c, tb:tb + S_LEN], rk, ["kT"])
        for c in range(4):
            dma(S, 'sp', V[:, c * 8:(c + 1) * 8, :],
                dr["d_v"][tb + c * 1024: tb + (c + 1) * 1024, :].rearrange("(s p) d -> p s d", p=128),
                [("d_v", i) for i in range(b * 8, b * 8 + 8)], ["V"])
        units = [(blk, h) for blk in range(nblk) for h in range(8)]
        bufi = {}

        def geom(blk):
            nk = 1 if blk == 0 else 2
            return nk, nk * 128, (blk + 1 - nk) * 128, 256 - nk * 128

        def stA(u):
            nonlocal hn
            blk, h = u
            nk, W_, k0, m0 = geom(blk)
            pr, base = h // 2, 64 * (h % 2)
            par = hn % 8
            z, zk = pz[hn % 4], ("pz", hn % 4)
            bufi[u] = (par, hn % 2)
            hn += 1
            mm(S, z[:, 0:W_], qT[base:base + 64, pr, blk * 128:(blk + 1) * 128], kT[base:base + 64, pr, k0:k0 + W_],
               True, True, ["qT", "kT"], [zk])
            act(S, ex[par][:, 0:W_], z[:, 0:W_], AF.Exp, [zk], [("ex", par)], scale=0.125)
            act(S, sp[par][:, 0:W_], ex[par][:, 0:W_], AF.Ln, [("ex", par), "one1"], [("sp", par)], bias=one1[:, 0:1])
            tt(S, 'dve', sp[par][:, 0:W_], sp[par][:, 0:W_], mask[:, m0:256], ALU.mult, [("sp", par), "mask"], [("sp", par)])
            S.op('dve', (lambda o, d0, d1: (lambda e: e.tensor_tensor_scan(o, d0, d1, 0.0, ALU.mult, ALU.add)))(
                Rc[par][:, 0:W_][:, ::-1], ones[:, 0:W_], sp[par][:, 0:W_][:, ::-1]), [("sp", par), "ones"], [("Rc", par)])
            stt(S, lw[par][:, 0:W_], z[:, 0:W_], 0.125, Rc[par][:, 0:W_], ALU.mult, ALU.subtract, [zk, ("Rc", par)], [("lw", par)])

        def stA2(u):
            blk, h = u
            nk, W_, k0, m0 = geom(blk)
            par, p2 = bufi[u]
            act(S, lw[par][:, 0:W_], lw[par][:, 0:W_], AF.Exp, [("lw", par)], [("lw", par)])
            tt(S, 'pool', wm[par][:, 0:W_], lw[par][:, 0:W_], mask[:, m0:256], ALU.mult, [("lw", par), "mask"], [("wm", par)])

        def stB(u):
            blk, h = u
            nk, W_, k0, m0 = geom(blk)
            par, p2 = bufi[u]
            pw2, pwk = pwt[p2], ("pwt", p2)
            for n in range(nk):
                tr(S, pw2[:, n, :], wm[par][:, n * 128:(n + 1) * 128], ident[:], [("wm", par), "ident"], [pwk])
            cp(S, 'act', wT[par][:, 0:nk, :], pw2[:, 0:nk, :], [pwk], [("wT", par)])

        def stC(u):
            blk, h = u
            nk, W_, k0, m0 = geom(blk)
            par, p2 = bufi.pop(u)
            for n in range(nk):
                kb = blk + 1 - nk + n
                mm(S, po[:, h, :], wT[par][:, n, :], V[:, kb, h * 64:(h + 1) * 64], n == 0, n == nk - 1,
                   [("wT", par), "V"], ["po"])
            if h == 7:
                epi(blk)

        def epi(blk):
            cp(S, 'dve', do_b[:], po[:].rearrange("p h d -> p (h d)"), ["po"], ["do_b"])
            for c in range(4):
                tr(S, ptp[:, c, :], do_b[:, c * 128:(c + 1) * 128], ident[:], ["do_b", "ident"], ["ptp"])
            apar = (blk // 4) % 2
            cp(S, 'act', dT[apar][:, :, (blk % 4) * 128:(blk % 4 + 1) * 128], ptp[:], ["ptp"], [("dT", apar)])
            if blk % 4 == 3 or blk == nblk - 1:
                q0 = (blk // 4) * 4
                n = (blk - q0 + 1) * 128
                dma(S, 'sp', cat_d[:, 4:8, tb + q0 * 128: tb + q0 * 128 + n], dT[apar][:, :, 0:n], [("dT", apar)],
                    [("catT1", "d", b, blk // 4)])

        for i in range(len(units) + 5):
            if i < len(units):
                stA(units[i])
            if 0 <= i - 1 < len(units):
                stA2(units[i - 1])
            if 0 <= i - 3 < len(units):
                stB(units[i - 3])
            if 0 <= i - 5 < len(units):
                stC(units[i - 5])
    ph.close()


def phase_ret(cx, nseq=NSEQ, nblk=32):
    nc, S, dr = cx.nc, cx.S, cx.dr
    ph = Phase(cx, "pc")
    ident = load_consts(ph, S, dr)
    decT = ph.sb("decT", [128, 4, 128], F32)
    dma(S, 'sp', decT[:], dr["ret_decT"], (), ["decT"])
    ng = ph.sb("ng", [128, 512], F32)
    rn = dr["ret_norm_g"]
    dma(S, 'sp', ng[:], bass.AP(tensor=rn.tensor, offset=rn.offset, ap=[[0, 128], [1, 512]]), (), ["ng"])
    mhalf = ph.sb("mhalf", [128, 4], F32)
    memset(S, 'pool', mhalf[:], -0.5, ["mhalf"])
    SEG = 8
    fm = [ph.sb("fm%d" % i, [128, 12, SEG * 128], BF16) for i in range(2)]
    tk = [ph.sb("tk%d" % i, [128, SEG, 3, 512], BF16) for i in range(2)]
    st32 = [ph.sb("st32_%d" % h, [128, 128], F32) for h in range(4)]
    stb = [[ph.sb("stb_%d_%d" % (h, i), [128, 128], BF16) for i in range(2)] for h in range(4)]
    PT = [ph.sb("PT%d" % i, [128, 4, 128], BF16) for i in range(2)]
    osb = ph.sb("osb", [128, 4, 128], F32)
    sq = ph.sb("sq", [128, 4, 128], F32)
    s12 = ph.sb("s12", [128, 2, 4], F32)
    mv = ph.sb("mv", [128, 3, 4], F32)
    cn = ph.sb("cn", [128, 512], F32)
    co_b = ph.sb("co_b", [128, 512], BF16)
    cT = [ph.sb("cT%d" % i, [128, 4, 512], BF16) for i in range(2)]
    pS = [ph.ps("pS%d" % i, [128, 4, 128], F32) for i in range(1)]
    pO = [ph.ps("pO%d" % i, [128, 4, 128], F32) for i in range(2)]
    pK = [ph.ps("pK%d" % i, [128, 128], F32) for i in range(4)]
    ptp = ph.ps("ptp", [128, 4, 128], BF16)
    cf_d = dr["c_fm"].rearrange("k p t -> p k t")
    ct_d = dr["c_tok"]
    cat_d = dr["catT1"].rearrange("(c p) t -> p c t", p=128)
    gam = [1.0 - 2.0 ** (-5.0 - h) for h in range(4)]
    sn = 0
    kn = 0
    for b in range(nseq):
        tb = b * S_LEN
        for h in range(4):
            memset(S, 'pool', st32[h][:], 0.0, [("st32", h)])
            memset(S, 'pool', stb[h][0][:], 0.0, [("stb", h, 0)])
        scnt = [0, 0, 0, 0]
        for blk in range(nblk):
            seg, sb_ = blk // SEG, blk % SEG
            sp_ = seg % 2
            if sb_ == 0:
                t0 = tb + seg * SEG * 128
                nb = min(SEG, nblk - seg * SEG)
                rkeys = [("c_fm", (t0 // 512) + i) for i in range(2)]
                dma(S, 'sp', fm[sp_][:, :, 0:nb * 128], cf_d[:, :, t0:t0 + nb * 128], rkeys, [("fm", sp_)])
                dma(S, 'sp', tk[sp_][:, 0:nb], ct_d[t0:t0 + nb * 128].rearrange("(s p) k d -> p s k d", p=128),
                    [("c_tok", (t0 // 512) + i) for i in range(2)], [("tk", sp_)])
            cs = slice(sb_ * 128, (sb_ + 1) * 128)
            bp = blk % 2
            po_ = pO[bp]
            pok = ("pO", bp)
            for h in range(4):
                mm(S, pS[0][:, h, :], fm[sp_][:, 8 + h, cs], fm[sp_][:, 0 + h, cs], True, True, [("fm", sp_)], ["pS"])
            tt(S, 'dve', PT[bp][:], pS[0][:], decT[:], ALU.mult, ["pS", "decT"], [("PT", bp)])
            for h in range(4):
                hs = slice(h * 128, (h + 1) * 128)
                mm(S, po_[:, h, :], PT[bp][:, h, :], tk[sp_][:, sb_, 1, hs], h == 0, False, [("PT", bp), ("tk", sp_)], [pok])
            for half in range(2):
                ps_ = slice(half * 64, (half + 1) * 64)
                kp = kn % 2
                kn += 1
                for h in range(4):
                    hs = slice(h * 128, (h + 1) * 128)
                    cur = scnt[h] % 2
                    mm(S, po_[ps_, h, :], fm[sp_][:, 4 + h, sb_ * 128 + half * 64: sb_ * 128 + (half + 1) * 64], stb[h][cur][:],
                       False, h == 3, [("fm", sp_), ("stb", h, cur)], [pok])
                    mm(S, pK[h][:], tk[sp_][ps_, sb_, 0, hs], tk[sp_][ps_, sb_, 1, hs], True, True, [("tk", sp_)], [("pK", h)])
                    stt(S, st32[h][:], st32[h][:], gam[h] ** 64, pK[h][:], ALU.mult, ALU.add, [("pK", h), ("st32", h)], [("st32", h)])
                    cp(S, 'act', stb[h][1 - cur][:], st32[h][:], [("st32", h)], [("stb", h, 1 - cur)])
                    scnt[h] += 1
            cp(S, 'act', osb[:], po_[:], [pok], ["osb"])
            S.op('dve', lambda e: e.reduce_sum(s12[:, 0, :], osb[:], axis=AX.X), ["osb"], [("s12", 0)])
            tt(S, 'pool', sq[:], osb[:], osb[:], ALU.mult, ["osb"], ["sq"])
            S.op('dve', lambda e: e.reduce_sum(s12[:, 1, :], sq[:], axis=AX.X), ["sq"], [("s12", 1)])
            ts(S, 'dve', mv[:, 0, :], s12[:, 0, :], 1.0 / 128, None, ALU.mult, None, [("s12", 0)], [("mv", 0)])
            tt(S, 'dve', mv[:, 1, :], mv[:, 0, :], mv[:, 0, :], ALU.mult, [("mv", 0)], [("mv", 1)])
            stt(S, mv[:, 2, :], s12[:, 1, :], 1.0 / 128, mv[:, 1, :], ALU.mult, ALU.subtract, [("s12", 1), ("mv", 1)], [("mv", 2)])
            ts(S, 'dve', mv[:, 2, :], mv[:, 2, :], EPS, None, ALU.add, None, [("mv", 2)], [("mv", 2)])
            tt(S, 'pool', mv[:, 2, :], mv[:, 2, :], mhalf[:], ALU.pow, [("mv", 2), "mhalf"], [("mv", 2)])
            cn3 = cn[:].rearrange("p (h d) -> p h d", h=4)
            tt(S, 'dve', cn3, osb[:], apx(mv[:, 0, :], [[0, 128]]), ALU.subtract, ["osb", ("mv", 0)], ["cn"])
            tt(S, 'dve', cn3, cn3, apx(mv[:, 2, :], [[0, 128]]), ALU.mult, ["cn", ("mv", 2)], ["cn"])
            tt(S, 'pool', cn[:], cn[:], ng[:], ALU.mult, ["cn", "ng"], ["cn"])
            tt(S, 'pool', co_b[:], cn[:], tk[sp_][:, sb_, 2, :], ALU.mult, ["cn", ("tk", sp_)], ["co_b"])
            for c in range(4):
                tr(S, ptp[:, c, :], co_b[:, c * 128:(c + 1) * 128], ident[:], ["co_b", "ident"], ["ptp"])
            apar = (blk // 4) % 2
            cp(S, 'act', cT[apar][:, :, (blk % 4) * 128:(blk % 4 + 1) * 128], ptp[:], ["ptp"], [("cT", apar)])
            if blk % 4 == 3 or blk == nblk - 1:
                q0 = (blk // 4) * 4
                n = (blk - q0 + 1) * 128
                dma(S, 'sp', cat_d[:, 0:4, tb + q0 * 128: tb + q0 * 128 + n], cT[apar][:, :, 0:n], [("cT", apar)],
                    [("catT1", "c", b, blk // 4)])
    ph.close()


def fap(t, off, dims, p0=0, pn=None):
    base = t[:]
    pstep, pcnt = base.ap[0]
    if pn is None:
        pn = pcnt - p0
    return bass.AP(tensor=base.tensor, offset=base.offset + p0 * pstep + off, ap=[[pstep, pn]] + [list(d) for d in dims])


def phase_s5(cx, nseq=NSEQ):
    nc, S, dr = cx.nc, cx.S, cx.dr
    ph = Phase(cx, "pb")
    ident = load_consts(ph, S, dr)
    identf = ph.sb("identf", [128, 128], F32)
    dma(S, 'sp', identf[:], dr["ident"], (), ["identf"])
    W_intra = ph.sb("W_intra", [128, 32, 2, 256], BF16)
    W_BU = ph.sb("W_BU", [128, 32, 2, 2, 64], BF16)
    W_CX = ph.sb("W_CX", [64, 2, 32, 256], BF16)
    Aa = ph.sb("Aa", [64, 2, 32], F32)
    A2 = ph.sb("A2", [64, 2, 32], F32)
    Wg = load_w(S, ph, "Wg", dr["s5_w_glu"], 512, 512)
    bg = ph.sb("bg", [128, 4], F32)
    dma(S, 'sp', bg[:], dr["s5_bgT"], (), ["bg"])
    memset(S, 'pool', W_intra[:], 0.0, ["W_intra"])

    pp = Phase(cx, "pbp")
    cnt = [0]

    def T(shape=(64, 32)):
        cnt[0] += 1
        return pp.sb("t%d" % cnt[0], list(shape), F32)

    def k(t):
        return t.name if hasattr(t, "name") else id(t)

    def mul(o, a, b_):
        tt(S, 'dve', o[:], a[:], b_[:], ALU.mult, [k(a), k(b_)], [k(o)])

    def add(o, a, b_):
        tt(S, 'dve', o[:], a[:], b_[:], ALU.add, [k(a), k(b_)], [k(o)])

    def sub(o, a, b_):
        tt(S, 'dve', o[:], a[:], b_[:], ALU.subtract, [k(a), k(b_)], [k(o)])

    def tsa(o, a, s1, s2, op0, op1):
        ts(S, 'dve', o[:], a[:], s1, s2, op0, op1, [k(a)], [k(o)])

    lre, lim, ldt = T(), T(), T()
    dma(S, 'sp', lre[:], dr["s5_lamT_re"], (), [k(lre)])
    dma(S, 'sp', lim[:], dr["s5_lamT_im"], (), [k(lim)])
    dma(S, 'sp', ldt[:], dr["s5_ldt_bc"], (), [k(ldt)])
    dt = T()
    act(S, dt[:], ldt[:], AF.Exp, [k(ldt)], [k(dt)])
    lr, ang, mag = T(), T(), T()
    mul(lr, lre, dt)
    mul(ang, lim, dt)
    act(S, mag[:], lr[:], AF.Exp, [k(lr)], [k(mag)])
    kf, r = T(), T()
    MAGIC = 12582912.0
    tsa(kf, ang, 1.0 / (2 * np.pi), None, ALU.mult, None)
    tsa(kf, kf, MAGIC, None, ALU.add, None)
    tsa(kf, kf, MAGIC, None, ALU.subtract, None)
    C1 = 6.28125
    C2 = 2 * np.pi - C1
    stt(S, r[:], kf[:], -C1, ang[:], ALU.mult, ALU.add, [k(kf), k(ang)], [k(r)])
    stt(S, r[:], kf[:], -C2, r[:], ALU.mult, ALU.add, [k(kf), k(r)], [k(r)])
    y, y2, sn, cs, tmp = T(), T(), T(), T(), T()
    tsa(y, r, 0.125, None, ALU.mult, None)
    mul(y2, y, y)
    f = [1.0]
    for i in range(1, 12):
        f.append(f[-1] * i)
    tsa(sn, y2, 1.0 / f[9], -1.0 / f[7], ALU.mult, ALU.add)
    for c_ in (1.0 / f[5], -1.0 / f[3], 1.0):
        mul(sn, sn, y2)
        tsa(sn, sn, c_, None, ALU.add, None)
    mul(sn, sn, y)
    tsa(cs, y2, -1.0 / f[10], 1.0 / f[8], ALU.mult, ALU.add)
    for c_ in (-1.0 / f[6], 1.0 / f[4], -0.5, 1.0):
        mul(cs, cs, y2)
        tsa(cs, cs, c_, None, ALU.add, None)
    for _ in range(3):
        mul(tmp, sn, cs)
        mul(cs, sn, sn)
        tsa(sn, tmp, 2.0, None, ALU.mult, None)
        tsa(cs, cs, -2.0, 1.0, ALU.mult, ALU.add)
    ar, ai = T(), T()
    mul(ar, mag, cs)
    mul(ai, mag, sn)
    am1, den, fre, fim, t1, t2 = T(), T(), T(), T(), T(), T()
    tsa(am1, ar, -1.0, None, ALU.add, None)
    mul(den, lre, lre)
    mul(t1, lim, lim)
    add(den, den, t1)
    S.op('dve', lambda e: e.reciprocal(den[:], den[:]), [k(den)], [k(den)])
    mul(t1, am1, lre)
    mul(t2, ai, lim)
    add(fre, t1, t2)
    mul(fre, fre, den)
    mul(t1, ai, lre)
    mul(t2, am1, lim)
    sub(fim, t1, t2)
    mul(fim, fim, den)
    pr = pp.sb("pr", [64, 32, 17], F32)
    pi_ = pp.sb("pi", [64, 32, 17], F32)
    memset(S, 'dve', pr[:, :, 0:1], 1.0, ["pr"])
    memset(S, 'dve', pi_[:, :, 0:1], 0.0, ["pi"])
    q1, q2 = T(), T()
    for j in range(16):
        tt(S, 'dve', q1[:], pr[:, :, j], ar[:], ALU.mult, ["pr", k(ar)], [k(q1)])
        tt(S, 'dve', q2[:], pi_[:, :, j], ai[:], ALU.mult, ["pi", k(ai)], [k(q2)])
        tt(S, 'dve', pr[:, :, j + 1], q1[:], q2[:], ALU.subtract, [k(q1), k(q2)], ["pr"])
        tt(S, 'dve', q1[:], pr[:, :, j], ai[:], ALU.mult, ["pr", k(ai)], [k(q1)])
        tt(S, 'dve', q2[:], pi_[:, :, j], ar[:], ALU.mult, ["pi", k(ar)], [k(q2)])
        tt(S, 'dve', pi_[:, :, j + 1], q1[:], q2[:], ALU.add, [k(q1), k(q2)], ["pi"])
    for ri in range(2):
        cp(S, 'dve', Aa[:, ri, :], pr[:, :, 16], ["pr"], ["Aa"])
    ts(S, 'dve', A2[:, 0, :], pi_[:, :, 16], -1.0, None, ALU.mult, None, ["pi"], ["A2"])
    cp(S, 'dve', A2[:, 1, :], pi_[:, :, 16], ["pi"], ["A2"])
    bre, bim = pp.sb("bre", [64, 32, 16], F32), pp.sb("bim", [64, 32, 16], F32)
    cre, cim = pp.sb("cre", [64, 32, 16], F32), pp.sb("cim", [64, 32, 16], F32)
    dma(S, 'sp', bre[:], dr["s5_bT_re"], (), ["bre"])
    dma(S, 'sp', bim[:], dr["s5_bT_im"], (), ["bim"])
    dma(S, 'sp', cre[:], dr["s5_cT_re"], (), ["cre"])
    dma(S, 'sp', cim[:], dr["s5_cT_im"], (), ["cim"])
    Bre, Bim, nBim, u1, u2 = [pp.sb(n_, [64, 32, 16], F32) for n_ in ("Bre", "Bim", "nBim", "u1", "u2")]
    fre_b, fim_b = apx(fre[:], [[0, 16]]), apx(fim[:], [[0, 16]])
    tt(S, 'dve', u1[:], bre[:], fre_b, ALU.mult, ["bre", k(fre)], ["u1"])
    tt(S, 'dve', u2[:], bim[:], fim_b, ALU.mult, ["bim", k(fim)], ["u2"])
    tt(S, 'dve', Bre[:], u1[:], u2[:], ALU.subtract, ["u1", "u2"], ["Bre"])
    tt(S, 'dve', u1[:], bim[:], fre_b, ALU.mult, ["bim", k(fre)], ["u1"])
    tt(S, 'dve', u2[:], bre[:], fim_b, ALU.mult, ["bre", k(fim)], ["u2"])
    tt(S, 'dve', Bim[:], u1[:], u2[:], ALU.add, ["u1", "u2"], ["Bim"])
    ts(S, 'dve', nBim[:], Bim[:], -1.0, None, ALU.mult, None, ["Bim"], ["nBim"])
    HG = 16
    pp1 = Phase(cx, "pbp1")
    T1 = pp1.sb("T1", [64, HG, 17, 16], F32)
    T2 = pp1.sb("T2", [64, HG, 17, 16], F32)
    T3 = pp1.sb("T3", [64, HG, 17, 16], F32)
    Kb = pp1.sb("Kb", [16, 32, 256], BF16)
    dbc = pp1.sb("dbc", [16, 32, 16], F32)
    dma(S, 'sp', dbc[:], dr["s5_d_bc"][0:16], (), ["dbc"])
    Dg = pp1.sb("Dg", [16, 32, 16], F32)
    tt(S, 'dve', Dg[:], dbc[:], fap(identf, 0, [(0, 32), (1, 16)], 0, 16), ALU.mult, ["dbc", "identf"], ["Dg"])
    pk = [pp1.ps("pk%d" % i, [16, 2, 256], F32) for i in range(2)]
    for gh in range(2):
        gs = slice(gh * HG, (gh + 1) * HG)
        Cre_b = fap(cre, gh * HG * 16, [(16, HG), (0, 17), (1, 16)])
        Cim_b = fap(cim, gh * HG * 16, [(16, HG), (0, 17), (1, 16)])
        pr_b = fap(pr, gh * HG * 17, [(17, HG), (1, 17), (0, 16)])
        pi_b = fap(pi_, gh * HG * 17, [(17, HG), (1, 17), (0, 16)])
        tt(S, 'dve', T1[:], Cre_b, pr_b, ALU.mult, ["cre", "pr"], ["T1"])
        tt(S, 'dve', T3[:], Cim_b, pi_b, ALU.mult, ["cim", "pi"], ["T3"])
        tt(S, 'pool', T1[:], T1[:], T3[:], ALU.subtract, ["T1", "T3"], ["T1"])
        tt(S, 'dve', T2[:], Cre_b, pi_b, ALU.mult, ["cre", "pi"], ["T2"])
        tt(S, 'dve', T3[:], Cim_b, pr_b, ALU.mult, ["cim", "pr"], ["T3"])
        tt(S, 'pool', T2[:], T2[:], T3[:], ALU.add, ["T2", "T3"], ["T2"])
        cp(S, 'act', W_CX[:, 0, gs, :].rearrange("p g (t h) -> p g t h", t=16), T1[:, :, 1:17, :], ["T1"], ["W_CX"])
        ts(S, 'pool', W_CX[:, 1, gs, :].rearrange("p g (t h) -> p g t h", t=16), T2[:, :, 1:17, :], -1.0, None, ALU.mult, None,
           ["T2"], ["W_CX"])
        for gl in range(HG):
            g = gh * HG + gl
            pkt = pk[(g // 2) % 2]
            pkk = ("pk", (g // 2) % 2)
            mm(S, pkt[:, g % 2, :], Bre[:, g, :], T1[:, gl, 0:16, :].rearrange("p t h -> p (t h)"), True, False, ["Bre", "T1"], [pkk])
            mm(S, pkt[:, g % 2, :], nBim[:, g, :], T2[:, gl, 0:16, :].rearrange("p t h -> p (t h)"), False, True, ["nBim", "T2"], [pkk])
            if g % 2 == 1:
                cp(S, 'act', Kb[:, g - 1:g + 1, 16:256], pkt[:, :, 16:256], [pkk], ["Kb"])
                tt(S, 'dve', Kb[:, g - 1:g + 1, 0:16], pkt[:, :, 0:16], Dg[:, g - 1:g + 1, :], ALU.add, [pkk, "Dg"], ["Kb"])
    dma(S, 'sp', dr["s5_kall"], Kb[:], ["Kb"], ["kall_d"])
    kd = dr["s5_kall"]
    for s in range(16):
        half, sl = s // 8, s % 8
        n = (16 - s) * 16
        dma(S, 'sp', W_intra[16 * sl:16 * sl + 16, :, half, 16 * s:256], kd[:, :, 0:n], ["kall_d", "W_intra"], ["W_intra"])
    pp1.close()
    pp2 = Phase(cx, "pbp2")
    WTb = [pp2.sb("WTb%d" % i, [64, 32, 256], BF16) for i in range(2)]
    ptw = [pp2.ps("ptw%d" % i, [128, 8, 64], BF16) for i in range(2)]
    Q1 = pp2.sb("Q1", [64, HG, 16, 16], F32)
    Q2 = pp2.sb("Q2", [64, HG, 16, 16], F32)
    for gh in range(2):
        gs = slice(gh * HG, (gh + 1) * HG)
        prr = fap(pr, gh * HG * 17 + 15, [(17, HG), (-1, 16), (0, 16)])
        pir = fap(pi_, gh * HG * 17 + 15, [(17, HG), (-1, 16), (0, 16)])
        Bre_b = fap(Bre, gh * HG * 16, [(16, HG), (0, 16), (1, 16)])
        Bim_b = fap(Bim, gh * HG * 16, [(16, HG), (0, 16), (1, 16)])
        tt(S, 'dve', Q1[:], prr, Bre_b, ALU.mult, ["pr", "Bre"], ["Q1"])
        tt(S, 'dve', Q2[:], pir, Bim_b, ALU.mult, ["pi", "Bim"], ["Q2"])
        tt(S, 'pool', WTb[0][:, gs, :].rearrange("p g (s h) -> p g s h", s=16), Q1[:], Q2[:], ALU.subtract, ["Q1", "Q2"], [("WTb", 0)])
        tt(S, 'dve', Q1[:], prr, Bim_b, ALU.mult, ["pr", "Bim"], ["Q1"])
        tt(S, 'dve', Q2[:], pir, Bre_b, ALU.mult, ["pi", "Bre"], ["Q2"])
        tt(S, 'pool', WTb[1][:, gs, :].rearrange("p g (s h) -> p g s h", s=16), Q1[:], Q2[:], ALU.add, ["Q1", "Q2"], [("WTb", 1)])
    for g2 in range(16):
        pt_ = ptw[g2 % 2]
        for gi in range(2):
            g = 2 * g2 + gi
            for half in range(2):
                for ri in range(2):
                    tr(S, pt_[:, gi * 4 + half * 2 + ri, :], WTb[ri][:, g, half * 128:(half + 1) * 128], ident[0:64, 0:64],
                       [("WTb", ri), "ident"], [("ptw", g2 % 2)])
        cp(S, 'act' if g2 % 2 == 0 else 'dve', W_BU[:, 2 * g2:2 * g2 + 2].rearrange("p g a r q -> p (g a r) q"), pt_[:],
           [("ptw", g2 % 2)], ["W_BU"])
    pp2.close()
    pp.close()

    RA = ph.sb("RA", [128, 16384], BF16)
    RB = ph.sb("RB", [128, 16384], BF16)
    RC = ph.sb("RC", [128, 16448], BF16)
    Ublk = RA[:].rearrange("p (cb s ch) -> p cb s ch", cb=2, s=16)
    Xb = RA[0:64, :].rearrange("p (r g c) -> p r g c", r=2, g=32)
    UT = RB[:].rearrange("p (g a c) -> p g a c", g=32, a=2)
    ygT = RB[:].rearrange("p (k t) -> p k t", k=4)
    GX = RC[0:64, :].bitcast(F32).rearrange("p (r g c) -> p r g c", r=2, g=16)
    Ytok = RC[:, 0:16384].rearrange("p (cb t ch) -> p cb t ch", cb=2, t=16)
    P1 = ph.sb("P1", [64, 2, 16], F32)
    P2 = ph.sb("P2", [64, 2, 16], F32)
    sg = [ph.sb("sg%d" % i, [128, 512], F32) for i in range(2)]
    boT = [ph.sb("boT%d" % i, [128, 4, 512], BF16) for i in range(2)]
    ptu = [ph.ps("ptu%d" % i, [128, 8, 128], BF16) for i in range(2)]
    pG = [ph.ps("pG%d" % i, [64, 2, 256], F32) for i in range(2)]
    pY = [ph.ps("pY%d" % i, [128, 256], F32) for i in range(2)]
    pL = [ph.ps("pL%d" % i, [128, 512], F32) for i in range(2)]
    cat_d = dr["catT0"].rearrange("(c p) t -> p c t", p=128)
    for b in range(nseq):
        tb = b * S_LEN
        dma(S, 'sp', RA[:].rearrange("p (cb x) -> p cb x", cb=2),
            dr["u0"][tb:tb + S_LEN, :].rearrange("(cb c s) ch -> c cb (s ch)", cb=2, c=128),
            [("u0", i) for i in range(b * 8, b * 8 + 8)], ["RA"])
        for cb in range(2):
            cp(S, 'dve' if cb == 0 else 'pool', fap(RC, cb * 8192, [(256, 32), (16, 16), (1, 16)]),
               fap(RA, cb * 8192, [(16, 32), (512, 16), (1, 16)]), ["RA"], ["RC"])
        for g2 in range(16):
            pt_ = ptu[g2 % 2]
            for gi in range(2):
                g = 2 * g2 + gi
                for half in range(2):
                    for cb in range(2):
                        tr(S, pt_[:, gi * 4 + half * 2 + cb, :], fap(RC, cb * 8192 + g * 256 + half * 128, [(1, 128)]), ident[:],
                           ["RC", "ident"], [("ptu", g2 % 2)])
            cp(S, 'act' if g2 % 2 == 0 else 'dve', UT[:, 2 * g2:2 * g2 + 2].rearrange("p g a c -> p (g a c)"),
               pt_[:].rearrange("p a c -> p (a c)"), [("ptu", g2 % 2)], ["RB"])
        for gh in range(2):
            memset(S, 'pool', GX[:, :, :, 0:1], 0.0, ["RC"])
            for gl in range(16):
                g = gh * 16 + gl
                pg_ = pG[gl % 2]
                for ri in range(2):
                    for half in range(2):
                        mm(S, pg_[:, ri, :], W_BU[:, g, half, ri, :], UT[:, g, half, :], half == 0, half == 1,
                           ["W_BU", "RB"], [("pG", gl % 2)])
                cp(S, 'act' if gl % 2 == 0 else 'dve', GX[:, :, gl, 1:257], pg_[:], [("pG", gl % 2)], ["RC"])
            Aa_h = Aa[:, :, gh * 16:(gh + 1) * 16]
            A2_h = A2[:, :, gh * 16:(gh + 1) * 16]
            for c in range(256):
                Xc = GX[:, :, :, c]
                Xs = GX[:, ::-1, :, c]
                tt(S, 'dve', P1[:], Xc, Aa_h, ALU.mult, ["RC", "Aa"], ["P1"])
                tt(S, 'dve', P2[:], Xs, A2_h, ALU.mult, ["RC", "A2"], ["P2"])
                tt(S, 'dve', P1[:], P1[:], P2[:], ALU.add, ["P1", "P2"], ["P1"])
                tt(S, 'dve', GX[:, :, :, c + 1], GX[:, :, :, c + 1], P1[:], ALU.add, ["RC", "P1"], ["RC"])
            cp(S, 'pool', Xb[:, :, gh * 16:(gh + 1) * 16, :], GX[:, :, :, 0:256], ["RC"], ["RA"])
        yn = 0
        for g in range(32):
            for cb in range(2):
                py = pY[yn % 2]
                pyk = ("pY", yn % 2)
                yn += 1
                cs_ = slice(cb * 128, (cb + 1) * 128)
                mm(S, py[:], UT[:, g, 0, cs_], W_intra[:, g, 0, :], True, False, ["RB", "W_intra"], [pyk])
                mm(S, py[:], UT[:, g, 1, cs_], W_intra[:, g, 1, :], False, False, ["RB", "W_intra"], [pyk])
                mm(S, py[:], Xb[:, 0, g, cs_], W_CX[:, 0, g, :], False, False, ["RA", "W_CX"], [pyk])
                mm(S, py[:], Xb[:, 1, g, cs_], W_CX[:, 1, g, :], False, True, ["RA", "W_CX"], [pyk])
                act(S, Ytok[:, cb, :, 16 * g:16 * g + 16], py[:].rearrange("p (t h) -> p t h", t=16), AF.Gelu_apprx_tanh, [pyk], ["RC"])
        tn = 0
        for cb in range(2):
            for kc in range(4):
                for th in range(2):
                    pt_ = ptu[tn % 2]
                    ptk = ("ptu", tn % 2)
                    tn += 1
                    for tl in range(8):
                        t_ = th * 8 + tl
                        tr(S, pt_[:, tl, :], Ytok[:, cb, t_, kc * 128:(kc + 1) * 128], ident[:], ["RC", "ident"], [ptk])
                    dst = fap(RB, kc * 4096 + cb * 2048 + th * 8, [(1, 8), (16, 128)])
                    cp(S, 'act' if tn % 2 == 0 else 'dve', dst, pt_[:], [ptk], ["RB"])
        ln = 0
        for ti in range(8):
            tsl = slice(ti * 512, (ti + 1) * 512)
            bp = ti % 2
            for oc in range(4):
                pl = pL[ln % 2]
                plk = ("pL", ln % 2)
                sgt = sg[ln % 2]
                sgk = ("sg", ln % 2)
                ln += 1
                for kc in range(4):
                    mm(S, pl[:], Wg[:, kc, oc * 128:(oc + 1) * 128], ygT[:, kc, tsl], kc == 0, kc == 3, [("Wg", kc), "RB"], [plk])
                act(S, sgt[:], pl[:], AF.Sigmoid, [plk, "bg"], [sgk], bias=bg[:, oc:oc + 1])
                tt(S, 'dve', boT[bp][:, oc, :], sgt[:], ygT[:, oc, tsl], ALU.mult, [sgk, "RB"], [("boT", bp)])
            dma(S, 'sp', cat_d[:, 4:8, tb + ti * 512: tb + (ti + 1) * 512], boT[bp][:], [("boT", bp)], [("catT0", "b", b, ti)])
    ph.close()


def prep_core(inp, core):
    b0 = 2 * core
    d = {}
    d["x"] = np.ascontiguousarray(inp["x"][b0:b0 + 2].reshape(8192, 1024))
    c2 = inp["c"][b0:b0 + 2]
    d["cT"] = np.ascontiguousarray(c2.T.reshape(8, 128, 2).transpose(1, 0, 2))
    d["ada_w"] = inp["ada_w"]
    d["ada_b"] = inp["ada_b"]
    d["ada_bT"] = np.ascontiguousarray(inp["ada_b"].reshape(2, 48, 128).transpose(0, 2, 1))
    lg = np.stack([inp["ln_mix_g"], inp["ln_ffn_g"]], 0)
    d["ln_gT"] = np.ascontiguousarray(lg.reshape(2, 2, 8, 128).transpose(0, 1, 3, 2))
    return d

def prep_l0(inp, d):
    d["ab_w_in"] = inp["ab_w_in"][0]
    d["qkg"] = np.ascontiguousarray(np.stack([np.tile(inp["a_q_gain"][0], 2), np.tile(inp["a_k_gain"][0], 2)], 1))
    d["ident"] = np.eye(128, dtype=np.float32)
    bo = np.zeros((128, 128), np.float32); bo[:64, :64] = 1; bo[64:, 64:] = 1
    d["blockones"] = bo
    return d

def prep_consts(d):
    d["ident"] = np.eye(128, dtype=np.float32)
    pos = np.arange(4096, dtype=np.float64)[:, None]
    fr = 10000.0 ** (-np.arange(64, dtype=np.float64) / 64)[None, :]
    ang = pos * fr
    d["rot"] = np.stack([np.cos(ang), np.sin(ang)], 1).astype(np.float32)
    lg = np.log(1.0 - 2.0 ** (-5.0 - np.arange(4, dtype=np.float64)))
    idx = np.arange(128) % 64
    d["ret_qd"] = np.exp(lg[None, :] * (idx[:, None] + 1.0)).astype(np.float32)
    d["ret_kd"] = (np.exp(lg[None, :] * (63.0 - idx[:, None])) * 128.0 ** -0.5).astype(np.float32)
    j = np.arange(128)[:, None]; i = np.arange(128)[None, :]
    same = (j // 64) == (i // 64)
    dec = np.exp(lg[:, None, None] * np.abs(i - j)[None]) * same[None]
    d["ret_decT"] = np.ascontiguousarray(dec.transpose(1, 0, 2)).astype(np.float32)
    m = np.ones((128, 256), np.float32)
    m[:, 128:] = (np.arange(128)[None, :] < np.arange(128)[:, None]).astype(np.float32)
    d["sb_mask"] = m
    return d

def prep_s5(inp, d):
    d["s5_lamT_re"] = np.ascontiguousarray(inp["s5_lambda_re"][0].T)
    d["s5_lamT_im"] = np.ascontiguousarray(inp["s5_lambda_im"][0].T)
    d["s5_ldt_bc"] = np.ascontiguousarray(np.broadcast_to(inp["s5_log_dt"][0][None, :], (64, 32)))
    d["s5_bT_re"] = np.ascontiguousarray(inp["s5_b_re"][0].transpose(1, 0, 2))
    d["s5_bT_im"] = np.ascontiguousarray(inp["s5_b_im"][0].transpose(1, 0, 2))
    d["s5_cT_re"] = np.ascontiguousarray(inp["s5_c_re"][0].transpose(2, 0, 1))
    d["s5_cT_im"] = np.ascontiguousarray(inp["s5_c_im"][0].transpose(2, 0, 1))
    d["s5_d_bc"] = np.ascontiguousarray(np.broadcast_to(inp["s5_d"][0][None], (128, 32, 16)))
    d["s5_w_glu"] = inp["s5_w_glu"][0]
    d["s5_bgT"] = np.ascontiguousarray(inp["s5_b_glu"][0].reshape(4, 128).T)
    return d


IN_SPECS = [
    ("x", [8192, 1024]), ("cT", [128, 8, 2]), ("ada_w", [2, 1024, 6144]), ("ada_b", [2, 6144]), ("ada_bT", [2, 128, 48]),
    ("ln_gT", [2, 2, 128, 8]), ("ab_w_in", [1024, 2048]), ("qkg", [128, 2]), ("ident", [128, 128]), ("blockones", [128, 128]),
    ("rel_bias", [8, 257]),
    ("s5_lamT_re", [64, 32]), ("s5_lamT_im", [64, 32]), ("s5_ldt_bc", [64, 32]), ("s5_bT_re", [64, 32, 16]), ("s5_bT_im", [64, 32, 16]),
    ("s5_cT_re", [64, 32, 16]), ("s5_cT_im", [64, 32, 16]), ("s5_d_bc", [128, 32, 16]), ("s5_w_glu", [512, 512]), ("s5_bgT", [128, 4]),
    ("ab_w_out", [1024, 1024]), ("ffn_w_in", [2, 1024, 5632]), ("ffn_w_out", [2, 2816, 1024]),
    ("cd_w_in", [1024, 3584]), ("cd_w_out", [1024, 1024]), ("ret_norm_g", [512]),
    ("rot", [4096, 2, 64]), ("ret_qd", [128, 4]), ("ret_kd", [128, 4]), ("ret_decT", [128, 4, 128]), ("sb_mask", [128, 256]),
]


def build_program():
    nc = bass.Bass("TRN2", target_bir_lowering=False)
    cx = Ctx(nc)
    for n_, s_ in IN_SPECS:
        cx.dram_in(n_, s_)
    cx.dram_out("out", [T_CORE, 1024])
    cx.dram_scr("modfm", [2, 128, 4, 8, 2], F32)
    cx.dram_scr("gbc", [2, 2, 2, 128, 1024], F32)
    cx.dram_scr("qkT", [8, 128, T_CORE], BF16)
    cx.dram_scr("v0", [T_CORE, 512], BF16)
    cx.dram_scr("u0", [T_CORE, 512], BF16)
    cx.dram_scr("relext", [8, 1024], F32)
    cx.dram_scr("s5_kall", [16, 32, 256], BF16)
    cx.dram_scr("catT0", [1024, T_CORE], BF16)
    cx.dram_scr("x1", [T_CORE, 1024], F32)
    cx.dram_scr("c_fm", [12, 128, T_CORE], BF16)
    cx.dram_scr("c_tok", [T_CORE, 3, 512], BF16)
    cx.dram_scr("d_qkT", [8, 128, T_CORE], BF16)
    cx.dram_scr("d_v", [T_CORE, 512], BF16)
    cx.dram_scr("catT1", [1024, T_CORE], BF16)
    with cx.st:
        phase_adaln(cx)
        phase_p1_l0(cx)
        phase_attn(cx)
        phase_s5(cx)
        phase_p3(cx, 0, "catT0", "x", "x1", "ab_w_out")
        phase_p1_l1(cx, "x1")
        phase_sb(cx)
        phase_ret(cx)
        phase_p3(cx, 1, "catT1", "x1", "out", "cd_w_out", final=True)
    return nc


def prep_all(inp, core):
    d = prep_core(inp, core)
    prep_l0(inp, d)
    prep_consts(d)
    prep_s5(inp, d)
    d["rel_bias"] = inp["a_rel_bias"][0]
    d["ab_w_out"] = inp["ab_w_out"][0]
    d["ffn_w_in"] = inp["ffn_w_in"]
    d["ffn_w_out"] = inp["ffn_w_out"]
    d["cd_w_in"] = inp["cd_w_in"][0]
    d["cd_w_out"] = inp["cd_w_out"][0]
    d["ret_norm_g"] = inp["ret_norm_g"][0]
    return {k_: np.ascontiguousarray(np.asarray(d[k_], dtype=np.float32)) for k_, _ in IN_SPECS}


def kernel(**inputs):
    inp = {k_: np.asarray(v_) for k_, v_ in inputs.items()}
    nc = build_program()
    in_maps = [prep_all(inp, core) for core in range(8)]
    res = run_bass_kernel_spmd(nc, in_maps, core_ids=list(range(8)))
    outs = [np.asarray(res.results[i]["out"]).reshape(2, S_LEN, 1024) for i in range(8)]
    return np.concatenate(outs, axis=0).astype(np.float32)
```

```python
import contextlib
from contextlib import ExitStack
import numpy as np
import concourse.bass as bass
import concourse.mybir as mybir
from concourse.bass_utils import run_bass_kernel_spmd

F32 = mybir.dt.float32
BF16 = mybir.dt.bfloat16
I32 = mybir.dt.int32
AF = mybir.ActivationFunctionType
ALU = mybir.AluOpType
AX = mybir.AxisListType

ENGS = ['pe', 'act', 'dve', 'pool', 'sp']
NDS = 6


class Sched:
    def __init__(self, nc, stack, same_engine_sync=True):
        self.nc = nc
        self.ops = {e: [] for e in ENGS}
        self.cnt = {e: 0 for e in ENGS}
        self.seen = {e: {} for e in ENGS}
        self.lastw = {}
        self.readers = {}
        self.same = same_engine_sync
        self.csem = {e: stack.enter_context(nc.semaphore("c_" + e)) for e in ['pe', 'act', 'dve', 'pool']}
        self.dsem = {q: [stack.enter_context(nc.semaphore("d_%s%d" % (q, i))) for i in range(NDS)]
                     for q in ['sp', 'pool', 'act']}
        self.dma_n = {q: 0 for q in ['sp', 'pool', 'act']}
        self.out_tokens = []

    def _deps(self, reads, writes):
        deps = []
        for k in reads:
            if k in self.lastw:
                deps.append(self.lastw[k])
        for k in writes:
            if k in self.lastw:
                deps.append(self.lastw[k])
            deps.extend(self.readers.get(k, []))
        return deps

    def _waits(self, eng, deps):
        need = {}
        for (semkey, sem, val, deng) in deps:
            if deng == eng and semkey[0] == 'c' and (eng == 'pe' or not self.same):
                continue
            if self.seen[eng].get(semkey, 0) >= val:
                continue
            if semkey not in need or need[semkey][1] < val:
                need[semkey] = (sem, val)
        for semkey, (sem, val) in need.items():
            self.seen[eng][semkey] = val
        return list(need.values())

    def _record(self, tok, reads, writes):
        for k in reads:
            self.readers.setdefault(k, []).append(tok)
        for k in writes:
            self.lastw[k] = tok
            self.readers[k] = []

    def op(self, eng, fn, reads=(), writes=()):
        deps = self._deps(reads, writes)
        waits = self._waits(eng, deps)
        self.cnt[eng] += 1
        tok = (('c', eng), self.csem[eng], self.cnt[eng], eng)
        self.ops[eng].append((waits, fn, (self.csem[eng], 1)))
        self._record(tok, reads, writes)
        return tok

    def dma(self, q, fn, reads=(), writes=(), is_output=False):
        deps = self._deps(reads, writes)
        n = self.dma_n[q]
        self.dma_n[q] += 1
        slot = n % NDS
        sem = self.dsem[q][slot]
        semkey = ('d', q, slot)
        prev = 16 * (n // NDS)
        if prev > 0:
            deps.append((semkey, sem, prev, 'dma'))
        waits = self._waits(q, deps)
        tok = (semkey, sem, prev + 16, 'dma')
        self.ops[q].append((waits, fn, (sem, 16)))
        self._record(tok, reads, writes)
        if is_output:
            self.out_tokens.append(tok)
        return tok

    def flush(self, block):
        toks = []
        for e in ['pe', 'act', 'dve', 'pool']:
            if self.cnt[e] > 0:
                toks.append((('c', e), self.csem[e], self.cnt[e], 'x'))
        for q in ['sp', 'pool', 'act']:
            n = self.dma_n[q]
            for j in range(max(0, n - NDS), n):
                slot = j % NDS
                toks.append((('d', q, slot), self.dsem[q][slot], 16 * (j // NDS + 1), 'dma'))
        for e in ENGS:
            self.ops[e].append((self._waits(e, toks), None, None))
        self.lastw = {}
        self.readers = {}

        def run(eng_name):
            lst = self.ops[eng_name]

            def body(e):
                for (waits, fn, inc) in lst:
                    for (sem, val) in waits:
                        e.wait_ge(sem, val)
                    if fn is None:
                        continue
                    ins = fn(e)
                    ins.then_inc(inc[0], inc[1])
            return body

        block.tensor(run('pe'))
        block.scalar(run('act'))
        block.vector(run('dve'))
        block.gpsimd(run('pool'))
        block.sync(run('sp'))
        self.ops = {e: [] for e in ENGS}


EPS = 1e-6
S_LEN = 4096
NSEQ = 2
T_CORE = NSEQ * S_LEN
D = 1024
FF = 2816


def apx(ap, extra):
    return bass.AP(tensor=ap.tensor, offset=ap.offset, ap=[list(a) for a in ap.ap] + [list(e) for e in extra])


class Ctx:
    def __init__(self, nc, debug_out=()):
        self.nc = nc
        self.st = ExitStack()
        self.S = Sched(nc, self.st)
        self.dr = {}
        self.debug_out = set(debug_out)
        self.uid = 0

    def dram_in(self, name, shape, dt=F32):
        self.dr[name] = self.nc.dram_tensor(name, list(shape), dt, kind="ExternalInput").ap()
        return self.dr[name]

    def dram_out(self, name, shape, dt=F32):
        self.dr[name] = self.nc.dram_tensor(name, list(shape), dt, kind="ExternalOutput").ap()
        return self.dr[name]

    def dram_scr(self, name, shape, dt):
        kind = "ExternalOutput" if name in self.debug_out else "Internal"
        self.dr[name] = self.nc.dram_tensor(name, list(shape), dt, kind=kind).ap()
        return self.dr[name]

    def flush(self):
        with self.nc.Block() as block:
            self.S.flush(block)


class Phase:
    def __init__(self, cx, name):
        self.cx = cx
        self.nc = cx.nc
        self.S = cx.S
        self.name = name
        self.st = ExitStack()

    def sb(self, name, shape, dt):
        return self.st.enter_context(self.nc.sbuf_tensor(self.name + "_" + name, list(shape), dt))

    def ps(self, name, shape, dt=F32):
        return self.st.enter_context(self.nc.psum_tensor(self.name + "_" + name, list(shape), dt))

    def close(self):
        self.cx.flush()
        self.st.close()


def mm(S, out, lhsT, rhs, start, stop, r, w):
    S.op('pe', lambda e: e.matmul(out, lhsT, rhs, start=start, stop=stop), r, w)


def tr(S, out, in_, ident, r, w):
    S.op('pe', lambda e: e.transpose(out, in_, ident), r, w)


def act(S, out, in_, func, r, w, scale=1.0, bias=None, accum_out=None, eng='act'):
    kw = {}
    if bias is not None:
        kw['bias'] = bias
    if accum_out is not None:
        kw['accum_out'] = accum_out
    S.op('act', lambda e: e.activation(out=out, in_=in_, func=func, scale=scale, **kw), r, w)


def ts(S, eng, out, in0, s1, s2, op0, op1, r, w, accum_out=None):
    if op1 is None:
        S.op(eng, lambda e: e.tensor_scalar(out, in0, s1, None, op0), r, w)
    elif accum_out is not None:
        S.op(eng, lambda e: e.tensor_scalar(out, in0, s1, s2, op0, op1, accum_out), r, w)
    else:
        S.op(eng, lambda e: e.tensor_scalar(out, in0, s1, s2, op0, op1), r, w)


def tt(S, eng, out, in0, in1, op, r, w):
    S.op(eng, lambda e: e.tensor_tensor(out, in0, in1, op), r, w)


def stt(S, out, in0, scalar, in1, op0, op1, r, w):
    S.op('dve', lambda e: e.scalar_tensor_tensor(out, in0, scalar, in1, op0, op1), r, w)


def cp(S, eng, out, in_, r, w):
    if eng == 'act':
        S.op('act', lambda e: e.copy(out, in_), r, w)
    else:
        S.op(eng, lambda e: e.tensor_copy(out, in_), r, w)


def memset(S, eng, ap, val, w):
    S.op(eng, lambda e: e.memset(ap, val), (), w)


def dma(S, q, out, in_, r, w, is_output=False, slow=False):
    if slow:
        S.dma(q, lambda e: e.dma_start(out=out, in_=in_, allow_slow_non_contiguous=True), r, w, is_output)
    else:
        S.dma(q, lambda e: e.dma_start(out=out, in_=in_), r, w, is_output)


def load_w(S, ph, name, w_dram, K, N, q='pool', nsplit=None):
    kc = K // 128
    t = ph.sb(name, [128, kc, N], BF16)
    src = w_dram.rearrange("(c p) n -> p c n", p=128)
    for c in range(kc):
        dma(S, q, t[:, c, :], src[:, c, :], (), [(name, c)])
    return t


def phase_adaln(cx):
    nc, S, dr = cx.nc, cx.S, cx.dr
    ph = Phase(cx, "p0")
    cT = ph.sb("cT", [128, 8, 2], F32)
    condT = ph.sb("condT", [128, 8, 2], BF16)
    condbc = ph.sb("condbc", [128, 8, 2, 128], BF16)
    aw = ph.sb("aw", [128, 8, 6144], BF16)
    abT = ph.sb("abT", [128, 48], F32)
    lng = ph.sb("lng", [128, 2, 8], F32)
    abbc = ph.sb("abbc", [128, 2, 1024], F32)
    modsb = ph.sb("modsb", [128, 4, 8, 2], F32)
    tmp = ph.sb("tmp", [128, 8, 2], F32)
    gsb = [ph.sb("gsb%d" % i, [128, 1024], F32) for i in range(2)]
    pm = ph.ps("pm", [128, 32, 2], F32)
    pg = [ph.ps("pg%d" % i, [128, 512], F32) for i in range(2)]

    dma(S, 'sp', cT[:], dr["cT"], (), ["cT"])
    act(S, condT[:], cT[:], AF.Silu, ["cT"], ["condT"])
    cp(S, 'dve', condbc[:], apx(condT[:], [[0, 128]]), ["condT"], ["condbc"])
    gi = 0
    for l in range(2):
        src = dr["ada_w"][l].rearrange("(c p) n -> p c n", p=128)
        for c in range(8):
            dma(S, 'pool', aw[:, c, :], src[:, c, :], (), [("aw", c)])
        dma(S, 'sp', abT[:], dr["ada_bT"][l], (), ["abT"])
        dma(S, 'sp', lng[:, 0, :], dr["ln_gT"][0, l], (), ["lng"])
        dma(S, 'sp', lng[:, 1, :], dr["ln_gT"][1, l], (), ["lng"])
        for wi, blk in enumerate((2, 5)):
            dma(S, 'sp', abbc[:, wi, :], dr["ada_b"][l:l + 1, blk * 1024:(blk + 1) * 1024].partition_broadcast(128)
                if False else apx_pb(dr["ada_b"][l, blk * 1024:(blk + 1) * 1024]), (), ["abbc"])
        for jj, blk in enumerate((0, 1, 3, 4)):
            for fc in range(8):
                col = blk * 1024 + fc * 128
                for k in range(8):
                    mm(S, pm[:, jj * 8 + fc, :], aw[:, k, col:col + 128], condT[:, k, :], k == 0, k == 7,
                       [("aw", k), "condT"], ["pm"])
        for jj, blk in enumerate((0, 1, 3, 4)):
            bias = apx(abT[:, blk * 8:(blk + 1) * 8], [[0, 2]])
            if jj in (0, 2):
                tt(S, 'dve', modsb[:, jj + 1], pm[:, jj * 8:(jj + 1) * 8, :], bias, ALU.add, ["pm", "abT"], ["modsb"])
            else:
                tt(S, 'dve', tmp[:], pm[:, jj * 8:(jj + 1) * 8, :], bias, ALU.add, ["pm", "abT"], ["tmp"])
                stt(S, modsb[:, jj - 1], tmp[:], 1.0, apx(lng[:, jj // 2, :], [[0, 2]]), ALU.add, ALU.mult,
                    ["tmp", "lng"], ["modsb"])
        dma(S, 'sp', dr["modfm"][l], modsb[:], ["modsb"], [("modfm", l)])
        for b in range(2):
            for wi, blk in enumerate((2, 5)):
                g = gsb[gi % 2]
                gk = ("gsb", gi % 2)
                for half in range(2):
                    p = pg[half]
                    col = blk * 1024 + half * 512
                    for k in range(8):
                        mm(S, p[:], condbc[:, k, b, :], aw[:, k, col:col + 512], k == 0, k == 7,
                           [("aw", k), "condbc"], [("pg", half)])
                    tt(S, 'dve', g[:, half * 512:(half + 1) * 512], p[:], abbc[:, wi, half * 512:(half + 1) * 512],
                       ALU.add, [("pg", half), "abbc"], [gk])
                dma(S, 'sp', dr["gbc"][l, b, wi], g[:], [gk], [("gbc", l, b, wi)])
                gi += 1
    ph.close()


def apx_pb(ap1d):
    return bass.AP(tensor=ap1d.tensor, offset=ap1d.offset, ap=[[0, 128]] + [list(a) for a in ap1d.ap])


class NormT:
    def __init__(self, ph, nsub, ident):
        self.ph, self.S, self.nsub, self.ident = ph, ph.S, nsub, ident
        self.junk = ph.sb("nt_junk", [128, 1024], BF16)
        self.ss = [ph.sb("nt_ss%d" % i, [128, nsub], F32) for i in range(2)]
        self.rstd = [ph.sb("nt_rstd%d" % i, [128, nsub], F32) for i in range(2)]
        self.mhalf = ph.sb("nt_mhalf", [128, nsub], F32)
        self.xn = ph.sb("nt_xn", [128, nsub, 1024], BF16)
        self.tp = [ph.ps("nt_tp%d" % i, [128, nsub * 128], BF16) for i in range(2)]
        memset(self.S, 'pool', self.mhalf[:], -0.5, ["nt_mhalf"])
        self.n = 0

    def run(self, xt, xkey, hT, hkey, A, B, b):
        self.run_a(xt, xkey)
        self.run_b(hT, hkey, A, B, b)

    def run_a(self, xt, xkey):
        S, nsub = self.S, self.nsub
        par = self.n % 2
        self.n += 1
        ss, rstd = self.ss[par], self.rstd[par]
        for s in range(nsub):
            act(S, self.junk[:], xt[:, s, :], AF.Square, [xkey], ["nt_junk", ("nt_ss", par, s)], accum_out=ss[:, s:s + 1])
        ts(S, 'dve', rstd[:], ss[:], 1.0 / 1024, EPS, ALU.mult, ALU.add, [("nt_ss", par, s) for s in range(nsub)], [("nt_rstd", par)])
        tt(S, 'pool', rstd[:], rstd[:], self.mhalf[:], ALU.pow, [("nt_rstd", par), "nt_mhalf"], [("nt_rstd", par)])
        for s in range(nsub):
            ts(S, 'dve' if s % 2 == 0 else 'pool', self.xn[:, s, :], xt[:, s, :], rstd[:, s:s + 1], None, ALU.mult, None,
               [xkey, ("nt_rstd", par)], [("nt_xn", s)])

    def run_b(self, hT, hkey, A, B, b):
        S, nsub = self.S, self.nsub
        for c in range(8):
            tp = self.tp[c % 2]
            for s in range(nsub):
                tr(S, tp[:, s * 128:(s + 1) * 128], self.xn[:, s, c * 128:(c + 1) * 128], self.ident[:],
                   [("nt_xn", s), "ident"], [("nt_tp", c % 2)])
            if c % 2 == 0:
                ts(S, 'dve', hT[:, c, :], tp[:], A[:, c, b:b + 1], B[:, c, b:b + 1], ALU.mult, ALU.add,
                   [("nt_tp", c % 2), "modAB"], [(hkey, c)])
            else:
                act(S, hT[:, c, :], tp[:], AF.Identity, [("nt_tp", c % 2), "modAB"], [(hkey, c)],
                    scale=A[:, c, b:b + 1], bias=B[:, c, b:b + 1])


def load_consts(ph, S, dr):
    ident = ph.sb("ident", [128, 128], BF16)
    dma(S, 'pool', ident[:], dr["ident"], (), ["ident"])
    return ident


def phase_p1_l0(cx, ntiles=16):
    nc, S, dr = cx.nc, cx.S, cx.dr
    ph = Phase(cx, "p1a")
    ident = load_consts(ph, S, dr)
    bones = ph.sb("bones", [128, 128], BF16)
    dma(S, 'pool', bones[:], dr["blockones"], (), ["bones"])
    W = load_w(S, ph, "W", dr["ab_w_in"], 1024, 2048)
    wkeys = [("W", c) for c in range(8)]
    modAB = ph.sb("modAB", [128, 4, 8, 2], F32)
    dma(S, 'sp', modAB[:], dr["modfm"][0], [("modfm", 0)], ["modAB"])
    qkg = ph.sb("qkg", [128, 2], F32)
    dma(S, 'sp', qkg[:], dr["qkg"], (), ["qkg"])
    cb = ph.sb("cbias", [128, 2], F32)
    memset(S, 'pool', cb[:, 0:1], 64 * EPS, ["cbias"])
    memset(S, 'pool', cb[:, 1:2], EPS, ["cbias"])
    nt = NormT(ph, 4, ident)
    xt = [ph.sb("xt%d" % i, [128, 4, 1024], F32) for i in range(2)]
    hT = [ph.sb("hT%d" % i, [128, 8, 512], BF16) for i in range(2)]
    qkst = [ph.sb("qkst%d" % i, [128, 8, 512], BF16) for i in range(2)]
    vust = [ph.sb("vust%d" % i, [128, 4, 1024], BF16) for i in range(2)]
    sqk = [ph.sb("sqk%d" % i, [128, 512], BF16) for i in range(3)]
    rs = [ph.sb("rs%d" % i, [128, 512], F32) for i in range(3)]
    pq = [ph.ps("pq%d" % i, [128, 512], F32) for i in range(3)]
    pss = [ph.ps("pss%d" % i, [128, 512], F32) for i in range(1)]
    pv = [ph.ps("pv%d" % i, [128, 512], F32) for i in range(2)]
    qkT_d = dr["qkT"].rearrange("c p t -> p c t")
    def pre_a1(ti):
        par = ti % 2
        t0 = ti * 512
        dma(S, 'sp', xt[par][:], dr["x"][t0:t0 + 512, :].rearrange("(s p) d -> p s d", p=128), (), [("xt", par)])

    def pre_a2(ti):
        par = ti % 2
        nt.run_a(xt[par], ("xt", par))

    def pre_b(ti):
        par = ti % 2
        nt.run_b(hT[par], ("hT", par), modAB[:, 0], modAB[:, 1], ti // 8)

    pre_a1(0)
    pre_a2(0)
    pre_b(0)
    for ti in range(ntiles):
        par = ti % 2
        b = ti // 8
        t0 = ti * 512
        if ti + 1 < ntiles:
            pre_a1(ti + 1)
        hkeys = [(("hT", par), c) for c in range(8)]
        def qk_tail(oc):
            i3 = oc % 3
            p = pq[i3]
            pk = ("pq", i3)
            isk = 1 if oc >= 4 else 0
            sk = ("sqk", i3)
            mm(S, pss[0][:], bones[:], sqk[i3][:], True, True, [sk, "bones"], ["pss"])
            rk = ("rs", i3)
            act(S, rs[i3][:], pss[0][:], AF.Sqrt, ["pss", "cbias"], [rk],
                scale=(1.0 / 64 if isk else 1.0), bias=cb[:, isk:isk + 1])
            S.op('dve', (lambda o: (lambda e: e.reciprocal(o, o)))(rs[i3][:]), [rk], [rk])
            stt(S, qkst[par][:, oc, :], p[:], qkg[:, isk:isk + 1], rs[i3][:], ALU.mult, ALU.mult,
                [pk, rk, "qkg"], [("qkst", par)])

        for oc in range(8):
            i3 = oc % 3
            p = pq[i3]
            pk = ("pq", i3)
            for k in range(8):
                mm(S, p[:], W[:, k, oc * 128:(oc + 1) * 128], hT[par][:, k, :], k == 0, k == 7,
                   [wkeys[k], hkeys[k]], [pk])
            act(S, sqk[i3][:], p[:], AF.Square, [pk], [("sqk", i3)])
            if oc >= 1:
                qk_tail(oc - 1)
            if oc == 3 and ti + 1 < ntiles:
                pre_a2(ti + 1)
        tails_left = [7]
        for vi, (col, dname) in enumerate(((1024, "v0"), (1536, "u0"))):
            for s in range(4):
                p = pv[s % 2]
                pk = ("pv", s % 2)
                for k in range(8):
                    mm(S, p[:], hT[par][:, k, s * 128:(s + 1) * 128], W[:, k, col:col + 512], k == 0, k == 7,
                       [wkeys[k], hkeys[k]], [pk])
                if tails_left:
                    qk_tail(tails_left.pop(0))
                    if not tails_left:
                        dma(S, 'sp', qkT_d[:, :, t0:t0 + 512], qkst[par][:], [("qkst", par)], [("qkT", ti)])
                cp(S, 'act' if s % 2 == 0 else 'dve', vust[par][:, s, vi * 512:(vi + 1) * 512], p[:], [pk], [("vust", par, vi)])
            dma(S, 'sp', dr[dname][t0:t0 + 512, :].rearrange("(s p) d -> p s d", p=128),
                vust[par][:, :, vi * 512:(vi + 1) * 512], [("vust", par, vi)], [(dname, ti)])
        if ti + 1 < ntiles:
            pre_b(ti + 1)
    ph.close()


def phase_p3(cx, l, cat_name, x_name, out_name, wo_name, ntiles=32, final=False):
    nc, S, dr = cx.nc, cx.S, cx.dr
    ph = Phase(cx, "p3_%d" % l)
    ident = load_consts(ph, S, dr)
    Wo = load_w(S, ph, "Wo", dr[wo_name], 1024, 1024)
    Win = load_w(S, ph, "Win", dr["ffn_w_in"][l], 1024, 2 * FF)
    Wout = load_w(S, ph, "Wout", dr["ffn_w_out"][l], FF, 1024)
    modAB = ph.sb("modAB", [128, 4, 8, 2], F32)
    dma(S, 'sp', modAB[:], dr["modfm"][l], [("modfm", l)], ["modAB"])
    gb = ph.sb("gb", [128, 2, 1024], F32)
    nt = NormT(ph, 2, ident)
    xt2 = [ph.sb("xt%d" % i, [128, 2, 1024], F32) for i in range(2)]
    ct = [ph.sb("ct%d" % i, [128, 8, 256], BF16) for i in range(2)]
    hT = ph.sb("hT", [128, 8, 256], BF16)
    hact = ph.sb("hact", [128, 22, 256], BF16)
    sil = [ph.sb("sil%d" % i, [128, 256], F32) for i in range(2)]
    tmp = ph.sb("tmp", [128, 1024], F32)
    po = ph.ps("po", [128, 1024], F32)
    pw = ph.ps("pw", [128, 1024], F32)
    pgu = [ph.ps("pgu%d" % i, [128, 2, 256], F32) for i in range(2)]
    cat_d = dr[cat_name].rearrange("(c p) t -> p c t", p=128)
    hkeys = [("hT", c) for c in range(8)]

    def load(ti):
        par = ti % 2
        b = ti // 16
        t0 = ti * 256
        if ti % 16 == 0:
            for wi in range(2):
                dma(S, 'sp', gb[:, wi, :], dr["gbc"][l, b, wi], [("gbc", l, b, wi)], ["gb"])
        dma(S, 'sp', xt2[par][:], dr[x_name][t0:t0 + 256, :].rearrange("(s p) d -> p s d", p=128), [(x_name, ti)], [("xt", par)])
        dma(S, 'sp', ct[par][:], cat_d[:, :, t0:t0 + 256], [(cat_name, ti)], [("ct", par)])

    def outproj(ti):
        par = ti % 2
        xt = xt2[par]
        for s in range(2):
            for half in range(2):
                for k in range(8):
                    mm(S, po[:, half * 512:(half + 1) * 512], ct[par][:, k, s * 128:(s + 1) * 128],
                       Wo[:, k, half * 512:(half + 1) * 512], k == 0, k == 7, [("Wo", k), ("ct", par)], [("po", half)])
            for half in range(2):
                hs_ = slice(half * 512, (half + 1) * 512)
                tt(S, 'dve', tmp[:, hs_], po[:, hs_], gb[:, 0, hs_], ALU.mult, [("po", half), "gb"], [("tmp", half)])
                tt(S, 'pool', xt[:, s, hs_], xt[:, s, hs_], tmp[:, hs_], ALU.add, [("tmp", half), ("xt", par)], [("xt", par)])

    def ffn_in(ti):
        for j in range(22):
            gu = pgu[j % 2]
            gk = ("pgu", j % 2)
            for hh in range(2):
                col = hh * FF + j * 128
                for k in range(8):
                    mm(S, gu[:, hh, :], Win[:, k, col:col + 128], hT[:, k, :], k == 0, k == 7,
                       [("Win", k), hkeys[k]], [gk])
            sk = ("sil", j % 2)
            act(S, sil[j % 2][:], gu[:, 0, :], AF.Silu, [gk], [sk])
            tt(S, 'dve', hact[:, j, :], gu[:, 1, :], sil[j % 2][:], ALU.mult, [gk, sk], [("hact", j)])

    def ffn_out(ti):
        par = ti % 2
        xt = xt2[par]
        t0 = ti * 256
        for s in range(2):
            for half in range(2):
                for j in range(22):
                    mm(S, pw[:, half * 512:(half + 1) * 512], hact[:, j, s * 128:(s + 1) * 128],
                       Wout[:, j, half * 512:(half + 1) * 512], j == 0, j == 21, [("Wout", j), ("hact", j)], [("pw", half)])
            for half in range(2):
                hs_ = slice(half * 512, (half + 1) * 512)
                tt(S, 'dve', tmp[:, hs_], pw[:, hs_], gb[:, 1, hs_], ALU.mult, [("pw", half), "gb"], [("tmp", half)])
                tt(S, 'pool', xt[:, s, hs_], xt[:, s, hs_], tmp[:, hs_], ALU.add, [("tmp", half), ("xt", par)], [("xt", par)])
        dma(S, 'sp', dr[out_name][t0:t0 + 256, :].rearrange("(s p) d -> p s d", p=128), xt[:], [("xt", par)], [(out_name, ti)],
            is_output=final)

    load(0)
    outproj(0)
    nt.run_a(xt2[0], ("xt", 0))
    nt.run_b(hT, "hT", modAB[:, 2], modAB[:, 3], 0)
    for ti in range(ntiles):
        nxt = ti + 1 < ntiles
        if nxt and (ti + 1) % 16 != 0:
            load(ti + 1)
        ffn_in(ti)
        if nxt and (ti + 1) % 16 != 0:
            outproj(ti + 1)
            nt.run_a(xt2[(ti + 1) % 2], ("xt", (ti + 1) % 2))
        ffn_out(ti)
        if nxt:
            if (ti + 1) % 16 == 0:
                load(ti + 1)
                outproj(ti + 1)
                nt.run_a(xt2[(ti + 1) % 2], ("xt", (ti + 1) % 2))
            nt.run_b(hT, "hT", modAB[:, 2], modAB[:, 3], (ti + 1) // 16)
    ph.close()


def phase_attn(cx, nseq=NSEQ, nqb=32):
    nc, S, dr = cx.nc, cx.S, cx.dr
    ph = Phase(cx, "pa")
    ident = load_consts(ph, S, dr)
    NEG = -30000.0
    Er = ph.sb("Er", [8, 257], F32)
    E = ph.sb("E", [8, 1024], F32)
    c256 = ph.sb("c256", [128, 8], F32)
    dma(S, 'sp', Er[:], dr["rel_bias"], (), ["Er"])
    rb = dr["rel_bias"]
    dma(S, 'sp', c256[:], bass.AP(tensor=rb.tensor, offset=rb.offset + 256, ap=[[0, 128], [257, 8]]), (), ["c256"], slow=True)
    memset(S, 'dve', E[:], 0.0, ["E"])
    ts(S, 'dve', E[:, 0:767], E[:, 0:767], Er[:, 256:257], None, ALU.add, None, ["E", "Er"], ["E"])
    cp(S, 'dve', E[:, 767:1024], Er[:, ::-1], ["E", "Er"], ["E"])
    dma(S, 'sp', dr["relext"], E[:], ["E"], ["relext"])
    BT = ph.sb("BT", [128, 5, 8, 128], F32)
    memset(S, 'pool', BT[:], 0.0, ["BT"])
    ext = dr["relext"]
    for j in range(3):
        tt(S, 'pool', BT[:, j, :, :], BT[:, j, :, :], apx(c256[:, :], [[0, 128]]), ALU.add, ["BT", "c256"], ["BT"])
    for j in (3, 4):
        for h in range(8):
            src = bass.AP(tensor=ext.tensor, offset=ext.offset + h * 1024 + 1023 - (5 - j) * 128 - 127, ap=[[1, 128], [1, 128]])
            dma(S, 'sp', BT[:, j, h, :], src, ["relext", "BT"], ["BT"])
    memset(S, 'pool', BT[0:64, 0, :, 0:64], NEG, ["BT"])
    memset(S, 'pool', BT[64:128, 4, :, 64:128], NEG, ["BT"])
    qT = ph.sb("qT", [128, 4, S_LEN], BF16)
    kT = ph.sb("kT", [128, 4, S_LEN], BF16)
    Vr = ph.sb("Vr", [128, 32, 512], BF16)
    Va = ph.sb("Va", [128, 32, 8, 65], BF16)
    memset(S, 'pool', Va[:, :, :, 64:65], 1.0, ["Va1"])
    NBA = 3
    sbf = [ph.sb("sbf%d" % i, [128, 5, 128], F32) for i in range(NBA)]
    pT = [ph.sb("pT%d" % i, [128, 5, 128], BF16) for i in range(NBA)]
    rc = ph.sb("rc", [128, 8], F32)
    ao = ph.sb("ao", [128, 512], BF16)
    aT = [ph.sb("aT%d" % i, [128, 4, 512], BF16) for i in range(2)]
    ps = [ph.ps("ps%d" % i, [128, 8, 128], F32) for i in range(2)]
    po = [ph.ps("po%d" % i, [128, 4, 65], F32) for i in range(2)]
    tp = ph.ps("tp", [128, 4, 128], BF16)
    qk_d = dr["qkT"].rearrange("c p t -> p c t")
    cat_d = dr["catT0"].rearrange("(c p) t -> p c t", p=128)
    hn = 0
    for b in range(nseq):
        tb = b * S_LEN
        for c in range(4):
            dma(S, 'sp', qT[:, c, :], qk_d[:, c, tb:tb + S_LEN], [("qkT", i) for i in range(b * 8, b * 8 + 8)], ["qT"])
            dma(S, 'sp', kT[:, c, :], qk_d[:, 4 + c, tb:tb + S_LEN], [("qkT", i) for i in range(b * 8, b * 8 + 8)], ["kT"])
        for c in range(4):
            dma(S, 'sp', Vr[:, c * 8:(c + 1) * 8, :],
                dr["v0"][tb + c * 1024: tb + (c + 1) * 1024, :].rearrange("(s p) d -> p s d", p=128),
                [("v0", i) for i in range(b * 8, b * 8 + 8)], ["Vr"])
        for c in range(4):
            cp(S, 'pool', Va[:, c * 8:(c + 1) * 8, :, 0:64], Vr[:, c * 8:(c + 1) * 8, :].rearrange("p s (h d) -> p s h d", h=8),
               ["Vr"], ["Va"])
        units = [(qb, h) for qb in range(nqb) for h in range(8)]
        bufi = {}

        def st1(u):
            nonlocal hn
            qb, h = u
            kb0 = max(0, qb - 4)
            nkb = qb - kb0 + 1
            j0 = 5 - nkb
            pr, base = h // 2, 64 * (h % 2)
            par = hn % NBA
            p_s, pk = ps[hn % 2], ("ps", hn % 2)
            hn += 1
            bufi[u] = par
            for j in range(nkb):
                kb = kb0 + j
                mm(S, p_s[:, j, :], kT[base:base + 64, pr, kb * 128:(kb + 1) * 128],
                   qT[base:base + 64, pr, qb * 128:(qb + 1) * 128], True, True, ["kT", "qT"], [pk])
            tt(S, 'dve', sbf[par][:, 0:nkb, :], p_s[:, 0:nkb, :], BT[:, j0:5, h, ::-1], ALU.add, [pk, "BT"], [("sbf", par)])
            act(S, pT[par][:, 0:nkb, :], sbf[par][:, 0:nkb, :], AF.Exp, [("sbf", par)], [("pT", par)])

        def st2(u):
            qb, h = u
            kb0 = max(0, qb - 4)
            nkb = qb - kb0 + 1
            par = bufi.pop(u)
            for j in range(nkb):
                kb = kb0 + j
                mm(S, po[h // 4][:, h % 4, :], pT[par][:, j, :], Va[:, kb, h, :], j == 0, j == nkb - 1,
                   [("pT", par), "Va", "Va1"], [("po", h // 4)])
            if h == 7:
                epi(qb)

        def epi(qb):
            for g in range(2):
                S.op('dve', (lambda o, i: (lambda e: e.reciprocal(o, i)))(rc[:, g * 4:(g + 1) * 4], po[g][:, :, 64]),
                     [("po", g)], [("rc", g)])
                tt(S, 'dve', ao[:, g * 256:(g + 1) * 256].rearrange("p (h d) -> p h d", h=4), po[g][:, :, 0:64],
                   apx(rc[:, g * 4:(g + 1) * 4], [[0, 64]]), ALU.mult, [("po", g), ("rc", g)], [("ao", g)])
            for c in range(4):
                tr(S, tp[:, c, :], ao[:, c * 128:(c + 1) * 128], ident[:], [("ao", c // 2), "ident"], ["tp"])
            apar = (qb // 4) % 2
            cp(S, 'act', aT[apar][:, :, (qb % 4) * 128:(qb % 4 + 1) * 128], tp[:], ["tp"], [("aT", apar)])
            if qb % 4 == 3 or qb == nqb - 1:
                q0 = (qb // 4) * 4
                n = (qb - q0 + 1) * 128
                dma(S, 'sp', cat_d[:, 0:4, tb + q0 * 128: tb + q0 * 128 + n], aT[apar][:, :, 0:n], [("aT", apar)],
                    [("catT0", "a", b, qb // 4)])

        SK = 2
        for i in range(len(units) + SK):
            if i < len(units):
                st1(units[i])
            if i - SK >= 0:
                st2(units[i - SK])
    ph.close()


def phase_p1_l1(cx, x_name, ntiles=16):
    nc, S, dr = cx.nc, cx.S, cx.dr
    ph = Phase(cx, "p1b")
    ident = load_consts(ph, S, dr)
    W = load_w(S, ph, "W", dr["cd_w_in"], 1024, 3584)
    wkeys = [("W", c) for c in range(8)]
    modAB = ph.sb("modAB", [128, 4, 8, 2], F32)
    dma(S, 'sp', modAB[:], dr["modfm"][1], [("modfm", 1)], ["modAB"])
    QD = ph.sb("QD", [128, 4], F32)
    KD = ph.sb("KD", [128, 4], F32)
    dma(S, 'sp', QD[:], dr["ret_qd"], (), ["QD"])
    dma(S, 'sp', KD[:], dr["ret_kd"], (), ["KD"])
    nt = NormT(ph, 4, ident)
    xt = [ph.sb("xt%d" % i, [128, 4, 1024], F32) for i in range(2)]
    hT2 = [ph.sb("hT%d" % i, [128, 8, 512], BF16) for i in range(2)]
    rot = [ph.sb("rot%d" % i, [128, 4, 2, 64], F32) for i in range(2)]
    t12 = [ph.sb("t12_%d" % i, [128, 4, 64], F32) for i in range(4)]
    R = [ph.sb("R%d" % i, [128, 4, 2, 64], F32) for i in range(2)]
    qkb = ph.sb("qkb", [128, 3, 512], BF16)
    fst = [ph.sb("fst%d" % i, [128, 12, 512], BF16) for i in range(2)]
    tst = [ph.sb("tst%d" % i, [128, 4, 3, 512], BF16) for i in range(2)]
    dst = [ph.sb("dst%d" % i, [128, 8, 512], BF16) for i in range(2)]
    dvs = [ph.sb("dvs%d" % i, [128, 4, 512], BF16) for i in range(2)]
    NPT = 4
    pt = [ph.ps("pt%d" % i, [128, 512], F32) for i in range(NPT)]
    ptr2 = [ph.ps("ptr%d" % i, [128, 4, 128], BF16) for i in range(2)]
    cf_d = dr["c_fm"].rearrange("k p t -> p k t")
    ct_d = dr["c_tok"]
    dq_d = dr["d_qkT"].rearrange("c p t -> p c t")
    pn = 0
    def pre_a(ti):
        par = ti % 2
        t0 = ti * 512
        pos0 = t0 % S_LEN
        xk = ("xt", par)
        dma(S, 'sp', xt[par][:], dr[x_name][t0:t0 + 512, :].rearrange("(s p) d -> p s d", p=128), [(x_name, 2 * ti), (x_name, 2 * ti + 1)], [xk])
        dma(S, 'sp', rot[par][:], dr["rot"][pos0:pos0 + 512].rearrange("(s p) a f -> p s a f", p=128), (), [("rot", par)])

    def pre_a2(ti):
        par = ti % 2
        nt.run_a(xt[par], ("xt", par))

    def pre_b(ti):
        par = ti % 2
        nt.run_b(hT2[par], ("hT", par), modAB[:, 0], modAB[:, 1], ti // 8)

    pre_a(0)
    pre_a2(0)
    pre_b(0)
    trn = 0
    for ti in range(ntiles):
        par = ti % 2
        b = ti // 8
        t0 = ti * 512
        if ti + 1 < ntiles:
            pre_a(ti + 1)
        hT = hT2[par]
        hkeys = [(("hT", par), c) for c in range(8)]

        def tokmm(s, col):
            nonlocal pn
            p = pt[pn % NPT]
            pk = ("pt", pn % NPT)
            pn += 1
            for k in range(8):
                mm(S, p[:], hT[:, k, s * 128:(s + 1) * 128], W[:, k, col:col + 512], k == 0, k == 7, [wkeys[k], hkeys[k]], [pk])
            return p, pk

        def fm_chunk(oc):
            nonlocal pn
            p = pt[pn % NPT]
            pk = ("pt", pn % NPT)
            pn += 1
            col = 2048 + oc * 128
            for k in range(8):
                mm(S, p[:], W[:, k, col:col + 128], hT[:, k, :], k == 0, k == 7, [wkeys[k], hkeys[k]], [pk])
            cp(S, 'act' if oc % 2 == 0 else 'dve', dst[par][:, oc, :], p[:], [pk], [("dst", par)])

        for s in range(4):
            cosv = rot[par][:, s, 0, :]
            sinv = rot[par][:, s, 1, :]
            cos4 = bass.AP(tensor=cosv.tensor, offset=cosv.offset, ap=[list(cosv.ap[0]), [0, 4], list(cosv.ap[1])])
            sin4 = bass.AP(tensor=sinv.tensor, offset=sinv.offset, ap=[list(sinv.ap[0]), [0, 4], list(sinv.ap[1])])
            for qi, col in enumerate((0, 512)):
                p, pk = tokmm(s, col)
                pv4 = p[:].rearrange("p (h a f) -> p h a f", h=4, a=2)
                x1, x2 = pv4[:, :, 0, :], pv4[:, :, 1, :]
                rk = ("R", qi)
                tt(S, 'dve', t12[0][:], x1, cos4, ALU.mult, [pk, ("rot", par)], ["t0"])
                tt(S, 'dve', t12[1][:], x2, sin4, ALU.mult, [pk, ("rot", par)], ["t1"])
                tt(S, 'dve', t12[2][:], x1, sin4, ALU.mult, [pk, ("rot", par)], ["t2"])
                tt(S, 'dve', t12[3][:], x2, cos4, ALU.mult, [pk, ("rot", par)], ["t3"])
                tt(S, 'pool', R[qi][:, :, 0, :], t12[0][:], t12[1][:], ALU.subtract, ["t0", "t1"], [rk])
                tt(S, 'pool', R[qi][:, :, 1, :], t12[2][:], t12[3][:], ALU.add, ["t2", "t3"], [rk])
            Rq = R[0][:].rearrange("p h a f -> p h (a f)")
            Rk = R[1][:].rearrange("p h a f -> p h (a f)")
            q3 = qkb[:].rearrange("p k (h d) -> p k h d", h=4)
            cp(S, 'act', q3[:, 0], Rq, [("R", 0)], [("qkb", 0)])
            tt(S, 'dve', q3[:, 1], Rq, apx(QD[:, :], [[0, 128]]), ALU.mult, [("R", 0), "QD"], [("qkb", 1)])
            act(S, q3[:, 2], Rk, AF.Copy, [("R", 1)], [("qkb", 2)], scale=128.0 ** -0.5)
            tt(S, 'pool', tst[par][:, s, 0, :].rearrange("p (h d) -> p h d", h=4), Rk, apx(KD[:, :], [[0, 128]]), ALU.mult,
               [("R", 1), "KD"], [("tst", par)])
            p, pk = tokmm(s, 1024)
            cp(S, 'act', tst[par][:, s, 1, :], p[:], [pk], [("tst", par)])
            p, pk = tokmm(s, 1536)
            act(S, tst[par][:, s, 2, :], p[:], AF.Silu, [pk], [("tst", par)])
            p, pk = tokmm(s, 3072)
            cp(S, 'act', dvs[par][:, s, :], p[:], [pk], [("dvs", par)])
            for oc in (2 * s, 2 * s + 1):
                fm_chunk(oc)
            for kind in range(3):
                ptr = ptr2[trn % 2]
                ptk = ("ptr", trn % 2)
                for h in range(4):
                    tr(S, ptr[:, h, :], qkb[:, kind, h * 128:(h + 1) * 128], ident[:], [("qkb", kind), "ident"], [ptk])
                cp(S, 'act' if trn % 2 == 0 else 'dve', fst[par][:, kind * 4:(kind + 1) * 4, s * 128:(s + 1) * 128], ptr[:],
                   [ptk], [("fst", par)])
                trn += 1
            if s == 1 and ti + 1 < ntiles:
                pre_a2(ti + 1)
        dma(S, 'sp', cf_d[:, :, t0:t0 + 512], fst[par][:], [("fst", par)], [("c_fm", ti)])
        dma(S, 'sp', ct_d[t0:t0 + 512].rearrange("(s p) k d -> p s k d", p=128), tst[par][:], [("tst", par)], [("c_tok", ti)])
        dma(S, 'sp', dq_d[:, :, t0:t0 + 512], dst[par][:], [("dst", par)], [("d_qkT", ti)])
        dma(S, 'sp', dr["d_v"][t0:t0 + 512, :].rearrange("(s p) d -> p s d", p=128), dvs[par][:], [("dvs", par)], [("d_v", ti)])
        if ti + 1 < ntiles:
            pre_b(ti + 1)
    ph.close()


def phase_sb(cx, nseq=NSEQ, nblk=32):
    nc, S, dr = cx.nc, cx.S, cx.dr
    ph = Phase(cx, "pd")
    ident = load_consts(ph, S, dr)
    mask = ph.sb("mask", [128, 256], F32)
    dma(S, 'sp', mask[:], dr["sb_mask"], (), ["mask"])
    ones = ph.sb("ones", [128, 256], F32)
    memset(S, 'pool', ones[:], 1.0, ["ones"])
    one1 = ph.sb("one1", [128, 1], F32)
    memset(S, 'pool', one1[:], 1.0, ["one1"])
    qT = ph.sb("qT", [128, 4, S_LEN], BF16)
    kT = ph.sb("kT", [128, 4, S_LEN], BF16)
    V = ph.sb("V", [128, 32, 512], BF16)
    ex = [ph.sb("ex%d" % i, [128, 256], F32) for i in range(8)]
    sp = [ph.sb("sp%d" % i, [128, 256], F32) for i in range(8)]
    Rc = [ph.sb("Rc%d" % i, [128, 256], F32) for i in range(8)]
    lw = [ph.sb("lw%d" % i, [128, 256], F32) for i in range(8)]
    wm = [ph.sb("wm%d" % i, [128, 256], BF16) for i in range(8)]
    wT = [ph.sb("wT%d" % i, [128, 2, 128], BF16) for i in range(8)]
    do_b = ph.sb("do_b", [128, 512], BF16)
    dT = [ph.sb("dT%d" % i, [128, 4, 512], BF16) for i in range(2)]
    pz = [ph.ps("pz%d" % i, [128, 256], F32) for i in range(4)]
    pwt = [ph.ps("pwt%d" % i, [128, 2, 128], BF16) for i in range(2)]
    po = ph.ps("po", [128, 8, 64], F32)
    ptp = ph.ps("ptp", [128, 4, 128], BF16)
    qk_d = dr["d_qkT"].rearrange("c p t -> p c t")
    cat_d = dr["catT1"].rearrange("(c p) t -> p c t", p=128)
    hn = 0
    for b in range(nseq):
        tb = b * S_LEN
        rk = [("d_qkT", i) for i in range(b * 8, b * 8 + 8)]
        for c in range(4):
            dma(S, 'sp', qT[:, c, :], qk_d[:, c, tb:tb + S_LEN], rk, ["qT"])
            dma(S, 'sp', kT[:, c, :], qk_d[:, 4 + c, tb:tb + S_LEN], rk, ["kT"])
        for c in range(4):
            dma(S, 'sp', V[:, c * 8:(c + 1) * 8, :],
                dr["d_v"][tb + c * 1024: tb + (c + 1) * 1024, :].rearrange("(s p) d -> p s d", p=128),
                [("d_v", i) for i in range(b * 8, b * 8 + 8)], ["V"])
        units = [(blk, h) for blk in range(nblk) for h in range(8)]
        bufi = {}

        def geom(blk):
            nk = 1 if blk == 0 else 2
            return nk, nk * 128, (blk + 1 - nk) * 128, 256 - nk * 128

        def stA(u):
            nonlocal hn
            blk, h = u
            nk, W_, k0, m0 = geom(blk)
            pr, base = h // 2, 64 * (h % 2)
            par = hn % 8
            z, zk = pz[hn % 4], ("pz", hn % 4)
            bufi[u] = (par, hn % 2)
            hn += 1
            mm(S, z[:, 0:W_], qT[base:base + 64, pr, blk * 128:(blk + 1) * 128], kT[base:base + 64, pr, k0:k0 + W_],
               True, True, ["qT", "kT"], [zk])
            act(S, ex[par][:, 0:W_], z[:, 0:W_], AF.Exp, [zk], [("ex", par)], scale=0.125)
            act(S, sp[par][:, 0:W_], ex[par][:, 0:W_], AF.Ln, [("ex", par), "one1"], [("sp", par)], bias=one1[:, 0:1])
            tt(S, 'dve', sp[par][:, 0:W_], sp[par][:, 0:W_], mask[:, m0:256], ALU.mult, [("sp", par), "mask"], [("sp", par)])
            S.op('dve', (lambda o, d0, d1: (lambda e: e.tensor_tensor_scan(o, d0, d1, 0.0, ALU.mult, ALU.add)))(
                Rc[par][:, 0:W_][:, ::-1], ones[:, 0:W_], sp[par][:, 0:W_][:, ::-1]), [("sp", par), "ones"], [("Rc", par)])
            stt(S, lw[par][:, 0:W_], z[:, 0:W_], 0.125, Rc[par][:, 0:W_], ALU.mult, ALU.subtract, [zk, ("Rc", par)], [("lw", par)])

        def stA2(u):
            blk, h = u
            nk, W_, k0, m0 = geom(blk)
            par, p2 = bufi[u]
            act(S, lw[par][:, 0:W_], lw[par][:, 0:W_], AF.Exp, [("lw", par)], [("lw", par)])
            tt(S, 'pool', wm[par][:, 0:W_], lw[par][:, 0:W_], mask[:, m0:256], ALU.mult, [("lw", par), "mask"], [("wm", par)])

        def stB(u):
            blk, h = u
            nk, W_, k0, m0 = geom(blk)
            par, p2 = bufi[u]
            pw2, pwk = pwt[p2], ("pwt", p2)
            for n in range(nk):
                tr(S, pw2[:, n, :], wm[par][:, n * 128:(n + 1) * 128], ident[:], [("wm", par), "ident"], [pwk])
            cp(S, 'act', wT[par][:, 0:nk, :], pw2[:, 0:nk, :], [pwk], [("wT", par)])

        def stC(u):
            blk, h = u
            nk, W_, k0, m0 = geom(blk)
            par, p2 = bufi.pop(u)
            for n in range(nk):
                kb = blk + 1 - nk + n
                mm(S, po[:, h, :], wT[par][:, n, :], V[:, kb, h * 64:(h + 1) * 64], n == 0, n == nk - 1,
                   [("wT", par), "V"], ["po"])
            if h == 7:
                epi(blk)

        def epi(blk):
            cp(S, 'dve', do_b[:], po[:].rearrange("p h d -> p (h d)"), ["po"], ["do_b"])
            for c in range(4):
                tr(S, ptp[:, c, :], do_b[:, c * 128:(c + 1) * 128], ident[:], ["do_b", "ident"], ["ptp"])
            apar = (blk // 4) % 2
            cp(S, 'act', dT[apar][:, :, (blk % 4) * 128:(blk % 4 + 1) * 128], ptp[:], ["ptp"], [("dT", apar)])
            if blk % 4 == 3 or blk == nblk - 1:
                q0 = (blk // 4) * 4
                n = (blk - q0 + 1) * 128
                dma(S, 'sp', cat_d[:, 4:8, tb + q0 * 128: tb + q0 * 128 + n], dT[apar][:, :, 0:n], [("dT", apar)],
                    [("catT1", "d", b, blk // 4)])

        for i in range(len(units) + 5):
            if i < len(units):
                stA(units[i])
            if 0 <= i - 1 < len(units):
                stA2(units[i - 1])
            if 0 <= i - 3 < len(units):
                stB(units[i - 3])
            if 0 <= i - 5 < len(units):
                stC(units[i - 5])
    ph.close()


def phase_ret(cx, nseq=NSEQ, nblk=32):
    nc, S, dr = cx.nc, cx.S, cx.dr
    ph = Phase(cx, "pc")
    ident = load_consts(ph, S, dr)
    decT = ph.sb("decT", [128, 4, 128], F32)
    dma(S, 'sp', decT[:], dr["ret_decT"], (), ["decT"])
    ng = ph.sb("ng", [128, 512], F32)
    rn = dr["ret_norm_g"]
    dma(S, 'sp', ng[:], bass.AP(tensor=rn.tensor, offset=rn.offset, ap=[[0, 128], [1, 512]]), (), ["ng"])
    mhalf = ph.sb("mhalf", [128, 4], F32)
    memset(S, 'pool', mhalf[:], -0.5, ["mhalf"])
    SEG = 8
    fm = [ph.sb("fm%d" % i, [128, 12, SEG * 128], BF16) for i in range(2)]
    tk = [ph.sb("tk%d" % i, [128, SEG, 3, 512], BF16) for i in range(2)]
    st32 = [ph.sb("st32_%d" % h, [128, 128], F32) for h in range(4)]
    stb = [[ph.sb("stb_%d_%d" % (h, i), [128, 128], BF16) for i in range(2)] for h in range(4)]
    PT = [ph.sb("PT%d" % i, [128, 4, 128], BF16) for i in range(2)]
    osb = ph.sb("osb", [128, 4, 128], F32)
    sq = ph.sb("sq", [128, 4, 128], F32)
    s12 = ph.sb("s12", [128, 2, 4], F32)
    mv = ph.sb("mv", [128, 3, 4], F32)
    cn = ph.sb("cn", [128, 512], F32)
    co_b = ph.sb("co_b", [128, 512], BF16)
    cT = [ph.sb("cT%d" % i, [128, 4, 512], BF16) for i in range(2)]
    pS = [ph.ps("pS%d" % i, [128, 4, 128], F32) for i in range(1)]
    pO = [ph.ps("pO%d" % i, [128, 4, 128], F32) for i in range(2)]
    pK = [ph.ps("pK%d" % i, [128, 128], F32) for i in range(4)]
    ptp = ph.ps("ptp", [128, 4, 128], BF16)
    cf_d = dr["c_fm"].rearrange("k p t -> p k t")
    ct_d = dr["c_tok"]
    cat_d = dr["catT1"].rearrange("(c p) t -> p c t", p=128)
    gam = [1.0 - 2.0 ** (-5.0 - h) for h in range(4)]
    sn = 0
    kn = 0
    for b in range(nseq):
        tb = b * S_LEN
        for h in range(4):
            memset(S, 'pool', st32[h][:], 0.0, [("st32", h)])
            memset(S, 'pool', stb[h][0][:], 0.0, [("stb", h, 0)])
        scnt = [0, 0, 0, 0]
        for blk in range(nblk):
            seg, sb_ = blk // SEG, blk % SEG
            sp_ = seg % 2
            if sb_ == 0:
                t0 = tb + seg * SEG * 128
                nb = min(SEG, nblk - seg * SEG)
                rkeys = [("c_fm", (t0 // 512) + i) for i in range(2)]
                dma(S, 'sp', fm[sp_][:, :, 0:nb * 128], cf_d[:, :, t0:t0 + nb * 128], rkeys, [("fm", sp_)])
                dma(S, 'sp', tk[sp_][:, 0:nb], ct_d[t0:t0 + nb * 128].rearrange("(s p) k d -> p s k d", p=128),
                    [("c_tok", (t0 // 512) + i) for i in range(2)], [("tk", sp_)])
            cs = slice(sb_ * 128, (sb_ + 1) * 128)
            bp = blk % 2
            po_ = pO[bp]
            pok = ("pO", bp)
            for h in range(4):
                mm(S, pS[0][:, h, :], fm[sp_][:, 8 + h, cs], fm[sp_][:, 0 + h, cs], True, True, [("fm", sp_)], ["pS"])
            tt(S, 'dve', PT[bp][:], pS[0][:], decT[:], ALU.mult, ["pS", "decT"], [("PT", bp)])
            for h in range(4):
                hs = slice(h * 128, (h + 1) * 128)
                mm(S, po_[:, h, :], PT[bp][:, h, :], tk[sp_][:, sb_, 1, hs], h == 0, False, [("PT", bp), ("tk", sp_)], [pok])
            for half in range(2):
                ps_ = slice(half * 64, (half + 1) * 64)
                kp = kn % 2
                kn += 1
                for h in range(4):
                    hs = slice(h * 128, (h + 1) * 128)
                    cur = scnt[h] % 2
                    mm(S, po_[ps_, h, :], fm[sp_][:, 4 + h, sb_ * 128 + half * 64: sb_ * 128 + (half + 1) * 64], stb[h][cur][:],
                       False, h == 3, [("fm", sp_), ("stb", h, cur)], [pok])
                    mm(S, pK[h][:], tk[sp_][ps_, sb_, 0, hs], tk[sp_][ps_, sb_, 1, hs], True, True, [("tk", sp_)], [("pK", h)])
                    stt(S, st32[h][:], st32[h][:], gam[h] ** 64, pK[h][:], ALU.mult, ALU.add, [("pK", h), ("st32", h)], [("st32", h)])
                    cp(S, 'act', stb[h][1 - cur][:], st32[h][:], [("st32", h)], [("stb", h, 1 - cur)])
                    scnt[h] += 1
            cp(S, 'act', osb[:], po_[:], [pok], ["osb"])
            S.op('dve', lambda e: e.reduce_sum(s12[:, 0, :], osb[:], axis=AX.X), ["osb"], [("s12", 0)])
            tt(S, 'pool', sq[:], osb[:], osb[:], ALU.mult, ["osb"], ["sq"])
            S.op('dve', lambda e: e.reduce_sum(s12[:, 1, :], sq[:], axis=AX.X), ["sq"], [("s12", 1)])
            ts(S, 'dve', mv[:, 0, :], s12[:, 0, :], 1.0 / 128, None, ALU.mult, None, [("s12", 0)], [("mv", 0)])
            tt(S, 'dve', mv[:, 1, :], mv[:, 0, :], mv[:, 0, :], ALU.mult, [("mv", 0)], [("mv", 1)])
            stt(S, mv[:, 2, :], s12[:, 1, :], 1.0 / 128, mv[:, 1, :], ALU.mult, ALU.subtract, [("s12", 1), ("mv", 1)], [("mv", 2)])
            ts(S, 'dve', mv[:, 2, :], mv[:, 2, :], EPS, None, ALU.add, None, [("mv", 2)], [("mv", 2)])
            tt(S, 'pool', mv[:, 2, :], mv[:, 2, :], mhalf[:], ALU.pow, [("mv", 2), "mhalf"], [("mv", 2)])
            cn3 = cn[:].rearrange("p (h d) -> p h d", h=4)
            tt(S, 'dve', cn3, osb[:], apx(mv[:, 0, :], [[0, 128]]), ALU.subtract, ["osb", ("mv", 0)], ["cn"])
            tt(S, 'dve', cn3, cn3, apx(mv[:, 2, :], [[0, 128]]), ALU.mult, ["cn", ("mv", 2)], ["cn"])
            tt(S, 'pool', cn[:], cn[:], ng[:], ALU.mult, ["cn", "ng"], ["cn"])
            tt(S, 'pool', co_b[:], cn[:], tk[sp_][:, sb_, 2, :], ALU.mult, ["cn", ("tk", sp_)], ["co_b"])
            for c in range(4):
                tr(S, ptp[:, c, :], co_b[:, c * 128:(c + 1) * 128], ident[:], ["co_b", "ident"], ["ptp"])
            apar = (blk // 4) % 2
            cp(S, 'act', cT[apar][:, :, (blk % 4) * 128:(blk % 4 + 1) * 128], ptp[:], ["ptp"], [("cT", apar)])
            if blk % 4 == 3 or blk == nblk - 1:
                q0 = (blk // 4) * 4
                n = (blk - q0 + 1) * 128
                dma(S, 'sp', cat_d[:, 0:4, tb + q0 * 128: tb + q0 * 128 + n], cT[apar][:, :, 0:n], [("cT", apar)],
                    [("catT1", "c", b, blk // 4)])
    ph.close()


def fap(t, off, dims, p0=0, pn=None):
    base = t[:]
    pstep, pcnt = base.ap[0]
    if pn is None:
        pn = pcnt - p0
    return bass.AP(tensor=base.tensor, offset=base.offset + p0 * pstep + off, ap=[[pstep, pn]] + [list(d) for d in dims])


def phase_s5(cx, nseq=NSEQ):
    nc, S, dr = cx.nc, cx.S, cx.dr
    ph = Phase(cx, "pb")
    ident = load_consts(ph, S, dr)
    identf = ph.sb("identf", [128, 128], F32)
    dma(S, 'sp', identf[:], dr["ident"], (), ["identf"])
    W_intra = ph.sb("W_intra", [128, 32, 2, 256], BF16)
    W_BU = ph.sb("W_BU", [128, 32, 2, 2, 64], BF16)
    W_CX = ph.sb("W_CX", [64, 2, 32, 256], BF16)
    AaL = [ph.sb("Aa%d" % i, [64, 2, 32], F32) for i in range(3)]
    A2L = [ph.sb("A2%d" % i, [64, 2, 32], F32) for i in range(3)]
    Aa, A2 = AaL[0], A2L[0]
    Wg = load_w(S, ph, "Wg", dr["s5_w_glu"], 512, 512)
    bg = ph.sb("bg", [128, 4], F32)
    dma(S, 'sp', bg[:], dr["s5_bgT"], (), ["bg"])
    memset(S, 'pool', W_intra[:], 0.0, ["W_intra"])

    pp = Phase(cx, "pbp")
    cnt = [0]

    def T(shape=(64, 32)):
        cnt[0] += 1
        return pp.sb("t%d" % cnt[0], list(shape), F32)

    def k(t):
        return t.name if hasattr(t, "name") else id(t)

    def mul(o, a, b_):
        tt(S, 'dve', o[:], a[:], b_[:], ALU.mult, [k(a), k(b_)], [k(o)])

    def add(o, a, b_):
        tt(S, 'dve', o[:], a[:], b_[:], ALU.add, [k(a), k(b_)], [k(o)])

    def sub(o, a, b_):
        tt(S, 'dve', o[:], a[:], b_[:], ALU.subtract, [k(a), k(b_)], [k(o)])

    def tsa(o, a, s1, s2, op0, op1):
        ts(S, 'dve', o[:], a[:], s1, s2, op0, op1, [k(a)], [k(o)])

    lre, lim, ldt = T(), T(), T()
    dma(S, 'sp', lre[:], dr["s5_lamT_re"], (), [k(lre)])
    dma(S, 'sp', lim[:], dr["s5_lamT_im"], (), [k(lim)])
    dma(S, 'sp', ldt[:], dr["s5_ldt_bc"], (), [k(ldt)])
    dt = T()
    act(S, dt[:], ldt[:], AF.Exp, [k(ldt)], [k(dt)])
    lr, ang, mag = T(), T(), T()
    mul(lr, lre, dt)
    mul(ang, lim, dt)
    act(S, mag[:], lr[:], AF.Exp, [k(lr)], [k(mag)])
    kf, r = T(), T()
    MAGIC = 12582912.0
    tsa(kf, ang, 1.0 / (2 * np.pi), None, ALU.mult, None)
    tsa(kf, kf, MAGIC, None, ALU.add, None)
    tsa(kf, kf, MAGIC, None, ALU.subtract, None)
    C1 = 6.28125
    C2 = 2 * np.pi - C1
    stt(S, r[:], kf[:], -C1, ang[:], ALU.mult, ALU.add, [k(kf), k(ang)], [k(r)])
    stt(S, r[:], kf[:], -C2, r[:], ALU.mult, ALU.add, [k(kf), k(r)], [k(r)])
    y, y2, sn, cs, tmp = T(), T(), T(), T(), T()
    tsa(y, r, 0.125, None, ALU.mult, None)
    mul(y2, y, y)
    f = [1.0]
    for i in range(1, 12):
        f.append(f[-1] * i)
    tsa(sn, y2, 1.0 / f[9], -1.0 / f[7], ALU.mult, ALU.add)
    for c_ in (1.0 / f[5], -1.0 / f[3], 1.0):
        mul(sn, sn, y2)
        tsa(sn, sn, c_, None, ALU.add, None)
    mul(sn, sn, y)
    tsa(cs, y2, -1.0 / f[10], 1.0 / f[8], ALU.mult, ALU.add)
    for c_ in (-1.0 / f[6], 1.0 / f[4], -0.5, 1.0):
        mul(cs, cs, y2)
        tsa(cs, cs, c_, None, ALU.add, None)
    for _ in range(3):
        mul(tmp, sn, cs)
        mul(cs, sn, sn)
        tsa(sn, tmp, 2.0, None, ALU.mult, None)
        tsa(cs, cs, -2.0, 1.0, ALU.mult, ALU.add)
    ar, ai = T(), T()
    mul(ar, mag, cs)
    mul(ai, mag, sn)
    am1, den, fre, fim, t1, t2 = T(), T(), T(), T(), T(), T()
    tsa(am1, ar, -1.0, None, ALU.add, None)
    mul(den, lre, lre)
    mul(t1, lim, lim)
    add(den, den, t1)
    S.op('dve', lambda e: e.reciprocal(den[:], den[:]), [k(den)], [k(den)])
    mul(t1, am1, lre)
    mul(t2, ai, lim)
    add(fre, t1, t2)
    mul(fre, fre, den)
    mul(t1, ai, lre)
    mul(t2, am1, lim)
    sub(fim, t1, t2)
    mul(fim, fim, den)
    pr = pp.sb("pr", [64, 32, 17], F32)
    pi_ = pp.sb("pi", [64, 32, 17], F32)
    memset(S, 'dve', pr[:, :, 0:1], 1.0, ["pr"])
    memset(S, 'dve', pi_[:, :, 0:1], 0.0, ["pi"])
    q1, q2 = T(), T()
    for j in range(16):
        tt(S, 'dve', q1[:], pr[:, :, j], ar[:], ALU.mult, ["pr", k(ar)], [k(q1)])
        tt(S, 'dve', q2[:], pi_[:, :, j], ai[:], ALU.mult, ["pi", k(ai)], [k(q2)])
        tt(S, 'dve', pr[:, :, j + 1], q1[:], q2[:], ALU.subtract, [k(q1), k(q2)], ["pr"])
        tt(S, 'dve', q1[:], pr[:, :, j], ai[:], ALU.mult, ["pr", k(ai)], [k(q1)])
        tt(S, 'dve', q2[:], pi_[:, :, j], ar[:], ALU.mult, ["pi", k(ar)], [k(q2)])
        tt(S, 'dve', pi_[:, :, j + 1], q1[:], q2[:], ALU.add, [k(q1), k(q2)], ["pi"])
    cr, ci = T(), T()
    cp(S, 'dve', cr[:], pr[:, :, 16], ["pr"], [k(cr)])
    cp(S, 'dve', ci[:], pi_[:, :, 16], ["pi"], [k(ci)])
    for lv in range(3):
        for ri in range(2):
            cp(S, 'dve', AaL[lv][:, ri, :], cr[:], [k(cr)], ["Aa"])
        ts(S, 'dve', A2L[lv][:, 0, :], ci[:], -1.0, None, ALU.mult, None, [k(ci)], ["A2"])
        cp(S, 'dve', A2L[lv][:, 1, :], ci[:], [k(ci)], ["A2"])
        if lv < 2:
            mul(q1, cr, cr)
            mul(q2, ci, ci)
            mul(ci, cr, ci)
            tsa(ci, ci, 2.0, None, ALU.mult, None)
            sub(cr, q1, q2)
    bre, bim = pp.sb("bre", [64, 32, 16], F32), pp.sb("bim", [64, 32, 16], F32)
    cre, cim = pp.sb("cre", [64, 32, 16], F32), pp.sb("cim", [64, 32, 16], F32)
    dma(S, 'sp', bre[:], dr["s5_bT_re"], (), ["bre"])
    dma(S, 'sp', bim[:], dr["s5_bT_im"], (), ["bim"])
    dma(S, 'sp', cre[:], dr["s5_cT_re"], (), ["cre"])
    dma(S, 'sp', cim[:], dr["s5_cT_im"], (), ["cim"])
    Bre, Bim, nBim, u1, u2 = [pp.sb(n_, [64, 32, 16], F32) for n_ in ("Bre", "Bim", "nBim", "u1", "u2")]
    fre_b, fim_b = apx(fre[:], [[0, 16]]), apx(fim[:], [[0, 16]])
    tt(S, 'dve', u1[:], bre[:], fre_b, ALU.mult, ["bre", k(fre)], ["u1"])
    tt(S, 'dve', u2[:], bim[:], fim_b, ALU.mult, ["bim", k(fim)], ["u2"])
    tt(S, 'dve', Bre[:], u1[:], u2[:], ALU.subtract, ["u1", "u2"], ["Bre"])
    tt(S, 'dve', u1[:], bim[:], fre_b, ALU.mult, ["bim", k(fre)], ["u1"])
    tt(S, 'dve', u2[:], bre[:], fim_b, ALU.mult, ["bre", k(fim)], ["u2"])
    tt(S, 'dve', Bim[:], u1[:], u2[:], ALU.add, ["u1", "u2"], ["Bim"])
    ts(S, 'dve', nBim[:], Bim[:], -1.0, None, ALU.mult, None, ["Bim"], ["nBim"])
    HG = 16
    pp1 = Phase(cx, "pbp1")
    T1 = pp1.sb("T1", [64, HG, 17, 16], F32)
    T2 = pp1.sb("T2", [64, HG, 17, 16], F32)
    T3 = pp1.sb("T3", [64, HG, 17, 16], F32)
    Kb = pp1.sb("Kb", [16, 32, 256], BF16)
    dbc = pp1.sb("dbc", [16, 32, 16], F32)
    dma(S, 'sp', dbc[:], dr["s5_d_bc"][0:16], (), ["dbc"])
    Dg = pp1.sb("Dg", [16, 32, 16], F32)
    tt(S, 'dve', Dg[:], dbc[:], fap(identf, 0, [(0, 32), (1, 16)], 0, 16), ALU.mult, ["dbc", "identf"], ["Dg"])
    pk = [pp1.ps("pk%d" % i, [16, 2, 256], F32) for i in range(2)]
    for gh in range(2):
        gs = slice(gh * HG, (gh + 1) * HG)
        Cre_b = fap(cre, gh * HG * 16, [(16, HG), (0, 17), (1, 16)])
        Cim_b = fap(cim, gh * HG * 16, [(16, HG), (0, 17), (1, 16)])
        pr_b = fap(pr, gh * HG * 17, [(17, HG), (1, 17), (0, 16)])
        pi_b = fap(pi_, gh * HG * 17, [(17, HG), (1, 17), (0, 16)])
        tt(S, 'dve', T1[:], Cre_b, pr_b, ALU.mult, ["cre", "pr"], ["T1"])
        tt(S, 'dve', T3[:], Cim_b, pi_b, ALU.mult, ["cim", "pi"], ["T3"])
        tt(S, 'pool', T1[:], T1[:], T3[:], ALU.subtract, ["T1", "T3"], ["T1"])
        tt(S, 'dve', T2[:], Cre_b, pi_b, ALU.mult, ["cre", "pi"], ["T2"])
        tt(S, 'dve', T3[:], Cim_b, pr_b, ALU.mult, ["cim", "pr"], ["T3"])
        tt(S, 'pool', T2[:], T2[:], T3[:], ALU.add, ["T2", "T3"], ["T2"])
        cp(S, 'act', W_CX[:, 0, gs, :].rearrange("p g (t h) -> p g t h", t=16), T1[:, :, 1:17, :], ["T1"], ["W_CX"])
        ts(S, 'pool', W_CX[:, 1, gs, :].rearrange("p g (t h) -> p g t h", t=16), T2[:, :, 1:17, :], -1.0, None, ALU.mult, None,
           ["T2"], ["W_CX"])
        for gl in range(HG):
            g = gh * HG + gl
            pkt = pk[(g // 2) % 2]
            pkk = ("pk", (g // 2) % 2)
            mm(S, pkt[:, g % 2, :], Bre[:, g, :], T1[:, gl, 0:16, :].rearrange("p t h -> p (t h)"), True, False, ["Bre", "T1"], [pkk])
            mm(S, pkt[:, g % 2, :], nBim[:, g, :], T2[:, gl, 0:16, :].rearrange("p t h -> p (t h)"), False, True, ["nBim", "T2"], [pkk])
            if g % 2 == 1:
                cp(S, 'act', Kb[:, g - 1:g + 1, 16:256], pkt[:, :, 16:256], [pkk], ["Kb"])
                tt(S, 'dve', Kb[:, g - 1:g + 1, 0:16], pkt[:, :, 0:16], Dg[:, g - 1:g + 1, :], ALU.add, [pkk, "Dg"], ["Kb"])
    dma(S, 'sp', dr["s5_kall"], Kb[:], ["Kb"], ["kall_d"])
    kd = dr["s5_kall"]
    for s in range(16):
        half, sl = s // 8, s % 8
        n = (16 - s) * 16
        dma(S, 'sp', W_intra[16 * sl:16 * sl + 16, :, half, 16 * s:256], kd[:, :, 0:n], ["kall_d", "W_intra"], ["W_intra"])
    pp1.close()
    pp2 = Phase(cx, "pbp2")
    WTb = [pp2.sb("WTb%d" % i, [64, 32, 256], BF16) for i in range(2)]
    ptw = [pp2.ps("ptw%d" % i, [128, 8, 64], BF16) for i in range(2)]
    Q1 = pp2.sb("Q1", [64, HG, 16, 16], F32)
    Q2 = pp2.sb("Q2", [64, HG, 16, 16], F32)
    for gh in range(2):
        gs = slice(gh * HG, (gh + 1) * HG)
        prr = fap(pr, gh * HG * 17 + 15, [(17, HG), (-1, 16), (0, 16)])
        pir = fap(pi_, gh * HG * 17 + 15, [(17, HG), (-1, 16), (0, 16)])
        Bre_b = fap(Bre, gh * HG * 16, [(16, HG), (0, 16), (1, 16)])
        Bim_b = fap(Bim, gh * HG * 16, [(16, HG), (0, 16), (1, 16)])
        tt(S, 'dve', Q1[:], prr, Bre_b, ALU.mult, ["pr", "Bre"], ["Q1"])
        tt(S, 'dve', Q2[:], pir, Bim_b, ALU.mult, ["pi", "Bim"], ["Q2"])
        tt(S, 'pool', WTb[0][:, gs, :].rearrange("p g (s h) -> p g s h", s=16), Q1[:], Q2[:], ALU.subtract, ["Q1", "Q2"], [("WTb", 0)])
        tt(S, 'dve', Q1[:], prr, Bim_b, ALU.mult, ["pr", "Bim"], ["Q1"])
        tt(S, 'dve', Q2[:], pir, Bre_b, ALU.mult, ["pi", "Bre"], ["Q2"])
        tt(S, 'pool', WTb[1][:, gs, :].rearrange("p g (s h) -> p g s h", s=16), Q1[:], Q2[:], ALU.add, ["Q1", "Q2"], [("WTb", 1)])
    for g2 in range(16):
        pt_ = ptw[g2 % 2]
        for gi in range(2):
            g = 2 * g2 + gi
            for half in range(2):
                for ri in range(2):
                    tr(S, pt_[:, gi * 4 + half * 2 + ri, :], WTb[ri][:, g, half * 128:(half + 1) * 128], ident[0:64, 0:64],
                       [("WTb", ri), "ident"], [("ptw", g2 % 2)])
        cp(S, 'act' if g2 % 2 == 0 else 'dve', W_BU[:, 2 * g2:2 * g2 + 2].rearrange("p g a r q -> p (g a r) q"), pt_[:],
           [("ptw", g2 % 2)], ["W_BU"])
    pp2.close()
    pp.close()

    RA = ph.sb("RA", [128, 16384], BF16)
    RB = ph.sb("RB", [128, 16384], BF16)
    RC = ph.sb("RC", [128, 16448], BF16)
    Ublk = RA[:].rearrange("p (cb s ch) -> p cb s ch", cb=2, s=16)
    Xb = RA[0:64, :].rearrange("p (r g c) -> p r g c", r=2, g=32)
    UT = RB[:].rearrange("p (g a c) -> p g a c", g=32, a=2)
    ygT = RB[:].rearrange("p (k t) -> p k t", k=4)
    GX = RC[0:64, :].bitcast(F32).rearrange("p (r g c) -> p r g c", r=2, g=16)
    Ytok = RC[:, 0:16384].rearrange("p (cb t ch) -> p cb t ch", cb=2, t=16)
    P1 = ph.sb("P1", [64, 2, 16], F32)
    P2 = ph.sb("P2", [64, 2, 16], F32)
    Q1s = ph.sb("Q1s", [64, 2, 16, 32], F32)
    Q2s = ph.sb("Q2s", [64, 2, 16, 32], F32)
    sg = [ph.sb("sg%d" % i, [128, 512], F32) for i in range(2)]
    boT = [ph.sb("boT%d" % i, [128, 4, 512], BF16) for i in range(2)]
    ptu = [ph.ps("ptu%d" % i, [128, 8, 128], BF16) for i in range(2)]
    pG = [ph.ps("pG%d" % i, [64, 2, 256], F32) for i in range(2)]
    pY = [ph.ps("pY%d" % i, [128, 256], F32) for i in range(2)]
    pL = [ph.ps("pL%d" % i, [128, 512], F32) for i in range(2)]
    cat_d = dr["catT0"].rearrange("(c p) t -> p c t", p=128)
    for b in range(nseq):
        tb = b * S_LEN
        dma(S, 'sp', RA[:].rearrange("p (cb x) -> p cb x", cb=2),
            dr["u0"][tb:tb + S_LEN, :].rearrange("(cb c s) ch -> c cb (s ch)", cb=2, c=128),
            [("u0", i) for i in range(b * 8, b * 8 + 8)], ["RA"])
        for cb in range(2):
            cp(S, 'dve' if cb == 0 else 'pool', fap(RC, cb * 8192, [(256, 32), (16, 16), (1, 16)]),
               fap(RA, cb * 8192, [(16, 32), (512, 16), (1, 16)]), ["RA"], ["RC"])
        for g2 in range(16):
            pt_ = ptu[g2 % 2]
            for gi in range(2):
                g = 2 * g2 + gi
                for half in range(2):
                    for cb in range(2):
                        tr(S, pt_[:, gi * 4 + half * 2 + cb, :], fap(RC, cb * 8192 + g * 256 + half * 128, [(1, 128)]), ident[:],
                           ["RC", "ident"], [("ptu", g2 % 2)])
            cp(S, 'act' if g2 % 2 == 0 else 'dve', UT[:, 2 * g2:2 * g2 + 2].rearrange("p g a c -> p (g a c)"),
               pt_[:].rearrange("p a c -> p (a c)"), [("ptu", g2 % 2)], ["RB"])
        for gh in range(2):
            memset(S, 'pool', GX[:, :, :, 0:1], 0.0, ["RC"])
            for gl in range(16):
                g = gh * 16 + gl
                pg_ = pG[gl % 2]
                for ri in range(2):
                    for half in range(2):
                        mm(S, pg_[:, ri, :], W_BU[:, g, half, ri, :], UT[:, g, half, :], half == 0, half == 1,
                           ["W_BU", "RB"], [("pG", gl % 2)])
                cp(S, 'act' if gl % 2 == 0 else 'dve', GX[:, :, gl, 1:257], pg_[:], [("pG", gl % 2)], ["RC"])
            ghs = slice(gh * 16, (gh + 1) * 16)

            def sweep(d0, s0, stride, n, lv):
                for c0 in range(0, n, 32):
                    m = min(32, n - c0)
                    da, sa = d0 + stride * c0, s0 + stride * c0
                    dst = GX[:, :, :, da:da + stride * (m - 1) + 1:stride]
                    src = GX[:, :, :, sa:sa + stride * (m - 1) + 1:stride]
                    srs = GX[:, ::-1, :, sa:sa + stride * (m - 1) + 1:stride]
                    tt(S, 'dve', Q1s[:, :, :, 0:m], src, apx(AaL[lv][:, :, ghs], [[0, m]]), ALU.mult, ["RC", "Aa"], ["Q1s"])
                    tt(S, 'dve', Q2s[:, :, :, 0:m], srs, apx(A2L[lv][:, :, ghs], [[0, m]]), ALU.mult, ["RC", "A2"], ["Q2s"])
                    tt(S, 'dve', Q1s[:, :, :, 0:m], Q1s[:, :, :, 0:m], Q2s[:, :, :, 0:m], ALU.add, ["Q1s", "Q2s"], ["Q1s"])
                    tt(S, 'dve', dst, dst, Q1s[:, :, :, 0:m], ALU.add, ["RC", "Q1s"], ["RC"])

            sweep(2, 1, 2, 128, 0)
            sweep(4, 2, 4, 64, 1)
            Aa_h = AaL[2][:, :, ghs]
            A2_h = A2L[2][:, :, ghs]
            for j in range(2, 65):
                c = 4 * (j - 1)
                Xc = GX[:, :, :, c]
                Xs = GX[:, ::-1, :, c]
                tt(S, 'dve', P1[:], Xc, Aa_h, ALU.mult, ["RC", "Aa"], ["P1"])
                tt(S, 'dve', P2[:], Xs, A2_h, ALU.mult, ["RC", "A2"], ["P2"])
                tt(S, 'dve', P1[:], P1[:], P2[:], ALU.add, ["P1", "P2"], ["P1"])
                tt(S, 'dve', GX[:, :, :, c + 4], GX[:, :, :, c + 4], P1[:], ALU.add, ["RC", "P1"], ["RC"])
            sweep(2, 0, 4, 64, 1)
            sweep(1, 0, 2, 128, 0)
            cp(S, 'pool', Xb[:, :, gh * 16:(gh + 1) * 16, :], GX[:, :, :, 0:256], ["RC"], ["RA"])
        yn = 0
        for g in range(32):
            for cb in range(2):
                py = pY[yn % 2]
                pyk = ("pY", yn % 2)
                yn += 1
                cs_ = slice(cb * 128, (cb + 1) * 128)
                mm(S, py[:], UT[:, g, 0, cs_], W_intra[:, g, 0, :], True, False, ["RB", "W_intra"], [pyk])
                mm(S, py[:], UT[:, g, 1, cs_], W_intra[:, g, 1, :], False, False, ["RB", "W_intra"], [pyk])
                mm(S, py[:], Xb[:, 0, g, cs_], W_CX[:, 0, g, :], False, False, ["RA", "W_CX"], [pyk])
                mm(S, py[:], Xb[:, 1, g, cs_], W_CX[:, 1, g, :], False, True, ["RA", "W_CX"], [pyk])
                act(S, Ytok[:, cb, :, 16 * g:16 * g + 16], py[:].rearrange("p (t h) -> p t h", t=16), AF.Gelu_apprx_tanh, [pyk], ["RC"])
        tn = 0
        for cb in range(2):
            for kc in range(4):
                for th in range(2):
                    pt_ = ptu[tn % 2]
                    ptk = ("ptu", tn % 2)
                    tn += 1
                    for tl in range(8):
                        t_ = th * 8 + tl
                        tr(S, pt_[:, tl, :], Ytok[:, cb, t_, kc * 128:(kc + 1) * 128], ident[:], ["RC", "ident"], [ptk])
                    dst = fap(RB, kc * 4096 + cb * 2048 + th * 8, [(1, 8), (16, 128)])
                    cp(S, 'act' if tn % 2 == 0 else 'dve', dst, pt_[:], [ptk], ["RB"])
        ln = 0
        for ti in range(8):
            tsl = slice(ti * 512, (ti + 1) * 512)
            bp = ti % 2
            for oc in range(4):
                pl = pL[ln % 2]
                plk = ("pL", ln % 2)
                sgt = sg[ln % 2]
                sgk = ("sg", ln % 2)
                ln += 1
                for kc in range(4):
                    mm(S, pl[:], Wg[:, kc, oc * 128:(oc + 1) * 128], ygT[:, kc, tsl], kc == 0, kc == 3, [("Wg", kc), "RB"], [plk])
                act(S, sgt[:], pl[:], AF.Sigmoid, [plk, "bg"], [sgk], bias=bg[:, oc:oc + 1])
                tt(S, 'dve', boT[bp][:, oc, :], sgt[:], ygT[:, oc, tsl], ALU.mult, [sgk, "RB"], [("boT", bp)])
            dma(S, 'sp', cat_d[:, 4:8, tb + ti * 512: tb + (ti + 1) * 512], boT[bp][:], [("boT", bp)], [("catT0", "b", b, ti)])
    ph.close()


def prep_core(inp, core):
    b0 = 2 * core
    d = {}
    d["x"] = np.ascontiguousarray(inp["x"][b0:b0 + 2].reshape(8192, 1024))
    c2 = inp["c"][b0:b0 + 2]
    d["cT"] = np.ascontiguousarray(c2.T.reshape(8, 128, 2).transpose(1, 0, 2))
    d["ada_w"] = inp["ada_w"]
    d["ada_b"] = inp["ada_b"]
    d["ada_bT"] = np.ascontiguousarray(inp["ada_b"].reshape(2, 48, 128).transpose(0, 2, 1))
    lg = np.stack([inp["ln_mix_g"], inp["ln_ffn_g"]], 0)
    d["ln_gT"] = np.ascontiguousarray(lg.reshape(2, 2, 8, 128).transpose(0, 1, 3, 2))
    return d

def prep_l0(inp, d):
    d["ab_w_in"] = inp["ab_w_in"][0]
    d["qkg"] = np.ascontiguousarray(np.stack([np.tile(inp["a_q_gain"][0], 2), np.tile(inp["a_k_gain"][0], 2)], 1))
    d["ident"] = np.eye(128, dtype=np.float32)
    bo = np.zeros((128, 128), np.float32); bo[:64, :64] = 1; bo[64:, 64:] = 1
    d["blockones"] = bo
    return d

def prep_consts(d):
    d["ident"] = np.eye(128, dtype=np.float32)
    pos = np.arange(4096, dtype=np.float64)[:, None]
    fr = 10000.0 ** (-np.arange(64, dtype=np.float64) / 64)[None, :]
    ang = pos * fr
    d["rot"] = np.stack([np.cos(ang), np.sin(ang)], 1).astype(np.float32)
    lg = np.log(1.0 - 2.0 ** (-5.0 - np.arange(4, dtype=np.float64)))
    idx = np.arange(128) % 64
    d["ret_qd"] = np.exp(lg[None, :] * (idx[:, None] + 1.0)).astype(np.float32)
    d["ret_kd"] = (np.exp(lg[None, :] * (63.0 - idx[:, None])) * 128.0 ** -0.5).astype(np.float32)
    j = np.arange(128)[:, None]; i = np.arange(128)[None, :]
    same = (j // 64) == (i // 64)
    dec = np.exp(lg[:, None, None] * np.abs(i - j)[None]) * same[None]
    d["ret_decT"] = np.ascontiguousarray(dec.transpose(1, 0, 2)).astype(np.float32)
    m = np.ones((128, 256), np.float32)
    m[:, 128:] = (np.arange(128)[None, :] < np.arange(128)[:, None]).astype(np.float32)
    d["sb_mask"] = m
    return d

def prep_s5(inp, d):
    d["s5_lamT_re"] = np.ascontiguousarray(inp["s5_lambda_re"][0].T)
    d["s5_lamT_im"] = np.ascontiguousarray(inp["s5_lambda_im"][0].T)
    d["s5_ldt_bc"] = np.ascontiguousarray(np.broadcast_to(inp["s5_log_dt"][0][None, :], (64, 32)))
    d["s5_bT_re"] = np.ascontiguousarray(inp["s5_b_re"][0].transpose(1, 0, 2))
    d["s5_bT_im"] = np.ascontiguousarray(inp["s5_b_im"][0].transpose(1, 0, 2))
    d["s5_cT_re"] = np.ascontiguousarray(inp["s5_c_re"][0].transpose(2, 0, 1))
    d["s5_cT_im"] = np.ascontiguousarray(inp["s5_c_im"][0].transpose(2, 0, 1))
    d["s5_d_bc"] = np.ascontiguousarray(np.broadcast_to(inp["s5_d"][0][None], (128, 32, 16)))
    d["s5_w_glu"] = inp["s5_w_glu"][0]
    d["s5_bgT"] = np.ascontiguousarray(inp["s5_b_glu"][0].reshape(4, 128).T)
    return d


IN_SPECS = [
    ("x", [8192, 1024]), ("cT", [128, 8, 2]), ("ada_w", [2, 1024, 6144]), ("ada_b", [2, 6144]), ("ada_bT", [2, 128, 48]),
    ("ln_gT", [2, 2, 128, 8]), ("ab_w_in", [1024, 2048]), ("qkg", [128, 2]), ("ident", [128, 128]), ("blockones", [128, 128]),
    ("rel_bias", [8, 257]),
    ("s5_lamT_re", [64, 32]), ("s5_lamT_im", [64, 32]), ("s5_ldt_bc", [64, 32]), ("s5_bT_re", [64, 32, 16]), ("s5_bT_im", [64, 32, 16]),
    ("s5_cT_re", [64, 32, 16]), ("s5_cT_im", [64, 32, 16]), ("s5_d_bc", [128, 32, 16]), ("s5_w_glu", [512, 512]), ("s5_bgT", [128, 4]),
    ("ab_w_out", [1024, 1024]), ("ffn_w_in", [2, 1024, 5632]), ("ffn_w_out", [2, 2816, 1024]),
    ("cd_w_in", [1024, 3584]), ("cd_w_out", [1024, 1024]), ("ret_norm_g", [512]),
    ("rot", [4096, 2, 64]), ("ret_qd", [128, 4]), ("ret_kd", [128, 4]), ("ret_decT", [128, 4, 128]), ("sb_mask", [128, 256]),
]


def build_program():
    nc = bass.Bass("TRN2", target_bir_lowering=False)
    cx = Ctx(nc)
    for n_, s_ in IN_SPECS:
        cx.dram_in(n_, s_)
    cx.dram_out("out", [T_CORE, 1024])
    cx.dram_scr("modfm", [2, 128, 4, 8, 2], F32)
    cx.dram_scr("gbc", [2, 2, 2, 128, 1024], F32)
    cx.dram_scr("qkT", [8, 128, T_CORE], BF16)
    cx.dram_scr("v0", [T_CORE, 512], BF16)
    cx.dram_scr("u0", [T_CORE, 512], BF16)
    cx.dram_scr("relext", [8, 1024], F32)
    cx.dram_scr("s5_kall", [16, 32, 256], BF16)
    cx.dram_scr("catT0", [1024, T_CORE], BF16)
    cx.dram_scr("x1", [T_CORE, 1024], F32)
    cx.dram_scr("c_fm", [12, 128, T_CORE], BF16)
    cx.dram_scr("c_tok", [T_CORE, 3, 512], BF16)
    cx.dram_scr("d_qkT", [8, 128, T_CORE], BF16)
    cx.dram_scr("d_v", [T_CORE, 512], BF16)
    cx.dram_scr("catT1", [1024, T_CORE], BF16)
    with cx.st:
        phase_adaln(cx)
        phase_p1_l0(cx)
        phase_attn(cx)
        phase_s5(cx)
        phase_p3(cx, 0, "catT0", "x", "x1", "ab_w_out")
        phase_p1_l1(cx, "x1")
        phase_sb(cx)
        phase_ret(cx)
        phase_p3(cx, 1, "catT1", "x1", "out", "cd_w_out", final=True)
    return nc


def prep_all(inp, core):
    d = prep_core(inp, core)
    prep_l0(inp, d)
    prep_consts(d)
    prep_s5(inp, d)
    d["rel_bias"] = inp["a_rel_bias"][0]
    d["ab_w_out"] = inp["ab_w_out"][0]
    d["ffn_w_in"] = inp["ffn_w_in"]
    d["ffn_w_out"] = inp["ffn_w_out"]
    d["cd_w_in"] = inp["cd_w_in"][0]
    d["cd_w_out"] = inp["cd_w_out"][0]
    d["ret_norm_g"] = inp["ret_norm_g"][0]
    return {k_: np.ascontiguousarray(np.asarray(d[k_], dtype=np.float32)) for k_, _ in IN_SPECS}


def kernel(**inputs):
    inp = {k_: np.asarray(v_) for k_, v_ in inputs.items()}
    nc = build_program()
    in_maps = [prep_all(inp, core) for core in range(8)]
    res = run_bass_kernel_spmd(nc, in_maps, core_ids=list(range(8)))
    outs = [np.asarray(res.results[i]["out"]).reshape(2, S_LEN, 1024) for i in range(8)]
    return np.concatenate(outs, axis=0).astype(np.float32)
```
